# Optimizing a Trainium2 kernel written in Bass

```python
import jax
import jax.numpy as jnp
from jax import lax
import numpy as np

D_MODEL = 1024
BATCH = 32
SEQ = 2048
DEPTH = 2

CTX_LEN = 256
GRID_W = 64
N_EVEN = (DEPTH + 1) // 2
N_ODD = DEPTH // 2
EPS = 1e-6

ML_HEADS = 4
ML_DQK = 128
ML_DV = 128
ML_CHUNK = 64
ML_QK_W = ML_HEADS * ML_DQK
ML_W = ML_HEADS * ML_DV
ML_GATES = 4 * ML_HEADS

GLA_HEADS = 4
GLA_DK = 64
GLA_DV = 128
GLA_LR = 16
GLA_TAU = 16.0
GLA_CHUNK = 32
GLA_QK_W = GLA_HEADS * GLA_DK
GLA_W = GLA_HEADS * GLA_DV

MIX_W = ML_W + GLA_W
IN_SIZES = (ML_QK_W, ML_QK_W, ML_W, ML_W, ML_GATES, GLA_QK_W, GLA_QK_W, GLA_W, GLA_W, 2 * GLA_LR)
IN_WIDTH = sum(IN_SIZES)

RW_HEAD = 64
RW_HEADS = D_MODEL // RW_HEAD
RW_DECAY_LR = 64
RW_A_LR = 64
RW_G_LR = 128
RW_LN_EPS = 64e-5

D_FF = 2816

kernel_name = "hybrid_mlstm_gla_rwkv7_convffn_dit"


def split_points(sizes):
    pts, acc = [], 0
    for s in sizes[:-1]:
        acc += s
        pts.append(acc)
    return pts


def rmsnorm(x, g):
    xf = x.astype(jnp.float32)
    y = xf * lax.rsqrt(jnp.mean(xf * xf, axis=-1, keepdims=True) + EPS)
    return (y * g.astype(jnp.float32)).astype(x.dtype)


def head_rmsnorm(y, g, n_heads):
    bn, L, W = y.shape
    d = W // n_heads
    return rmsnorm(y.reshape(bn, L, n_heads, d), g.reshape(n_heads, d)).reshape(bn, L, W)


def head_layernorm(y, w, b, n_heads, out_dtype):
    bn, L, W = y.shape
    yf = y.astype(jnp.float32).reshape(bn, L, n_heads, W // n_heads)
    mu = jnp.mean(yf, axis=-1, keepdims=True)
    var = jnp.mean(jnp.square(yf - mu), axis=-1, keepdims=True)
    yn = ((yf - mu) * lax.rsqrt(var + RW_LN_EPS)).reshape(bn, L, W)
    return (yn * w.astype(jnp.float32) + b.astype(jnp.float32)).astype(out_dtype)


def modulate(h, shift, scale):
    return h * (1 + scale) + shift


def to_heads(t, n_heads):
    bn, L, W = t.shape
    return t.reshape(bn, L, n_heads, W // n_heads).transpose(0, 2, 1, 3)


def from_heads(t):
    bn, H, L, d = t.shape
    return t.transpose(0, 2, 1, 3).reshape(bn, L, H * d)


def to_chunks(t, chunk):
    bn, H, L = t.shape[:3]
    t = t.reshape((bn, H, L // chunk, chunk) + t.shape[3:])
    return jnp.moveaxis(t, 2, 0)


def from_chunks(t):
    t = jnp.moveaxis(t, 0, 2)
    return t.reshape(t.shape[:2] + (-1,) + t.shape[4:])


def short_conv(x, w, b):
    xp = jnp.pad(x, ((0, 0), (1, 1), (0, 0)))
    return xp[:, :-2] * w[0] + xp[:, 1:-1] * w[1] + xp[:, 2:] * w[2] + b


def dwconv_grid(h, w, b, grid):
    rows, width = grid
    bn, L, F = h.shape
    img = h.reshape(bn, rows, width, F)
    out = lax.conv_general_dilated(img, w[:, :, None, :].astype(h.dtype), window_strides=(1, 1), padding="SAME",
                                   dimension_numbers=("NHWC", "HWIO", "NHWC"), feature_group_count=F)
    return out.reshape(bn, L, F) + b


def grid_shift(x, grid):
    rows, width = grid
    bn, L, D = x.shape
    q = D // 4
    p = jnp.pad(x.reshape(bn, rows, width, D), ((0, 0), (1, 1), (1, 1), (0, 0)))
    out = jnp.concatenate([p[:, 1:-1, :-2, :q], p[:, 1:-1, 2:, q:2 * q],
                           p[:, :-2, 1:-1, 2 * q:3 * q], p[:, 2:, 1:-1, 3 * q:]], axis=-1)
    return out.reshape(bn, L, D)


def seq_shift(x):
    D = x.shape[-1]
    q = D // 4
    p = jnp.pad(x, ((0, 0), (1, 1), (0, 0)))
    prv, nxt = p[:, :-2], p[:, 2:]
    return jnp.concatenate([prv[..., :q], nxt[..., q:2 * q], prv[..., 2 * q:3 * q], nxt[..., 3 * q:]], axis=-1)


def bidir_scan(scan_fn, init, ctx_fwd, ctx_bwd, lat_fwd, lat_bwd):
    flip = lambda args: tuple(jnp.flip(a, axis=2) for a in args)
    y_cf, s_f = scan_fn(*ctx_fwd, init)
    y_cb, s_b = scan_fn(*flip(ctx_bwd), init)
    y_lf, _ = scan_fn(*lat_fwd, s_f)
    y_lb, _ = scan_fn(*flip(lat_bwd), s_b)
    return y_cf + jnp.flip(y_cb, axis=2), y_lf + jnp.flip(y_lb, axis=2)


def mlstm_scan(q, k, v, logi, logf, state):
    q, k, v, logi, logf = (t.astype(jnp.float32) for t in (q, k, v, logi, logf))
    mask = jnp.tril(jnp.ones((ML_CHUNK, ML_CHUNK), dtype=bool))

    def step(carry, blk):
        C, n, m = carry
        qc, kc, vc, ic, fc = blk
        b = jnp.cumsum(fc, axis=-1)
        inter = b + m[..., None]
        d = jnp.where(mask, b[..., :, None] - b[..., None, :] + ic[..., None, :], -jnp.inf)
        mt = jnp.maximum(inter, jnp.max(d, axis=-1))
        s = jnp.einsum("bhtd,bhsd->bhts", qc, kc) * jnp.exp(d - mt[..., None])
        wi = jnp.exp(inter - mt)
        num = jnp.einsum("bhts,bhsv->bhtv", s, vc) + wi[..., None] * jnp.einsum("bhtd,bhdv->bhtv", qc, C)
        den = jnp.sum(s, axis=-1) + wi * jnp.einsum("bhtd,bhd->bht", qc, n)
        h = num / jnp.maximum(jnp.abs(den), jnp.exp(-mt))[..., None]
        bl = b[..., -1]
        g = bl[..., None] - b + ic
        m_new = jnp.maximum(bl + m, jnp.max(g, axis=-1))
        ws = jnp.exp(g - m_new[..., None])
        wp = jnp.exp(bl + m - m_new)
        C = wp[..., None, None] * C + jnp.einsum("bhs,bhsd,bhsv->bhdv", ws, kc, vc)
        n = wp[..., None] * n + jnp.einsum("bhs,bhsd->bhd", ws, kc)
        return (C, n, m_new), h

    state, hs = lax.scan(step, state, tuple(to_chunks(t, ML_CHUNK) for t in (q, k, v, logi, logf)))
    return from_chunks(hs), state


def gla_scan(q, k, v, la, S):
    q, k, v, la = (t.astype(jnp.float32) for t in (q, k, v, la))
    mask = jnp.tril(jnp.ones((GLA_CHUNK, GLA_CHUNK), dtype=bool))[:, :, None]

    def step(S, blk):
        qc, kc, vc, ac = blk
        b = jnp.cumsum(ac, axis=2)
        decay = jnp.exp(jnp.where(mask, b[:, :, :, None, :] - b[:, :, None, :, :], -jnp.inf))
        A = jnp.einsum("bhtd,bhsd,bhtsd->bhts", qc, kc, decay)
        o = jnp.einsum("bhts,bhsv->bhtv", A, vc) + jnp.einsum("bhtd,bhdv->bhtv", qc * jnp.exp(b), S)
        bl = b[:, :, -1:, :]
        S = jnp.exp(bl[:, :, 0, :])[..., None] * S + jnp.einsum("bhsd,bhsv->bhdv", kc * jnp.exp(bl - b), vc)
        return S, o

    S, os_ = lax.scan(step, S, tuple(to_chunks(t, GLA_CHUNK) for t in (q, k, v, la)))
    return from_chunks(os_), S


def rwkv7_scan(r, w, k, v, kk, a, S):
    xs = tuple(jnp.moveaxis(t.astype(jnp.float32), 2, 0) for t in (r, w, k, v, kk, a))

    def step(S, inp):
        rt, wt, kt, vt, kkt, at = inp
        sa = jnp.einsum("bhvk,bhk->bhv", S, -kkt)
        S = S * wt[:, :, None, :] + sa[..., None] * (kkt * at)[:, :, None, :] + vt[..., None] * kt[:, :, None, :]
        return S, jnp.einsum("bhvk,bhk->bhv", S, rt)

    S, ys = lax.scan(step, S, xs)
    return jnp.moveaxis(ys, 0, 2), S


def even_project(h, w_in, b_gates, conv_w, conv_b, gla_w2, gla_b):
    bn, L, _ = h.shape
    z = h @ w_in
    mq, mk, mv, mo, mg, gq, gk, gv, gg, glr = jnp.split(z, split_points(IN_SIZES), axis=-1)
    qk = jax.nn.silu(short_conv(jnp.concatenate([mq, mk], axis=-1), conv_w, conv_b))
    mq, mk = jnp.split(qk, 2, axis=-1)
    mg = (mg + b_gates).astype(jnp.float32).reshape(bn, L, 4, ML_HEADS).transpose(2, 0, 3, 1)
    glr = glr.reshape(bn, L, 2, GLA_LR)
    la = jax.nn.log_sigmoid((jnp.einsum("blzr,zrc->zblc", glr, gla_w2) + gla_b[:, None, None, :]).astype(jnp.float32)) / GLA_TAU
    return {
        "mq": to_heads(mq * ML_DQK ** -0.5, ML_HEADS), "mk": to_heads(mk, ML_HEADS), "mv": to_heads(mv, ML_HEADS),
        "i_f": mg[0], "f_f": jax.nn.log_sigmoid(mg[1]), "i_b": mg[2], "f_b": jax.nn.log_sigmoid(mg[3]),
        "gq": to_heads(gq * GLA_DK ** -0.5, GLA_HEADS), "gk": to_heads(gk, GLA_HEADS), "gv": to_heads(gv, GLA_HEADS),
        "la_f": to_heads(la[0], GLA_HEADS), "la_b": to_heads(la[1], GLA_HEADS),
        "mo": mo, "gg": gg,
    }


def even_out(ml_y, gla_y, p, head_g, w_out):
    dt = p["mo"].dtype
    m = head_rmsnorm(from_heads(ml_y).astype(dt), head_g[:ML_W], ML_HEADS) * jax.nn.sigmoid(p["mo"])
    g = head_rmsnorm(from_heads(gla_y).astype(dt), head_g[ML_W:], GLA_HEADS) * jax.nn.silu(p["gg"])
    return jnp.concatenate([m, g], axis=-1) @ w_out


def even_mixer(hc, hl, w_in, b_gates, conv_w, conv_b, gla_w2, gla_b, head_g, w_out, need_ctx):
    pc = even_project(hc, w_in, b_gates, conv_w, conv_b, gla_w2, gla_b)
    pl = even_project(hl, w_in, b_gates, conv_w, conv_b, gla_w2, gla_b)
    bn = hl.shape[0]
    f32 = jnp.float32
    ml_init = (jnp.zeros((bn, ML_HEADS, ML_DQK, ML_DV), f32), jnp.zeros((bn, ML_HEADS, ML_DQK), f32),
               jnp.zeros((bn, ML_HEADS), f32))
    gla_init = jnp.zeros((bn, GLA_HEADS, GLA_DK, GLA_DV), f32)
    ml_args = lambda p: ((p["mq"], p["mk"], p["mv"], p["i_f"], p["f_f"]), (p["mq"], p["mk"], p["mv"], p["i_b"], p["f_b"]))
    gla_args = lambda p: ((p["gq"], p["gk"], p["gv"], p["la_f"]), (p["gq"], p["gk"], p["gv"], p["la_b"]))
    ml_c, ml_l = bidir_scan(mlstm_scan, ml_init, *ml_args(pc), *ml_args(pl))
    gla_c, gla_l = bidir_scan(gla_scan, gla_init, *gla_args(pc), *gla_args(pl))
    y_lat = even_out(ml_l, gla_l, pl, head_g, w_out)
    y_ctx = even_out(ml_c, gla_c, pc, head_g, w_out) if need_ctx else None
    return y_ctx, y_lat


def rwkv_project(h, shifted, mu, w_rkv, w0, w1, w2, a0, a1, a2, g1, g2, kvec):
    xx = shifted - h
    xr, xw, xk, xv, xa, xg = (h + xx * mu[i] for i in range(6))
    r = xr @ w_rkv[0]
    k = xk @ w_rkv[1]
    v = xv @ w_rkv[2]
    lw = jnp.einsum("zblr,zrd->zbld", jnp.tanh(jnp.einsum("bld,zdr->zblr", xw, w1)), w2) + w0[:, None, None, :]
    decay = jnp.exp(-jnp.exp(-jax.nn.softplus(-lw.astype(jnp.float32)) - 0.5))
    a = jax.nn.sigmoid(a0 + (xa @ a1) @ a2)
    g = jax.nn.sigmoid(xg @ g1) @ g2
    kk = to_heads(k * kvec[0], RW_HEADS).astype(jnp.float32)
    kk = kk / jnp.maximum(jnp.sqrt(jnp.sum(kk * kk, axis=-1, keepdims=True)), 1e-12)
    k = k * (1 + (a - 1) * kvec[1])
    return {"r": r, "k": k, "v": v, "a": a, "g": g, "decay": decay, "kk": kk}


def rwkv_out(y, p, kvec, lnx, w_o):
    r, k, v, g = p["r"], p["k"], p["v"], p["g"]
    bn, L, D = r.shape
    y = head_layernorm(from_heads(y), lnx[0], lnx[1], RW_HEADS, r.dtype)
    bonus = jnp.sum((r * k * kvec[2]).reshape(bn, L, RW_HEADS, RW_HEAD), axis=-1, keepdims=True) * v.reshape(bn, L, RW_HEADS, RW_HEAD)
    return ((y + bonus.reshape(bn, L, D)) * g) @ w_o


def rwkv_mixer(hc, hl, lat_grid, mu, w_rkv, w_o, w0, w1, w2, a0, a1, a2, g1, g2, kvec, lnx, need_ctx):
    pc = rwkv_project(hc, seq_shift(hc), mu, w_rkv, w0, w1, w2, a0, a1, a2, g1, g2, kvec)
    pl = rwkv_project(hl, grid_shift(hl, lat_grid), mu, w_rkv, w0, w1, w2, a0, a1, a2, g1, g2, kvec)
    init = jnp.zeros((hl.shape[0], RW_HEADS, RW_HEAD, RW_HEAD), jnp.float32)

    def scan_args(p):
        r, k, v, a = (to_heads(p[n], RW_HEADS) for n in ("r", "k", "v", "a"))
        return ((r, to_heads(p["decay"][0], RW_HEADS), k, v, p["kk"], a),
                (r, to_heads(p["decay"][1], RW_HEADS), k, v, p["kk"], a))

    y_c, y_l = bidir_scan(rwkv7_scan, init, *scan_args(pc), *scan_args(pl))
    out_l = rwkv_out(y_l, pl, kvec, lnx, w_o)
    out_c = rwkv_out(y_c, pc, kvec, lnx, w_o) if need_ctx else None
    return out_c, out_l


def conv_ffn(h, w_up, conv_w, conv_b, w_down, grid):
    gate, up = jnp.split(h @ w_up, 2, axis=-1)
    gate = dwconv_grid(gate, conv_w, conv_b, grid)
    return (jax.nn.gelu(gate, approximate=True) * up) @ w_down


def setup_inputs(seed: int = 0) -> dict:
    key = jax.random.key(seed)
    keys = iter(jax.random.split(key, 48))

    def nrm(shape, scale):
        return jax.random.normal(next(keys), shape, jnp.float32) * scale

    def unif(shape, lo, hi):
        return jax.random.uniform(next(keys), shape, jnp.float32, lo, hi)

    D = D_MODEL
    gate_i = nrm((N_EVEN, 2, 1, ML_HEADS), 0.1)
    gate_f = jnp.linspace(3.0, 6.0, ML_HEADS, dtype=jnp.float32) + nrm((N_EVEN, 2, 1, ML_HEADS), 0.1)
    return {
        "x": nrm((BATCH, SEQ, D), 1.0),
        "c": nrm((BATCH, D), 1.0),
        "ctx": nrm((BATCH, CTX_LEN, D), 1.0),
        "c_ctx": nrm((D,), 1.0),
        "w_mod": nrm((DEPTH, D, 6 * D), 0.5 * D ** -0.5),
        "b_mod": nrm((DEPTH, 6 * D), 0.02),
        "norm_g": 1.0 + nrm((DEPTH, 4, D), 0.02),
        "ffn_w_up": nrm((DEPTH, D, 2 * D_FF), D ** -0.5),
        "ffn_conv_w": nrm((DEPTH, 3, 3, D_FF), 1.0 / 3.0),
        "ffn_conv_b": nrm((DEPTH, D_FF), 0.02),
        "ffn_w_down": nrm((DEPTH, D_FF, D), D_FF ** -0.5),
        "ev_w_in": nrm((N_EVEN, D, IN_WIDTH), D ** -0.5),
        "ev_b_gates": jnp.concatenate([gate_i, gate_f], axis=2).reshape(N_EVEN, ML_GATES),
        "ev_conv_w": nrm((N_EVEN, 3, 2 * ML_QK_W), 3 ** -0.5),
        "ev_conv_b": nrm((N_EVEN, 2 * ML_QK_W), 0.02),
        "ev_gla_w2": nrm((N_EVEN, 2, GLA_LR, GLA_QK_W), GLA_LR ** -0.5),
        "ev_gla_b": nrm((N_EVEN, 2, GLA_QK_W), 0.1),
        "ev_head_g": 1.0 + nrm((N_EVEN, MIX_W), 0.02),
        "ev_w_out": nrm((N_EVEN, MIX_W, D), MIX_W ** -0.5),
        "rw_mu": unif((N_ODD, 6, D), 0.0, 1.0),
        "rw_w_rkv": nrm((N_ODD, 3, D, D), D ** -0.5),
        "rw_w_o": nrm((N_ODD, D, D), D ** -0.5),
        "rw_w0": unif((N_ODD, 2, D), -6.0, 1.0),
        "rw_w1": nrm((N_ODD, 2, D, RW_DECAY_LR), D ** -0.5),
        "rw_w2": nrm((N_ODD, 2, RW_DECAY_LR, D), 0.1 * RW_DECAY_LR ** -0.5),
        "rw_a0": nrm((N_ODD, D), 0.1),
        "rw_a1": nrm((N_ODD, D, RW_A_LR), D ** -0.5),
        "rw_a2": nrm((N_ODD, RW_A_LR, D), 0.1 * RW_A_LR ** -0.5),
        "rw_g1": nrm((N_ODD, D, RW_G_LR), D ** -0.5),
        "rw_g2": nrm((N_ODD, RW_G_LR, D), RW_G_LR ** -0.5),
        "rw_kvec": jnp.stack([0.85 + nrm((N_ODD, D), 0.02), 1.0 + nrm((N_ODD, D), 0.02), nrm((N_ODD, D), 0.1)], axis=1),
        "rw_lnx": jnp.stack([1.0 + nrm((N_ODD, D), 0.02), nrm((N_ODD, D), 0.02)], axis=1),
    }


def reference(x, c, ctx, c_ctx, w_mod, b_mod, norm_g, ffn_w_up, ffn_conv_w, ffn_conv_b, ffn_w_down,
              ev_w_in, ev_b_gates, ev_conv_w, ev_conv_b, ev_gla_w2, ev_gla_b, ev_head_g, ev_w_out,
              rw_mu, rw_w_rkv, rw_w_o, rw_w0, rw_w1, rw_w2, rw_a0, rw_a1, rw_a2, rw_g1, rw_g2, rw_kvec, rw_lnx):
    rows = x.shape[1] // GRID_W
    lat_grid = (rows, GRID_W)
    ctx_grid = (1, ctx.shape[1])
    xl, xc = x, ctx
    for layer in range(DEPTH):
        last = layer == DEPTH - 1
        j = layer // 2
        mod_l = jnp.split((jax.nn.silu(c) @ w_mod[layer] + b_mod[layer])[:, None, :], 6, axis=-1)
        mod_c = jnp.split((jax.nn.silu(c_ctx) @ w_mod[layer] + b_mod[layer])[None, None, :], 6, axis=-1)
        hl = modulate(rmsnorm(xl, norm_g[layer, 0]), mod_l[0], mod_l[1])
        hc = modulate(rmsnorm(xc, norm_g[layer, 0]), mod_c[0], mod_c[1])
        if layer % 2 == 0:
            yc, yl = even_mixer(hc, hl, ev_w_in[j], ev_b_gates[j], ev_conv_w[j], ev_conv_b[j], ev_gla_w2[j],
                                ev_gla_b[j], ev_head_g[j], ev_w_out[j], not last)
        else:
            yc, yl = rwkv_mixer(hc, hl, lat_grid, rw_mu[j], rw_w_rkv[j], rw_w_o[j], rw_w0[j], rw_w1[j], rw_w2[j],
                                rw_a0[j], rw_a1[j], rw_a2[j], rw_g1[j], rw_g2[j], rw_kvec[j], rw_lnx[j], not last)
        xl = xl + mod_l[2] * rmsnorm(yl, norm_g[layer, 1])
        hl = modulate(rmsnorm(xl, norm_g[layer, 2]), mod_l[3], mod_l[4])
        xl = xl + mod_l[5] * rmsnorm(conv_ffn(hl, ffn_w_up[layer], ffn_conv_w[layer], ffn_conv_b[layer],
                                              ffn_w_down[layer], lat_grid), norm_g[layer, 3])
        if not last:
            xc = xc + mod_c[2] * rmsnorm(yc, norm_g[layer, 1])
            hc = modulate(rmsnorm(xc, norm_g[layer, 2]), mod_c[3], mod_c[4])
            xc = xc + mod_c[5] * rmsnorm(conv_ffn(hc, ffn_w_up[layer], ffn_conv_w[layer], ffn_conv_b[layer],
                                                  ffn_w_down[layer], ctx_grid), norm_g[layer, 3])
    return xl
```

```python
import numpy as np
import concourse.bass as bass
import concourse.mybir as mybir
from concourse.bass_utils import run_bass_kernel_spmd

F32 = mybir.dt.float32
BF16 = mybir.dt.bfloat16
AF = mybir.ActivationFunctionType
ALU = mybir.AluOpType
AX = mybir.AxisListType
ENG = ('sp', 'act', 'dve', 'pool', 'pe')
DSZ = {F32: 4, BF16: 2}
SAME_SYNC = True
NDMASEM = 8


class Buf:
    __slots__ = ('name', 'w', 'r')

    def __init__(self, name):
        self.name = name
        self.w = None
        self.r = {}


class V:
    __slots__ = ('ap', 'buf')

    def __init__(self, ap, buf):
        self.ap = ap
        self.buf = buf

    def __getitem__(self, idx):
        return V(self.ap[idx], self.buf)

    def rr(self, pat, **kw):
        return V(self.ap.rearrange(pat, **kw), self.buf)

    def bc(self, dt):
        return V(self.ap.bitcast(dt), self.buf)

    def on(self, buf):
        return V(self.ap, buf)

    def bcast(self, shape):
        return V(self.ap.broadcast_to(list(shape)), self.buf)

    def pbcast(self, n):
        return V(self.ap.partition_broadcast(n), self.buf)

    @property
    def shape(self):
        return tuple(self.ap.shape)


class Prog:
    def __init__(self):
        nc = self.nc = bass.Bass("TRN2", target_bir_lowering=False)
        self.q = {e: [] for e in ENG}
        self.cnt = {e: 0 for e in ENG}
        self.sem = {e: nc.alloc_semaphore('sem_' + e) for e in ENG}
        self.seen = {e: {} for e in ENG}
        self.dsem = [nc.alloc_semaphore('dsem%d' % i) for i in range(NDMASEM)]
        self.dval = [0] * NDMASEM
        self.dnext = 0
        self.sb_off = 16512
        self.sb_max = 0
        self.nalloc = 0
        self.ninst = 0

    def dram(self, name, shape, dt, kind="Internal"):
        t = self.nc.dram_tensor(name, list(shape), dt, kind=kind)
        return V(t.ap(), Buf(name))

    def sb(self, name, shape, dt, nbuf=None):
        per = int(np.prod(shape[1:])) * DSZ[dt]
        per = (per + 63) // 64 * 64
        off = self.sb_off
        self.sb_off += per
        self.sb_max = max(self.sb_max, self.sb_off)
        assert self.sb_off <= 229344, (name, self.sb_off)
        self.nalloc += 1
        t = self.nc.alloc_sbuf_tensor_at("%s_%d" % (name, self.nalloc), list(shape), dt, offset=off)
        return V(t.ap(), Buf(name))

    def mark(self):
        return self.sb_off

    def release(self, m):
        self.sb_off = m

    def psum_banks(self):
        banks = []
        for i in range(8):
            t = self.nc.alloc_psum_tensor("psb%d" % i, [128, 512], F32)
            banks.append(V(t.ap(), Buf("psb%d" % i)))
        return banks

    def _emit(self, eng, fn, reads, writes, dma=False):
        waits = {}

        def need(tok, waw_pe=False):
            if tok is None:
                return
            sem, val, te = tok
            if te == eng and not dma:
                if eng == 'pe' or not SAME_SYNC:
                    return
            k = id(sem)
            if k not in waits or waits[k][1] < val:
                waits[k] = (sem, val)

        for b in reads:
            need(b.w)
        for b in writes:
            need(b.w)
            for t in b.r.values():
                need(t)
        if dma:
            k = self.dnext
            self.dnext = (self.dnext + 1) % NDMASEM
            if self.dval[k] > 0:
                need((self.dsem[k], self.dval[k], 'dma'))
            self.dval[k] += 16
            tok = (self.dsem[k], self.dval[k], 'dma')
            inc = (self.dsem[k], 16)
        else:
            self.cnt[eng] += 1
            tok = (self.sem[eng], self.cnt[eng], eng)
            inc = (self.sem[eng], 1)
        seen = self.seen[eng]
        final = []
        for k, (sem, val) in waits.items():
            if seen.get(k, 0) < val:
                seen[k] = val
                final.append((sem, val))
        for b in reads:
            b.r[id(tok[0])] = tok
        for b in writes:
            b.w = tok
            b.r = {}
        self.q[eng].append((final, fn, inc))
        self.ninst += 1

    def barrier(self):
        toks = [(self.sem[e], self.cnt[e]) for e in ENG if self.cnt[e] > 0]
        toks += [(self.dsem[k], self.dval[k]) for k in range(NDMASEM) if self.dval[k] > 0]
        for e in ENG:
            final = []
            for sem, val in toks:
                if sem is self.sem[e]:
                    continue
                if self.seen[e].get(id(sem), 0) < val:
                    self.seen[e][id(sem)] = val
                    final.append((sem, val))
            if final:
                self.q[e].append((final, None, None))

    def finish(self):
        self.barrier()
        nc = self.nc
        q = self.q

        def replay(e, lst):
            for waits, fn, inc in lst:
                for sem, val in waits:
                    e.wait_ge(sem, val)
                if fn is not None:
                    ins = fn(e)
                    ins.then_inc(inc[0], inc[1])

        with nc.Block() as block:
            @block.sync
            def _(e):
                replay(e, q['sp'])

            @block.scalar
            def _(e):
                replay(e, q['act'])

            @block.vector
            def _(e):
                replay(e, q['dve'])

            @block.gpsimd
            def _(e):
                replay(e, q['pool'])

            @block.tensor
            def _(e):
                replay(e, q['pe'])
        return nc

    @staticmethod
    def _a(x):
        return x.ap if isinstance(x, V) else x

    @staticmethod
    def _bufs(*xs):
        out = []
        for x in xs:
            if isinstance(x, V) and x.buf not in out:
                out.append(x.buf)
        return out

    def dma(self, out, in_, eng='sp', **kw):
        o, i = out.ap, in_.ap
        self._emit(eng, lambda e: e.dma_start(out=o, in_=i, **kw), self._bufs(in_), self._bufs(out), dma=True)

    def mm(self, out, lhsT, rhs, start=True, stop=True):
        o, l, r = out.ap, lhsT.ap, rhs.ap
        self._emit('pe', lambda e: e.matmul(o, lhsT=l, rhs=r, start=start, stop=stop),
                   self._bufs(lhsT, rhs), self._bufs(out))

    def tr(self, out, in_, ident):
        o, i, d = out.ap, in_.ap, ident.ap
        self._emit('pe', lambda e: e.transpose(o, i, d), self._bufs(in_, ident), self._bufs(out))

    def act(self, out, in_, func, bias=None, scale=None, accum=None):
        kw = {}
        if bias is not None:
            kw['bias'] = self._a(bias)
        if scale is not None:
            kw['scale'] = self._a(scale)
        if accum is not None:
            kw['accum_out'] = accum.ap
        o, i = out.ap, in_.ap
        self._emit('act', lambda e: e.activation(out=o, in_=i, func=func, **kw),
                   self._bufs(in_, bias, scale), self._bufs(out, accum))

    def tt(self, out, in0, in1, op, eng='dve'):
        o, a, b = out.ap, in0.ap, in1.ap
        self._emit(eng, lambda e: e.tensor_tensor(out=o, in0=a, in1=b, op=op),
                   self._bufs(in0, in1), self._bufs(out))

    def ts(self, out, in0, s1, op0, s2=None, op1=None, eng='dve', accum=None):
        o, a = out.ap, in0.ap
        a1, a2 = self._a(s1), self._a(s2)
        kw = {}
        if op1 is not None:
            kw['op1'] = op1
        if accum is not None:
            kw['accum_out'] = accum.ap
        self._emit(eng, lambda e: e.tensor_scalar(out=o, in0=a, scalar1=a1, scalar2=a2, op0=op0, **kw),
                   self._bufs(in0, s1, s2), self._bufs(out, accum))

    def stt(self, out, in0, scalar, in1, op0, op1, eng='dve'):
        o, a, b = out.ap, in0.ap, in1.ap
        s = self._a(scalar)
        self._emit(eng, lambda e: e.scalar_tensor_tensor(out=o, in0=a, scalar=s, in1=b, op0=op0, op1=op1),
                   self._bufs(in0, scalar, in1), self._bufs(out))

    def copy(self, out, in_, eng='dve'):
        o, i = out.ap, in_.ap
        if eng == 'act':
            self._emit('act', lambda e: e.activation(out=o, in_=i, func=AF.Copy), self._bufs(in_), self._bufs(out))
        else:
            self._emit(eng, lambda e: e.tensor_copy(out=o, in_=i), self._bufs(in_), self._bufs(out))

    def memset(self, out, val, eng='dve'):
        o = out.ap
        self._emit(eng, lambda e: e.memset(o, val), [], self._bufs(out))

    def reduce(self, out, in_, op, axis=AX.X, eng='dve'):
        o, i = out.ap, in_.ap
        self._emit(eng, lambda e: e.tensor_reduce(out=o, in_=i, axis=axis, op=op), self._bufs(in_), self._bufs(out))

    def recip(self, out, in_):
        o, i = out.ap, in_.ap
        self._emit('dve', lambda e: e.reciprocal(out=o, in_=i), self._bufs(in_), self._bufs(out))

    def aselect(self, out, in_, pattern, cmp, fill, base, cm):
        o, i = out.ap, in_.ap
        self._emit('pool', lambda e: e.affine_select(out=o, in_=i, pattern=pattern, compare_op=cmp, fill=fill,
                                                     base=base, channel_multiplier=cm),
                   self._bufs(in_), self._bufs(out))

    def scan(self, out, d0, d1, init, op0, op1):
        o, a, b = out.ap, d0.ap, d1.ap
        self._emit('dve', lambda e: e.tensor_tensor_scan(out=o, data0=a, data1=b, initial=init, op0=op0, op1=op1),
                   self._bufs(d0, d1), self._bufs(out))


D = 1024
NT = 18
LTOK = 2304
EPS = 1e-6
DFF = 2816
NJ = 22


class Packer:
    def __init__(self):
        self.cols = {}
        self.n = 0
        self.arrs = []

    def add(self, name, arr):
        arr = np.ascontiguousarray(arr, dtype=np.float32).reshape(128, -1)
        self.cols[name] = (self.n, arr.shape[1])
        self.n += arr.shape[1]
        self.arrs.append(arr)

    def pack(self):
        return np.ascontiguousarray(np.concatenate(self.arrs, axis=1))


def fm(v):
    v = np.asarray(v, dtype=np.float32)
    lead = v.shape[:-1]
    c = v.shape[-1] // 128
    v = v.reshape(lead + (c, 128))
    return np.moveaxis(v, -1, 0).reshape(128, -1)


def host_small(inp, layer_cols_only=False):
    pk = Packer()
    for l in range(2):
        pk.add('bmod%d' % l, fm(inp['b_mod'][l]))
        pk.add('ng%d' % l, fm(inp['norm_g'][l]))
        pk.add('cw%d' % l, np.moveaxis(inp['ffn_conv_w'][l].reshape(9, NJ, 128), 2, 0).transpose(0, 2, 1).reshape(128, NJ * 9))
        pk.add('cb%d' % l, fm(inp['ffn_conv_b'][l]))
    return pk


class K:
    def __init__(self, NB, small_cols, nsmall, dbg=False):
        self.NB = NB
        self.dbg = dbg
        p = self.p = Prog()
        self.sc = small_cols
        self.X0 = p.dram("xcat", [NB, LTOK, D], F32, kind="ExternalInput")
        self.ccT_d = p.dram("ccT", [128, 8 * 6], F32, kind="ExternalInput")
        self.small_d = p.dram("small", [128, nsmall], F32, kind="ExternalInput")
        self.rows_d = p.dram("rows", [16, D], F32, kind="ExternalInput")
        self.w_mod = p.dram("w_mod", [2, D, 6 * D], F32, kind="ExternalInput")
        self.w_up = p.dram("ffn_w_up", [2, D, 2 * DFF], F32, kind="ExternalInput")
        self.w_down = p.dram("ffn_w_down", [2, DFF, D], F32, kind="ExternalInput")
        self.Xs = [self.X0]
        for i in range(1, 4):
            self.Xs.append(p.dram("xs%d" % i, [NB, LTOK, D], F32, kind="ExternalOutput" if dbg else "Internal"))
        self.out = p.dram("out", [NB, 2048, D], F32, kind="ExternalOutput")
        self.xbufs = [[[Buf("x%d_%d_%d" % (i, b, t)) for t in range(NT)] for b in range(NB)] for i in range(5)]
        self.banks = p.psum_banks()
        self.nbank = 0
        self.small = p.sb("small", [128, nsmall], F32)
        p.dma(self.small, self.small_d)
        self.identf = p.sb("identf", [128, 128], F32)
        p.memset(self.identf, 1.0, eng='pool')
        p.aselect(self.identf, self.identf, [[-1, 128]], ALU.is_equal, 0.0, 0, 1)
        self.identb = p.sb("identb", [128, 128], BF16)
        p.copy(self.identb, self.identf, eng='pool')
        self.scT = p.sb("scT", [128, 48], F32)
        p.dma(self.scT, self.ccT_d)
        p.act(self.scT, self.scT, AF.Silu)
        self.modT = p.sb("modT", [128, 48 * 6], F32)
        self.A1 = p.sb("A1", [128, 48], F32)
        self.A2 = p.sb("A2", [128, 48], F32)
        self.base_mark = p.mark()

    def bank(self):
        b = self.banks[self.nbank]
        self.nbank = (self.nbank + 1) % 8
        return b

    def col(self, name, a=None, b=None):
        o, w = self.sc[name]
        if a is None:
            return self.small[:, o:o + w]
        return self.small[:, o + a:o + b]

    def mod_stage(self, l):
        p = self.p
        m = p.mark()
        wst = [p.sb("wmod_st%d" % i, [128, 8, 512], F32) for i in range(2)]
        ps = self.bank()
        for blk in range(12):
            w = wst[blk % 2]
            p.dma(w, self.w_mod[l, :, blk * 512:(blk + 1) * 512].rr("(c p) n -> p c n", p=128))
            for f in range(4):
                fc = blk * 4 + f
                for kc in range(8):
                    p.mm(ps[:, fc * 6:(fc + 1) * 6], w[:, kc, f * 128:(f + 1) * 128], self.scT[:, kc * 6:(kc + 1) * 6],
                         start=(kc == 0), stop=(kc == 7))
        p.tt(self.modT.rr("p (c r) -> p c r", r=6), ps[:, 0:288].rr("p (c r) -> p c r", r=6),
             self.col('bmod%d' % l).rr("p (c o) -> p c o", o=1).bcast([128, 48, 6]), ALU.add)
        mv = self.modT.rr("p (i c r) -> p i c r", i=6, c=8)
        ng = self.col('ng%d' % l).rr("p (i c o) -> p i c o", i=4, o=1)
        for A, mi, gi in ((self.A1, 1, 0), (self.A2, 4, 2)):
            Av = A.rr("p (c r) -> p c r", r=6)
            p.ts(Av, mv[:, mi], 1.0, ALU.add)
            p.tt(Av, Av, ng[:, gi].bcast([128, 8, 6]), ALU.mult)
        p.barrier()
        p.release(m)

    def modvec(self, l, which):
        mv = self.modT.rr("p (i c r) -> p i c r", i=6, c=8)
        if which == 0:
            return self.A1.rr("p (c r) -> p c r", r=6), mv[:, 0]
        return self.A2.rr("p (c r) -> p c r", r=6), mv[:, 3]

    def gate_row(self, dst, l, gi, row, tmp):
        p = self.p
        mv = self.modT.rr("p (i c r) -> p i c r", i=6, c=8)
        nrow = l * 4 + (1 if gi == 2 else 3)
        p.dma(dst, self.rows_d[nrow:nrow + 1, :].pbcast(128))
        for half in range(2):
            ps = self.bank()
            for cc in range(4):
                c = half * 4 + cc
                p.copy(tmp, mv[:, gi, c, row:row + 1].bcast([128, 128]))
                p.mm(ps[:, cc * 128:(cc + 1) * 128], tmp, self.identf)
            p.tt(dst[:, half * 512:(half + 1) * 512], dst[:, half * 512:(half + 1) * 512], ps, ALU.mult)

    def prenorm(self, b, xi, l, which, hT, tiles=range(NT)):
        p = self.p
        A, Bv = self.modvec(l, which)
        m = p.mark()
        xt = [p.sb("pn_x%d" % i, [128, D], F32) for i in range(2)]
        junk = p.sb("pn_junk", [128, D], BF16)
        xn = [p.sb("pn_xn%d" % i, [128, D], BF16) for i in range(2)]
        ss = [p.sb("pn_ss%d" % i, [128, 1], F32) for i in range(2)]
        tmp = [p.sb("pn_tmp%d" % i, [128, 8, 128], F32) for i in range(2)]
        for n, tt in enumerate(tiles):
            x = xt[n % 2]
            s = ss[n % 2]
            p.dma(x, self.Xs[xi][b, tt * 128:(tt + 1) * 128, :].on(self.xbufs[xi][b][tt]))
            p.act(junk, x, AF.Square, accum=s)
            p.act(s, s, AF.Sqrt, bias=EPS, scale=1.0 / D)
            p.recip(s, s)
            p.act(xn[n % 2], x, AF.Identity, scale=s)
            ps = self.bank()
            psb = ps.bc(BF16)
            for c in range(8):
                p.tr(psb[:, c * 128:(c + 1) * 128], xn[n % 2][:, c * 128:(c + 1) * 128], self.identb)
            row = 4 if tt < 2 else b
            p.tt(tmp[n % 2], psb.rr("p (c t) -> p c t", c=8), A[:, :, row:row + 1].bcast([128, 8, 128]), ALU.mult)
            p.tt(hT[:, :, tt * 128:(tt + 1) * 128], tmp[n % 2], Bv[:, :, row:row + 1].bcast([128, 8, 128]), ALU.add,
                 eng='pool')
        p.barrier()
        p.release(m)

    def residual_tile(self, b, tt, xi, xo, ybanks, G, st):
        p = self.p
        x, junk, ss2, s, tmp = st
        p.dma(x, self.Xs[xi][b, tt * 128:(tt + 1) * 128, :].on(self.xbufs[xi][b][tt]))
        for h in range(2):
            p.act(junk, ybanks[h], AF.Square, accum=ss2[:, h:h + 1])
        p.tt(s, ss2[:, 0:1], ss2[:, 1:2], ALU.add)
        p.act(s, s, AF.Sqrt, bias=EPS, scale=1.0 / D)
        p.recip(s, s)
        for h in range(2):
            p.stt(tmp[:, h * 512:(h + 1) * 512], ybanks[h], s, G[:, h * 512:(h + 1) * 512], ALU.mult, ALU.mult)
        p.tt(tmp, tmp, x, ALU.add, eng='pool')
        if xo == 4:
            dst = self.out[b, (tt - 2) * 128:(tt - 1) * 128, :]
        else:
            dst = self.Xs[xo][b, tt * 128:(tt + 1) * 128, :]
        p.dma(dst.on(self.xbufs[xo][b][tt]), tmp)

    def res_state(self, n=2):
        p = self.p
        return [(p.sb("rs_x%d" % i, [128, D], F32), p.sb("rs_junk%d" % i, [128, 512], BF16),
                 p.sb("rs_ss2%d" % i, [128, 2], F32), p.sb("rs_s%d" % i, [128, 1], F32),
                 p.sb("rs_tmp%d" % i, [128, D], F32)) for i in range(n)]

    def ffn(self, b, l, xi, xo, do_ctx=True):
        p = self.p
        m = p.mark()
        hT = p.sb("ffn_hT", [128, 8, LTOK], BF16)
        self.prenorm(b, xi, l, 1, hT, tiles=range(NT) if do_ctx else range(2, NT))
        wdown = p.sb("ffn_wdown", [128, NJ, D], BF16)
        for j in range(NJ):
            p.dma(wdown[:, j, :], self.w_down[l, j * 128:(j + 1) * 128, :], eng='pool')
        actT = p.sb("ffn_actT", [128, NJ, 1280], BF16)
        wg = [p.sb("ffn_wg%d" % i, [128, 8, 128], BF16) for i in range(2)]
        wu = [p.sb("ffn_wu%d" % i, [128, 8, 128], BF16) for i in range(2)]
        gpad = [p.sb("ffn_gpad%d" % i, [128, 18, 66], BF16) for i in range(2)]
        cpad = p.sb("ffn_cpad", [128, 258], BF16)
        gg = [p.sb("ffn_gg%d" % i, [128, 512], BF16) for i in range(2)]
        diag = [p.sb("ffn_diag%d" % i, [128, 9, 128], BF16) for i in range(2)]
        G = [p.sb("ffn_G%d" % i, [128, D], F32) for i in range(2)]
        gtmp = p.sb("ffn_gtmp", [128, 128], F32)
        st = self.res_state(2)
        for g in gpad:
            p.memset(g, 0.0, eng='pool')
        p.memset(cpad, 0.0, eng='pool')
        self.gate_row(G[0], l, 5, b, gtmp)
        if do_ctx:
            self.gate_row(G[1], l, 5, 4, gtmp)
        cw = self.col('cw%d' % l)
        cb = self.col('cb%d' % l)
        nn = 0
        for seg in range(2):
            r0 = 16 * seg
            g0 = 0 if seg == 0 else 15
            prow0 = 1 if seg == 0 else 0
            gp = gpad[seg]
            with_ctx = (seg == 0 and do_ctx)
            for j in range(NJ):
                wgj, wuj = wg[j % 2], wu[j % 2]
                p.dma(wgj, self.w_up[l, :, j * 128:(j + 1) * 128].rr("(c p) n -> p c n", p=128), eng='pool')
                p.dma(wuj, self.w_up[l, :, DFF + j * 128:DFF + (j + 1) * 128].rr("(c p) n -> p c n", p=128), eng='pool')
                dg = diag[j % 2]
                for t in range(9):
                    p.ts(dg[:, t, :], self.identb, cw[:, j * 9 + t:j * 9 + t + 1], ALU.mult, eng='pool')
                tok0 = 256 + g0 * 64
                for (o, n) in ((0, 512), (512, 512), (1024, 64)):
                    ps = self.bank()
                    for kc in range(8):
                        p.mm(ps[:, 0:n], wgj[:, kc, :], hT[:, kc, tok0 + o:tok0 + o + n], start=(kc == 0), stop=(kc == 7))
                    pr = prow0 + o // 64
                    p.copy(gp[:, pr:pr + n // 64, 1:65], ps[:, 0:n].rr("p (r c) -> p r c", c=64), eng='act')
                for blk in range(2):
                    ps = self.bank()
                    for t in range(9):
                        dy, dx = t // 3, t % 3
                        p.mm(ps, dg[:, t, :], gp[:, 8 * blk + dy:8 * blk + dy + 8, dx:dx + 64], start=(t == 0), stop=(t == 8))
                    g_ = gg[nn % 2]
                    nn += 1
                    p.act(g_, ps, AF.Gelu_apprx_tanh, bias=cb[:, j:j + 1])
                    ps2 = self.bank()
                    t0 = 256 + r0 * 64 + blk * 512
                    for kc in range(8):
                        p.mm(ps2, wuj[:, kc, :], hT[:, kc, t0:t0 + 512], start=(kc == 0), stop=(kc == 7))
                    p.tt(actT[:, j, 256 + blk * 512:256 + (blk + 1) * 512], ps2, g_, ALU.mult)
                if with_ctx:
                    ps = self.bank()
                    for kc in range(8):
                        p.mm(ps[:, 0:256], wgj[:, kc, :], hT[:, kc, 0:256], start=(kc == 0), stop=(kc == 7))
                    p.copy(cpad[:, 1:257], ps[:, 0:256], eng='act')
                    ps = self.bank()
                    for dx in range(3):
                        p.mm(ps[:, 0:256], dg[:, 3 + dx, :], cpad[:, dx:dx + 256], start=(dx == 0), stop=(dx == 2))
                    g_ = gg[nn % 2]
                    nn += 1
                    p.act(g_[:, 0:256], ps[:, 0:256], AF.Gelu_apprx_tanh, bias=cb[:, j:j + 1])
                    ps2 = self.bank()
                    for kc in range(8):
                        p.mm(ps2[:, 0:256], wuj[:, kc, :], hT[:, kc, 0:256], start=(kc == 0), stop=(kc == 7))
                    p.tt(actT[:, j, 0:256], ps2[:, 0:256], g_[:, 0:256], ALU.mult)
            tiles = ([0, 1] if with_ctx else []) + [2 + 8 * seg + i for i in range(8)]
            for n, tt in enumerate(tiles):
                a0 = tt * 128 if tt < 2 else 256 + (tt - 2 - 8 * seg) * 128
                yb = [self.bank(), self.bank()]
                for h in range(2):
                    for j in range(NJ):
                        p.mm(yb[h], actT[:, j, a0:a0 + 128], wdown[:, j, h * 512:(h + 1) * 512], start=(j == 0), stop=(j == NJ - 1))
                self.residual_tile(b, tt, xi, xo, yb, G[1] if tt < 2 else G[0], st[n % 2])
        p.barrier()
        p.release(m)


LNS_ML = float(np.log(128.0 ** -0.5))
LNS_GLA = float(np.log(64.0 ** -0.5))
ORD = [list(range(NT)), [1, 0] + list(range(17, 1, -1))]
C_MQ, C_MK, C_MV, C_MO, C_MG, C_GQ, C_GK, C_GV, C_GG, C_GLR = 0, 512, 1024, 1536, 2048, 2064, 2320, 2576, 3088, 3600


def host_small_even(pk, inp):
    pk.add('ecw', fm(inp['ev_conv_w'][0]))
    pk.add('ecb', fm(inp['ev_conv_b'][0]))
    pk.add('glab', fm(inp['ev_gla_b'][0]))
    pk.add('hg', fm(inp['ev_head_g'][0]))


class KE(K):
    def setup_even(self):
        p = self.p
        self.w_in = p.dram("ev_w_in", [D, 3632], F32, kind="ExternalInput")
        self.w_out = p.dram("ev_w_out", [D, D], F32, kind="ExternalInput")
        self.gla_w2 = p.dram("ev_gla_w2", [2, 16, 256], F32, kind="ExternalInput")
        self.ones = p.sb("ones", [128, 128], F32)
        p.memset(self.ones, 1.0, eng='pool')
        self.tri = []
        for z in range(2):
            t = p.sb("tri%d" % z, [128, 128], F32)
            pat, cm = ([[1, 128]], -1) if z == 0 else ([[-1, 128]], 1)
            p.aselect(t, self.ones, pat, ALU.is_ge, 0.0, 0, cm)
            self.tri.append(t)
        self.common_mark = p.mark()
        self.mask4 = []
        self.maskb = []
        for z in range(2):
            t = self.tri[z]
            m4 = p.sb("mask4%d" % z, [128, 4, 128], F32)
            p.ts(m4, t.rr("p (o t) -> p o t", o=1).bcast([128, 4, 128]), -1.0, ALU.add, 30000.0, ALU.mult, eng='pool')
            self.mask4.append(m4)
            mb = p.sb("maskb%d" % z, [128, 128], BF16)
            p.copy(mb, t, eng='pool')
            self.maskb.append(mb)
        self.bgrow = p.sb("bgrow", [128, 16], F32)
        p.dma(self.bgrow, self.rows_d[8:9, 0:16].pbcast(128))
        self.base_mark = p.mark()

    def headnorm_gate(self, y, ybufs, hd, gate_col, gate_fn, hT, mixT, scratch):
        p = self.p
        sq, ss, yn, sg, wgt = scratch
        p.dma(wgt, self.w_in[:, gate_col:gate_col + 128].rr("(c p) n -> p c n", p=128), eng='pool')
        for blk in range(5):
            t0, n = blk * 512, (512 if blk < 4 else 256)
            ps = self.bank()
            for kc in range(8):
                p.mm(ps[:, 0:n], wgt[:, kc, :], hT[:, kc, t0:t0 + n], start=(kc == 0), stop=(kc == 7))
            p.act(sg[:, t0:t0 + n], ps[:, 0:n], gate_fn)
        yall = V(y.ap, Buf('yall'))
        for tt in range(NT):
            p.tt(sq[:, tt, :], y[:, tt, :].on(ybufs[tt]), y[:, tt, :].on(ybufs[tt]), ALU.mult)
        p.reduce(ss, sq, ALU.add)
        p.act(ss, ss, AF.Sqrt, bias=EPS, scale=1.0 / 128)
        p.recip(ss, ss)
        for tt in range(NT):
            p.ts(yn[:, tt, :], y[:, tt, :].on(ybufs[tt]), ss[:, tt:tt + 1], ALU.mult)
        hg = self.col('hg')
        for g0 in (0, 8, 16):
            n = min(8, NT - g0)
            ps = self.bank()
            psb = ps.bc(BF16)
            for i in range(n):
                p.tr(psb[:, i * 128:(i + 1) * 128], yn[:, g0 + i, :], self.identb)
            p.stt(mixT[:, hd, g0 * 128:(g0 + n) * 128], psb[:, 0:n * 128], hg[:, hd:hd + 1], sg[:, g0 * 128:(g0 + n) * 128],
                  ALU.mult, ALU.mult)

    def even_mixer(self, b, xi, xo):
        p = self.p
        l = 0
        m = p.mark()
        hT = p.sb("ev_hT", [128, 8, LTOK], BF16)
        self.prenorm(b, xi, l, 0, hT)
        mixT = p.sb("ev_mixT", [128, 8, LTOK], BF16)
        wgate = p.sb("ev_wgate", [128, 8, 16], BF16)
        p.dma(wgate, self.w_in[:, C_MG:C_MG + 16].rr("(c p) n -> p c n", p=128), eng='pool')
        graw = p.sb("ev_graw", [128, NT, 16], F32)
        ps = self.bank()
        for tt in range(NT):
            for kc in range(8):
                p.mm(ps[:, tt * 16:(tt + 1) * 16], hT[:, kc, tt * 128:(tt + 1) * 128], wgate[:, kc, :], start=(kc == 0), stop=(kc == 7))
        p.tt(graw, ps[:, 0:NT * 16].rr("p (t n) -> p t n", n=16), self.bgrow.rr("p (o n) -> p o n", o=1).bcast([128, NT, 16]), ALU.add)
        g5 = graw.rr("p t (z y h) -> p t z y h", z=2, y=2)
        I8 = g5[:, :, :, 0, :]
        F8 = g5[:, :, :, 1, :]
        def s8(name):
            return p.sb(name, [128, NT, 2, 4], F32)
        lf8, Fc8, Ft8, bs8, qs8, wk8, dc8 = [s8("ev_" + n) for n in ("lf8", "Fc8", "Ft8", "bs8", "qs8", "wk8", "dc8")]
        p.act(lf8, F8, AF.Exp, scale=-1.0)
        p.act(lf8, lf8, AF.Ln, bias=1.0)
        p.ts(lf8, lf8, -1.0, ALU.mult)
        ps = self.bank()
        for z in range(2):
            p.mm(ps[:, z * 72:(z + 1) * 72], self.tri[z], lf8[:, :, z, :])
        p.mm(ps[:, 144:288], self.ones, lf8)
        for z in range(2):
            p.copy(Fc8[:, :, z, :], ps[:, z * 72:(z + 1) * 72].rr("p (t h) -> p t h", h=4))
        p.copy(Ft8, ps[:, 144:288].rr("p (t z h) -> p t z h", z=2, h=4))
        p.tt(bs8, I8, Fc8, ALU.subtract)
        p.tt(wk8, Ft8, bs8, ALU.add)
        p.act(wk8, wk8, AF.Exp)
        p.ts(bs8, bs8, LNS_ML, ALU.add)
        p.act(qs8, Fc8, AF.Exp, bias=LNS_ML)
        p.act(dc8, Ft8, AF.Exp)
        Fc5 = Fc8.rr("p t z (h o) -> p t z h o", o=1)

        cw = self.col('ecw').rr("p (t c) -> p t c", t=3)
        cbias = self.col('ecb')
        for hp in range(2):
            m2 = p.mark()
            heads = [2 * hp, 2 * hp + 1]
            qT = [p.sb("ml_qT%d" % i, [128, LTOK], BF16) for i in range(2)]
            kT = [p.sb("ml_kT%d" % i, [128, LTOK], BF16) for i in range(2)]
            vv = [p.sb("ml_v%d" % i, [128, NT, 130], BF16) for i in range(2)]
            yy = [p.sb("ml_y%d" % i, [128, NT, 128], F32) for i in range(2)]
            ybufs = [[Buf("mly%d_%d" % (i, t)) for t in range(NT)] for i in range(2)]
            m3 = p.mark()
            xpad = p.sb("ml_xpad", [128, 2308], F32)
            t1 = p.sb("ml_t1", [128, 2306], F32)
            t2 = p.sb("ml_t2", [128, 2306], F32)
            wq = [p.sb("ml_wq%d" % i, [128, 8, 128], BF16) for i in range(2)]
            p.memset(xpad, 0.0, eng='pool')
            nw = 0
            for i, h in enumerate(heads):
                for dst, c0, cc in ((qT[i], C_MQ + h * 128, h), (kT[i], C_MK + h * 128, 4 + h)):
                    w = wq[nw % 2]
                    nw += 1
                    p.dma(w, self.w_in[:, c0:c0 + 128].rr("(c p) n -> p c n", p=128), eng='pool')
                    for blk in range(5):
                        t0, n = (0, 256) if blk == 0 else (256 + (blk - 1) * 512, 512)
                        pos = 1 if blk == 0 else 259 + (blk - 1) * 512
                        ps = self.bank()
                        for kc in range(8):
                            p.mm(ps[:, 0:n], w[:, kc, :], hT[:, kc, t0:t0 + n], start=(kc == 0), stop=(kc == 7))
                        p.copy(xpad[:, pos:pos + n], ps[:, 0:n], eng='act')
                    p.ts(t1, xpad[:, 0:2306], cw[:, 0, cc:cc + 1], ALU.mult, cbias[:, cc:cc + 1], ALU.add)
                    p.stt(t2, xpad[:, 1:2307], cw[:, 1, cc:cc + 1], t1, ALU.mult, ALU.add)
                    p.stt(t1, xpad[:, 2:2308], cw[:, 2, cc:cc + 1], t2, ALU.mult, ALU.add)
                    p.act(dst[:, 0:256], t1[:, 0:256], AF.Silu)
                    p.act(dst[:, 256:LTOK], t1[:, 258:2306], AF.Silu)
                w = wq[nw % 2]
                nw += 1
                p.dma(w, self.w_in[:, C_MV + h * 128:C_MV + (h + 1) * 128].rr("(c p) n -> p c n", p=128), eng='pool')
                p.memset(vv[i][:, :, 128:130], 1.0, eng='pool')
                for g0 in range(0, NT, 4):
                    n = min(4, NT - g0)
                    ps = self.bank()
                    for j in range(n):
                        tt = g0 + j
                        for kc in range(8):
                            p.mm(ps[:, j * 128:(j + 1) * 128], hT[:, kc, tt * 128:(tt + 1) * 128], w[:, kc, :], start=(kc == 0), stop=(kc == 7))
                    p.copy(vv[i][:, g0:g0 + n, 0:128], ps[:, 0:n * 128].rr("p (t n) -> p t n", n=128), eng='act')
                p.memset(yy[i], 0.0, eng='pool')
            p.barrier()
            p.release(m3)
            Cs = [[p.sb("ml_C%d%d" % (z, i), [128, 132], F32) for i in range(2)] for z in range(2)]
            Cb = [[p.sb("ml_Cb%d%d" % (z, i), [128, 132], BF16) for i in range(2)] for z in range(2)]
            for z in range(2):
                for i in range(2):
                    p.memset(Cs[z][i], 0.0, eng='pool')
                    p.memset(Cb[z][i], 0.0, eng='pool')
            diag4 = [p.sb("ml_diag4%d" % i, [128, 4, 128], F32) for i in range(2)]
            Dm = [p.sb("ml_Dm%d" % i, [128, 128], BF16) for i in range(4)]
            sT = [p.sb("ml_sT%d" % i, [128, 128], BF16) for i in range(4)]
            tmpo = [p.sb("ml_tmpo%d" % i, [128, 132], F32) for i in range(4)]
            num = [p.sb("ml_num%d" % i, [128, 132], F32) for i in range(4)]
            den = [p.sb("ml_den%d" % i, [128, 1], F32) for i in range(4)]
            ktil = [p.sb("ml_ktil%d" % i, [128, 128], BF16) for i in range(4)]
            identf4 = self.identf.rr("p (o t) -> p o t", o=1).bcast([128, 4, 128])
            n4 = 0
            for step in range(NT):
                for z in range(2):
                    tt = ORD[z][step]
                    ts_ = slice(tt * 128, (tt + 1) * 128)
                    dg = diag4[(step * 2 + z) % 2]
                    p.tt(dg, identf4, Fc5[:, tt, z].bcast([128, 4, 128]), ALU.mult)
                    rb = self.bank()
                    p.mm(rb, self.ones, dg, start=True, stop=False)
                    p.mm(rb, self.identf, self.mask4[z], start=False, stop=True)
                    for i, h in enumerate(heads):
                        k4 = n4 % 4
                        n4 += 1
                        sc = self.bank()
                        p.mm(sc[:, 0:128], kT[i][:, ts_], qT[i][:, ts_])
                        p.act(Dm[k4], rb[:, h * 128:(h + 1) * 128], AF.Exp, bias=bs8[:, tt, z, h:h + 1])
                        p.tt(sT[k4], sc[:, 0:128], Dm[k4], ALU.mult)
                        o = self.bank()
                        p.mm(o[:, 0:129], sT[k4], vv[i][:, tt, 0:129])
                        p.mm(o[:, 256:385], qT[i][:, ts_], Cb[z][i][:, 0:129])
                        p.act(tmpo[k4][:, 0:129], o[:, 256:385], AF.Identity, scale=qs8[:, tt, z, h:h + 1])
                        p.tt(num[k4][:, 0:129], tmpo[k4][:, 0:129], o[:, 0:129], ALU.add)
                        p.act(den[k4], num[k4][:, 128:129], AF.Abs)
                        p.ts(den[k4], den[k4], 1.0, ALU.max)
                        p.recip(den[k4], den[k4])
                        yv = yy[i][:, tt, :].on(ybufs[i][tt])
                        p.stt(yv, num[k4][:, 0:128], den[k4], yv, ALU.mult, ALU.add)
                        if step < NT - 1:
                            kp = self.bank()
                            kpb = kp.bc(BF16)
                            p.tr(kpb[:, 0:128], kT[i][:, ts_], self.identb)
                            p.act(ktil[k4], kpb[:, 0:128], AF.Identity, scale=wk8[:, tt, z, h:h + 1])
                            cu = self.bank()
                            p.mm(cu[:, 0:129], ktil[k4], vv[i][:, tt, 0:129])
                            p.stt(Cs[z][i][:, 0:129], Cs[z][i][:, 0:129], dc8[:, tt, z, h:h + 1], cu[:, 0:129], ALU.mult, ALU.add)
                            p.copy(Cb[z][i][:, 0:129], Cs[z][i][:, 0:129], eng='pool')
            m4_ = p.mark()
            scratch = (p.sb("hn_sq", [128, NT, 128], F32), p.sb("hn_ss", [128, NT], F32), p.sb("hn_yn", [128, NT, 128], BF16),
                       p.sb("hn_sg", [128, LTOK], BF16), p.sb("hn_wgt", [128, 8, 128], BF16))
            for i, h in enumerate(heads):
                self.headnorm_gate(yy[i], ybufs[i], h, C_MO + h * 128, AF.Sigmoid, hT, mixT, scratch)
            p.barrier()
            p.release(m2)
        self.ev_hT, self.ev_mixT, self.ev_mark = hT, mixT, m
        return hT, mixT, m

    def even_out(self, b, xi, xo, hT, mixT, m):
        p = self.p
        l = 0
        wout = p.sb("ev_wout", [128, 8, D], BF16)
        for c in range(8):
            p.dma(wout[:, c, :], self.w_out[c * 128:(c + 1) * 128, :], eng='pool')
        G = [p.sb("evo_G%d" % i, [128, D], F32) for i in range(2)]
        gtmp = p.sb("evo_gtmp", [128, 128], F32)
        st = self.res_state(2)
        self.gate_row(G[0], l, 2, b, gtmp)
        self.gate_row(G[1], l, 2, 4, gtmp)
        for tt in range(NT):
            yb = [self.bank(), self.bank()]
            for h in range(2):
                for c in range(8):
                    p.mm(yb[h], mixT[:, c, tt * 128:(tt + 1) * 128], wout[:, c, h * 512:(h + 1) * 512], start=(c == 0), stop=(c == 7))
            self.residual_tile(b, tt, xi, xo, yb, G[1] if tt < 2 else G[0], st[tt % 2])
        p.barrier()
        p.release(m)


def _gla(self, b, hT, mixT):
    p = self.p
    m = p.mark()
    rmask = p.sb("gl_rmask", [128, LTOK], BF16)
    p.memset(rmask, 1.0, eng='pool')
    p.memset(rmask.rr("p (n t) -> p n t", t=128)[:, :, 0:1], 0.0, eng='pool')
    w2b = p.sb("gl_w2b", [16, 2, 256], BF16)
    p.dma(w2b, self.gla_w2.rr("z r c -> r z c"), eng='pool')
    wlr = p.sb("gl_wlr", [128, 8, 32], BF16)
    p.dma(wlr, self.w_in[:, C_GLR:C_GLR + 32].rr("(c p) n -> p c n", p=128), eng='pool')
    glrT = p.sb("gl_glrT", [16, 2, LTOK], BF16)
    for z in range(2):
        for blk in range(5):
            t0, n = blk * 512, (512 if blk < 4 else 256)
            ps = self.bank()
            for kc in range(8):
                p.mm(ps[0:16, 0:n], wlr[:, kc, z * 16:(z + 1) * 16], hT[:, kc, t0:t0 + n], start=(kc == 0), stop=(kc == 7))
            p.copy(glrT[:, z, t0:t0 + n], ps[0:16, 0:n], eng='act')
    nb = p.sb("gl_nb", [128, 4], F32)
    p.ts(nb, self.col('glab'), -1.0, ALU.mult)
    for cp in range(2):
        m2 = p.mark()
        qtil = [p.sb("gl_qtil%d" % z, [128, LTOK], BF16) for z in range(2)]
        khat = [p.sb("gl_khat%d" % z, [128, LTOK], BF16) for z in range(2)]
        ktl = [p.sb("gl_ktl%d" % z, [128, LTOK], BF16) for z in range(2)]
        dec = [p.sb("gl_dec%d" % z, [128, NT], F32) for z in range(2)]
        vv = [p.sb("gl_v%d" % i, [128, NT, 128], BF16) for i in range(2)]
        yy = [p.sb("gl_y%d" % i, [128, NT, 128], F32) for i in range(2)]
        ybufs = [[Buf("gly%d_%d" % (i, t)) for t in range(NT)] for i in range(2)]
        m3 = p.mark()
        qT = p.sb("gl_qT", [128, LTOK], BF16)
        kT = p.sb("gl_kT", [128, LTOK], BF16)
        lb = p.sb("gl_l", [128, LTOK], F32)
        P = p.sb("gl_P", [128, LTOK], F32)
        tmp = p.sb("gl_tmp", [128, LTOK], F32)
        E = p.sb("gl_E", [128, LTOK], BF16)
        w = [p.sb("gl_w%d" % i, [128, 8, 128], BF16) for i in range(2)]
        for dst, c0, wi in ((qT, C_GQ + cp * 128, 0), (kT, C_GK + cp * 128, 1)):
            p.dma(w[wi], self.w_in[:, c0:c0 + 128].rr("(c p) n -> p c n", p=128), eng='pool')
            for blk in range(5):
                t0, n = blk * 512, (512 if blk < 4 else 256)
                ps = self.bank()
                for kc in range(8):
                    p.mm(ps[:, 0:n], w[wi][:, kc, :], hT[:, kc, t0:t0 + n], start=(kc == 0), stop=(kc == 7))
                p.copy(dst[:, t0:t0 + n], ps[:, 0:n], eng='act')
        P3 = P.rr("p (n t) -> p n t", t=128)
        Ptot = P3[:, :, 127:128]
        for z in range(2):
            for blk in range(5):
                t0, n = blk * 512, (512 if blk < 4 else 256)
                ps = self.bank()
                p.mm(ps[:, 0:n], w2b[:, z, cp * 128:(cp + 1) * 128], glrT[:, z, t0:t0 + n])
                p.act(lb[:, t0:t0 + n], ps[:, 0:n], AF.Exp, scale=-1.0, bias=nb[:, z * 2 + cp:z * 2 + cp + 1])
            p.act(lb, lb, AF.Ln, bias=1.0)
            p.scan(P, rmask, lb, 0.0, ALU.mult, ALU.add)
            p.act(dec[z], Ptot.rr("p n o -> p (n o)"), AF.Exp, scale=-1.0 / 16)
            t3 = tmp.rr("p (n t) -> p n t", t=128)
            if z == 0:
                p.act(E, P, AF.Exp, scale=-1.0 / 16, bias=LNS_GLA)
                p.tt(qtil[z], qT, E, ALU.mult)
                p.act(E, P, AF.Exp, scale=1.0 / 16)
                p.tt(khat[z], kT, E, ALU.mult, eng='pool')
                p.tt(t3, P3, Ptot.bcast([128, NT, 128]), ALU.subtract)
                p.act(E, tmp, AF.Exp, scale=1.0 / 16)
                p.tt(ktl[z], kT, E, ALU.mult)
            else:
                p.tt(t3, Ptot.bcast([128, NT, 128]), P3, ALU.subtract)
                p.tt(tmp, tmp, lb, ALU.add, eng='pool')
                p.act(E, tmp, AF.Exp, scale=-1.0 / 16, bias=LNS_GLA)
                p.tt(qtil[z], qT, E, ALU.mult)
                p.act(E, tmp, AF.Exp, scale=1.0 / 16)
                p.tt(khat[z], kT, E, ALU.mult, eng='pool')
                p.tt(tmp, lb, P, ALU.subtract)
                p.act(E, tmp, AF.Exp, scale=1.0 / 16)
                p.tt(ktl[z], kT, E, ALU.mult)
        for i in range(2):
            h = cp * 2 + i
            p.dma(w[i], self.w_in[:, C_GV + h * 128:C_GV + (h + 1) * 128].rr("(c p) n -> p c n", p=128), eng='pool')
            for g0 in range(0, NT, 4):
                n = min(4, NT - g0)
                ps = self.bank()
                for j in range(n):
                    tt = g0 + j
                    for kc in range(8):
                        p.mm(ps[:, j * 128:(j + 1) * 128], hT[:, kc, tt * 128:(tt + 1) * 128], w[i][:, kc, :], start=(kc == 0), stop=(kc == 7))
                p.copy(vv[i][:, g0:g0 + n, :], ps[:, 0:n * 128].rr("p (t n) -> p t n", n=128), eng='act')
            p.memset(yy[i], 0.0, eng='pool')
        p.barrier()
        p.release(m3)
        S = [[p.sb("gl_S%d%d" % (z, i), [128, 128], F32) for i in range(2)] for z in range(2)]
        Sb = [[p.sb("gl_Sb%d%d" % (z, i), [128, 128], BF16) for i in range(2)] for z in range(2)]
        for z in range(2):
            for i in range(2):
                p.memset(S[z][i], 0.0, eng='pool')
                p.memset(Sb[z][i], 0.0, eng='pool')
        AT = [p.sb("gl_AT%d" % i, [128, 128], BF16) for i in range(4)]
        ktok = [p.sb("gl_ktok%d" % i, [128, 64], BF16) for i in range(4)]
        n4 = 0
        for step in range(NT):
            for z in range(2):
                tt = ORD[z][step]
                ts_ = slice(tt * 128, (tt + 1) * 128)
                for i in range(2):
                    pr = slice(i * 64, (i + 1) * 64)
                    k4 = n4 % 4
                    n4 += 1
                    sc = self.bank()
                    p.mm(sc[:, 0:128], khat[z][pr, ts_], qtil[z][pr, ts_])
                    p.tt(AT[k4], sc[:, 0:128], self.maskb[z], ALU.mult)
                    o = self.bank()
                    p.mm(o[:, 0:128], AT[k4], vv[i][:, tt, :], start=True, stop=False)
                    p.mm(o[:, 0:128], qtil[z][pr, ts_], Sb[z][i][pr, :], start=False, stop=True)
                    yv = yy[i][:, tt, :].on(ybufs[i][tt])
                    p.tt(yv, yv, o[:, 0:128], ALU.add)
                    if step < NT - 1:
                        kp = self.bank()
                        kpb = kp.bc(BF16)
                        p.tr(kpb[:, 0:64], ktl[z][pr, ts_], self.identb[pr, pr])
                        p.copy(ktok[k4], kpb[:, 0:64], eng='act')
                        su = self.bank()
                        p.mm(su[pr, 0:128], ktok[k4], vv[i][:, tt, :])
                        p.stt(S[z][i][pr, :], S[z][i][pr, :], dec[z][pr, tt:tt + 1], su[pr, 0:128], ALU.mult, ALU.add)
                        p.copy(Sb[z][i][pr, :], S[z][i][pr, :], eng='pool')
        scratch = (p.sb("hn_sq", [128, NT, 128], F32), p.sb("hn_ss", [128, NT], F32), p.sb("hn_yn", [128, NT, 128], BF16),
                   p.sb("hn_sg", [128, LTOK], BF16), p.sb("hn_wgt", [128, 8, 128], BF16))
        for i in range(2):
            h = cp * 2 + i
            self.headnorm_gate(yy[i], ybufs[i], 4 + h, C_GG + h * 128, AF.Silu, hT, mixT, scratch)
        p.barrier()
        p.release(m2)
    p.barrier()
    p.release(m)


KE.gla = _gla


C0 = float(np.exp(-0.5))
RW_LN_EPS = 64e-5


def host_small_rw(pk, inp):
    pk.add('mu', fm(inp['rw_mu'][0]))
    pk.add('w0', fm(inp['rw_w0'][0]))
    pk.add('a0', fm(inp['rw_a0'][0]))
    pk.add('kv', fm(inp['rw_kvec'][0]))
    pk.add('lnx', fm(inp['rw_lnx'][0]))


def _setup_rw(self):
    p = self.p
    NB = self.NB
    self.w_rkv = p.dram("rw_w_rkv", [3, D, D], F32, kind="ExternalInput")
    self.w_o = p.dram("rw_w_o", [D, D], F32, kind="ExternalInput")
    self.rw_w1 = p.dram("rw_w1", [2, D, 64], F32, kind="ExternalInput")
    self.rw_w2 = p.dram("rw_w2", [2, 64, D], F32, kind="ExternalInput")
    self.rw_a1 = p.dram("rw_a1", [D, 64], F32, kind="ExternalInput")
    self.rw_a2 = p.dram("rw_a2", [64, D], F32, kind="ExternalInput")
    self.rw_g1 = p.dram("rw_g1", [D, 128], F32, kind="ExternalInput")
    self.rw_g2 = p.dram("rw_g2", [128, D], F32, kind="ExternalInput")
    self.scr = []
    if getattr(self, 'dbg', False):
        self.dbg_y = p.dram("dbg_y", [8, 128, NT * 128], F32, kind="ExternalOutput")
    for b in range(NB):
        d = {}
        for nm in ('r', 'k', 'v', 'o'):
            d[nm] = p.dram("scr_%s%d" % (nm, b), [8, 128, LTOK], BF16, kind="ExternalOutput" if getattr(self, 'dbg', False) else "Internal")
            d[nm + 'buf'] = [Buf("scr_%s%d_%d" % (nm, b, c)) for c in range(8)]
        self.scr.append(d)


def _setup_rw_consts(self):
    p = self.p
    p.barrier()
    p.release(self.common_mark)
    self.strict = []
    self.M4 = []
    for z in range(2):
        st = p.sb("strict%d" % z, [128, 128], F32)
        pat, cm = ([[1, 128]], -1) if z == 0 else ([[-1, 128]], 1)
        p.aselect(st, self.ones, pat, ALU.is_gt, 0.0, 0, cm)
        self.strict.append(st)
    for z in range(2):
        m4 = p.sb("M4%d" % z, [128, 4, 128], F32)
        p.ts(m4[:, 0, :], self.strict[z], -1.0, ALU.mult, eng='pool')
        p.ts(m4[:, 1, :], self.tri[z], -1.0, ALU.mult, eng='pool')
        p.copy(m4[:, 2, :], self.strict[z], eng='pool')
        p.copy(m4[:, 3, :], self.tri[z], eng='pool')
        self.M4.append(m4)
    self.nstrictT = []
    for z in range(2):
        t = p.sb("nstrictT%d" % z, [128, 128], F32)
        p.ts(t, self.strict[1 - z], -1.0, ALU.mult, eng='pool')
        self.nstrictT.append(t)
    self.masks3 = p.sb("masks3", [128, 3, 128], BF16)
    bd64 = p.sb("bd64", [128, 128], BF16)
    p.memset(self.masks3[:, 0, :], 0.0, eng='pool')
    p.memset(bd64, 0.0, eng='pool')
    for i in range(4):
        p.memset(self.masks3[32 * i:32 * i + 32, 0, 32 * i:32 * i + 32], 1.0, eng='pool')
    for i in range(2):
        p.memset(bd64[64 * i:64 * i + 64, 64 * i:64 * i + 64], 1.0, eng='pool')
    p.tt(self.masks3[:, 1, :], bd64, self.masks3[:, 0, :], ALU.subtract, eng='pool')
    p.ts(self.masks3[:, 2, :], bd64, -1.0, ALU.mult, 1.0, ALU.add, eng='pool')
    self.bones = p.sb("bones", [128, 128], BF16)
    p.memset(self.bones, 0.0, eng='pool')
    p.memset(self.bones[0:64, 0:64], 1.0, eng='pool')
    p.memset(self.bones[64:128, 64:128], 1.0, eng='pool')
    self.omu = p.sb("omu", [128, 48], F32)
    p.ts(self.omu, self.col('mu'), -1.0, ALU.mult, 1.0, ALU.add)
    self.okv1 = p.sb("okv1", [128, 8], F32)
    p.ts(self.okv1, self.col('kv')[:, 8:16], -1.0, ALU.mult, 1.0, ALU.add)
    self.base_mark = p.mark()


def _rw_mix_block(self, xb, hT, mi, blk):
    p = self.p
    mu = self.col('mu')
    t0, n = (0, 256) if blk == 0 else (256 + (blk - 1) * 512, 512)
    for c in range(8):
        muc = mu[:, mi * 8 + c:mi * 8 + c + 1]
        p.ts(xb[:, c, 0:n], hT[:, c, t0:t0 + n], self.omu[:, mi * 8 + c:mi * 8 + c + 1], ALU.mult, eng='pool')
        kind = c // 2
        if blk == 0:
            if kind in (0, 2):
                p.stt(xb[:, c, 1:256], hT[:, c, 0:255], muc, xb[:, c, 1:256], ALU.mult, ALU.add)
            else:
                p.stt(xb[:, c, 0:255], hT[:, c, 1:256], muc, xb[:, c, 0:255], ALU.mult, ALU.add)
        else:
            r0 = (blk - 1) * 8
            xv = xb[:, c, :].rr("p (r w) -> p r w", w=64)
            hv = hT[:, c, 256:LTOK].rr("p (r w) -> p r w", w=64)
            if kind == 0:
                p.stt(xv[:, :, 1:64], hv[:, r0:r0 + 8, 0:63], muc, xv[:, :, 1:64], ALU.mult, ALU.add)
            elif kind == 1:
                p.stt(xv[:, :, 0:63], hv[:, r0:r0 + 8, 1:64], muc, xv[:, :, 0:63], ALU.mult, ALU.add)
            elif kind == 2:
                lo = 1 if r0 == 0 else 0
                p.stt(xv[:, lo:8, :], hv[:, r0 + lo - 1:r0 + 7, :], muc, xv[:, lo:8, :], ALU.mult, ALU.add)
            else:
                hi = 7 if r0 == 24 else 8
                p.stt(xv[:, 0:hi, :], hv[:, r0 + 1:r0 + hi + 1, :], muc, xv[:, 0:hi, :], ALU.mult, ALU.add)
    return t0, n


def _rw_phase1(self, b, hT, tw, ta, tg):
    p = self.p
    scr = self.scr[b]
    m = p.mark()
    xblk = [p.sb("rw_xb%d" % i, [128, 8, 512], BF16) for i in range(2)]
    W = p.sb("rw_W", [128, 8, D], BF16)
    stage = [p.sb("rw_stage%d" % i, [128, 512], BF16) for i in range(4)]
    ns = 0
    nx = 0
    for mi, kind in ((0, 'r'), (2, 'k'), (3, 'v'), (1, 'w'), (4, 'a'), (5, 'g')):
        if kind in 'rkv':
            idx = 'rkv'.index(kind)
            for c in range(8):
                p.dma(W[:, c, :], self.w_rkv[idx, c * 128:(c + 1) * 128, :], eng='pool')
        elif kind == 'w':
            for z in range(2):
                p.dma(W[:, :, z * 64:(z + 1) * 64], self.rw_w1[z].rr("(c p) r -> p c r", p=128), eng='pool')
        elif kind == 'a':
            p.dma(W[:, :, 0:64], self.rw_a1.rr("(c p) r -> p c r", p=128), eng='pool')
        else:
            p.dma(W[:, :, 0:128], self.rw_g1.rr("(c p) r -> p c r", p=128), eng='pool')
        for blk in range(5):
            xb = xblk[nx % 2]
            nx += 1
            t0, n = self.rw_mix_block(xb, hT, mi, blk)
            if kind in 'rkv':
                for oc in range(8):
                    ps = self.bank()
                    for kc in range(8):
                        p.mm(ps[:, 0:n], W[:, kc, oc * 128:(oc + 1) * 128], xb[:, kc, 0:n], start=(kc == 0), stop=(kc == 7))
                    sg = stage[ns % 4]
                    ns += 1
                    p.copy(sg[:, 0:n], ps[:, 0:n], eng='act')
                    p.dma(scr[kind][oc, :, t0:t0 + n].on(scr[kind + 'buf'][oc]), sg[:, 0:n])
            elif kind == 'w':
                for z in range(2):
                    ps = self.bank()
                    for kc in range(8):
                        p.mm(ps[0:64, 0:n], W[:, kc, z * 64:(z + 1) * 64], xb[:, kc, 0:n], start=(kc == 0), stop=(kc == 7))
                    p.act(tw[z][0:64, t0:t0 + n], ps[0:64, 0:n], AF.Tanh)
            elif kind == 'a':
                ps = self.bank()
                for kc in range(8):
                    p.mm(ps[0:64, 0:n], W[:, kc, 0:64], xb[:, kc, 0:n], start=(kc == 0), stop=(kc == 7))
                p.copy(ta[0:64, t0:t0 + n], ps[0:64, 0:n], eng='act')
            else:
                ps = self.bank()
                for kc in range(8):
                    p.mm(ps[:, 0:n], W[:, kc, 0:128], xb[:, kc, 0:n], start=(kc == 0), stop=(kc == 7))
                p.act(tg[:, t0:t0 + n], ps[:, 0:n], AF.Sigmoid)
    p.barrier()
    p.release(m)


KE.setup_rw = _setup_rw
KE.setup_rw_consts = _setup_rw_consts
KE.rw_mix_block = _rw_mix_block
KE.rw_phase1 = _rw_phase1


BLKS = [(0, 512), (512, 512), (1024, 512), (1536, 512), (2048, 256)]


def _rw_chunk(self, b, cc, tw, ta, tg, last):
    p = self.p
    scr = self.scr[b]
    m = p.mark()
    kv = self.col('kv')
    rT = p.sb("rc_rT", [128, LTOK], BF16)
    kT = p.sb("rc_kT", [128, LTOK], BF16)
    vT = p.sb("rc_vT", [128, LTOK], BF16)
    for t_, nm in ((rT, 'r'), (kT, 'k'), (vT, 'v')):
        p.dma(t_, scr[nm][cc].on(scr[nm + 'buf'][cc]))
    aT = p.sb("rc_aT", [128, LTOK], BF16)
    kap = p.sb("rc_kap", [128, LTOK], BF16)
    bet = p.sb("rc_bet", [128, LTOK], BF16)
    KR = [p.sb("rc_KR%d" % z, [128, NT, 2, 128], BF16) for z in range(2)]
    khat = [p.sb("rc_khat%d" % z, [128, LTOK], BF16) for z in range(2)]
    bhat = [p.sb("rc_bhat%d" % z, [128, LTOK], BF16) for z in range(2)]
    kT4 = [p.sb("rc_kT4%d" % z, [128, LTOK], BF16) for z in range(2)]
    nbT4 = [p.sb("rc_nbT4%d" % z, [128, LTOK], BF16) for z in range(2)]
    gam = [p.sb("rc_gam%d" % z, [128, NT], F32) for z in range(2)]
    Vtok = p.sb("rc_Vtok", [128, NT, 128], BF16)
    y = p.sb("rc_y", [128, NT, 128], F32)
    ybufs = [[Buf("rcy%d_%d" % (t, hh)) for hh in range(2)] for t in range(NT)]
    p.memset(y, 0.0, eng='pool')
    m3 = p.mark()
    sig = p.sb("rc_sig", [128, LTOK], F32)
    P = p.sb("rc_P", [128, LTOK], F32)
    t1 = p.sb("rc_t1", [128, LTOK], F32)
    t2 = p.sb("rc_t2", [128, LTOK], F32)
    E = p.sb("rc_E", [128, LTOK], BF16)
    rmask = p.sb("rc_rmask", [128, LTOK], BF16)
    p.memset(rmask, 1.0, eng='pool')
    p.memset(rmask.rr("p (n t) -> p n t", t=128)[:, :, 0:1], 0.0, eng='pool')
    w2b = p.sb("rc_w2b", [64, 2, 128], BF16)
    p.dma(w2b, self.rw_w2[:, :, cc * 128:(cc + 1) * 128].rr("z r c -> r z c"), eng='pool')
    a2b = p.sb("rc_a2b", [64, 128], BF16)
    p.dma(a2b, self.rw_a2[:, cc * 128:(cc + 1) * 128], eng='pool')
    for (t0, n) in BLKS:
        ps = self.bank()
        p.mm(ps[:, 0:n], a2b, ta[0:64, t0:t0 + n])
        p.act(aT[:, t0:t0 + n], ps[:, 0:n], AF.Sigmoid, bias=self.col('a0')[:, cc:cc + 1])
    p.ts(t1, kT, kv[:, cc:cc + 1], ALU.mult)
    p.tt(E, t1, t1, ALU.mult, eng='pool')
    for (t0, n) in BLKS:
        ps = self.bank()
        p.mm(ps[:, 0:n], self.bones, E[:, t0:t0 + n])
        p.act(t2[:, t0:t0 + n], ps[:, 0:n], AF.Sqrt)
    p.ts(t2, t2, 1e-12, ALU.max)
    p.recip(t2, t2)
    p.tt(kap, t1, t2, ALU.mult)
    p.tt(bet, kap, aT, ALU.mult, eng='pool')
    p.ts(t1, aT, kv[:, 8 + cc:8 + cc + 1], ALU.mult, self.okv1[:, cc:cc + 1], ALU.add)
    p.tt(kT, kT, t1, ALU.mult)
    P3 = P.rr("p (n t) -> p n t", t=128)
    Ptot = P3[:, :, 127:128]
    Pb = Ptot.bcast([128, NT, 128])
    t13 = t1.rr("p (n t) -> p n t", t=128)
    t23 = t2.rr("p (n t) -> p n t", t=128)
    v3 = lambda x: x.rr("p (n t) -> p n t", t=128)
    for z in range(2):
        for (t0, n) in BLKS:
            ps = self.bank()
            p.mm(ps[:, 0:n], w2b[:, z, :], tw[z][0:64, t0:t0 + n])
            p.act(sig[:, t0:t0 + n], ps[:, 0:n], AF.Sigmoid, bias=self.col('w0')[:, z * 8 + cc:z * 8 + cc + 1])
        p.scan(P, rmask, sig, 0.0, ALU.mult, ALU.add)
        p.act(gam[z], Ptot.rr("p n o -> p (n o)"), AF.Exp, scale=-C0)
        if z == 0:
            G = P
            p.tt(t1, P, sig, ALU.subtract)
            p.tt(t23, P3, Pb, ALU.subtract)
        else:
            p.tt(t13, Pb, P3, ALU.subtract)
            p.tt(t2, sig, P, ALU.subtract)
            G = sig
            p.tt(sig, t1, sig, ALU.add, eng='pool')
        p.act(E, G, AF.Exp, scale=-C0)
        p.tt(KR[z][:, :, 1, :], v3(rT), v3(E), ALU.mult)
        p.act(E, t1, AF.Exp, scale=-C0)
        p.tt(KR[z][:, :, 0, :], v3(kap), v3(E), ALU.mult, eng='pool')
        p.act(E, G, AF.Exp, scale=C0)
        p.tt(khat[z], kT, E, ALU.mult)
        p.tt(bhat[z], bet, E, ALU.mult, eng='pool')
        p.act(E, t2, AF.Exp, scale=C0)
        p.tt(kT4[z], kT, E, ALU.mult)
        p.stt(nbT4[z], bet, -1.0, E, ALU.mult, ALU.mult)
    for g0 in range(0, NT, 8):
        n = min(8, NT - g0)
        ps = self.bank()
        psb = ps.bc(BF16)
        for i in range(n):
            p.tr(psb[:, i * 128:(i + 1) * 128], vT[:, (g0 + i) * 128:(g0 + i + 1) * 128], self.identb)
        p.copy(Vtok[:, g0:g0 + n, :], psb[:, 0:n * 128].rr("p (t c) -> p t c", c=128), eng='act')
    p.barrier()
    p.release(m3)
    NR = 4
    A4 = [p.sb("rs_A4%d" % i, [128, 4, 128], BF16) for i in range(NR)]
    SQ = [[p.sb("rs_SQ%d_%d" % (i, j), [128, 2, 128], BF16) for j in range(2)] for i in range(NR)]
    XXr = [p.sb("rs_XX%d" % i, [128, 2, 128], BF16) for i in range(NR)]
    Q0T = [p.sb("rs_Q0T%d" % i, [128, 128], BF16) for i in range(NR)]
    QM = [p.sb("rs_QM%d" % i, [128, 3, 128], BF16) for i in range(NR)]
    QMT = [p.sb("rs_QMT%d" % i, [128, 3, 128], BF16) for i in range(NR)]
    Y1r = [p.sb("rs_Y1%d" % i, [128, 128], BF16) for i in range(NR)]
    KB = [p.sb("rs_KB%d" % i, [128, 2, 128], BF16) for i in range(2)]
    Wb = [p.sb("rs_Wb%d" % i, [128, 64], BF16) for i in range(NR)]
    Ub = [p.sb("rs_Ub%d" % i, [128, 64], BF16) for i in range(NR)]
    H = [[p.sb("rs_H%d%d" % (z, hh), [128, 64], F32) for hh in range(2)] for z in range(2)]
    Hb = [[p.sb("rs_Hb%d%d" % (z, hh), [128, 64], BF16) for hh in range(2)] for z in range(2)]
    for z in range(2):
        for hh in range(2):
            p.memset(H[z][hh], 0.0, eng='pool')
            p.memset(Hb[z][hh], 0.0, eng='pool')
    units = [(step, z, hh) for step in range(NT) for z in range(2) for hh in range(2)]
    prep = {}

    def stage_a(u):
        step, z, hh = units[u]
        tt = ORD[z][step]
        ts_ = slice(tt * 128, (tt + 1) * 128)
        pr = slice(hh * 64, (hh + 1) * 64)
        r = u % NR
        if hh == 0:
            kb = KB[(u // 2) % 2]
            ps = self.bank()
            psb = ps.bc(BF16)
            p.tr(psb[:, 0:128], kT4[z][:, ts_], self.identb)
            p.tr(psb[:, 128:256], nbT4[z][:, ts_], self.identb)
            p.copy(kb, psb[:, 0:256].rr("p (j c) -> p j c", j=2), eng='act')
        else:
            kb = KB[(u // 2) % 2]
        sc = self.bank()
        kr = KR[z][pr, tt].rr("p j t -> p (j t)")
        p.mm(sc[:, 0:256], bhat[z][pr, ts_], kr)
        p.mm(sc[:, 256:512], khat[z][pr, ts_], kr)
        p.tt(A4[r], sc.rr("p (j t) -> p j t", j=4), self.M4[z], ALU.mult)
        q0 = self.bank()
        p.mm(q0[:, 0:128], KR[z][pr, tt, 0, :], bhat[z][pr, ts_])
        p.tt(Q0T[r], q0[:, 0:128], self.nstrictT[z], ALU.mult)
        p.tt(QM[r], A4[r][:, 0:1, :].bcast([128, 3, 128]), self.masks3, ALU.mult, eng='pool')
        p.tt(QMT[r], Q0T[r].rr("p (o t) -> p o t", o=1).bcast([128, 3, 128]), self.masks3, ALU.mult, eng='pool')
        cq, ct = QM[r][:, 0, :], QMT[r][:, 0, :]
        xq, xt = cq, ct
        XX = XXr[r]
        XXf = XX.rr("p j t -> p (j t)")
        for lev in range(4):
            ps = self.bank()
            p.mm(ps[:, 0:128], ct, cq)
            p.mm(ps[:, 128:256], cq, ct)
            nxt = SQ[r][lev % 2]
            p.copy(nxt.rr("p j t -> p (j t)"), ps[:, 0:256], eng='act')
            cq, ct = nxt[:, 0, :], nxt[:, 1, :]
            ps2 = self.bank()
            p.mm(ps2[:, 0:128], ct, xq, start=True, stop=False)
            p.mm(ps2[:, 0:128], ct, self.identb, start=False, stop=True)
            p.mm(ps2[:, 128:256], xq, ct, start=True, stop=False)
            p.mm(ps2[:, 128:256], cq, self.identb, start=False, stop=True)
            if lev == 0:
                p.tt(XX[:, 0, :], xq, ps2[:, 0:128], ALU.add)
                p.tt(XX[:, 1, :], xt, ps2[:, 128:256], ALU.add)
            else:
                p.tt(XXf, XXf, ps2[:, 0:256], ALU.add)
            xq, xt = XX[:, 0, :], XX[:, 1, :]
        for lvl in (1, 2):
            C, CT = QM[r][:, lvl, :], QMT[r][:, lvl, :]
            ps = self.bank()
            p.mm(ps[:, 0:128], CT, xq, start=True, stop=False)
            p.mm(ps[:, 0:128], CT, self.identb, start=False, stop=True)
            p.copy(Y1r[r], ps[:, 0:128], eng='act')
            ps2 = self.bank()
            p.mm(ps2[:, 0:128], xt, Y1r[r], start=True, stop=False)
            p.mm(ps2[:, 0:128], self.identb, Y1r[r], start=False, stop=True)
            if lvl == 1:
                p.mm(ps2[:, 128:256], Y1r[r], xt, start=True, stop=False)
                p.mm(ps2[:, 128:256], Y1r[r], self.identb, start=False, stop=True)
                p.tt(XXf, XXf, ps2[:, 0:256], ALU.add)
            else:
                p.tt(XX[:, 0, :], XX[:, 0, :], ps2[:, 0:128], ALU.add)
        xc = XX[:, 0, :]
        prep[u] = (r, kb, xc)

    def stage_b(u):
        step, z, hh = units[u]
        tt = ORD[z][step]
        ts_ = slice(tt * 128, (tt + 1) * 128)
        pr = slice(hh * 64, (hh + 1) * 64)
        cs = slice(hh * 64, (hh + 1) * 64)
        r, kb, xfin = prep.pop(u)
        a4 = A4[r]
        hb = Hb[z][hh]
        vt = Vtok[:, tt, cs]
        w = self.bank()
        p.mm(w[:, 0:64], KR[z][pr, tt, 0, :], hb[pr, :], start=True, stop=False)
        p.mm(w[:, 0:64], a4[:, 2, :], vt, start=False, stop=True)
        p.copy(Wb[r], w[:, 0:64], eng='act')
        uu = self.bank()
        p.mm(uu[:, 0:64], xfin, Wb[r], start=True, stop=False)
        p.mm(uu[:, 0:64], self.identb, Wb[r], start=False, stop=True)
        p.copy(Ub[r], uu[:, 0:64], eng='act')
        yb = self.bank()
        p.mm(yb[:, 0:64], KR[z][pr, tt, 1, :], hb[pr, :], start=True, stop=False)
        p.mm(yb[:, 0:64], a4[:, 3, :], vt, start=False, stop=False)
        p.mm(yb[:, 0:64], a4[:, 1, :], Ub[r], start=False, stop=True)
        yv = y[:, tt, cs].on(ybufs[tt][hh])
        p.tt(yv, yv, yb[:, 0:64], ALU.add)
        if step < NT - 1:
            hu = self.bank()
            p.mm(hu[pr, 0:64], kb[:, 0, cs], vt, start=True, stop=False)
            p.mm(hu[pr, 0:64], kb[:, 1, cs], Ub[r], start=False, stop=True)
            p.stt(H[z][hh][pr, :], H[z][hh][pr, :], gam[z][pr, tt:tt + 1], hu[pr, 0:64], ALU.mult, ALU.add)
            p.copy(hb[pr, :], H[z][hh][pr, :], eng='pool')

    LOOK = 2
    for u in range(len(units) + LOOK):
        if u < len(units):
            stage_a(u)
        if u >= LOOK:
            stage_b(u - LOOK)
    if getattr(self, 'dbg', False):
        for tt in range(NT):
            for hh in range(2):
                p.tt(y[:, tt, hh * 64:(hh + 1) * 64], y[:, tt, hh * 64:(hh + 1) * 64].on(ybufs[tt][hh]), y[:, tt, hh * 64:(hh + 1) * 64].on(ybufs[tt][hh]), ALU.max)
        p.dma(self.dbg_y[cc], y.rr("p t c -> p (t c)"))
    g2b = p.sb("rp_g2b", [128, 128], BF16)
    p.dma(g2b, self.rw_g2[:, cc * 128:(cc + 1) * 128], eng='pool')
    s1 = p.sb("rp_s1", [128, 36], F32)
    s2 = p.sb("rp_s2", [128, 36], F32)
    sq = p.sb("rp_sq", [128, 36, 64], F32)
    yn = p.sb("rp_yn", [128, NT, 128], BF16)
    lnT = p.sb("rp_lnT", [128, LTOK], F32)
    prod = p.sb("rp_prod", [128, LTOK], BF16)
    y3 = y.rr("p t (h c) -> p (t h) c", c=64)
    for tt in range(NT):
        for hh in range(2):
            yv = y[:, tt, hh * 64:(hh + 1) * 64].on(ybufs[tt][hh])
            p.tt(sq[:, tt * 2 + hh, :], yv, yv, ALU.mult)
    p.reduce(s2, sq, ALU.add)
    p.reduce(s1, y3, ALU.add)
    p.ts(s1, s1, 1.0 / 64, ALU.mult)
    p.tt(sq[:, :, 0], s1, s1, ALU.mult)
    p.stt(s2, s2, 1.0 / 64, sq[:, :, 0], ALU.mult, ALU.subtract)
    p.act(s2, s2, AF.Sqrt, bias=RW_LN_EPS)
    p.recip(s2, s2)
    yn3 = yn.rr("p t (h c) -> p (t h) c", c=64)
    p.tt(sq, y3, s1.rr("p (n o) -> p n o", o=1).bcast([128, 36, 64]), ALU.subtract)
    p.tt(yn3, sq, s2.rr("p (n o) -> p n o", o=1).bcast([128, 36, 64]), ALU.mult)
    lnx = self.col('lnx')
    for g0 in range(0, NT, 8):
        n = min(8, NT - g0)
        ps = self.bank()
        psb = ps.bc(BF16)
        for i in range(n):
            p.tr(psb[:, i * 128:(i + 1) * 128], yn[:, g0 + i, :], self.identb)
        p.ts(lnT[:, g0 * 128:(g0 + n) * 128], psb[:, 0:n * 128], lnx[:, cc:cc + 1], ALU.mult, lnx[:, 8 + cc:8 + cc + 1], ALU.add)
    p.stt(prod, rT, kv[:, 16 + cc:16 + cc + 1], kT, ALU.mult, ALU.mult)
    stage = [p.sb("rp_stage%d" % i, [128, 512], BF16) for i in range(2)]
    tmpb = [p.sb("rp_tmp%d" % i, [128, 512], F32) for i in range(2)]
    for i, (t0, n) in enumerate(BLKS):
        psA = self.bank()
        p.mm(psA[:, 0:n], self.bones, prod[:, t0:t0 + n])
        psG = self.bank()
        p.mm(psG[:, 0:n], g2b, tg[:, t0:t0 + n])
        tb = tmpb[i % 2]
        p.tt(tb[:, 0:n], psA[:, 0:n], vT[:, t0:t0 + n], ALU.mult)
        p.tt(tb[:, 0:n], tb[:, 0:n], lnT[:, t0:t0 + n], ALU.add, eng='pool')
        sg = stage[i % 2]
        p.tt(sg[:, 0:n], psG[:, 0:n], tb[:, 0:n], ALU.mult)
        p.dma(scr['o'][cc, :, t0:t0 + n].on(scr['obuf'][cc]), sg[:, 0:n])
    p.barrier()
    p.release(m)


def _rw_mixer(self, b, xi, xo, last=True):
    p = self.p
    l = 1
    m = p.mark()
    tw = [p.sb("rw_tw%d" % z, [128, LTOK], BF16) for z in range(2)]
    ta = p.sb("rw_ta", [128, LTOK], BF16)
    tg = p.sb("rw_tg", [128, LTOK], BF16)
    mh = p.mark()
    hT = p.sb("rw_hT", [128, 8, LTOK], BF16)
    self.prenorm(b, xi, l, 0, hT)
    self.rw_phase1(b, hT, tw, ta, tg)
    p.barrier()
    p.release(mh)
    for cc in range(8):
        self.rw_chunk(b, cc, tw, ta, tg, last)
    p.barrier()
    p.release(m)
    m = p.mark()
    oT = p.sb("rw_oT", [128, 8, LTOK], BF16)
    for c in range(8):
        p.dma(oT[:, c, :], self.scr[b]['o'][c].on(self.scr[b]['obuf'][c]))
    wo = p.sb("rw_wo", [128, 8, D], BF16)
    for c in range(8):
        p.dma(wo[:, c, :], self.w_o[c * 128:(c + 1) * 128, :], eng='pool')
    G = [p.sb("rwo_G%d" % i, [128, D], F32) for i in range(2)]
    gtmp = p.sb("rwo_gtmp", [128, 128], F32)
    st = self.res_state(2)
    self.gate_row(G[0], l, 2, b, gtmp)
    if not last:
        self.gate_row(G[1], l, 2, 4, gtmp)
    for tt in (range(2, NT) if last else range(NT)):
        yb = [self.bank(), self.bank()]
        for h in range(2):
            for c in range(8):
                p.mm(yb[h], oT[:, c, tt * 128:(tt + 1) * 128], wo[:, c, h * 512:(h + 1) * 512], start=(c == 0), stop=(c == 7))
        self.residual_tile(b, tt, xi, xo, yb, G[1] if tt < 2 else G[0], st[tt % 2])
    p.barrier()
    p.release(m)


KE.rw_chunk = _rw_chunk
KE.rw_mixer = _rw_mixer


_NC_CACHE = {}
W_NAMES = ('w_mod', 'ffn_w_up', 'ffn_w_down')
EV_NAMES = ('ev_w_in', 'ev_w_out', 'ev_gla_w2')
RW_NAMES = ('rw_w_rkv', 'rw_w_o', 'rw_w1', 'rw_w2', 'rw_a1', 'rw_a2', 'rw_g1', 'rw_g2')


def build_program(NB, pk):
    k = KE(NB, pk.cols, pk.n, dbg=False)
    k.setup_even()
    k.setup_rw()
    k.mod_stage(0)
    for b in range(NB):
        hT, mixT, m = k.even_mixer(b, 0, 1)
        k.gla(b, hT, mixT)
        k.even_out(b, 0, 1, hT, mixT, m)
        k.ffn(b, 0, 1, 2, do_ctx=True)
    k.setup_rw_consts()
    k.mod_stage(1)
    for b in range(NB):
        k.rw_mixer(b, 2, 3, last=True)
        k.ffn(b, 1, 3, 4, do_ctx=False)
    return k.p.finish()


def kernel(**inp):
    inp = {k_: np.asarray(v_, dtype=np.float32) for k_, v_ in inp.items()}
    NCORE = 8
    B = inp['x'].shape[0]
    NB = B // NCORE
    pk = host_small(inp)
    host_small_even(pk, inp)
    host_small_rw(pk, inp)
    small = pk.pack()
    rows = np.zeros((16, D), np.float32)
    rows[0:8] = inp['norm_g'].reshape(8, D)
    rows[8, :16] = inp['ev_b_gates'][0]
    nc = build_program(NB, pk)
    shared = {"small": small, "rows": rows}
    for nm in W_NAMES:
        shared[nm] = np.ascontiguousarray(inp[nm])
    for nm in EV_NAMES + RW_NAMES:
        shared[nm] = np.ascontiguousarray(inp[nm][0])
    in_maps = []
    for c in range(NCORE):
        sl = slice(c * NB, (c + 1) * NB)
        cc = np.zeros((6, D), np.float32)
        cc[0:NB] = inp['c'][sl]
        cc[4] = inp['c_ctx']
        ccT = np.ascontiguousarray(cc.reshape(6, 8, 128).transpose(2, 1, 0).reshape(128, 48))
        xcat = np.ascontiguousarray(np.concatenate([inp['ctx'][sl], inp['x'][sl]], axis=1))
        d = dict(shared)
        d["xcat"] = xcat
        d["ccT"] = ccT
        in_maps.append(d)
    res = run_bass_kernel_spmd(nc, in_maps, core_ids=list(range(NCORE)))
    out = np.concatenate([np.asarray(r["out"]) for r in res.results], axis=0)
    return out.astype(np.float32)
```

```python
import numpy as np
import concourse.bass as bass
import concourse.mybir as mybir
from concourse.bass_utils import run_bass_kernel_spmd

F32 = mybir.dt.float32
BF16 = mybir.dt.bfloat16
AF = mybir.ActivationFunctionType
ALU = mybir.AluOpType
AX = mybir.AxisListType
ENG = ('sp', 'act', 'dve', 'pool', 'pe')
DSZ = {F32: 4, BF16: 2}
SAME_SYNC = True
NDMASEM = 8


class Buf:
    __slots__ = ('name', 'w', 'r', 'excl')

    def __init__(self, name, excl=False):
        self.name = name
        self.w = None
        self.r = {}
        self.excl = excl


class V:
    __slots__ = ('ap', 'buf')

    def __init__(self, ap, buf):
        self.ap = ap
        self.buf = buf

    def __getitem__(self, idx):
        return V(self.ap[idx], self.buf)

    def rr(self, pat, **kw):
        return V(self.ap.rearrange(pat, **kw), self.buf)

    def bc(self, dt):
        return V(self.ap.bitcast(dt), self.buf)

    def on(self, buf):
        return V(self.ap, buf)

    def bcast(self, shape):
        return V(self.ap.broadcast_to(list(shape)), self.buf)

    def pbcast(self, n):
        return V(self.ap.partition_broadcast(n), self.buf)

    @property
    def shape(self):
        return tuple(self.ap.shape)


class Prog:
    def __init__(self):
        nc = self.nc = bass.Bass("TRN2", target_bir_lowering=False)
        self.q = {e: [] for e in ENG}
        self.cnt = {e: 0 for e in ENG}
        self.sem = {e: nc.alloc_semaphore('sem_' + e) for e in ENG}
        self.seen = {e: {} for e in ENG}
        self.dsem = [nc.alloc_semaphore('dsem%d' % i) for i in range(NDMASEM)]
        self.dval = [0] * NDMASEM
        self.dnext = 0
        self.sb_off = 16512
        self.sb_max = 0
        self.nalloc = 0
        self.ninst = 0

    def dram(self, name, shape, dt, kind="Internal"):
        t = self.nc.dram_tensor(name, list(shape), dt, kind=kind)
        return V(t.ap(), Buf(name))

    def sb(self, name, shape, dt, nbuf=None):
        per = int(np.prod(shape[1:])) * DSZ[dt]
        per = (per + 63) // 64 * 64
        off = self.sb_off
        self.sb_off += per
        self.sb_max = max(self.sb_max, self.sb_off)
        assert self.sb_off <= 229344, (name, self.sb_off)
        self.nalloc += 1
        t = self.nc.alloc_sbuf_tensor_at("%s_%d" % (name, self.nalloc), list(shape), dt, offset=off)
        return V(t.ap(), Buf(name))

    def mark(self):
        return self.sb_off

    def release(self, m):
        self.sb_off = m

    def psum_banks(self):
        banks = []
        self.pslot_bufs = []
        self.pslot_free = [True] * 32
        for i in range(8):
            t = self.nc.alloc_psum_tensor("psb%d" % i, [128, 512], F32)
            bl = Buf("psb%d" % i, excl=True)
            self.pslot_bufs.append(bl)
            banks.append(V(t.ap(), bl))
        self.pbanks = banks
        return banks

    def palloc(self, nq, bank=None):
        banks = range(8) if bank is None else (bank,)
        for bk in banks:
            for q0 in range(0, 4, nq):
                if all(self.pslot_free[bk * 4 + q0 + j] for j in range(nq)):
                    for j in range(nq):
                        self.pslot_free[bk * 4 + q0 + j] = False
                    v = V(self.pbanks[bk].ap[:, q0 * 128:(q0 + nq) * 128], self.pslot_bufs[bk])
                    return v, (bk, q0, nq)
        raise RuntimeError("out of PSUM slots")

    def pfree(self, h):
        bk, q0, nq = h
        for j in range(nq):
            self.pslot_free[bk * 4 + q0 + j] = True

    def _emit(self, eng, fn, reads, writes, dma=False):
        waits = {}

        def need(tok, waw_pe=False):
            if tok is None:
                return
            sem, val, te = tok
            if te == eng and not dma:
                if eng == 'pe' or not SAME_SYNC:
                    return
            k = id(sem)
            if k not in waits or waits[k][1] < val:
                waits[k] = (sem, val)

        ex = [b for b in reads if b.excl and b not in writes]
        if ex:
            reads = [b for b in reads if not b.excl]
            writes = list(writes) + ex
        for b in reads:
            need(b.w)
        for b in writes:
            need(b.w)
            for t in b.r.values():
                need(t)
        if dma:
            k = self.dnext
            self.dnext = (self.dnext + 1) % NDMASEM
            if self.dval[k] > 0:
                need((self.dsem[k], self.dval[k], 'dma'))
            self.dval[k] += 16
            tok = (self.dsem[k], self.dval[k], 'dma')
            inc = (self.dsem[k], 16)
        else:
            self.cnt[eng] += 1
            tok = (self.sem[eng], self.cnt[eng], eng)
            inc = (self.sem[eng], 1)
        seen = self.seen[eng]
        final = []
        for k, (sem, val) in waits.items():
            if seen.get(k, 0) < val:
                seen[k] = val
                final.append((sem, val))
        for b in reads:
            b.r[id(tok[0])] = tok
        for b in writes:
            b.w = tok
            b.r = {}
        self.q[eng].append((final, fn, inc))
        self.ninst += 1

    def barrier(self):
        toks = [(self.sem[e], self.cnt[e]) for e in ENG if self.cnt[e] > 0]
        toks += [(self.dsem[k], self.dval[k]) for k in range(NDMASEM) if self.dval[k] > 0]
        for e in ENG:
            final = []
            for sem, val in toks:
                if sem is self.sem[e]:
                    continue
                if self.seen[e].get(id(sem), 0) < val:
                    self.seen[e][id(sem)] = val
                    final.append((sem, val))
            if final:
                self.q[e].append((final, None, None))

    def finish(self):
        self.barrier()
        nc = self.nc
        q = self.q

        def replay(e, lst):
            for waits, fn, inc in lst:
                for sem, val in waits:
                    e.wait_ge(sem, val)
                if fn is not None:
                    ins = fn(e)
                    ins.then_inc(inc[0], inc[1])

        with nc.Block() as block:
            @block.sync
            def _(e):
                replay(e, q['sp'])

            @block.scalar
            def _(e):
                replay(e, q['act'])

            @block.vector
            def _(e):
                replay(e, q['dve'])

            @block.gpsimd
            def _(e):
                replay(e, q['pool'])

            @block.tensor
            def _(e):
                replay(e, q['pe'])
        return nc

    @staticmethod
    def _a(x):
        return x.ap if isinstance(x, V) else x

    @staticmethod
    def _bufs(*xs):
        out = []
        for x in xs:
            if isinstance(x, V):
                bl = x.buf if isinstance(x.buf, (list, tuple)) else (x.buf,)
                for b in bl:
                    if b not in out:
                        out.append(b)
        return out

    def dma(self, out, in_, eng='sp', **kw):
        o, i = out.ap, in_.ap
        self._emit(eng, lambda e: e.dma_start(out=o, in_=i, **kw), self._bufs(in_), self._bufs(out), dma=True)

    def mm(self, out, lhsT, rhs, start=True, stop=True):
        o, l, r = out.ap, lhsT.ap, rhs.ap
        self._emit('pe', lambda e: e.matmul(o, lhsT=l, rhs=r, start=start, stop=stop),
                   self._bufs(lhsT, rhs), self._bufs(out))

    def tr(self, out, in_, ident):
        o, i, d = out.ap, in_.ap, ident.ap
        self._emit('pe', lambda e: e.transpose(o, i, d), self._bufs(in_, ident), self._bufs(out))

    def act(self, out, in_, func, bias=None, scale=None, accum=None):
        kw = {}
        if bias is not None:
            kw['bias'] = self._a(bias)
        if scale is not None:
            kw['scale'] = self._a(scale)
        if accum is not None:
            kw['accum_out'] = accum.ap
        o, i = out.ap, in_.ap
        self._emit('act', lambda e: e.activation(out=o, in_=i, func=func, **kw),
                   self._bufs(in_, bias, scale), self._bufs(out, accum))

    def tt(self, out, in0, in1, op, eng='dve'):
        o, a, b = out.ap, in0.ap, in1.ap
        self._emit(eng, lambda e: e.tensor_tensor(out=o, in0=a, in1=b, op=op),
                   self._bufs(in0, in1), self._bufs(out))

    def ts(self, out, in0, s1, op0, s2=None, op1=None, eng='dve', accum=None):
        o, a = out.ap, in0.ap
        a1, a2 = self._a(s1), self._a(s2)
        kw = {}
        if op1 is not None:
            kw['op1'] = op1
        if accum is not None:
            kw['accum_out'] = accum.ap
        self._emit(eng, lambda e: e.tensor_scalar(out=o, in0=a, scalar1=a1, scalar2=a2, op0=op0, **kw),
                   self._bufs(in0, s1, s2), self._bufs(out, accum))

    def stt(self, out, in0, scalar, in1, op0, op1, eng='dve'):
        o, a, b = out.ap, in0.ap, in1.ap
        s = self._a(scalar)
        self._emit(eng, lambda e: e.scalar_tensor_tensor(out=o, in0=a, scalar=s, in1=b, op0=op0, op1=op1),
                   self._bufs(in0, scalar, in1), self._bufs(out))

    def copy(self, out, in_, eng='dve'):
        o, i = out.ap, in_.ap
        if eng == 'act':
            self._emit('act', lambda e: e.activation(out=o, in_=i, func=AF.Copy), self._bufs(in_), self._bufs(out))
        else:
            self._emit(eng, lambda e: e.tensor_copy(out=o, in_=i), self._bufs(in_), self._bufs(out))

    def memset(self, out, val, eng='dve'):
        o = out.ap
        self._emit(eng, lambda e: e.memset(o, val), [], self._bufs(out))

    def reduce(self, out, in_, op, axis=AX.X, eng='dve'):
        o, i = out.ap, in_.ap
        self._emit(eng, lambda e: e.tensor_reduce(out=o, in_=i, axis=axis, op=op), self._bufs(in_), self._bufs(out))

    def recip(self, out, in_):
        o, i = out.ap, in_.ap
        self._emit('dve', lambda e: e.reciprocal(out=o, in_=i), self._bufs(in_), self._bufs(out))

    def aselect(self, out, in_, pattern, cmp, fill, base, cm):
        o, i = out.ap, in_.ap
        self._emit('pool', lambda e: e.affine_select(out=o, in_=i, pattern=pattern, compare_op=cmp, fill=fill,
                                                     base=base, channel_multiplier=cm),
                   self._bufs(in_), self._bufs(out))

    def scan(self, out, d0, d1, init, op0, op1):
        o, a, b = out.ap, d0.ap, d1.ap
        self._emit('dve', lambda e: e.tensor_tensor_scan(out=o, data0=a, data1=b, initial=init, op0=op0, op1=op1),
                   self._bufs(d0, d1), self._bufs(out))


D = 1024
NT = 18
LTOK = 2304
EPS = 1e-6
DFF = 2816
NJ = 22


class Packer:
    def __init__(self):
        self.cols = {}
        self.n = 0
        self.arrs = []

    def add(self, name, arr):
        arr = np.ascontiguousarray(arr, dtype=np.float32).reshape(128, -1)
        self.cols[name] = (self.n, arr.shape[1])
        self.n += arr.shape[1]
        self.arrs.append(arr)

    def pack(self):
        return np.ascontiguousarray(np.concatenate(self.arrs, axis=1))


def fm(v):
    v = np.asarray(v, dtype=np.float32)
    lead = v.shape[:-1]
    c = v.shape[-1] // 128
    v = v.reshape(lead + (c, 128))
    return np.moveaxis(v, -1, 0).reshape(128, -1)


def host_small(inp, layer_cols_only=False):
    pk = Packer()
    for l in range(2):
        pk.add('bmod%d' % l, fm(inp['b_mod'][l]))
        pk.add('ng%d' % l, fm(inp['norm_g'][l]))
        pk.add('cw%d' % l, np.moveaxis(inp['ffn_conv_w'][l].reshape(9, NJ, 128), 2, 0).transpose(0, 2, 1).reshape(128, NJ * 9))
        pk.add('cb%d' % l, fm(inp['ffn_conv_b'][l]))
    return pk


class K:
    def __init__(self, NB, small_cols, nsmall, dbg=False):
        self.NB = NB
        self.dbg = dbg
        p = self.p = Prog()
        self.sc = small_cols
        self.X0 = p.dram("xcat", [NB, LTOK, D], F32, kind="ExternalInput")
        self.ccT_d = p.dram("ccT", [128, 8 * 6], F32, kind="ExternalInput")
        self.small_d = p.dram("small", [128, nsmall], F32, kind="ExternalInput")
        self.rows_d = p.dram("rows", [16, D], F32, kind="ExternalInput")
        self.w_mod = p.dram("w_mod", [2, D, 6 * D], F32, kind="ExternalInput")
        self.w_up = p.dram("ffn_w_up", [2, D, 2 * DFF], F32, kind="ExternalInput")
        self.w_down = p.dram("ffn_w_down", [2, DFF, D], F32, kind="ExternalInput")
        self.Xs = [self.X0]
        for i in range(1, 4):
            self.Xs.append(p.dram("xs%d" % i, [NB, LTOK, D], F32, kind="ExternalOutput" if dbg else "Internal"))
        self.out = p.dram("out", [NB, 2048, D], F32, kind="ExternalOutput")
        self.xbufs = [[[Buf("x%d_%d_%d" % (i, b, t)) for t in range(NT)] for b in range(NB)] for i in range(5)]
        self.banks = p.psum_banks()
        self.nbank = 0
        self.small = p.sb("small", [128, nsmall], F32)
        p.dma(self.small, self.small_d)
        self.identf = p.sb("identf", [128, 128], F32)
        p.memset(self.identf, 1.0, eng='pool')
        p.aselect(self.identf, self.identf, [[-1, 128]], ALU.is_equal, 0.0, 0, 1)
        self.identb = p.sb("identb", [128, 128], BF16)
        p.copy(self.identb, self.identf, eng='pool')
        self.scT = p.sb("scT", [128, 48], F32)
        p.dma(self.scT, self.ccT_d)
        p.act(self.scT, self.scT, AF.Silu)
        self.modT = p.sb("modT", [128, 48 * 6], F32)
        self.A1 = p.sb("A1", [128, 48], F32)
        self.A2 = p.sb("A2", [128, 48], F32)
        self.base_mark = p.mark()

    def bank(self):
        b = self.banks[self.nbank]
        self.nbank = (self.nbank + 1) % 8
        return b

    def col(self, name, a=None, b=None):
        o, w = self.sc[name]
        if a is None:
            return self.small[:, o:o + w]
        return self.small[:, o + a:o + b]

    def mod_stage(self, l):
        p = self.p
        m = p.mark()
        wst = [p.sb("wmod_st%d" % i, [128, 8, 512], F32) for i in range(2)]
        ps = self.bank()
        for blk in range(12):
            w = wst[blk % 2]
            p.dma(w, self.w_mod[l, :, blk * 512:(blk + 1) * 512].rr("(c p) n -> p c n", p=128))
            for f in range(4):
                fc = blk * 4 + f
                for kc in range(8):
                    p.mm(ps[:, fc * 6:(fc + 1) * 6], w[:, kc, f * 128:(f + 1) * 128], self.scT[:, kc * 6:(kc + 1) * 6],
                         start=(kc == 0), stop=(kc == 7))
        p.tt(self.modT.rr("p (c r) -> p c r", r=6), ps[:, 0:288].rr("p (c r) -> p c r", r=6),
             self.col('bmod%d' % l).rr("p (c o) -> p c o", o=1).bcast([128, 48, 6]), ALU.add)
        mv = self.modT.rr("p (i c r) -> p i c r", i=6, c=8)
        ng = self.col('ng%d' % l).rr("p (i c o) -> p i c o", i=4, o=1)
        for A, mi, gi in ((self.A1, 1, 0), (self.A2, 4, 2)):
            Av = A.rr("p (c r) -> p c r", r=6)
            p.ts(Av, mv[:, mi], 1.0, ALU.add)
            p.tt(Av, Av, ng[:, gi].bcast([128, 8, 6]), ALU.mult)
        p.barrier()
        p.release(m)

    def modvec(self, l, which):
        mv = self.modT.rr("p (i c r) -> p i c r", i=6, c=8)
        if which == 0:
            return self.A1.rr("p (c r) -> p c r", r=6), mv[:, 0]
        return self.A2.rr("p (c r) -> p c r", r=6), mv[:, 3]

    def gate_row(self, dst, l, gi, row, tmp):
        p = self.p
        mv = self.modT.rr("p (i c r) -> p i c r", i=6, c=8)
        nrow = l * 4 + (1 if gi == 2 else 3)
        p.dma(dst, self.rows_d[nrow:nrow + 1, :].pbcast(128))
        for half in range(2):
            ps = self.bank()
            for cc in range(4):
                c = half * 4 + cc
                p.copy(tmp, mv[:, gi, c, row:row + 1].bcast([128, 128]))
                p.mm(ps[:, cc * 128:(cc + 1) * 128], tmp, self.identf)
            p.tt(dst[:, half * 512:(half + 1) * 512], dst[:, half * 512:(half + 1) * 512], ps, ALU.mult)

    def prenorm(self, b, xi, l, which, hT, tiles=range(NT)):
        p = self.p
        A, Bv = self.modvec(l, which)
        m = p.mark()
        xt = [p.sb("pn_x%d" % i, [128, D], F32) for i in range(2)]
        junk = p.sb("pn_junk", [128, D], BF16)
        xn = [p.sb("pn_xn%d" % i, [128, D], BF16) for i in range(2)]
        ss = [p.sb("pn_ss%d" % i, [128, 1], F32) for i in range(2)]
        tmp = [p.sb("pn_tmp%d" % i, [128, 8, 128], F32) for i in range(2)]
        for n, tt in enumerate(tiles):
            x = xt[n % 2]
            s = ss[n % 2]
            p.dma(x, self.Xs[xi][b, tt * 128:(tt + 1) * 128, :].on(self.xbufs[xi][b][tt]))
            p.act(junk, x, AF.Square, accum=s)
            p.act(s, s, AF.Sqrt, bias=EPS, scale=1.0 / D)
            p.recip(s, s)
            p.act(xn[n % 2], x, AF.Identity, scale=s)
            ps = self.bank()
            psb = ps.bc(BF16)
            for c in range(8):
                p.tr(psb[:, c * 128:(c + 1) * 128], xn[n % 2][:, c * 128:(c + 1) * 128], self.identb)
            row = 4 if tt < 2 else b
            p.tt(tmp[n % 2], psb.rr("p (c t) -> p c t", c=8), A[:, :, row:row + 1].bcast([128, 8, 128]), ALU.mult)
            p.tt(hT[:, :, tt * 128:(tt + 1) * 128], tmp[n % 2], Bv[:, :, row:row + 1].bcast([128, 8, 128]), ALU.add,
                 eng='pool')
        p.barrier()
        p.release(m)

    def residual_tile(self, b, tt, xi, xo, ybanks, G, st):
        p = self.p
        x, junk, ss2, s, tmp = st
        p.dma(x, self.Xs[xi][b, tt * 128:(tt + 1) * 128, :].on(self.xbufs[xi][b][tt]))
        for h in range(2):
            p.act(junk, ybanks[h], AF.Square, accum=ss2[:, h:h + 1])
        p.tt(s, ss2[:, 0:1], ss2[:, 1:2], ALU.add)
        p.act(s, s, AF.Sqrt, bias=EPS, scale=1.0 / D)
        p.recip(s, s)
        for h in range(2):
            p.stt(tmp[:, h * 512:(h + 1) * 512], ybanks[h], s, G[:, h * 512:(h + 1) * 512], ALU.mult, ALU.mult)
        p.tt(tmp, tmp, x, ALU.add, eng='pool')
        if xo == 4:
            dst = self.out[b, (tt - 2) * 128:(tt - 1) * 128, :]
        else:
            dst = self.Xs[xo][b, tt * 128:(tt + 1) * 128, :]
        p.dma(dst.on(self.xbufs[xo][b][tt]), tmp)

    def res_state(self, n=2):
        p = self.p
        return [(p.sb("rs_x%d" % i, [128, D], F32), p.sb("rs_junk%d" % i, [128, 512], BF16),
                 p.sb("rs_ss2%d" % i, [128, 2], F32), p.sb("rs_s%d" % i, [128, 1], F32),
                 p.sb("rs_tmp%d" % i, [128, D], F32)) for i in range(n)]

    def ffn(self, b, l, xi, xo, do_ctx=True):
        p = self.p
        m = p.mark()
        hT = p.sb("ffn_hT", [128, 8, LTOK], BF16)
        self.prenorm(b, xi, l, 1, hT, tiles=range(NT) if do_ctx else range(2, NT))
        wdown = p.sb("ffn_wdown", [128, NJ, D], BF16)
        for j in range(NJ):
            p.dma(wdown[:, j, :], self.w_down[l, j * 128:(j + 1) * 128, :], eng='pool')
        actT = p.sb("ffn_actT", [128, NJ, 1280], BF16)
        wg = [p.sb("ffn_wg%d" % i, [128, 8, 128], BF16) for i in range(2)]
        wu = [p.sb("ffn_wu%d" % i, [128, 8, 128], BF16) for i in range(2)]
        gpad = [p.sb("ffn_gpad%d" % i, [128, 18, 66], BF16) for i in range(2)]
        cpad = p.sb("ffn_cpad", [128, 258], BF16)
        gg = [p.sb("ffn_gg%d" % i, [128, 512], BF16) for i in range(2)]
        diag = [p.sb("ffn_diag%d" % i, [128, 9, 128], BF16) for i in range(2)]
        G = [p.sb("ffn_G%d" % i, [128, D], F32) for i in range(2)]
        gtmp = p.sb("ffn_gtmp", [128, 128], F32)
        st = self.res_state(2)
        for g in gpad:
            p.memset(g, 0.0, eng='pool')
        p.memset(cpad, 0.0, eng='pool')
        self.gate_row(G[0], l, 5, b, gtmp)
        if do_ctx:
            self.gate_row(G[1], l, 5, 4, gtmp)
        cw = self.col('cw%d' % l)
        cb = self.col('cb%d' % l)
        nn = 0
        for seg in range(2):
            r0 = 16 * seg
            g0 = 0 if seg == 0 else 15
            prow0 = 1 if seg == 0 else 0
            gp = gpad[seg]
            with_ctx = (seg == 0 and do_ctx)
            for j in range(NJ):
                wgj, wuj = wg[j % 2], wu[j % 2]
                p.dma(wgj, self.w_up[l, :, j * 128:(j + 1) * 128].rr("(c p) n -> p c n", p=128), eng='pool')
                p.dma(wuj, self.w_up[l, :, DFF + j * 128:DFF + (j + 1) * 128].rr("(c p) n -> p c n", p=128), eng='pool')
                dg = diag[j % 2]
                p.tt(dg, self.identb.rr("p (o t) -> p o t", o=1).bcast([128, 9, 128]),
                     cw[:, j * 9:(j + 1) * 9].rr("p (t o) -> p t o", o=1).bcast([128, 9, 128]), ALU.mult, eng='pool')
                tok0 = 256 + g0 * 64
                for (o, n) in ((0, 512), (512, 512), (1024, 64)):
                    ps = self.bank()
                    for kc in range(8):
                        p.mm(ps[:, 0:n], wgj[:, kc, :], hT[:, kc, tok0 + o:tok0 + o + n], start=(kc == 0), stop=(kc == 7))
                    pr = prow0 + o // 64
                    p.copy(gp[:, pr:pr + n // 64, 1:65], ps[:, 0:n].rr("p (r c) -> p r c", c=64), eng='act')
                for blk in range(2):
                    ps = self.bank()
                    for t in range(9):
                        dy, dx = t // 3, t % 3
                        p.mm(ps, dg[:, t, :], gp[:, 8 * blk + dy:8 * blk + dy + 8, dx:dx + 64], start=(t == 0), stop=(t == 8))
                    g_ = gg[nn % 2]
                    nn += 1
                    p.act(g_, ps, AF.Gelu_apprx_tanh, bias=cb[:, j:j + 1])
                    ps2 = self.bank()
                    t0 = 256 + r0 * 64 + blk * 512
                    for kc in range(8):
                        p.mm(ps2, wuj[:, kc, :], hT[:, kc, t0:t0 + 512], start=(kc == 0), stop=(kc == 7))
                    p.tt(actT[:, j, 256 + blk * 512:256 + (blk + 1) * 512], ps2, g_, ALU.mult)
                if with_ctx:
                    ps = self.bank()
                    for kc in range(8):
                        p.mm(ps[:, 0:256], wgj[:, kc, :], hT[:, kc, 0:256], start=(kc == 0), stop=(kc == 7))
                    p.copy(cpad[:, 1:257], ps[:, 0:256], eng='act')
                    ps = self.bank()
                    for dx in range(3):
                        p.mm(ps[:, 0:256], dg[:, 3 + dx, :], cpad[:, dx:dx + 256], start=(dx == 0), stop=(dx == 2))
                    g_ = gg[nn % 2]
                    nn += 1
                    p.act(g_[:, 0:256], ps[:, 0:256], AF.Gelu_apprx_tanh, bias=cb[:, j:j + 1])
                    ps2 = self.bank()
                    for kc in range(8):
                        p.mm(ps2[:, 0:256], wuj[:, kc, :], hT[:, kc, 0:256], start=(kc == 0), stop=(kc == 7))
                    p.tt(actT[:, j, 0:256], ps2[:, 0:256], g_[:, 0:256], ALU.mult)
            tiles = ([0, 1] if with_ctx else []) + [2 + 8 * seg + i for i in range(8)]
            for n, tt in enumerate(tiles):
                a0 = tt * 128 if tt < 2 else 256 + (tt - 2 - 8 * seg) * 128
                yb = [self.bank(), self.bank()]
                for h in range(2):
                    for j in range(NJ):
                        p.mm(yb[h], actT[:, j, a0:a0 + 128], wdown[:, j, h * 512:(h + 1) * 512], start=(j == 0), stop=(j == NJ - 1))
                self.residual_tile(b, tt, xi, xo, yb, G[1] if tt < 2 else G[0], st[n % 2])
        p.barrier()
        p.release(m)


LNS_ML = float(np.log(128.0 ** -0.5))
LNS_GLA = float(np.log(64.0 ** -0.5))
ORD = [list(range(NT)), [1, 0] + list(range(17, 1, -1))]
C_MQ, C_MK, C_MV, C_MO, C_MG, C_GQ, C_GK, C_GV, C_GG, C_GLR = 0, 512, 1024, 1536, 2048, 2064, 2320, 2576, 3088, 3600


def host_small_even(pk, inp):
    pk.add('ecw', fm(inp['ev_conv_w'][0]))
    pk.add('ecb', fm(inp['ev_conv_b'][0]))
    pk.add('glab', fm(inp['ev_gla_b'][0]))
    pk.add('hg', fm(inp['ev_head_g'][0]))


class KE(K):
    def setup_even(self):
        p = self.p
        self.w_in = p.dram("ev_w_in", [D, 3632], F32, kind="ExternalInput")
        self.w_out = p.dram("ev_w_out", [D, D], F32, kind="ExternalInput")
        self.gla_w2 = p.dram("ev_gla_w2", [2, 16, 256], F32, kind="ExternalInput")
        self.ones = p.sb("ones", [128, 128], F32)
        p.memset(self.ones, 1.0, eng='pool')
        self.tri = []
        for z in range(2):
            t = p.sb("tri%d" % z, [128, 128], F32)
            pat, cm = ([[1, 128]], -1) if z == 0 else ([[-1, 128]], 1)
            p.aselect(t, self.ones, pat, ALU.is_ge, 0.0, 0, cm)
            self.tri.append(t)
        self.common_mark = p.mark()
        self.mask4 = []
        self.maskb = []
        for z in range(2):
            t = self.tri[z]
            m4 = p.sb("mask4%d" % z, [128, 4, 128], F32)
            p.ts(m4, t.rr("p (o t) -> p o t", o=1).bcast([128, 4, 128]), -1.0, ALU.add, 30000.0, ALU.mult, eng='pool')
            self.mask4.append(m4)
            mb = p.sb("maskb%d" % z, [128, 128], BF16)
            p.copy(mb, t, eng='pool')
            self.maskb.append(mb)
        self.bgrow = p.sb("bgrow", [128, 16], F32)
        p.dma(self.bgrow, self.rows_d[8:9, 0:16].pbcast(128))
        self.base_mark = p.mark()

    def headnorm_gate(self, y, ybufs, hd, gate_col, gate_fn, hT, mixT, scratch):
        p = self.p
        sq, ss, yn, sg, wgt = scratch
        p.dma(wgt, self.w_in[:, gate_col:gate_col + 128].rr("(c p) n -> p c n", p=128), eng='pool')
        for blk in range(5):
            t0, n = blk * 512, (512 if blk < 4 else 256)
            ps = self.bank()
            for kc in range(8):
                p.mm(ps[:, 0:n], wgt[:, kc, :], hT[:, kc, t0:t0 + n], start=(kc == 0), stop=(kc == 7))
            p.act(sg[:, t0:t0 + n], ps[:, 0:n], gate_fn)
        yall = V(y.ap, Buf('yall'))
        for tt in range(NT):
            p.tt(sq[:, tt, :], y[:, tt, :].on(ybufs[tt]), y[:, tt, :].on(ybufs[tt]), ALU.mult)
        p.reduce(ss, sq, ALU.add)
        p.act(ss, ss, AF.Sqrt, bias=EPS, scale=1.0 / 128)
        p.recip(ss, ss)
        for tt in range(NT):
            p.ts(yn[:, tt, :], y[:, tt, :].on(ybufs[tt]), ss[:, tt:tt + 1], ALU.mult)
        hg = self.col('hg')
        for g0 in (0, 8, 16):
            n = min(8, NT - g0)
            ps = self.bank()
            psb = ps.bc(BF16)
            for i in range(n):
                p.tr(psb[:, i * 128:(i + 1) * 128], yn[:, g0 + i, :], self.identb)
            p.stt(mixT[:, hd, g0 * 128:(g0 + n) * 128], psb[:, 0:n * 128], hg[:, hd:hd + 1], sg[:, g0 * 128:(g0 + n) * 128],
                  ALU.mult, ALU.mult)

    def even_mixer(self, b, xi, xo):
        p = self.p
        l = 0
        m = p.mark()
        hT = p.sb("ev_hT", [128, 8, LTOK], BF16)
        self.prenorm(b, xi, l, 0, hT)
        mixT = p.sb("ev_mixT", [128, 8, LTOK], BF16)
        wgate = p.sb("ev_wgate", [128, 8, 16], BF16)
        p.dma(wgate, self.w_in[:, C_MG:C_MG + 16].rr("(c p) n -> p c n", p=128), eng='pool')
        graw = p.sb("ev_graw", [128, NT, 16], F32)
        ps = self.bank()
        for tt in range(NT):
            for kc in range(8):
                p.mm(ps[:, tt * 16:(tt + 1) * 16], hT[:, kc, tt * 128:(tt + 1) * 128], wgate[:, kc, :], start=(kc == 0), stop=(kc == 7))
        p.tt(graw, ps[:, 0:NT * 16].rr("p (t n) -> p t n", n=16), self.bgrow.rr("p (o n) -> p o n", o=1).bcast([128, NT, 16]), ALU.add)
        g5 = graw.rr("p t (z y h) -> p t z y h", z=2, y=2)
        I8 = g5[:, :, :, 0, :]
        F8 = g5[:, :, :, 1, :]
        def s8(name):
            return p.sb(name, [128, NT, 2, 4], F32)
        lf8, Fc8, Ft8, bs8, qs8, wk8, dc8 = [s8("ev_" + n) for n in ("lf8", "Fc8", "Ft8", "bs8", "qs8", "wk8", "dc8")]
        p.act(lf8, F8, AF.Exp, scale=-1.0)
        p.act(lf8, lf8, AF.Ln, bias=1.0)
        p.ts(lf8, lf8, -1.0, ALU.mult)
        ps = self.bank()
        for z in range(2):
            p.mm(ps[:, z * 72:(z + 1) * 72], self.tri[z], lf8[:, :, z, :])
        p.mm(ps[:, 144:288], self.ones, lf8)
        for z in range(2):
            p.copy(Fc8[:, :, z, :], ps[:, z * 72:(z + 1) * 72].rr("p (t h) -> p t h", h=4))
        p.copy(Ft8, ps[:, 144:288].rr("p (t z h) -> p t z h", z=2, h=4))
        p.tt(bs8, I8, Fc8, ALU.subtract)
        p.tt(wk8, Ft8, bs8, ALU.add)
        p.act(wk8, wk8, AF.Exp)
        p.ts(bs8, bs8, LNS_ML, ALU.add)
        p.act(qs8, Fc8, AF.Exp, bias=LNS_ML)
        p.act(dc8, Ft8, AF.Exp)
        Fc5 = Fc8.rr("p t z (h o) -> p t z h o", o=1)

        cw = self.col('ecw').rr("p (t c) -> p t c", t=3)
        cbias = self.col('ecb')
        for hp in range(2):
            m2 = p.mark()
            heads = [2 * hp, 2 * hp + 1]
            qT = [p.sb("ml_qT%d" % i, [128, LTOK], BF16) for i in range(2)]
            kT = [p.sb("ml_kT%d" % i, [128, LTOK], BF16) for i in range(2)]
            vv = [p.sb("ml_v%d" % i, [128, NT, 130], BF16) for i in range(2)]
            yy = [p.sb("ml_y%d" % i, [128, NT, 128], F32) for i in range(2)]
            ybufs = [[Buf("mly%d_%d" % (i, t)) for t in range(NT)] for i in range(2)]
            m3 = p.mark()
            xpad = p.sb("ml_xpad", [128, 2308], F32)
            t1 = p.sb("ml_t1", [128, 2306], F32)
            t2 = p.sb("ml_t2", [128, 2306], F32)
            wq = [p.sb("ml_wq%d" % i, [128, 8, 128], BF16) for i in range(2)]
            p.memset(xpad, 0.0, eng='pool')
            nw = 0
            for i, h in enumerate(heads):
                for dst, c0, cc in ((qT[i], C_MQ + h * 128, h), (kT[i], C_MK + h * 128, 4 + h)):
                    w = wq[nw % 2]
                    nw += 1
                    p.dma(w, self.w_in[:, c0:c0 + 128].rr("(c p) n -> p c n", p=128), eng='pool')
                    for blk in range(5):
                        t0, n = (0, 256) if blk == 0 else (256 + (blk - 1) * 512, 512)
                        pos = 1 if blk == 0 else 259 + (blk - 1) * 512
                        ps = self.bank()
                        for kc in range(8):
                            p.mm(ps[:, 0:n], w[:, kc, :], hT[:, kc, t0:t0 + n], start=(kc == 0), stop=(kc == 7))
                        p.copy(xpad[:, pos:pos + n], ps[:, 0:n], eng='act')
                    p.ts(t1, xpad[:, 0:2306], cw[:, 0, cc:cc + 1], ALU.mult, cbias[:, cc:cc + 1], ALU.add)
                    p.stt(t2, xpad[:, 1:2307], cw[:, 1, cc:cc + 1], t1, ALU.mult, ALU.add)
                    p.stt(t1, xpad[:, 2:2308], cw[:, 2, cc:cc + 1], t2, ALU.mult, ALU.add)
                    p.act(dst[:, 0:256], t1[:, 0:256], AF.Silu)
                    p.act(dst[:, 256:LTOK], t1[:, 258:2306], AF.Silu)
                w = wq[nw % 2]
                nw += 1
                p.dma(w, self.w_in[:, C_MV + h * 128:C_MV + (h + 1) * 128].rr("(c p) n -> p c n", p=128), eng='pool')
                p.memset(vv[i][:, :, 128:130], 1.0, eng='pool')
                for g0 in range(0, NT, 4):
                    n = min(4, NT - g0)
                    ps = self.bank()
                    for j in range(n):
                        tt = g0 + j
                        for kc in range(8):
                            p.mm(ps[:, j * 128:(j + 1) * 128], hT[:, kc, tt * 128:(tt + 1) * 128], w[:, kc, :], start=(kc == 0), stop=(kc == 7))
                    p.copy(vv[i][:, g0:g0 + n, 0:128], ps[:, 0:n * 128].rr("p (t n) -> p t n", n=128), eng='act')
                p.memset(yy[i], 0.0, eng='pool')
            p.barrier()
            p.release(m3)
            Cs = [[p.sb("ml_C%d%d" % (z, i), [128, 132], F32) for i in range(2)] for z in range(2)]
            Cb = [[p.sb("ml_Cb%d%d" % (z, i), [128, 132], BF16) for i in range(2)] for z in range(2)]
            for z in range(2):
                for i in range(2):
                    p.memset(Cs[z][i], 0.0, eng='pool')
                    p.memset(Cb[z][i], 0.0, eng='pool')
            diag4 = [p.sb("ml_diag4%d" % i, [128, 4, 128], F32) for i in range(2)]
            Dm = [p.sb("ml_Dm%d" % i, [128, 128], BF16) for i in range(4)]
            sT = [p.sb("ml_sT%d" % i, [128, 128], BF16) for i in range(4)]
            tmpo = [p.sb("ml_tmpo%d" % i, [128, 132], F32) for i in range(4)]
            num = [p.sb("ml_num%d" % i, [128, 132], F32) for i in range(4)]
            den = [p.sb("ml_den%d" % i, [128, 1], F32) for i in range(4)]
            ktil = [p.sb("ml_ktil%d" % i, [128, 128], BF16) for i in range(4)]
            identf4 = self.identf.rr("p (o t) -> p o t", o=1).bcast([128, 4, 128])
            n4 = 0
            for step in range(NT):
                for z in range(2):
                    tt = ORD[z][step]
                    ts_ = slice(tt * 128, (tt + 1) * 128)
                    dg = diag4[(step * 2 + z) % 2]
                    p.tt(dg, identf4, Fc5[:, tt, z].bcast([128, 4, 128]), ALU.mult)
                    rb = self.bank()
                    p.mm(rb, self.ones, dg, start=True, stop=False)
                    p.mm(rb, self.identf, self.mask4[z], start=False, stop=True)
                    for i, h in enumerate(heads):
                        k4 = n4 % 4
                        n4 += 1
                        sc = self.bank()
                        p.mm(sc[:, 0:128], kT[i][:, ts_], qT[i][:, ts_])
                        p.act(Dm[k4], rb[:, h * 128:(h + 1) * 128], AF.Exp, bias=bs8[:, tt, z, h:h + 1])
                        p.tt(sT[k4], sc[:, 0:128], Dm[k4], ALU.mult)
                        o = self.bank()
                        p.mm(o[:, 0:129], sT[k4], vv[i][:, tt, 0:129])
                        p.mm(o[:, 256:385], qT[i][:, ts_], Cb[z][i][:, 0:129])
                        p.act(tmpo[k4][:, 0:129], o[:, 256:385], AF.Identity, scale=qs8[:, tt, z, h:h + 1])
                        p.tt(num[k4][:, 0:129], tmpo[k4][:, 0:129], o[:, 0:129], ALU.add)
                        p.act(den[k4], num[k4][:, 128:129], AF.Abs)
                        p.ts(den[k4], den[k4], 1.0, ALU.max)
                        p.recip(den[k4], den[k4])
                        yv = yy[i][:, tt, :].on(ybufs[i][tt])
                        p.stt(yv, num[k4][:, 0:128], den[k4], yv, ALU.mult, ALU.add)
                        if step < NT - 1:
                            kp = self.bank()
                            kpb = kp.bc(BF16)
                            p.tr(kpb[:, 0:128], kT[i][:, ts_], self.identb)
                            p.act(ktil[k4], kpb[:, 0:128], AF.Identity, scale=wk8[:, tt, z, h:h + 1])
                            cu = self.bank()
                            p.mm(cu[:, 0:129], ktil[k4], vv[i][:, tt, 0:129])
                            p.stt(Cs[z][i][:, 0:129], Cs[z][i][:, 0:129], dc8[:, tt, z, h:h + 1], cu[:, 0:129], ALU.mult, ALU.add)
                            p.copy(Cb[z][i][:, 0:129], Cs[z][i][:, 0:129], eng='pool')
            m4_ = p.mark()
            scratch = (p.sb("hn_sq", [128, NT, 128], F32), p.sb("hn_ss", [128, NT], F32), p.sb("hn_yn", [128, NT, 128], BF16),
                       p.sb("hn_sg", [128, LTOK], BF16), p.sb("hn_wgt", [128, 8, 128], BF16))
            for i, h in enumerate(heads):
                self.headnorm_gate(yy[i], ybufs[i], h, C_MO + h * 128, AF.Sigmoid, hT, mixT, scratch)
            p.barrier()
            p.release(m2)
        self.ev_hT, self.ev_mixT, self.ev_mark = hT, mixT, m
        return hT, mixT, m

    def even_out(self, b, xi, xo, hT, mixT, m):
        p = self.p
        l = 0
        wout = p.sb("ev_wout", [128, 8, D], BF16)
        for c in range(8):
            p.dma(wout[:, c, :], self.w_out[c * 128:(c + 1) * 128, :], eng='pool')
        G = [p.sb("evo_G%d" % i, [128, D], F32) for i in range(2)]
        gtmp = p.sb("evo_gtmp", [128, 128], F32)
        st = self.res_state(2)
        self.gate_row(G[0], l, 2, b, gtmp)
        self.gate_row(G[1], l, 2, 4, gtmp)
        for tt in range(NT):
            yb = [self.bank(), self.bank()]
            for h in range(2):
                for c in range(8):
                    p.mm(yb[h], mixT[:, c, tt * 128:(tt + 1) * 128], wout[:, c, h * 512:(h + 1) * 512], start=(c == 0), stop=(c == 7))
            self.residual_tile(b, tt, xi, xo, yb, G[1] if tt < 2 else G[0], st[tt % 2])
        p.barrier()
        p.release(m)


def _gla(self, b, hT, mixT):
    p = self.p
    m = p.mark()
    rmask = p.sb("gl_rmask", [128, LTOK], BF16)
    p.memset(rmask, 1.0, eng='pool')
    p.memset(rmask.rr("p (n t) -> p n t", t=128)[:, :, 0:1], 0.0, eng='pool')
    w2b = p.sb("gl_w2b", [16, 2, 256], BF16)
    p.dma(w2b, self.gla_w2.rr("z r c -> r z c"), eng='pool')
    wlr = p.sb("gl_wlr", [128, 8, 32], BF16)
    p.dma(wlr, self.w_in[:, C_GLR:C_GLR + 32].rr("(c p) n -> p c n", p=128), eng='pool')
    glrT = p.sb("gl_glrT", [16, 2, LTOK], BF16)
    for z in range(2):
        for blk in range(5):
            t0, n = blk * 512, (512 if blk < 4 else 256)
            ps = self.bank()
            for kc in range(8):
                p.mm(ps[0:16, 0:n], wlr[:, kc, z * 16:(z + 1) * 16], hT[:, kc, t0:t0 + n], start=(kc == 0), stop=(kc == 7))
            p.copy(glrT[:, z, t0:t0 + n], ps[0:16, 0:n], eng='act')
    nb = p.sb("gl_nb", [128, 4], F32)
    p.ts(nb, self.col('glab'), -1.0, ALU.mult)
    for cp in range(2):
        m2 = p.mark()
        qtil = [p.sb("gl_qtil%d" % z, [128, LTOK], BF16) for z in range(2)]
        khat = [p.sb("gl_khat%d" % z, [128, LTOK], BF16) for z in range(2)]
        ktl = [p.sb("gl_ktl%d" % z, [128, LTOK], BF16) for z in range(2)]
        dec = [p.sb("gl_dec%d" % z, [128, NT], F32) for z in range(2)]
        vv = [p.sb("gl_v%d" % i, [128, NT, 128], BF16) for i in range(2)]
        yy = [p.sb("gl_y%d" % i, [128, NT, 128], F32) for i in range(2)]
        ybufs = [[Buf("gly%d_%d" % (i, t)) for t in range(NT)] for i in range(2)]
        m3 = p.mark()
        qT = p.sb("gl_qT", [128, LTOK], BF16)
        kT = p.sb("gl_kT", [128, LTOK], BF16)
        lb = p.sb("gl_l", [128, LTOK], F32)
        P = p.sb("gl_P", [128, LTOK], F32)
        tmp = p.sb("gl_tmp", [128, LTOK], F32)
        E = p.sb("gl_E", [128, LTOK], BF16)
        w = [p.sb("gl_w%d" % i, [128, 8, 128], BF16) for i in range(2)]
        for dst, c0, wi in ((qT, C_GQ + cp * 128, 0), (kT, C_GK + cp * 128, 1)):
            p.dma(w[wi], self.w_in[:, c0:c0 + 128].rr("(c p) n -> p c n", p=128), eng='pool')
            for blk in range(5):
                t0, n = blk * 512, (512 if blk < 4 else 256)
                ps = self.bank()
                for kc in range(8):
                    p.mm(ps[:, 0:n], w[wi][:, kc, :], hT[:, kc, t0:t0 + n], start=(kc == 0), stop=(kc == 7))
                p.copy(dst[:, t0:t0 + n], ps[:, 0:n], eng='act')
        P3 = P.rr("p (n t) -> p n t", t=128)
        Ptot = P3[:, :, 127:128]
        for z in range(2):
            for blk in range(5):
                t0, n = blk * 512, (512 if blk < 4 else 256)
                ps = self.bank()
                p.mm(ps[:, 0:n], w2b[:, z, cp * 128:(cp + 1) * 128], glrT[:, z, t0:t0 + n])
                p.act(lb[:, t0:t0 + n], ps[:, 0:n], AF.Exp, scale=-1.0, bias=nb[:, z * 2 + cp:z * 2 + cp + 1])
            p.act(lb, lb, AF.Ln, bias=1.0)
            p.scan(P, rmask, lb, 0.0, ALU.mult, ALU.add)
            p.act(dec[z], Ptot.rr("p n o -> p (n o)"), AF.Exp, scale=-1.0 / 16)
            t3 = tmp.rr("p (n t) -> p n t", t=128)
            if z == 0:
                p.act(E, P, AF.Exp, scale=-1.0 / 16, bias=LNS_GLA)
                p.tt(qtil[z], qT, E, ALU.mult)
                p.act(E, P, AF.Exp, scale=1.0 / 16)
                p.tt(khat[z], kT, E, ALU.mult, eng='pool')
                p.tt(t3, P3, Ptot.bcast([128, NT, 128]), ALU.subtract)
                p.act(E, tmp, AF.Exp, scale=1.0 / 16)
                p.tt(ktl[z], kT, E, ALU.mult)
            else:
                p.tt(t3, Ptot.bcast([128, NT, 128]), P3, ALU.subtract)
                p.tt(tmp, tmp, lb, ALU.add, eng='pool')
                p.act(E, tmp, AF.Exp, scale=-1.0 / 16, bias=LNS_GLA)
                p.tt(qtil[z], qT, E, ALU.mult)
                p.act(E, tmp, AF.Exp, scale=1.0 / 16)
                p.tt(khat[z], kT, E, ALU.mult, eng='pool')
                p.tt(tmp, lb, P, ALU.subtract)
                p.act(E, tmp, AF.Exp, scale=1.0 / 16)
                p.tt(ktl[z], kT, E, ALU.mult)
        for i in range(2):
            h = cp * 2 + i
            p.dma(w[i], self.w_in[:, C_GV + h * 128:C_GV + (h + 1) * 128].rr("(c p) n -> p c n", p=128), eng='pool')
            for g0 in range(0, NT, 4):
                n = min(4, NT - g0)
                ps = self.bank()
                for j in range(n):
                    tt = g0 + j
                    for kc in range(8):
                        p.mm(ps[:, j * 128:(j + 1) * 128], hT[:, kc, tt * 128:(tt + 1) * 128], w[i][:, kc, :], start=(kc == 0), stop=(kc == 7))
                p.copy(vv[i][:, g0:g0 + n, :], ps[:, 0:n * 128].rr("p (t n) -> p t n", n=128), eng='act')
            p.memset(yy[i], 0.0, eng='pool')
        p.barrier()
        p.release(m3)
        S = [[p.sb("gl_S%d%d" % (z, i), [128, 128], F32) for i in range(2)] for z in range(2)]
        Sb = [[p.sb("gl_Sb%d%d" % (z, i), [128, 128], BF16) for i in range(2)] for z in range(2)]
        for z in range(2):
            for i in range(2):
                p.memset(S[z][i], 0.0, eng='pool')
                p.memset(Sb[z][i], 0.0, eng='pool')
        AT = [p.sb("gl_AT%d" % i, [128, 128], BF16) for i in range(4)]
        ktok = [p.sb("gl_ktok%d" % i, [128, 64], BF16) for i in range(4)]
        n4 = 0
        for step in range(NT):
            for z in range(2):
                tt = ORD[z][step]
                ts_ = slice(tt * 128, (tt + 1) * 128)
                for i in range(2):
                    pr = slice(i * 64, (i + 1) * 64)
                    k4 = n4 % 4
                    n4 += 1
                    sc = self.bank()
                    p.mm(sc[:, 0:128], khat[z][pr, ts_], qtil[z][pr, ts_])
                    p.tt(AT[k4], sc[:, 0:128], self.maskb[z], ALU.mult)
                    o = self.bank()
                    p.mm(o[:, 0:128], AT[k4], vv[i][:, tt, :], start=True, stop=False)
                    p.mm(o[:, 0:128], qtil[z][pr, ts_], Sb[z][i][pr, :], start=False, stop=True)
                    yv = yy[i][:, tt, :].on(ybufs[i][tt])
                    p.tt(yv, yv, o[:, 0:128], ALU.add)
                    if step < NT - 1:
                        kp = self.bank()
                        kpb = kp.bc(BF16)
                        p.tr(kpb[:, 0:64], ktl[z][pr, ts_], self.identb[pr, pr])
                        p.copy(ktok[k4], kpb[:, 0:64], eng='act')
                        su = self.bank()
                        p.mm(su[pr, 0:128], ktok[k4], vv[i][:, tt, :])
                        p.stt(S[z][i][pr, :], S[z][i][pr, :], dec[z][pr, tt:tt + 1], su[pr, 0:128], ALU.mult, ALU.add)
                        p.copy(Sb[z][i][pr, :], S[z][i][pr, :], eng='pool')
        scratch = (p.sb("hn_sq", [128, NT, 128], F32), p.sb("hn_ss", [128, NT], F32), p.sb("hn_yn", [128, NT, 128], BF16),
                   p.sb("hn_sg", [128, LTOK], BF16), p.sb("hn_wgt", [128, 8, 128], BF16))
        for i in range(2):
            h = cp * 2 + i
            self.headnorm_gate(yy[i], ybufs[i], 4 + h, C_GG + h * 128, AF.Silu, hT, mixT, scratch)
        p.barrier()
        p.release(m2)
    p.barrier()
    p.release(m)


KE.gla = _gla


C0 = float(np.exp(-0.5))
RW_LN_EPS = 64e-5


def host_small_rw(pk, inp):
    pk.add('mu', fm(inp['rw_mu'][0]))
    pk.add('w0', fm(inp['rw_w0'][0]))
    pk.add('a0', fm(inp['rw_a0'][0]))
    pk.add('kv', fm(inp['rw_kvec'][0]))
    pk.add('lnx', fm(inp['rw_lnx'][0]))


def _setup_rw(self):
    p = self.p
    NB = self.NB
    self.w_rkv = p.dram("rw_w_rkv", [3, D, D], F32, kind="ExternalInput")
    self.w_o = p.dram("rw_w_o", [D, D], F32, kind="ExternalInput")
    self.rw_w1 = p.dram("rw_w1", [2, D, 64], F32, kind="ExternalInput")
    self.rw_w2 = p.dram("rw_w2", [2, 64, D], F32, kind="ExternalInput")
    self.rw_a1 = p.dram("rw_a1", [D, 64], F32, kind="ExternalInput")
    self.rw_a2 = p.dram("rw_a2", [64, D], F32, kind="ExternalInput")
    self.rw_g1 = p.dram("rw_g1", [D, 128], F32, kind="ExternalInput")
    self.rw_g2 = p.dram("rw_g2", [128, D], F32, kind="ExternalInput")
    self.scr = []
    if getattr(self, 'dbg', False):
        self.dbg_y = p.dram("dbg_y", [8, 128, NT * 128], F32, kind="ExternalOutput")
    for b in range(NB):
        d = {}
        for nm in ('r', 'k', 'v', 'o'):
            d[nm] = p.dram("scr_%s%d" % (nm, b), [8, 128, LTOK], BF16, kind="ExternalOutput" if getattr(self, 'dbg', False) else "Internal")
            d[nm + 'buf'] = [Buf("scr_%s%d_%d" % (nm, b, c)) for c in range(8)]
        self.scr.append(d)


def _setup_rw_consts(self):
    p = self.p
    p.barrier()
    p.release(self.common_mark)
    self.strict = []
    self.M4 = []
    for z in range(2):
        st = p.sb("strict%d" % z, [128, 128], F32)
        pat, cm = ([[1, 128]], -1) if z == 0 else ([[-1, 128]], 1)
        p.aselect(st, self.ones, pat, ALU.is_gt, 0.0, 0, cm)
        self.strict.append(st)
    for z in range(2):
        m4 = p.sb("M4%d" % z, [128, 4, 128], F32)
        p.ts(m4[:, 0, :], self.strict[z], -1.0, ALU.mult, eng='pool')
        p.ts(m4[:, 1, :], self.tri[z], -1.0, ALU.mult, eng='pool')
        p.copy(m4[:, 2, :], self.strict[z], eng='pool')
        p.copy(m4[:, 3, :], self.tri[z], eng='pool')
        self.M4.append(m4)
    self.nstrictT = []
    for z in range(2):
        t = p.sb("nstrictT%d" % z, [128, 128], F32)
        p.ts(t, self.strict[1 - z], -1.0, ALU.mult, eng='pool')
        self.nstrictT.append(t)
    self.masks3 = p.sb("masks3", [128, 3, 128], BF16)
    bd64 = p.sb("bd64", [128, 128], BF16)
    p.memset(self.masks3[:, 0, :], 0.0, eng='pool')
    p.memset(bd64, 0.0, eng='pool')
    for i in range(4):
        p.memset(self.masks3[32 * i:32 * i + 32, 0, 32 * i:32 * i + 32], 1.0, eng='pool')
    for i in range(2):
        p.memset(bd64[64 * i:64 * i + 64, 64 * i:64 * i + 64], 1.0, eng='pool')
    p.tt(self.masks3[:, 1, :], bd64, self.masks3[:, 0, :], ALU.subtract, eng='pool')
    p.ts(self.masks3[:, 2, :], bd64, -1.0, ALU.mult, 1.0, ALU.add, eng='pool')
    self.bones = p.sb("bones", [128, 128], BF16)
    p.memset(self.bones, 0.0, eng='pool')
    p.memset(self.bones[0:64, 0:64], 1.0, eng='pool')
    p.memset(self.bones[64:128, 64:128], 1.0, eng='pool')
    self.omu = p.sb("omu", [128, 48], F32)
    p.ts(self.omu, self.col('mu'), -1.0, ALU.mult, 1.0, ALU.add)
    self.okv1 = p.sb("okv1", [128, 8], F32)
    p.ts(self.okv1, self.col('kv')[:, 8:16], -1.0, ALU.mult, 1.0, ALU.add)
    self.base_mark = p.mark()


def _rw_mix_block(self, xb, hT, mi, blk):
    p = self.p
    mu = self.col('mu')
    t0, n = (0, 256) if blk == 0 else (256 + (blk - 1) * 512, 512)
    for c in range(8):
        muc = mu[:, mi * 8 + c:mi * 8 + c + 1]
        p.act(xb[:, c, 0:n], hT[:, c, t0:t0 + n], AF.Identity, scale=self.omu[:, mi * 8 + c:mi * 8 + c + 1])
        kind = c // 2
        if blk == 0:
            if kind in (0, 2):
                p.stt(xb[:, c, 1:256], hT[:, c, 0:255], muc, xb[:, c, 1:256], ALU.mult, ALU.add)
            else:
                p.stt(xb[:, c, 0:255], hT[:, c, 1:256], muc, xb[:, c, 0:255], ALU.mult, ALU.add)
        else:
            r0 = (blk - 1) * 8
            xv = xb[:, c, :].rr("p (r w) -> p r w", w=64)
            hv = hT[:, c, 256:LTOK].rr("p (r w) -> p r w", w=64)
            if kind == 0:
                p.stt(xv[:, :, 1:64], hv[:, r0:r0 + 8, 0:63], muc, xv[:, :, 1:64], ALU.mult, ALU.add)
            elif kind == 1:
                p.stt(xv[:, :, 0:63], hv[:, r0:r0 + 8, 1:64], muc, xv[:, :, 0:63], ALU.mult, ALU.add)
            elif kind == 2:
                lo = 1 if r0 == 0 else 0
                p.stt(xv[:, lo:8, :], hv[:, r0 + lo - 1:r0 + 7, :], muc, xv[:, lo:8, :], ALU.mult, ALU.add)
            else:
                hi = 7 if r0 == 24 else 8
                p.stt(xv[:, 0:hi, :], hv[:, r0 + 1:r0 + hi + 1, :], muc, xv[:, 0:hi, :], ALU.mult, ALU.add)
    return t0, n


def _rw_phase1(self, b, hT, tw, ta, tg):
    p = self.p
    scr = self.scr[b]
    m = p.mark()
    xblk = [p.sb("rw_xb%d" % i, [128, 8, 512], BF16) for i in range(2)]
    W = p.sb("rw_W", [128, 8, D], BF16)
    stage = [p.sb("rw_stage%d" % i, [128, 512], BF16) for i in range(4)]
    ns = 0
    nx = 0
    for mi, kind in ((0, 'r'), (2, 'k'), (3, 'v'), (1, 'w'), (4, 'a'), (5, 'g')):
        if kind in 'rkv':
            idx = 'rkv'.index(kind)
            for c in range(8):
                p.dma(W[:, c, :], self.w_rkv[idx, c * 128:(c + 1) * 128, :], eng='pool')
        elif kind == 'w':
            for z in range(2):
                p.dma(W[:, :, z * 64:(z + 1) * 64], self.rw_w1[z].rr("(c p) r -> p c r", p=128), eng='pool')
        elif kind == 'a':
            p.dma(W[:, :, 0:64], self.rw_a1.rr("(c p) r -> p c r", p=128), eng='pool')
        else:
            p.dma(W[:, :, 0:128], self.rw_g1.rr("(c p) r -> p c r", p=128), eng='pool')
        for blk in range(5):
            xb = xblk[nx % 2]
            nx += 1
            t0, n = self.rw_mix_block(xb, hT, mi, blk)
            if kind in 'rkv':
                for oc in range(8):
                    ps = self.bank()
                    for kc in range(8):
                        p.mm(ps[:, 0:n], W[:, kc, oc * 128:(oc + 1) * 128], xb[:, kc, 0:n], start=(kc == 0), stop=(kc == 7))
                    sg = stage[ns % 4]
                    ns += 1
                    p.copy(sg[:, 0:n], ps[:, 0:n], eng='act')
                    p.dma(scr[kind][oc, :, t0:t0 + n].on(scr[kind + 'buf'][oc]), sg[:, 0:n])
            elif kind == 'w':
                for z in range(2):
                    ps = self.bank()
                    for kc in range(8):
                        p.mm(ps[0:64, 0:n], W[:, kc, z * 64:(z + 1) * 64], xb[:, kc, 0:n], start=(kc == 0), stop=(kc == 7))
                    p.act(tw[z][0:64, t0:t0 + n], ps[0:64, 0:n], AF.Tanh)
            elif kind == 'a':
                ps = self.bank()
                for kc in range(8):
                    p.mm(ps[0:64, 0:n], W[:, kc, 0:64], xb[:, kc, 0:n], start=(kc == 0), stop=(kc == 7))
                p.copy(ta[0:64, t0:t0 + n], ps[0:64, 0:n], eng='act')
            else:
                ps = self.bank()
                for kc in range(8):
                    p.mm(ps[:, 0:n], W[:, kc, 0:128], xb[:, kc, 0:n], start=(kc == 0), stop=(kc == 7))
                p.act(tg[:, t0:t0 + n], ps[:, 0:n], AF.Sigmoid)
    p.barrier()
    p.release(m)


KE.setup_rw = _setup_rw
KE.setup_rw_consts = _setup_rw_consts
KE.rw_mix_block = _rw_mix_block
KE.rw_phase1 = _rw_phase1


BLKS = [(0, 512), (512, 512), (1024, 512), (1536, 512), (2048, 256)]


def _rw_chunk(self, b, cc, tw, ta, tg, last):
    p = self.p
    scr = self.scr[b]
    m = p.mark()
    kv = self.col('kv')
    rT = p.sb("rc_rT", [128, LTOK], BF16)
    kT = p.sb("rc_kT", [128, LTOK], BF16)
    vT = p.sb("rc_vT", [128, LTOK], BF16)
    for t_, nm in ((rT, 'r'), (kT, 'k'), (vT, 'v')):
        p.dma(t_, scr[nm][cc].on(scr[nm + 'buf'][cc]))
    aT = p.sb("rc_aT", [128, LTOK], BF16)
    kap = p.sb("rc_kap", [128, LTOK], BF16)
    bet = p.sb("rc_bet", [128, LTOK], BF16)
    KR = [p.sb("rc_KR%d" % z, [128, NT, 2, 128], BF16) for z in range(2)]
    khat = [p.sb("rc_khat%d" % z, [128, LTOK], BF16) for z in range(2)]
    bhat = [p.sb("rc_bhat%d" % z, [128, LTOK], BF16) for z in range(2)]
    kT4 = [p.sb("rc_kT4%d" % z, [128, LTOK], BF16) for z in range(2)]
    nbT4 = [p.sb("rc_nbT4%d" % z, [128, LTOK], BF16) for z in range(2)]
    gam = [p.sb("rc_gam%d" % z, [128, NT], F32) for z in range(2)]
    Vtok = p.sb("rc_Vtok", [128, NT, 128], BF16)
    y = p.sb("rc_y", [128, NT, 128], F32)
    ybufs = [[Buf("rcy%d_%d" % (t, hh)) for hh in range(2)] for t in range(NT)]
    p.memset(y, 0.0, eng='pool')
    m3 = p.mark()
    sig = p.sb("rc_sig", [128, LTOK], F32)
    P = p.sb("rc_P", [128, LTOK], F32)
    t1 = p.sb("rc_t1", [128, LTOK], F32)
    t2 = p.sb("rc_t2", [128, LTOK], F32)
    E = p.sb("rc_E", [128, LTOK], BF16)
    rmask = p.sb("rc_rmask", [128, LTOK], BF16)
    p.memset(rmask, 1.0, eng='pool')
    p.memset(rmask.rr("p (n t) -> p n t", t=128)[:, :, 0:1], 0.0, eng='pool')
    w2b = p.sb("rc_w2b", [64, 2, 128], BF16)
    p.dma(w2b, self.rw_w2[:, :, cc * 128:(cc + 1) * 128].rr("z r c -> r z c"), eng='pool')
    a2b = p.sb("rc_a2b", [64, 128], BF16)
    p.dma(a2b, self.rw_a2[:, cc * 128:(cc + 1) * 128], eng='pool')
    for (t0, n) in BLKS:
        ps = self.bank()
        p.mm(ps[:, 0:n], a2b, ta[0:64, t0:t0 + n])
        p.act(aT[:, t0:t0 + n], ps[:, 0:n], AF.Sigmoid, bias=self.col('a0')[:, cc:cc + 1])
    p.ts(t1, kT, kv[:, cc:cc + 1], ALU.mult)
    p.tt(E, t1, t1, ALU.mult, eng='pool')
    for (t0, n) in BLKS:
        ps = self.bank()
        p.mm(ps[:, 0:n], self.bones, E[:, t0:t0 + n])
        p.act(t2[:, t0:t0 + n], ps[:, 0:n], AF.Sqrt)
    p.ts(t2, t2, 1e-12, ALU.max)
    p.recip(t2, t2)
    p.tt(kap, t1, t2, ALU.mult)
    p.tt(bet, kap, aT, ALU.mult, eng='pool')
    p.ts(t1, aT, kv[:, 8 + cc:8 + cc + 1], ALU.mult, self.okv1[:, cc:cc + 1], ALU.add)
    p.tt(kT, kT, t1, ALU.mult)
    P3 = P.rr("p (n t) -> p n t", t=128)
    Ptot = P3[:, :, 127:128]
    Pb = Ptot.bcast([128, NT, 128])
    t13 = t1.rr("p (n t) -> p n t", t=128)
    t23 = t2.rr("p (n t) -> p n t", t=128)
    v3 = lambda x: x.rr("p (n t) -> p n t", t=128)
    for z in range(2):
        for (t0, n) in BLKS:
            ps = self.bank()
            p.mm(ps[:, 0:n], w2b[:, z, :], tw[z][0:64, t0:t0 + n])
            p.act(sig[:, t0:t0 + n], ps[:, 0:n], AF.Sigmoid, bias=self.col('w0')[:, z * 8 + cc:z * 8 + cc + 1])
        p.scan(P, rmask, sig, 0.0, ALU.mult, ALU.add)
        p.act(gam[z], Ptot.rr("p n o -> p (n o)"), AF.Exp, scale=-C0)
        if z == 0:
            G = P
            p.tt(t1, P, sig, ALU.subtract)
            p.tt(t23, P3, Pb, ALU.subtract)
        else:
            p.tt(t13, Pb, P3, ALU.subtract)
            p.tt(t2, sig, P, ALU.subtract)
            G = sig
            p.tt(sig, t1, sig, ALU.add, eng='pool')
        p.act(E, G, AF.Exp, scale=-C0)
        p.tt(KR[z][:, :, 1, :], v3(rT), v3(E), ALU.mult)
        p.act(E, t1, AF.Exp, scale=-C0)
        p.tt(KR[z][:, :, 0, :], v3(kap), v3(E), ALU.mult, eng='pool')
        p.act(E, G, AF.Exp, scale=C0)
        p.tt(khat[z], kT, E, ALU.mult)
        p.tt(bhat[z], bet, E, ALU.mult, eng='pool')
        p.act(E, t2, AF.Exp, scale=C0)
        p.tt(kT4[z], kT, E, ALU.mult)
        p.stt(nbT4[z], bet, -1.0, E, ALU.mult, ALU.mult)
    for g0 in range(0, NT, 8):
        n = min(8, NT - g0)
        ps = self.bank()
        psb = ps.bc(BF16)
        for i in range(n):
            p.tr(psb[:, i * 128:(i + 1) * 128], vT[:, (g0 + i) * 128:(g0 + i + 1) * 128], self.identb)
        p.copy(Vtok[:, g0:g0 + n, :], psb[:, 0:n * 128].rr("p (t c) -> p t c", c=128), eng='act')
    p.barrier()
    p.release(m3)
    NR = 8
    A4 = [p.sb("rs_A4%d" % i, [128, 4, 128], BF16) for i in range(NR)]
    SQ = [[p.sb("rs_SQ%d_%d" % (i, j), [128, 2, 128], BF16) for j in range(2)] for i in range(NR)]
    XXr = [p.sb("rs_XX%d" % i, [128, 2, 128], BF16) for i in range(NR)]
    Q0T = [p.sb("rs_Q0T%d" % i, [128, 128], BF16) for i in range(NR)]
    QM = [p.sb("rs_QM%d" % i, [128, 3, 128], BF16) for i in range(NR)]
    QMT = [p.sb("rs_QMT%d" % i, [128, 3, 128], BF16) for i in range(NR)]
    Y1r = [p.sb("rs_Y1%d" % i, [128, 128], BF16) for i in range(NR)]
    KB = [p.sb("rs_KB%d" % i, [128, 2, 128], BF16) for i in range(4)]
    Wb = [p.sb("rs_Wb%d" % i, [128, 64], BF16) for i in range(NR)]
    Ub = [p.sb("rs_Ub%d" % i, [128, 64], BF16) for i in range(NR)]
    H = [[p.sb("rs_H%d%d" % (z, hh), [128, 64], F32) for hh in range(2)] for z in range(2)]
    Hb = [[p.sb("rs_Hb%d%d" % (z, hh), [128, 64], BF16) for hh in range(2)] for z in range(2)]
    for z in range(2):
        for hh in range(2):
            p.memset(H[z][hh], 0.0, eng='pool')
            p.memset(Hb[z][hh], 0.0, eng='pool')
    units = [(step, z, hh) for step in range(NT) for z in range(2) for hh in range(2)]
    prep = {}

    def stage_a(u, bk):
        step, z, hh = units[u]
        tt = ORD[z][step]
        ts_ = slice(tt * 128, (tt + 1) * 128)
        pr = slice(hh * 64, (hh + 1) * 64)
        r = u % NR
        kb = KB[(u // 2) % 4]
        if hh == 0:
            kps, hk = p.palloc(1, bk)
            psb = kps.bc(BF16)
            p.tr(psb[:, 0:128], kT4[z][:, ts_], self.identb)
            p.tr(psb[:, 128:256], nbT4[z][:, ts_], self.identb)
        kr = KR[z][pr, tt].rr("p j t -> p (j t)")
        sc1, hsc1 = p.palloc(2, bk)
        p.mm(sc1[:, 0:256], bhat[z][pr, ts_], kr)
        q0, hq0 = p.palloc(1, bk)
        p.mm(q0[:, 0:128], KR[z][pr, tt, 0, :], bhat[z][pr, ts_])
        yield
        if hh == 0:
            p.copy(kb, psb[:, 0:256].rr("p (j c) -> p j c", j=2), eng='act')
            p.pfree(hk)
        p.tt(A4[r][:, 0:2, :], sc1.rr("p (j t) -> p j t", j=2), self.M4[z][:, 0:2, :], ALU.mult)
        p.pfree(hsc1)
        p.tt(Q0T[r], q0[:, 0:128], self.nstrictT[z], ALU.mult)
        p.pfree(hq0)
        sc2, hsc2 = p.palloc(2, bk)
        p.mm(sc2[:, 0:256], khat[z][pr, ts_], kr)
        yield
        p.tt(A4[r][:, 2:4, :], sc2.rr("p (j t) -> p j t", j=2), self.M4[z][:, 2:4, :], ALU.mult)
        p.pfree(hsc2)
        p.tt(QM[r], A4[r][:, 0:1, :].bcast([128, 3, 128]), self.masks3, ALU.mult, eng='pool')
        p.tt(QMT[r], Q0T[r].rr("p (o t) -> p o t", o=1).bcast([128, 3, 128]), self.masks3, ALU.mult, eng='pool')
        yield
        cq, ct = QM[r][:, 0, :], QMT[r][:, 0, :]
        xq, xt = cq, ct
        XX = XXr[r]
        XXf = XX.rr("p j t -> p (j t)")
        for lev in range(4):
            ps, h1 = p.palloc(2, bk)
            p.mm(ps[:, 0:128], ct, cq)
            p.mm(ps[:, 128:256], cq, ct)
            yield
            nxt = SQ[r][lev % 2]
            p.copy(nxt.rr("p j t -> p (j t)"), ps[:, 0:256], eng='act')
            p.pfree(h1)
            yield
            cq, ct = nxt[:, 0, :], nxt[:, 1, :]
            ps2, h2 = p.palloc(2, bk)
            p.mm(ps2[:, 0:128], ct, xq, start=True, stop=False)
            p.mm(ps2[:, 0:128], ct, self.identb, start=False, stop=True)
            p.mm(ps2[:, 128:256], xq, ct, start=True, stop=False)
            p.mm(ps2[:, 128:256], cq, self.identb, start=False, stop=True)
            yield
            if lev == 0:
                p.tt(XX[:, 0, :], xq, ps2[:, 0:128], ALU.add)
                p.tt(XX[:, 1, :], xt, ps2[:, 128:256], ALU.add)
            else:
                p.tt(XXf, XXf, ps2[:, 0:256], ALU.add)
            p.pfree(h2)
            xq, xt = XX[:, 0, :], XX[:, 1, :]
            yield
        for lvl in (1, 2):
            C, CT = QM[r][:, lvl, :], QMT[r][:, lvl, :]
            ps, h1 = p.palloc(1, bk)
            p.mm(ps[:, 0:128], CT, xq, start=True, stop=False)
            p.mm(ps[:, 0:128], CT, self.identb, start=False, stop=True)
            yield
            p.copy(Y1r[r], ps[:, 0:128], eng='act')
            p.pfree(h1)
            yield
            ps2, h2 = p.palloc(2, bk)
            p.mm(ps2[:, 0:128], xt, Y1r[r], start=True, stop=False)
            p.mm(ps2[:, 0:128], self.identb, Y1r[r], start=False, stop=True)
            if lvl == 1:
                p.mm(ps2[:, 128:256], Y1r[r], xt, start=True, stop=False)
                p.mm(ps2[:, 128:256], Y1r[r], self.identb, start=False, stop=True)
            yield
            if lvl == 1:
                p.tt(XXf, XXf, ps2[:, 0:256], ALU.add)
            else:
                p.tt(XX[:, 0, :], XX[:, 0, :], ps2[:, 0:128], ALU.add)
            p.pfree(h2)
            yield
        prep[u] = (r, kb, XX[:, 0, :])

    def stage_b(u, bk):
        step, z, hh = units[u]
        tt = ORD[z][step]
        ts_ = slice(tt * 128, (tt + 1) * 128)
        pr = slice(hh * 64, (hh + 1) * 64)
        cs = slice(hh * 64, (hh + 1) * 64)
        r, kb, xfin = prep.pop(u)
        a4 = A4[r]
        hb = Hb[z][hh]
        vt = Vtok[:, tt, cs]
        w, hw = p.palloc(1, bk)
        p.mm(w[:, 0:64], KR[z][pr, tt, 0, :], hb[pr, :], start=True, stop=False)
        p.mm(w[:, 0:64], a4[:, 2, :], vt, start=False, stop=True)
        yield
        p.copy(Wb[r], w[:, 0:64], eng='act')
        p.pfree(hw)
        yield
        uu, hu_ = p.palloc(1, bk)
        p.mm(uu[:, 0:64], xfin, Wb[r], start=True, stop=False)
        p.mm(uu[:, 0:64], self.identb, Wb[r], start=False, stop=True)
        yield
        p.copy(Ub[r], uu[:, 0:64], eng='act')
        p.pfree(hu_)
        yield
        yb, hy = p.palloc(1, bk)
        p.mm(yb[:, 0:64], KR[z][pr, tt, 1, :], hb[pr, :], start=True, stop=False)
        p.mm(yb[:, 0:64], a4[:, 3, :], vt, start=False, stop=False)
        p.mm(yb[:, 0:64], a4[:, 1, :], Ub[r], start=False, stop=True)
        if step < NT - 1:
            hu, hh_ = p.palloc(1, bk)
            p.mm(hu[pr, 0:64], kb[:, 0, cs], vt, start=True, stop=False)
            p.mm(hu[pr, 0:64], kb[:, 1, cs], Ub[r], start=False, stop=True)
        yield
        if step < NT - 1:
            p.stt(H[z][hh][pr, :], H[z][hh][pr, :], gam[z][pr, tt:tt + 1], hu[pr, 0:64], ALU.mult, ALU.add)
            p.pfree(hh_)
            p.copy(hb[pr, :], H[z][hh][pr, :], eng='pool')
        yv = y[:, tt, cs].on(ybufs[tt][hh])
        p.tt(yv, yv, yb[:, 0:64], ALU.add)
        p.pfree(hy)
        yield

    def run_interleaved(gens):
        active = list(gens)
        while active:
            nxt_ = []
            for g in active:
                try:
                    next(g)
                    nxt_.append(g)
                except StopIteration:
                    pass
            active = nxt_

    for step in range(-1, NT):
        gens = []
        if step + 1 < NT:
            gens += [stage_a((step + 1) * 4 + j, j) for j in range(4)]
        if step >= 0:
            gens += [stage_b(step * 4 + j, 4 + j) for j in range(4)]
        run_interleaved(gens)
    if getattr(self, 'dbg', False):
        for tt in range(NT):
            for hh in range(2):
                p.tt(y[:, tt, hh * 64:(hh + 1) * 64], y[:, tt, hh * 64:(hh + 1) * 64].on(ybufs[tt][hh]), y[:, tt, hh * 64:(hh + 1) * 64].on(ybufs[tt][hh]), ALU.max)
        p.dma(self.dbg_y[cc], y.rr("p t c -> p (t c)"))
    g2b = p.sb("rp_g2b", [128, 128], BF16)
    p.dma(g2b, self.rw_g2[:, cc * 128:(cc + 1) * 128], eng='pool')
    s1 = p.sb("rp_s1", [128, 36], F32)
    s2 = p.sb("rp_s2", [128, 36], F32)
    sq = p.sb("rp_sq", [128, 36, 64], F32)
    yn = p.sb("rp_yn", [128, NT, 128], BF16)
    lnT = p.sb("rp_lnT", [128, LTOK], F32)
    prod = p.sb("rp_prod", [128, LTOK], BF16)
    y3 = y.rr("p t (h c) -> p (t h) c", c=64)
    for tt in range(NT):
        for hh in range(2):
            yv = y[:, tt, hh * 64:(hh + 1) * 64].on(ybufs[tt][hh])
            p.tt(sq[:, tt * 2 + hh, :], yv, yv, ALU.mult)
    p.reduce(s2, sq, ALU.add)
    p.reduce(s1, y3, ALU.add)
    p.ts(s1, s1, 1.0 / 64, ALU.mult)
    p.tt(sq[:, :, 0], s1, s1, ALU.mult)
    p.stt(s2, s2, 1.0 / 64, sq[:, :, 0], ALU.mult, ALU.subtract)
    p.act(s2, s2, AF.Sqrt, bias=RW_LN_EPS)
    p.recip(s2, s2)
    yn3 = yn.rr("p t (h c) -> p (t h) c", c=64)
    p.tt(sq, y3, s1.rr("p (n o) -> p n o", o=1).bcast([128, 36, 64]), ALU.subtract)
    p.tt(yn3, sq, s2.rr("p (n o) -> p n o", o=1).bcast([128, 36, 64]), ALU.mult)
    lnx = self.col('lnx')
    for g0 in range(0, NT, 8):
        n = min(8, NT - g0)
        ps = self.bank()
        psb = ps.bc(BF16)
        for i in range(n):
            p.tr(psb[:, i * 128:(i + 1) * 128], yn[:, g0 + i, :], self.identb)
        p.ts(lnT[:, g0 * 128:(g0 + n) * 128], psb[:, 0:n * 128], lnx[:, cc:cc + 1], ALU.mult, lnx[:, 8 + cc:8 + cc + 1], ALU.add)
    p.stt(prod, rT, kv[:, 16 + cc:16 + cc + 1], kT, ALU.mult, ALU.mult)
    stage = [p.sb("rp_stage%d" % i, [128, 512], BF16) for i in range(2)]
    tmpb = [p.sb("rp_tmp%d" % i, [128, 512], F32) for i in range(2)]
    for i, (t0, n) in enumerate(BLKS):
        psA = self.bank()
        p.mm(psA[:, 0:n], self.bones, prod[:, t0:t0 + n])
        psG = self.bank()
        p.mm(psG[:, 0:n], g2b, tg[:, t0:t0 + n])
        tb = tmpb[i % 2]
        p.tt(tb[:, 0:n], psA[:, 0:n], vT[:, t0:t0 + n], ALU.mult)
        p.tt(tb[:, 0:n], tb[:, 0:n], lnT[:, t0:t0 + n], ALU.add, eng='pool')
        sg = stage[i % 2]
        p.tt(sg[:, 0:n], psG[:, 0:n], tb[:, 0:n], ALU.mult)
        p.dma(scr['o'][cc, :, t0:t0 + n].on(scr['obuf'][cc]), sg[:, 0:n])
    p.barrier()
    p.release(m)


def _rw_mixer(self, b, xi, xo, last=True):
    p = self.p
    l = 1
    m = p.mark()
    tw = [p.sb("rw_tw%d" % z, [128, LTOK], BF16) for z in range(2)]
    ta = p.sb("rw_ta", [128, LTOK], BF16)
    tg = p.sb("rw_tg", [128, LTOK], BF16)
    mh = p.mark()
    hT = p.sb("rw_hT", [128, 8, LTOK], BF16)
    self.prenorm(b, xi, l, 0, hT)
    self.rw_phase1(b, hT, tw, ta, tg)
    p.barrier()
    p.release(mh)
    for cc in range(8):
        self.rw_chunk(b, cc, tw, ta, tg, last)
    p.barrier()
    p.release(m)
    m = p.mark()
    oT = p.sb("rw_oT", [128, 8, LTOK], BF16)
    for c in range(8):
        p.dma(oT[:, c, :], self.scr[b]['o'][c].on(self.scr[b]['obuf'][c]))
    wo = p.sb("rw_wo", [128, 8, D], BF16)
    for c in range(8):
        p.dma(wo[:, c, :], self.w_o[c * 128:(c + 1) * 128, :], eng='pool')
    G = [p.sb("rwo_G%d" % i, [128, D], F32) for i in range(2)]
    gtmp = p.sb("rwo_gtmp", [128, 128], F32)
    st = self.res_state(2)
    self.gate_row(G[0], l, 2, b, gtmp)
    if not last:
        self.gate_row(G[1], l, 2, 4, gtmp)
    for tt in (range(2, NT) if last else range(NT)):
        yb = [self.bank(), self.bank()]
        for h in range(2):
            for c in range(8):
                p.mm(yb[h], oT[:, c, tt * 128:(tt + 1) * 128], wo[:, c, h * 512:(h + 1) * 512], start=(c == 0), stop=(c == 7))
        self.residual_tile(b, tt, xi, xo, yb, G[1] if tt < 2 else G[0], st[tt % 2])
    p.barrier()
    p.release(m)


KE.rw_chunk = _rw_chunk
KE.rw_mixer = _rw_mixer


_NC_CACHE = {}
W_NAMES = ('w_mod', 'ffn_w_up', 'ffn_w_down')
EV_NAMES = ('ev_w_in', 'ev_w_out', 'ev_gla_w2')
RW_NAMES = ('rw_w_rkv', 'rw_w_o', 'rw_w1', 'rw_w2', 'rw_a1', 'rw_a2', 'rw_g1', 'rw_g2')


def build_program(NB, pk):
    k = KE(NB, pk.cols, pk.n, dbg=False)
    k.setup_even()
    k.setup_rw()
    k.mod_stage(0)
    for b in range(NB):
        hT, mixT, m = k.even_mixer(b, 0, 1)
        k.gla(b, hT, mixT)
        k.even_out(b, 0, 1, hT, mixT, m)
        k.ffn(b, 0, 1, 2, do_ctx=True)
    k.setup_rw_consts()
    k.mod_stage(1)
    for b in range(NB):
        k.rw_mixer(b, 2, 3, last=True)
        k.ffn(b, 1, 3, 4, do_ctx=False)
    return k.p.finish()


def kernel(**inp):
    inp = {k_: np.asarray(v_, dtype=np.float32) for k_, v_ in inp.items()}
    NCORE = 8
    B = inp['x'].shape[0]
    NB = B // NCORE
    pk = host_small(inp)
    host_small_even(pk, inp)
    host_small_rw(pk, inp)
    small = pk.pack()
    rows = np.zeros((16, D), np.float32)
    rows[0:8] = inp['norm_g'].reshape(8, D)
    rows[8, :16] = inp['ev_b_gates'][0]
    nc = build_program(NB, pk)
    shared = {"small": small, "rows": rows}
    for nm in W_NAMES:
        shared[nm] = np.ascontiguousarray(inp[nm])
    for nm in EV_NAMES + RW_NAMES:
        shared[nm] = np.ascontiguousarray(inp[nm][0])
    in_maps = []
    for c in range(NCORE):
        sl = slice(c * NB, (c + 1) * NB)
        cc = np.zeros((6, D), np.float32)
        cc[0:NB] = inp['c'][sl]
        cc[4] = inp['c_ctx']
        ccT = np.ascontiguousarray(cc.reshape(6, 8, 128).transpose(2, 1, 0).reshape(128, 48))
        xcat = np.ascontiguousarray(np.concatenate([inp['ctx'][sl], inp['x'][sl]], axis=1))
        d = dict(shared)
        d["xcat"] = xcat
        d["ccT"] = ccT
        in_maps.append(d)
    res = run_bass_kernel_spmd(nc, in_maps, core_ids=list(range(NCORE)))
    out = np.concatenate([np.asarray(r["out"]) for r in res.results], axis=0)
    return out.astype(np.float32)
```

```python
import numpy as np
import concourse.bass as bass
import concourse.mybir as mybir
from concourse.bass_utils import run_bass_kernel_spmd

F32 = mybir.dt.float32
BF16 = mybir.dt.bfloat16
AF = mybir.ActivationFunctionType
ALU = mybir.AluOpType
AX = mybir.AxisListType
ENG = ('sp', 'act', 'dve', 'pool', 'pe')
DSZ = {F32: 4, BF16: 2}
SAME_SYNC = True
NDMASEM = 8


class Buf:
    __slots__ = ('name', 'w', 'r', 'excl')

    def __init__(self, name, excl=False):
        self.name = name
        self.w = None
        self.r = {}
        self.excl = excl


class V:
    __slots__ = ('ap', 'buf')

    def __init__(self, ap, buf):
        self.ap = ap
        self.buf = buf

    def __getitem__(self, idx):
        return V(self.ap[idx], self.buf)

    def rr(self, pat, **kw):
        return V(self.ap.rearrange(pat, **kw), self.buf)

    def bc(self, dt):
        return V(self.ap.bitcast(dt), self.buf)

    def on(self, buf):
        return V(self.ap, buf)

    def bcast(self, shape):
        return V(self.ap.broadcast_to(list(shape)), self.buf)

    def pbcast(self, n):
        return V(self.ap.partition_broadcast(n), self.buf)

    @property
    def shape(self):
        return tuple(self.ap.shape)


class Prog:
    def __init__(self):
        nc = self.nc = bass.Bass("TRN2", target_bir_lowering=False)
        self.q = {e: [] for e in ENG}
        self.cnt = {e: 0 for e in ENG}
        self.sem = {e: nc.alloc_semaphore('sem_' + e) for e in ENG}
        self.seen = {e: {} for e in ENG}
        self.dsem = [nc.alloc_semaphore('dsem%d' % i) for i in range(NDMASEM)]
        self.dval = [0] * NDMASEM
        self.dnext = 0
        self.sb_off = 16512
        self.sb_max = 0
        self.nalloc = 0
        self.ninst = 0

    def dram(self, name, shape, dt, kind="Internal"):
        t = self.nc.dram_tensor(name, list(shape), dt, kind=kind)
        return V(t.ap(), Buf(name))

    def sb(self, name, shape, dt, nbuf=None):
        per = int(np.prod(shape[1:])) * DSZ[dt]
        per = (per + 63) // 64 * 64
        off = self.sb_off
        self.sb_off += per
        self.sb_max = max(self.sb_max, self.sb_off)
        assert self.sb_off <= 229344, (name, self.sb_off)
        self.nalloc += 1
        t = self.nc.alloc_sbuf_tensor_at("%s_%d" % (name, self.nalloc), list(shape), dt, offset=off)
        return V(t.ap(), Buf(name))

    def mark(self):
        return self.sb_off

    def release(self, m):
        self.sb_off = m

    def psum_banks(self):
        banks = []
        self.pslot_bufs = []
        self.pslot_free = [True] * 32
        for i in range(8):
            t = self.nc.alloc_psum_tensor("psb%d" % i, [128, 512], F32)
            bl = Buf("psb%d" % i, excl=True)
            self.pslot_bufs.append(bl)
            banks.append(V(t.ap(), bl))
        self.pbanks = banks
        return banks

    def palloc(self, nq, bank=None):
        banks = range(8) if bank is None else (bank,)
        for bk in banks:
            for q0 in range(0, 4, nq):
                if all(self.pslot_free[bk * 4 + q0 + j] for j in range(nq)):
                    for j in range(nq):
                        self.pslot_free[bk * 4 + q0 + j] = False
                    v = V(self.pbanks[bk].ap[:, q0 * 128:(q0 + nq) * 128], self.pslot_bufs[bk])
                    return v, (bk, q0, nq)
        raise RuntimeError("out of PSUM slots")

    def pfree(self, h):
        bk, q0, nq = h
        for j in range(nq):
            self.pslot_free[bk * 4 + q0 + j] = True

    def _emit(self, eng, fn, reads, writes, dma=False):
        waits = {}

        def need(tok, waw_pe=False):
            if tok is None:
                return
            sem, val, te = tok
            if te == eng and not dma:
                if eng == 'pe' or not SAME_SYNC:
                    return
            k = id(sem)
            if k not in waits or waits[k][1] < val:
                waits[k] = (sem, val)

        ex = [b for b in reads if b.excl and b not in writes]
        if ex:
            reads = [b for b in reads if not b.excl]
            writes = list(writes) + ex
        for b in reads:
            need(b.w)
        for b in writes:
            need(b.w)
            for t in b.r.values():
                need(t)
        if dma:
            k = self.dnext
            self.dnext = (self.dnext + 1) % NDMASEM
            if self.dval[k] > 0:
                need((self.dsem[k], self.dval[k], 'dma'))
            self.dval[k] += 16
            tok = (self.dsem[k], self.dval[k], 'dma')
            inc = (self.dsem[k], 16)
        else:
            self.cnt[eng] += 1
            tok = (self.sem[eng], self.cnt[eng], eng)
            inc = (self.sem[eng], 1)
        seen = self.seen[eng]
        final = []
        for k, (sem, val) in waits.items():
            if seen.get(k, 0) < val:
                seen[k] = val
                final.append((sem, val))
        for b in reads:
            b.r[id(tok[0])] = tok
        for b in writes:
            b.w = tok
            b.r = {}
        self.q[eng].append((final, fn, inc))
        self.ninst += 1

    def barrier(self):
        toks = [(self.sem[e], self.cnt[e]) for e in ENG if self.cnt[e] > 0]
        toks += [(self.dsem[k], self.dval[k]) for k in range(NDMASEM) if self.dval[k] > 0]
        for e in ENG:
            final = []
            for sem, val in toks:
                if sem is self.sem[e]:
                    continue
                if self.seen[e].get(id(sem), 0) < val:
                    self.seen[e][id(sem)] = val
                    final.append((sem, val))
            if final:
                self.q[e].append((final, None, None))

    def finish(self):
        self.barrier()
        nc = self.nc
        q = self.q

        def replay(e, lst):
            for waits, fn, inc in lst:
                for sem, val in waits:
                    e.wait_ge(sem, val)
                if fn is not None:
                    ins = fn(e)
                    ins.then_inc(inc[0], inc[1])

        with nc.Block() as block:
            @block.sync
            def _(e):
                replay(e, q['sp'])

            @block.scalar
            def _(e):
                replay(e, q['act'])

            @block.vector
            def _(e):
                replay(e, q['dve'])

            @block.gpsimd
            def _(e):
                replay(e, q['pool'])

            @block.tensor
            def _(e):
                replay(e, q['pe'])
        return nc

    @staticmethod
    def _a(x):
        return x.ap if isinstance(x, V) else x

    @staticmethod
    def _bufs(*xs):
        out = []
        for x in xs:
            if isinstance(x, V):
                bl = x.buf if isinstance(x.buf, (list, tuple)) else (x.buf,)
                for b in bl:
                    if b not in out:
                        out.append(b)
        return out

    def dma(self, out, in_, eng='sp', **kw):
        o, i = out.ap, in_.ap
        self._emit(eng, lambda e: e.dma_start(out=o, in_=i, **kw), self._bufs(in_), self._bufs(out), dma=True)

    def mm(self, out, lhsT, rhs, start=True, stop=True):
        o, l, r = out.ap, lhsT.ap, rhs.ap
        self._emit('pe', lambda e: e.matmul(o, lhsT=l, rhs=r, start=start, stop=stop),
                   self._bufs(lhsT, rhs), self._bufs(out))

    def tr(self, out, in_, ident):
        o, i, d = out.ap, in_.ap, ident.ap
        self._emit('pe', lambda e: e.transpose(o, i, d), self._bufs(in_, ident), self._bufs(out))

    def act(self, out, in_, func, bias=None, scale=None, accum=None):
        kw = {}
        if bias is not None:
            kw['bias'] = self._a(bias)
        if scale is not None:
            kw['scale'] = self._a(scale)
        if accum is not None:
            kw['accum_out'] = accum.ap
        o, i = out.ap, in_.ap
        self._emit('act', lambda e: e.activation(out=o, in_=i, func=func, **kw),
                   self._bufs(in_, bias, scale), self._bufs(out, accum))

    def tt(self, out, in0, in1, op, eng='dve'):
        o, a, b = out.ap, in0.ap, in1.ap
        self._emit(eng, lambda e: e.tensor_tensor(out=o, in0=a, in1=b, op=op),
                   self._bufs(in0, in1), self._bufs(out))

    def ts(self, out, in0, s1, op0, s2=None, op1=None, eng='dve', accum=None):
        o, a = out.ap, in0.ap
        a1, a2 = self._a(s1), self._a(s2)
        kw = {}
        if op1 is not None:
            kw['op1'] = op1
        if accum is not None:
            kw['accum_out'] = accum.ap
        self._emit(eng, lambda e: e.tensor_scalar(out=o, in0=a, scalar1=a1, scalar2=a2, op0=op0, **kw),
                   self._bufs(in0, s1, s2), self._bufs(out, accum))

    def stt(self, out, in0, scalar, in1, op0, op1, eng='dve'):
        o, a, b = out.ap, in0.ap, in1.ap
        s = self._a(scalar)
        self._emit(eng, lambda e: e.scalar_tensor_tensor(out=o, in0=a, scalar=s, in1=b, op0=op0, op1=op1),
                   self._bufs(in0, scalar, in1), self._bufs(out))

    def copy(self, out, in_, eng='dve'):
        o, i = out.ap, in_.ap
        if eng == 'act':
            self._emit('act', lambda e: e.activation(out=o, in_=i, func=AF.Copy), self._bufs(in_), self._bufs(out))
        else:
            self._emit(eng, lambda e: e.tensor_copy(out=o, in_=i), self._bufs(in_), self._bufs(out))

    def memset(self, out, val, eng='dve'):
        o = out.ap
        self._emit(eng, lambda e: e.memset(o, val), [], self._bufs(out))

    def reduce(self, out, in_, op, axis=AX.X, eng='dve'):
        o, i = out.ap, in_.ap
        self._emit(eng, lambda e: e.tensor_reduce(out=o, in_=i, axis=axis, op=op), self._bufs(in_), self._bufs(out))

    def recip(self, out, in_):
        o, i = out.ap, in_.ap
        self._emit('dve', lambda e: e.reciprocal(out=o, in_=i), self._bufs(in_), self._bufs(out))

    def aselect(self, out, in_, pattern, cmp, fill, base, cm):
        o, i = out.ap, in_.ap
        self._emit('pool', lambda e: e.affine_select(out=o, in_=i, pattern=pattern, compare_op=cmp, fill=fill,
                                                     base=base, channel_multiplier=cm),
                   self._bufs(in_), self._bufs(out))

    def scan(self, out, d0, d1, init, op0, op1):
        o, a, b = out.ap, d0.ap, d1.ap
        self._emit('dve', lambda e: e.tensor_tensor_scan(out=o, data0=a, data1=b, initial=init, op0=op0, op1=op1),
                   self._bufs(d0, d1), self._bufs(out))


D = 1024
NT = 18
LTOK = 2304
EPS = 1e-6
DFF = 2816
NJ = 22


class Packer:
    def __init__(self):
        self.cols = {}
        self.n = 0
        self.arrs = []

    def add(self, name, arr):
        arr = np.ascontiguousarray(arr, dtype=np.float32).reshape(128, -1)
        self.cols[name] = (self.n, arr.shape[1])
        self.n += arr.shape[1]
        self.arrs.append(arr)

    def pack(self):
        return np.ascontiguousarray(np.concatenate(self.arrs, axis=1))


def fm(v):
    v = np.asarray(v, dtype=np.float32)
    lead = v.shape[:-1]
    c = v.shape[-1] // 128
    v = v.reshape(lead + (c, 128))
    return np.moveaxis(v, -1, 0).reshape(128, -1)


def host_small(inp, layer_cols_only=False):
    pk = Packer()
    for l in range(2):
        pk.add('bmod%d' % l, fm(inp['b_mod'][l]))
        pk.add('ng%d' % l, fm(inp['norm_g'][l]))
        pk.add('cw%d' % l, np.moveaxis(inp['ffn_conv_w'][l].reshape(9, NJ, 128), 2, 0).transpose(0, 2, 1).reshape(128, NJ * 9))
        pk.add('cb%d' % l, fm(inp['ffn_conv_b'][l]))
    return pk


class K:
    def __init__(self, NB, small_cols, nsmall, dbg=False):
        self.NB = NB
        self.dbg = dbg
        p = self.p = Prog()
        self.sc = small_cols
        self.X0 = p.dram("xcat", [NB, LTOK, D], F32, kind="ExternalInput")
        self.ccT_d = p.dram("ccT", [128, 8 * 6], F32, kind="ExternalInput")
        self.small_d = p.dram("small", [128, nsmall], F32, kind="ExternalInput")
        self.rows_d = p.dram("rows", [16, D], F32, kind="ExternalInput")
        self.w_mod = p.dram("w_mod", [2, D, 6 * D], F32, kind="ExternalInput")
        self.w_up = p.dram("ffn_w_up", [2, D, 2 * DFF], F32, kind="ExternalInput")
        self.w_down = p.dram("ffn_w_down", [2, DFF, D], F32, kind="ExternalInput")
        self.Xs = [self.X0]
        for i in range(1, 4):
            self.Xs.append(p.dram("xs%d" % i, [NB, LTOK, D], F32, kind="ExternalOutput" if dbg else "Internal"))
        self.out = p.dram("out", [NB, 2048, D], F32, kind="ExternalOutput")
        self.xbufs = [[[Buf("x%d_%d_%d" % (i, b, t)) for t in range(NT)] for b in range(NB)] for i in range(5)]
        self.banks = p.psum_banks()
        self.nbank = 0
        self.small = p.sb("small", [128, nsmall], F32)
        p.dma(self.small, self.small_d)
        self.identf = p.sb("identf", [128, 128], F32)
        p.memset(self.identf, 1.0, eng='pool')
        p.aselect(self.identf, self.identf, [[-1, 128]], ALU.is_equal, 0.0, 0, 1)
        self.identb = p.sb("identb", [128, 128], BF16)
        p.copy(self.identb, self.identf, eng='pool')
        self.scT = p.sb("scT", [128, 48], F32)
        p.dma(self.scT, self.ccT_d)
        p.act(self.scT, self.scT, AF.Silu)
        self.modT = p.sb("modT", [128, 48 * 6], F32)
        self.A1 = p.sb("A1", [128, 48], F32)
        self.A2 = p.sb("A2", [128, 48], F32)
        self.base_mark = p.mark()

    def bank(self):
        b = self.banks[self.nbank]
        self.nbank = (self.nbank + 1) % 8
        return b

    def col(self, name, a=None, b=None):
        o, w = self.sc[name]
        if a is None:
            return self.small[:, o:o + w]
        return self.small[:, o + a:o + b]

    def mod_stage(self, l):
        p = self.p
        m = p.mark()
        wst = [p.sb("wmod_st%d" % i, [128, 8, 512], F32) for i in range(2)]
        ps = self.bank()
        for blk in range(12):
            w = wst[blk % 2]
            p.dma(w, self.w_mod[l, :, blk * 512:(blk + 1) * 512].rr("(c p) n -> p c n", p=128))
            for f in range(4):
                fc = blk * 4 + f
                for kc in range(8):
                    p.mm(ps[:, fc * 6:(fc + 1) * 6], w[:, kc, f * 128:(f + 1) * 128], self.scT[:, kc * 6:(kc + 1) * 6],
                         start=(kc == 0), stop=(kc == 7))
        p.tt(self.modT.rr("p (c r) -> p c r", r=6), ps[:, 0:288].rr("p (c r) -> p c r", r=6),
             self.col('bmod%d' % l).rr("p (c o) -> p c o", o=1).bcast([128, 48, 6]), ALU.add)
        mv = self.modT.rr("p (i c r) -> p i c r", i=6, c=8)
        ng = self.col('ng%d' % l).rr("p (i c o) -> p i c o", i=4, o=1)
        for A, mi, gi in ((self.A1, 1, 0), (self.A2, 4, 2)):
            Av = A.rr("p (c r) -> p c r", r=6)
            p.ts(Av, mv[:, mi], 1.0, ALU.add)
            p.tt(Av, Av, ng[:, gi].bcast([128, 8, 6]), ALU.mult)
        p.barrier()
        p.release(m)

    def modvec(self, l, which):
        mv = self.modT.rr("p (i c r) -> p i c r", i=6, c=8)
        if which == 0:
            return self.A1.rr("p (c r) -> p c r", r=6), mv[:, 0]
        return self.A2.rr("p (c r) -> p c r", r=6), mv[:, 3]

    def gate_row(self, dst, l, gi, row, tmp):
        p = self.p
        mv = self.modT.rr("p (i c r) -> p i c r", i=6, c=8)
        nrow = l * 4 + (1 if gi == 2 else 3)
        p.dma(dst, self.rows_d[nrow:nrow + 1, :].pbcast(128))
        for half in range(2):
            ps = self.bank()
            for cc in range(4):
                c = half * 4 + cc
                p.copy(tmp, mv[:, gi, c, row:row + 1].bcast([128, 128]))
                p.mm(ps[:, cc * 128:(cc + 1) * 128], tmp, self.identf)
            p.tt(dst[:, half * 512:(half + 1) * 512], dst[:, half * 512:(half + 1) * 512], ps, ALU.mult)

    def prenorm(self, b, xi, l, which, hT, tiles=range(NT)):
        p = self.p
        A, Bv = self.modvec(l, which)
        m = p.mark()
        xt = [p.sb("pn_x%d" % i, [128, D], F32) for i in range(2)]
        junk = p.sb("pn_junk", [128, D], BF16)
        xn = [p.sb("pn_xn%d" % i, [128, D], BF16) for i in range(2)]
        ss = [p.sb("pn_ss%d" % i, [128, 1], F32) for i in range(2)]
        tmp = [p.sb("pn_tmp%d" % i, [128, 8, 128], F32) for i in range(2)]
        for n, tt in enumerate(tiles):
            x = xt[n % 2]
            s = ss[n % 2]
            p.dma(x, self.Xs[xi][b, tt * 128:(tt + 1) * 128, :].on(self.xbufs[xi][b][tt]))
            p.act(junk, x, AF.Square, accum=s)
            p.act(s, s, AF.Sqrt, bias=EPS, scale=1.0 / D)
            p.recip(s, s)
            p.act(xn[n % 2], x, AF.Identity, scale=s)
            ps = self.bank()
            psb = ps.bc(BF16)
            for c in range(8):
                p.tr(psb[:, c * 128:(c + 1) * 128], xn[n % 2][:, c * 128:(c + 1) * 128], self.identb)
            row = 4 if tt < 2 else b
            p.tt(tmp[n % 2], psb.rr("p (c t) -> p c t", c=8), A[:, :, row:row + 1].bcast([128, 8, 128]), ALU.mult)
            p.tt(hT[:, :, tt * 128:(tt + 1) * 128], tmp[n % 2], Bv[:, :, row:row + 1].bcast([128, 8, 128]), ALU.add,
                 eng='pool')
        p.barrier()
        p.release(m)

    def residual_tile(self, b, tt, xi, xo, ybanks, G, st):
        p = self.p
        x, junk, ss2, s, tmp = st
        p.dma(x, self.Xs[xi][b, tt * 128:(tt + 1) * 128, :].on(self.xbufs[xi][b][tt]))
        for h in range(2):
            p.act(junk, ybanks[h], AF.Square, accum=ss2[:, h:h + 1])
        p.tt(s, ss2[:, 0:1], ss2[:, 1:2], ALU.add)
        p.act(s, s, AF.Sqrt, bias=EPS, scale=1.0 / D)
        p.recip(s, s)
        for h in range(2):
            p.stt(tmp[:, h * 512:(h + 1) * 512], ybanks[h], s, G[:, h * 512:(h + 1) * 512], ALU.mult, ALU.mult)
        p.tt(tmp, tmp, x, ALU.add, eng='pool')
        if xo == 4:
            dst = self.out[b, (tt - 2) * 128:(tt - 1) * 128, :]
        else:
            dst = self.Xs[xo][b, tt * 128:(tt + 1) * 128, :]
        p.dma(dst.on(self.xbufs[xo][b][tt]), tmp)

    def res_state(self, n=2):
        p = self.p
        return [(p.sb("rs_x%d" % i, [128, D], F32), p.sb("rs_junk%d" % i, [128, 512], BF16),
                 p.sb("rs_ss2%d" % i, [128, 2], F32), p.sb("rs_s%d" % i, [128, 1], F32),
                 p.sb("rs_tmp%d" % i, [128, D], F32)) for i in range(n)]

    def ffn(self, b, l, xi, xo, do_ctx=True):
        p = self.p
        m = p.mark()
        hT = p.sb("ffn_hT", [128, 8, LTOK], BF16)
        self.prenorm(b, xi, l, 1, hT, tiles=range(NT) if do_ctx else range(2, NT))
        wdown = p.sb("ffn_wdown", [128, NJ, D], BF16)
        for j in range(NJ):
            p.dma(wdown[:, j, :], self.w_down[l, j * 128:(j + 1) * 128, :], eng='pool')
        actT = p.sb("ffn_actT", [128, NJ, 1280], BF16)
        wg = [p.sb("ffn_wg%d" % i, [128, 8, 128], BF16) for i in range(2)]
        wu = [p.sb("ffn_wu%d" % i, [128, 8, 128], BF16) for i in range(2)]
        gpad = [p.sb("ffn_gpad%d" % i, [128, 18, 66], BF16) for i in range(2)]
        cpad = p.sb("ffn_cpad", [128, 258], BF16)
        gg = [p.sb("ffn_gg%d" % i, [128, 512], BF16) for i in range(2)]
        diag = [p.sb("ffn_diag%d" % i, [128, 9, 128], BF16) for i in range(2)]
        G = [p.sb("ffn_G%d" % i, [128, D], F32) for i in range(2)]
        gtmp = p.sb("ffn_gtmp", [128, 128], F32)
        st = self.res_state(2)
        for g in gpad:
            p.memset(g, 0.0, eng='pool')
        p.memset(cpad, 0.0, eng='pool')
        self.gate_row(G[0], l, 5, b, gtmp)
        if do_ctx:
            self.gate_row(G[1], l, 5, 4, gtmp)
        cw = self.col('cw%d' % l)
        cb = self.col('cb%d' % l)
        nn = 0
        for seg in range(2):
            r0 = 16 * seg
            g0 = 0 if seg == 0 else 15
            prow0 = 1 if seg == 0 else 0
            gp = gpad[seg]
            with_ctx = (seg == 0 and do_ctx)
            for j in range(NJ):
                wgj, wuj = wg[j % 2], wu[j % 2]
                p.dma(wgj, self.w_up[l, :, j * 128:(j + 1) * 128].rr("(c p) n -> p c n", p=128), eng='pool')
                p.dma(wuj, self.w_up[l, :, DFF + j * 128:DFF + (j + 1) * 128].rr("(c p) n -> p c n", p=128), eng='pool')
                dg = diag[j % 2]
                p.tt(dg, self.identb.rr("p (o t) -> p o t", o=1).bcast([128, 9, 128]),
                     cw[:, j * 9:(j + 1) * 9].rr("p (t o) -> p t o", o=1).bcast([128, 9, 128]), ALU.mult, eng='pool')
                tok0 = 256 + g0 * 64
                for (o, n) in ((0, 512), (512, 512), (1024, 64)):
                    ps = self.bank()
                    for kc in range(8):
                        p.mm(ps[:, 0:n], wgj[:, kc, :], hT[:, kc, tok0 + o:tok0 + o + n], start=(kc == 0), stop=(kc == 7))
                    pr = prow0 + o // 64
                    p.copy(gp[:, pr:pr + n // 64, 1:65], ps[:, 0:n].rr("p (r c) -> p r c", c=64), eng='act')
                for blk in range(2):
                    ps = self.bank()
                    for t in range(9):
                        dy, dx = t // 3, t % 3
                        p.mm(ps, dg[:, t, :], gp[:, 8 * blk + dy:8 * blk + dy + 8, dx:dx + 64], start=(t == 0), stop=(t == 8))
                    g_ = gg[nn % 2]
                    nn += 1
                    p.act(g_, ps, AF.Gelu_apprx_tanh, bias=cb[:, j:j + 1])
                    ps2 = self.bank()
                    t0 = 256 + r0 * 64 + blk * 512
                    for kc in range(8):
                        p.mm(ps2, wuj[:, kc, :], hT[:, kc, t0:t0 + 512], start=(kc == 0), stop=(kc == 7))
                    p.tt(actT[:, j, 256 + blk * 512:256 + (blk + 1) * 512], ps2, g_, ALU.mult)
                if with_ctx:
                    ps = self.bank()
                    for kc in range(8):
                        p.mm(ps[:, 0:256], wgj[:, kc, :], hT[:, kc, 0:256], start=(kc == 0), stop=(kc == 7))
                    p.copy(cpad[:, 1:257], ps[:, 0:256], eng='act')
                    ps = self.bank()
                    for dx in range(3):
                        p.mm(ps[:, 0:256], dg[:, 3 + dx, :], cpad[:, dx:dx + 256], start=(dx == 0), stop=(dx == 2))
                    g_ = gg[nn % 2]
                    nn += 1
                    p.act(g_[:, 0:256], ps[:, 0:256], AF.Gelu_apprx_tanh, bias=cb[:, j:j + 1])
                    ps2 = self.bank()
                    for kc in range(8):
                        p.mm(ps2[:, 0:256], wuj[:, kc, :], hT[:, kc, 0:256], start=(kc == 0), stop=(kc == 7))
                    p.tt(actT[:, j, 0:256], ps2[:, 0:256], g_[:, 0:256], ALU.mult)
            tiles = ([0, 1] if with_ctx else []) + [2 + 8 * seg + i for i in range(8)]
            for n, tt in enumerate(tiles):
                a0 = tt * 128 if tt < 2 else 256 + (tt - 2 - 8 * seg) * 128
                yb = [self.bank(), self.bank()]
                for h in range(2):
                    for j in range(NJ):
                        p.mm(yb[h], actT[:, j, a0:a0 + 128], wdown[:, j, h * 512:(h + 1) * 512], start=(j == 0), stop=(j == NJ - 1))
                self.residual_tile(b, tt, xi, xo, yb, G[1] if tt < 2 else G[0], st[n % 2])
        p.barrier()
        p.release(m)


LNS_ML = float(np.log(128.0 ** -0.5))
LNS_GLA = float(np.log(64.0 ** -0.5))
ORD = [list(range(NT)), [1, 0] + list(range(17, 1, -1))]
C_MQ, C_MK, C_MV, C_MO, C_MG, C_GQ, C_GK, C_GV, C_GG, C_GLR = 0, 512, 1024, 1536, 2048, 2064, 2320, 2576, 3088, 3600


def host_small_even(pk, inp):
    pk.add('ecw', fm(inp['ev_conv_w'][0]))
    pk.add('ecb', fm(inp['ev_conv_b'][0]))
    pk.add('glab', fm(inp['ev_gla_b'][0]))
    pk.add('hg', fm(inp['ev_head_g'][0]))


class KE(K):
    def setup_even(self):
        p = self.p
        self.w_in = p.dram("ev_w_in", [D, 3632], F32, kind="ExternalInput")
        self.w_out = p.dram("ev_w_out", [D, D], F32, kind="ExternalInput")
        self.gla_w2 = p.dram("ev_gla_w2", [2, 16, 256], F32, kind="ExternalInput")
        self.ones = p.sb("ones", [128, 128], F32)
        p.memset(self.ones, 1.0, eng='pool')
        self.tri = []
        for z in range(2):
            t = p.sb("tri%d" % z, [128, 128], F32)
            pat, cm = ([[1, 128]], -1) if z == 0 else ([[-1, 128]], 1)
            p.aselect(t, self.ones, pat, ALU.is_ge, 0.0, 0, cm)
            self.tri.append(t)
        self.common_mark = p.mark()
        self.mask4 = []
        self.maskb = []
        for z in range(2):
            t = self.tri[z]
            m4 = p.sb("mask4%d" % z, [128, 4, 128], F32)
            p.ts(m4, t.rr("p (o t) -> p o t", o=1).bcast([128, 4, 128]), -1.0, ALU.add, 30000.0, ALU.mult, eng='pool')
            self.mask4.append(m4)
            mb = p.sb("maskb%d" % z, [128, 128], BF16)
            p.copy(mb, t, eng='pool')
            self.maskb.append(mb)
        self.bgrow = p.sb("bgrow", [128, 16], F32)
        p.dma(self.bgrow, self.rows_d[8:9, 0:16].pbcast(128))
        self.base_mark = p.mark()

    def headnorm_gate(self, y, ybufs, hd, gate_col, gate_fn, hT, mixT, scratch):
        p = self.p
        sq, ss, yn, sg, wgt = scratch
        p.dma(wgt, self.w_in[:, gate_col:gate_col + 128].rr("(c p) n -> p c n", p=128), eng='pool')
        for blk in range(5):
            t0, n = blk * 512, (512 if blk < 4 else 256)
            ps = self.bank()
            for kc in range(8):
                p.mm(ps[:, 0:n], wgt[:, kc, :], hT[:, kc, t0:t0 + n], start=(kc == 0), stop=(kc == 7))
            p.act(sg[:, t0:t0 + n], ps[:, 0:n], gate_fn)
        yall = V(y.ap, Buf('yall'))
        for tt in range(NT):
            p.tt(sq[:, tt, :], y[:, tt, :].on(ybufs[tt]), y[:, tt, :].on(ybufs[tt]), ALU.mult)
        p.reduce(ss, sq, ALU.add)
        p.act(ss, ss, AF.Sqrt, bias=EPS, scale=1.0 / 128)
        p.recip(ss, ss)
        for tt in range(NT):
            p.ts(yn[:, tt, :], y[:, tt, :].on(ybufs[tt]), ss[:, tt:tt + 1], ALU.mult)
        hg = self.col('hg')
        for g0 in (0, 8, 16):
            n = min(8, NT - g0)
            ps = self.bank()
            psb = ps.bc(BF16)
            for i in range(n):
                p.tr(psb[:, i * 128:(i + 1) * 128], yn[:, g0 + i, :], self.identb)
            p.stt(mixT[:, hd, g0 * 128:(g0 + n) * 128], psb[:, 0:n * 128], hg[:, hd:hd + 1], sg[:, g0 * 128:(g0 + n) * 128],
                  ALU.mult, ALU.mult)

    def even_mixer(self, b, xi, xo):
        p = self.p
        l = 0
        m = p.mark()
        hT = p.sb("ev_hT", [128, 8, LTOK], BF16)
        self.prenorm(b, xi, l, 0, hT)
        mixT = p.sb("ev_mixT", [128, 8, LTOK], BF16)
        wgate = p.sb("ev_wgate", [128, 8, 16], BF16)
        p.dma(wgate, self.w_in[:, C_MG:C_MG + 16].rr("(c p) n -> p c n", p=128), eng='pool')
        graw = p.sb("ev_graw", [128, NT, 16], F32)
        ps = self.bank()
        for tt in range(NT):
            for kc in range(8):
                p.mm(ps[:, tt * 16:(tt + 1) * 16], hT[:, kc, tt * 128:(tt + 1) * 128], wgate[:, kc, :], start=(kc == 0), stop=(kc == 7))
        p.tt(graw, ps[:, 0:NT * 16].rr("p (t n) -> p t n", n=16), self.bgrow.rr("p (o n) -> p o n", o=1).bcast([128, NT, 16]), ALU.add)
        g5 = graw.rr("p t (z y h) -> p t z y h", z=2, y=2)
        I8 = g5[:, :, :, 0, :]
        F8 = g5[:, :, :, 1, :]
        def s8(name):
            return p.sb(name, [128, NT, 2, 4], F32)
        lf8, Fc8, Ft8, bs8, qs8, wk8, dc8 = [s8("ev_" + n) for n in ("lf8", "Fc8", "Ft8", "bs8", "qs8", "wk8", "dc8")]
        p.act(lf8, F8, AF.Exp, scale=-1.0)
        p.act(lf8, lf8, AF.Ln, bias=1.0)
        p.ts(lf8, lf8, -1.0, ALU.mult)
        ps = self.bank()
        for z in range(2):
            p.mm(ps[:, z * 72:(z + 1) * 72], self.tri[z], lf8[:, :, z, :])
        p.mm(ps[:, 144:288], self.ones, lf8)
        for z in range(2):
            p.copy(Fc8[:, :, z, :], ps[:, z * 72:(z + 1) * 72].rr("p (t h) -> p t h", h=4))
        p.copy(Ft8, ps[:, 144:288].rr("p (t z h) -> p t z h", z=2, h=4))
        p.tt(bs8, I8, Fc8, ALU.subtract)
        p.tt(wk8, Ft8, bs8, ALU.add)
        p.act(wk8, wk8, AF.Exp)
        p.ts(bs8, bs8, LNS_ML, ALU.add)
        p.act(qs8, Fc8, AF.Exp, bias=LNS_ML)
        p.act(dc8, Ft8, AF.Exp)
        Fc5 = Fc8.rr("p t z (h o) -> p t z h o", o=1)

        cw = self.col('ecw').rr("p (t c) -> p t c", t=3)
        cbias = self.col('ecb')
        for hp in range(2):
            m2 = p.mark()
            heads = [2 * hp, 2 * hp + 1]
            qT = [p.sb("ml_qT%d" % i, [128, LTOK], BF16) for i in range(2)]
            kT = [p.sb("ml_kT%d" % i, [128, LTOK], BF16) for i in range(2)]
            vv = [p.sb("ml_v%d" % i, [128, NT, 130], BF16) for i in range(2)]
            yy = [p.sb("ml_y%d" % i, [128, NT, 128], F32) for i in range(2)]
            ybufs = [[Buf("mly%d_%d" % (i, t)) for t in range(NT)] for i in range(2)]
            m3 = p.mark()
            xpad = p.sb("ml_xpad", [128, 2308], F32)
            t1 = p.sb("ml_t1", [128, 2306], F32)
            t2 = p.sb("ml_t2", [128, 2306], F32)
            wq = [p.sb("ml_wq%d" % i, [128, 8, 128], BF16) for i in range(2)]
            p.memset(xpad, 0.0, eng='pool')
            nw = 0
            for i, h in enumerate(heads):
                for dst, c0, cc in ((qT[i], C_MQ + h * 128, h), (kT[i], C_MK + h * 128, 4 + h)):
                    w = wq[nw % 2]
                    nw += 1
                    p.dma(w, self.w_in[:, c0:c0 + 128].rr("(c p) n -> p c n", p=128), eng='pool')
                    for blk in range(5):
                        t0, n = (0, 256) if blk == 0 else (256 + (blk - 1) * 512, 512)
                        pos = 1 if blk == 0 else 259 + (blk - 1) * 512
                        ps = self.bank()
                        for kc in range(8):
                            p.mm(ps[:, 0:n], w[:, kc, :], hT[:, kc, t0:t0 + n], start=(kc == 0), stop=(kc == 7))
                        p.copy(xpad[:, pos:pos + n], ps[:, 0:n], eng='act')
                    p.ts(t1, xpad[:, 0:2306], cw[:, 0, cc:cc + 1], ALU.mult, cbias[:, cc:cc + 1], ALU.add)
                    p.stt(t2, xpad[:, 1:2307], cw[:, 1, cc:cc + 1], t1, ALU.mult, ALU.add)
                    p.stt(t1, xpad[:, 2:2308], cw[:, 2, cc:cc + 1], t2, ALU.mult, ALU.add)
                    p.act(dst[:, 0:256], t1[:, 0:256], AF.Silu)
                    p.act(dst[:, 256:LTOK], t1[:, 258:2306], AF.Silu)
                w = wq[nw % 2]
                nw += 1
                p.dma(w, self.w_in[:, C_MV + h * 128:C_MV + (h + 1) * 128].rr("(c p) n -> p c n", p=128), eng='pool')
                p.memset(vv[i][:, :, 128:130], 1.0, eng='pool')
                for g0 in range(0, NT, 4):
                    n = min(4, NT - g0)
                    ps = self.bank()
                    for j in range(n):
                        tt = g0 + j
                        for kc in range(8):
                            p.mm(ps[:, j * 128:(j + 1) * 128], hT[:, kc, tt * 128:(tt + 1) * 128], w[:, kc, :], start=(kc == 0), stop=(kc == 7))
                    p.copy(vv[i][:, g0:g0 + n, 0:128], ps[:, 0:n * 128].rr("p (t n) -> p t n", n=128), eng='act')
                p.memset(yy[i], 0.0, eng='pool')
            p.barrier()
            p.release(m3)
            Cs = [[p.sb("ml_C%d%d" % (z, i), [128, 132], F32) for i in range(2)] for z in range(2)]
            Cb = [[p.sb("ml_Cb%d%d" % (z, i), [128, 132], BF16) for i in range(2)] for z in range(2)]
            for z in range(2):
                for i in range(2):
                    p.memset(Cs[z][i], 0.0, eng='pool')
                    p.memset(Cb[z][i], 0.0, eng='pool')
            diag4 = [p.sb("ml_diag4%d" % i, [128, 4, 128], F32) for i in range(2)]
            Dm = [p.sb("ml_Dm%d" % i, [128, 128], BF16) for i in range(4)]
            sT = [p.sb("ml_sT%d" % i, [128, 128], BF16) for i in range(4)]
            tmpo = [p.sb("ml_tmpo%d" % i, [128, 132], F32) for i in range(4)]
            num = [p.sb("ml_num%d" % i, [128, 132], F32) for i in range(4)]
            den = [p.sb("ml_den%d" % i, [128, 1], F32) for i in range(4)]
            ktil = [p.sb("ml_ktil%d" % i, [128, 128], BF16) for i in range(4)]
            identf4 = self.identf.rr("p (o t) -> p o t", o=1).bcast([128, 4, 128])
            n4 = 0
            for step in range(NT):
                for z in range(2):
                    tt = ORD[z][step]
                    ts_ = slice(tt * 128, (tt + 1) * 128)
                    dg = diag4[(step * 2 + z) % 2]
                    p.tt(dg, identf4, Fc5[:, tt, z].bcast([128, 4, 128]), ALU.mult)
                    rb = self.bank()
                    p.mm(rb, self.ones, dg, start=True, stop=False)
                    p.mm(rb, self.identf, self.mask4[z], start=False, stop=True)
                    for i, h in enumerate(heads):
                        k4 = n4 % 4
                        n4 += 1
                        sc = self.bank()
                        p.mm(sc[:, 0:128], kT[i][:, ts_], qT[i][:, ts_])
                        p.act(Dm[k4], rb[:, h * 128:(h + 1) * 128], AF.Exp, bias=bs8[:, tt, z, h:h + 1])
                        p.tt(sT[k4], sc[:, 0:128], Dm[k4], ALU.mult)
                        o = self.bank()
                        p.mm(o[:, 0:129], sT[k4], vv[i][:, tt, 0:129])
                        p.mm(o[:, 256:385], qT[i][:, ts_], Cb[z][i][:, 0:129])
                        p.act(tmpo[k4][:, 0:129], o[:, 256:385], AF.Identity, scale=qs8[:, tt, z, h:h + 1])
                        p.tt(num[k4][:, 0:129], tmpo[k4][:, 0:129], o[:, 0:129], ALU.add)
                        p.act(den[k4], num[k4][:, 128:129], AF.Abs)
                        p.ts(den[k4], den[k4], 1.0, ALU.max)
                        p.recip(den[k4], den[k4])
                        yv = yy[i][:, tt, :].on(ybufs[i][tt])
                        p.stt(yv, num[k4][:, 0:128], den[k4], yv, ALU.mult, ALU.add)
                        if step < NT - 1:
                            kp = self.bank()
                            kpb = kp.bc(BF16)
                            p.tr(kpb[:, 0:128], kT[i][:, ts_], self.identb)
                            p.act(ktil[k4], kpb[:, 0:128], AF.Identity, scale=wk8[:, tt, z, h:h + 1])
                            cu = self.bank()
                            p.mm(cu[:, 0:129], ktil[k4], vv[i][:, tt, 0:129])
                            p.stt(Cs[z][i][:, 0:129], Cs[z][i][:, 0:129], dc8[:, tt, z, h:h + 1], cu[:, 0:129], ALU.mult, ALU.add)
                            p.copy(Cb[z][i][:, 0:129], Cs[z][i][:, 0:129], eng='pool')
            m4_ = p.mark()
            scratch = (p.sb("hn_sq", [128, NT, 128], F32), p.sb("hn_ss", [128, NT], F32), p.sb("hn_yn", [128, NT, 128], BF16),
                       p.sb("hn_sg", [128, LTOK], BF16), p.sb("hn_wgt", [128, 8, 128], BF16))
            for i, h in enumerate(heads):
                self.headnorm_gate(yy[i], ybufs[i], h, C_MO + h * 128, AF.Sigmoid, hT, mixT, scratch)
            p.barrier()
            p.release(m2)
        self.ev_hT, self.ev_mixT, self.ev_mark = hT, mixT, m
        return hT, mixT, m

    def even_out(self, b, xi, xo, hT, mixT, m):
        p = self.p
        l = 0
        wout = p.sb("ev_wout", [128, 8, D], BF16)
        for c in range(8):
            p.dma(wout[:, c, :], self.w_out[c * 128:(c + 1) * 128, :], eng='pool')
        G = [p.sb("evo_G%d" % i, [128, D], F32) for i in range(2)]
        gtmp = p.sb("evo_gtmp", [128, 128], F32)
        st = self.res_state(2)
        self.gate_row(G[0], l, 2, b, gtmp)
        self.gate_row(G[1], l, 2, 4, gtmp)
        for tt in range(NT):
            yb = [self.bank(), self.bank()]
            for h in range(2):
                for c in range(8):
                    p.mm(yb[h], mixT[:, c, tt * 128:(tt + 1) * 128], wout[:, c, h * 512:(h + 1) * 512], start=(c == 0), stop=(c == 7))
            self.residual_tile(b, tt, xi, xo, yb, G[1] if tt < 2 else G[0], st[tt % 2])
        p.barrier()
        p.release(m)


def _gla(self, b, hT, mixT):
    p = self.p
    m = p.mark()
    rmask = p.sb("gl_rmask", [128, LTOK], BF16)
    p.memset(rmask, 1.0, eng='pool')
    p.memset(rmask.rr("p (n t) -> p n t", t=128)[:, :, 0:1], 0.0, eng='pool')
    w2b = p.sb("gl_w2b", [16, 2, 256], BF16)
    p.dma(w2b, self.gla_w2.rr("z r c -> r z c"), eng='pool')
    wlr = p.sb("gl_wlr", [128, 8, 32], BF16)
    p.dma(wlr, self.w_in[:, C_GLR:C_GLR + 32].rr("(c p) n -> p c n", p=128), eng='pool')
    glrT = p.sb("gl_glrT", [16, 2, LTOK], BF16)
    for z in range(2):
        for blk in range(5):
            t0, n = blk * 512, (512 if blk < 4 else 256)
            ps = self.bank()
            for kc in range(8):
                p.mm(ps[0:16, 0:n], wlr[:, kc, z * 16:(z + 1) * 16], hT[:, kc, t0:t0 + n], start=(kc == 0), stop=(kc == 7))
            p.copy(glrT[:, z, t0:t0 + n], ps[0:16, 0:n], eng='act')
    nb = p.sb("gl_nb", [128, 4], F32)
    p.ts(nb, self.col('glab'), -1.0, ALU.mult)
    for cp in range(2):
        m2 = p.mark()
        qtil = [p.sb("gl_qtil%d" % z, [128, LTOK], BF16) for z in range(2)]
        khat = [p.sb("gl_khat%d" % z, [128, LTOK], BF16) for z in range(2)]
        ktl = [p.sb("gl_ktl%d" % z, [128, LTOK], BF16) for z in range(2)]
        dec = [p.sb("gl_dec%d" % z, [128, NT], F32) for z in range(2)]
        vv = [p.sb("gl_v%d" % i, [128, NT, 128], BF16) for i in range(2)]
        yy = [p.sb("gl_y%d" % i, [128, NT, 128], F32) for i in range(2)]
        ybufs = [[Buf("gly%d_%d" % (i, t)) for t in range(NT)] for i in range(2)]
        m3 = p.mark()
        qT = p.sb("gl_qT", [128, LTOK], BF16)
        kT = p.sb("gl_kT", [128, LTOK], BF16)
        lb = p.sb("gl_l", [128, LTOK], F32)
        P = p.sb("gl_P", [128, LTOK], F32)
        tmp = p.sb("gl_tmp", [128, LTOK], F32)
        E = p.sb("gl_E", [128, LTOK], BF16)
        w = [p.sb("gl_w%d" % i, [128, 8, 128], BF16) for i in range(2)]
        for dst, c0, wi in ((qT, C_GQ + cp * 128, 0), (kT, C_GK + cp * 128, 1)):
            p.dma(w[wi], self.w_in[:, c0:c0 + 128].rr("(c p) n -> p c n", p=128), eng='pool')
            for blk in range(5):
                t0, n = blk * 512, (512 if blk < 4 else 256)
                ps = self.bank()
                for kc in range(8):
                    p.mm(ps[:, 0:n], w[wi][:, kc, :], hT[:, kc, t0:t0 + n], start=(kc == 0), stop=(kc == 7))
                p.copy(dst[:, t0:t0 + n], ps[:, 0:n], eng='act')
        P3 = P.rr("p (n t) -> p n t", t=128)
        Ptot = P3[:, :, 127:128]
        for z in range(2):
            for blk in range(5):
                t0, n = blk * 512, (512 if blk < 4 else 256)
                ps = self.bank()
                p.mm(ps[:, 0:n], w2b[:, z, cp * 128:(cp + 1) * 128], glrT[:, z, t0:t0 + n])
                p.act(lb[:, t0:t0 + n], ps[:, 0:n], AF.Exp, scale=-1.0, bias=nb[:, z * 2 + cp:z * 2 + cp + 1])
            p.act(lb, lb, AF.Ln, bias=1.0)
            p.scan(P, rmask, lb, 0.0, ALU.mult, ALU.add)
            p.act(dec[z], Ptot.rr("p n o -> p (n o)"), AF.Exp, scale=-1.0 / 16)
            t3 = tmp.rr("p (n t) -> p n t", t=128)
            if z == 0:
                p.act(E, P, AF.Exp, scale=-1.0 / 16, bias=LNS_GLA)
                p.tt(qtil[z], qT, E, ALU.mult)
                p.act(E, P, AF.Exp, scale=1.0 / 16)
                p.tt(khat[z], kT, E, ALU.mult, eng='pool')
                p.tt(t3, P3, Ptot.bcast([128, NT, 128]), ALU.subtract)
                p.act(E, tmp, AF.Exp, scale=1.0 / 16)
                p.tt(ktl[z], kT, E, ALU.mult)
            else:
                p.tt(t3, Ptot.bcast([128, NT, 128]), P3, ALU.subtract)
                p.tt(tmp, tmp, lb, ALU.add, eng='pool')
                p.act(E, tmp, AF.Exp, scale=-1.0 / 16, bias=LNS_GLA)
                p.tt(qtil[z], qT, E, ALU.mult)
                p.act(E, tmp, AF.Exp, scale=1.0 / 16)
                p.tt(khat[z], kT, E, ALU.mult, eng='pool')
                p.tt(tmp, lb, P, ALU.subtract)
                p.act(E, tmp, AF.Exp, scale=1.0 / 16)
                p.tt(ktl[z], kT, E, ALU.mult)
        for i in range(2):
            h = cp * 2 + i
            p.dma(w[i], self.w_in[:, C_GV + h * 128:C_GV + (h + 1) * 128].rr("(c p) n -> p c n", p=128), eng='pool')
            for g0 in range(0, NT, 4):
                n = min(4, NT - g0)
                ps = self.bank()
                for j in range(n):
                    tt = g0 + j
                    for kc in range(8):
                        p.mm(ps[:, j * 128:(j + 1) * 128], hT[:, kc, tt * 128:(tt + 1) * 128], w[i][:, kc, :], start=(kc == 0), stop=(kc == 7))
                p.copy(vv[i][:, g0:g0 + n, :], ps[:, 0:n * 128].rr("p (t n) -> p t n", n=128), eng='act')
            p.memset(yy[i], 0.0, eng='pool')
        p.barrier()
        p.release(m3)
        S = [[p.sb("gl_S%d%d" % (z, i), [128, 128], F32) for i in range(2)] for z in range(2)]
        Sb = [[p.sb("gl_Sb%d%d" % (z, i), [128, 128], BF16) for i in range(2)] for z in range(2)]
        for z in range(2):
            for i in range(2):
                p.memset(S[z][i], 0.0, eng='pool')
                p.memset(Sb[z][i], 0.0, eng='pool')
        AT = [p.sb("gl_AT%d" % i, [128, 128], BF16) for i in range(4)]
        ktok = [p.sb("gl_ktok%d" % i, [128, 64], BF16) for i in range(4)]
        n4 = 0
        for step in range(NT):
            for z in range(2):
                tt = ORD[z][step]
                ts_ = slice(tt * 128, (tt + 1) * 128)
                for i in range(2):
                    pr = slice(i * 64, (i + 1) * 64)
                    k4 = n4 % 4
                    n4 += 1
                    sc = self.bank()
                    p.mm(sc[:, 0:128], khat[z][pr, ts_], qtil[z][pr, ts_])
                    p.tt(AT[k4], sc[:, 0:128], self.maskb[z], ALU.mult)
                    o = self.bank()
                    p.mm(o[:, 0:128], AT[k4], vv[i][:, tt, :], start=True, stop=False)
                    p.mm(o[:, 0:128], qtil[z][pr, ts_], Sb[z][i][pr, :], start=False, stop=True)
                    yv = yy[i][:, tt, :].on(ybufs[i][tt])
                    p.tt(yv, yv, o[:, 0:128], ALU.add)
                    if step < NT - 1:
                        kp = self.bank()
                        kpb = kp.bc(BF16)
                        p.tr(kpb[:, 0:64], ktl[z][pr, ts_], self.identb[pr, pr])
                        p.copy(ktok[k4], kpb[:, 0:64], eng='act')
                        su = self.bank()
                        p.mm(su[pr, 0:128], ktok[k4], vv[i][:, tt, :])
                        p.stt(S[z][i][pr, :], S[z][i][pr, :], dec[z][pr, tt:tt + 1], su[pr, 0:128], ALU.mult, ALU.add)
                        p.copy(Sb[z][i][pr, :], S[z][i][pr, :], eng='pool')
        scratch = (p.sb("hn_sq", [128, NT, 128], F32), p.sb("hn_ss", [128, NT], F32), p.sb("hn_yn", [128, NT, 128], BF16),
                   p.sb("hn_sg", [128, LTOK], BF16), p.sb("hn_wgt", [128, 8, 128], BF16))
        for i in range(2):
            h = cp * 2 + i
            self.headnorm_gate(yy[i], ybufs[i], 4 + h, C_GG + h * 128, AF.Silu, hT, mixT, scratch)
        p.barrier()
        p.release(m2)
    p.barrier()
    p.release(m)


KE.gla = _gla


C0 = float(np.exp(-0.5))
RW_LN_EPS = 64e-5


def host_small_rw(pk, inp):
    pk.add('mu', fm(inp['rw_mu'][0]))
    pk.add('w0', fm(inp['rw_w0'][0]))
    pk.add('a0', fm(inp['rw_a0'][0]))
    pk.add('kv', fm(inp['rw_kvec'][0]))
    pk.add('lnx', fm(inp['rw_lnx'][0]))


def _setup_rw(self):
    p = self.p
    NB = self.NB
    self.w_rkv = p.dram("rw_w_rkv", [3, D, D], F32, kind="ExternalInput")
    self.w_o = p.dram("rw_w_o", [D, D], F32, kind="ExternalInput")
    self.rw_w1 = p.dram("rw_w1", [2, D, 64], F32, kind="ExternalInput")
    self.rw_w2 = p.dram("rw_w2", [2, 64, D], F32, kind="ExternalInput")
    self.rw_a1 = p.dram("rw_a1", [D, 64], F32, kind="ExternalInput")
    self.rw_a2 = p.dram("rw_a2", [64, D], F32, kind="ExternalInput")
    self.rw_g1 = p.dram("rw_g1", [D, 128], F32, kind="ExternalInput")
    self.rw_g2 = p.dram("rw_g2", [128, D], F32, kind="ExternalInput")
    self.scr = []
    if getattr(self, 'dbg', False):
        self.dbg_y = p.dram("dbg_y", [8, 128, NT * 128], F32, kind="ExternalOutput")
    for b in range(NB):
        d = {}
        for nm in ('r', 'k', 'v', 'o'):
            d[nm] = p.dram("scr_%s%d" % (nm, b), [8, 128, LTOK], BF16, kind="ExternalOutput" if getattr(self, 'dbg', False) else "Internal")
            d[nm + 'buf'] = [Buf("scr_%s%d_%d" % (nm, b, c)) for c in range(8)]
        self.scr.append(d)


def _setup_rw_consts(self):
    p = self.p
    p.barrier()
    p.release(self.common_mark)
    self.strict = []
    self.M4 = []
    for z in range(2):
        st = p.sb("strict%d" % z, [128, 128], F32)
        pat, cm = ([[1, 128]], -1) if z == 0 else ([[-1, 128]], 1)
        p.aselect(st, self.ones, pat, ALU.is_gt, 0.0, 0, cm)
        self.strict.append(st)
    for z in range(2):
        m4 = p.sb("M4%d" % z, [128, 4, 128], F32)
        p.ts(m4[:, 0, :], self.strict[z], -1.0, ALU.mult, eng='pool')
        p.ts(m4[:, 1, :], self.tri[z], -1.0, ALU.mult, eng='pool')
        p.copy(m4[:, 2, :], self.strict[z], eng='pool')
        p.copy(m4[:, 3, :], self.tri[z], eng='pool')
        self.M4.append(m4)
    self.nstrictT = []
    for z in range(2):
        t = p.sb("nstrictT%d" % z, [128, 128], F32)
        p.ts(t, self.strict[1 - z], -1.0, ALU.mult, eng='pool')
        self.nstrictT.append(t)
    self.masks3 = p.sb("masks3", [128, 3, 128], BF16)
    bd64 = p.sb("bd64", [128, 128], BF16)
    p.memset(self.masks3[:, 0, :], 0.0, eng='pool')
    p.memset(bd64, 0.0, eng='pool')
    for i in range(4):
        p.memset(self.masks3[32 * i:32 * i + 32, 0, 32 * i:32 * i + 32], 1.0, eng='pool')
    for i in range(2):
        p.memset(bd64[64 * i:64 * i + 64, 64 * i:64 * i + 64], 1.0, eng='pool')
    p.tt(self.masks3[:, 1, :], bd64, self.masks3[:, 0, :], ALU.subtract, eng='pool')
    p.ts(self.masks3[:, 2, :], bd64, -1.0, ALU.mult, 1.0, ALU.add, eng='pool')
    self.bones = p.sb("bones", [128, 128], BF16)
    p.memset(self.bones, 0.0, eng='pool')
    p.memset(self.bones[0:64, 0:64], 1.0, eng='pool')
    p.memset(self.bones[64:128, 64:128], 1.0, eng='pool')
    self.omu = p.sb("omu", [128, 48], F32)
    p.ts(self.omu, self.col('mu'), -1.0, ALU.mult, 1.0, ALU.add)
    self.okv1 = p.sb("okv1", [128, 8], F32)
    p.ts(self.okv1, self.col('kv')[:, 8:16], -1.0, ALU.mult, 1.0, ALU.add)
    self.base_mark = p.mark()


def _rw_mix_block(self, xb, hT, mi, blk):
    p = self.p
    mu = self.col('mu')
    t0, n = (0, 256) if blk == 0 else (256 + (blk - 1) * 512, 512)
    for c in range(8):
        muc = mu[:, mi * 8 + c:mi * 8 + c + 1]
        p.act(xb[:, c, 0:n], hT[:, c, t0:t0 + n], AF.Identity, scale=self.omu[:, mi * 8 + c:mi * 8 + c + 1])
        kind = c // 2
        if blk == 0:
            if kind in (0, 2):
                p.stt(xb[:, c, 1:256], hT[:, c, 0:255], muc, xb[:, c, 1:256], ALU.mult, ALU.add)
            else:
                p.stt(xb[:, c, 0:255], hT[:, c, 1:256], muc, xb[:, c, 0:255], ALU.mult, ALU.add)
        else:
            r0 = (blk - 1) * 8
            xv = xb[:, c, :].rr("p (r w) -> p r w", w=64)
            hv = hT[:, c, 256:LTOK].rr("p (r w) -> p r w", w=64)
            if kind == 0:
                p.stt(xv[:, :, 1:64], hv[:, r0:r0 + 8, 0:63], muc, xv[:, :, 1:64], ALU.mult, ALU.add)
            elif kind == 1:
                p.stt(xv[:, :, 0:63], hv[:, r0:r0 + 8, 1:64], muc, xv[:, :, 0:63], ALU.mult, ALU.add)
            elif kind == 2:
                lo = 1 if r0 == 0 else 0
                p.stt(xv[:, lo:8, :], hv[:, r0 + lo - 1:r0 + 7, :], muc, xv[:, lo:8, :], ALU.mult, ALU.add)
            else:
                hi = 7 if r0 == 24 else 8
                p.stt(xv[:, 0:hi, :], hv[:, r0 + 1:r0 + hi + 1, :], muc, xv[:, 0:hi, :], ALU.mult, ALU.add)
    return t0, n


def _rw_phase1(self, b, hT, tw, ta, tg):
    p = self.p
    scr = self.scr[b]
    m = p.mark()
    xblk = [p.sb("rw_xb%d" % i, [128, 8, 512], BF16) for i in range(2)]
    Wr = [p.sb("rw_W%d" % i, [128, 8, D], BF16) for i in range(2)]
    stage = [p.sb("rw_stage%d" % i, [128, 512], BF16) for i in range(4)]
    ns = 0
    nx = 0
    for wi, (mi, kind) in enumerate(((0, 'r'), (2, 'k'), (3, 'v'), (1, 'w'), (4, 'a'), (5, 'g'))):
        W = Wr[wi % 2]
        if kind in 'rkv':
            idx = 'rkv'.index(kind)
            for c in range(8):
                p.dma(W[:, c, :], self.w_rkv[idx, c * 128:(c + 1) * 128, :], eng='pool')
        elif kind == 'w':
            for z in range(2):
                p.dma(W[:, :, z * 64:(z + 1) * 64], self.rw_w1[z].rr("(c p) r -> p c r", p=128), eng='pool')
        elif kind == 'a':
            p.dma(W[:, :, 0:64], self.rw_a1.rr("(c p) r -> p c r", p=128), eng='pool')
        else:
            p.dma(W[:, :, 0:128], self.rw_g1.rr("(c p) r -> p c r", p=128), eng='pool')
        for blk in range(5):
            xb = xblk[nx % 2]
            nx += 1
            t0, n = self.rw_mix_block(xb, hT, mi, blk)
            if kind in 'rkv':
                for oc in range(8):
                    ps = self.bank()
                    for kc in range(8):
                        p.mm(ps[:, 0:n], W[:, kc, oc * 128:(oc + 1) * 128], xb[:, kc, 0:n], start=(kc == 0), stop=(kc == 7))
                    sg = stage[ns % 4]
                    ns += 1
                    p.copy(sg[:, 0:n], ps[:, 0:n], eng='act')
                    p.dma(scr[kind][oc, :, t0:t0 + n].on(scr[kind + 'buf'][oc]), sg[:, 0:n])
            elif kind == 'w':
                for z in range(2):
                    ps = self.bank()
                    for kc in range(8):
                        p.mm(ps[0:64, 0:n], W[:, kc, z * 64:(z + 1) * 64], xb[:, kc, 0:n], start=(kc == 0), stop=(kc == 7))
                    p.act(tw[z][0:64, t0:t0 + n], ps[0:64, 0:n], AF.Tanh)
            elif kind == 'a':
                ps = self.bank()
                for kc in range(8):
                    p.mm(ps[0:64, 0:n], W[:, kc, 0:64], xb[:, kc, 0:n], start=(kc == 0), stop=(kc == 7))
                p.copy(ta[0:64, t0:t0 + n], ps[0:64, 0:n], eng='act')
            else:
                ps = self.bank()
                for kc in range(8):
                    p.mm(ps[:, 0:n], W[:, kc, 0:128], xb[:, kc, 0:n], start=(kc == 0), stop=(kc == 7))
                p.act(tg[:, t0:t0 + n], ps[:, 0:n], AF.Sigmoid)
    p.barrier()
    p.release(m)


KE.setup_rw = _setup_rw
KE.setup_rw_consts = _setup_rw_consts
KE.rw_mix_block = _rw_mix_block
KE.rw_phase1 = _rw_phase1


BLKS = [(0, 512), (512, 512), (1024, 512), (1536, 512), (2048, 256)]


def _rw_chunk(self, b, cc, tw, ta, tg, last):
    p = self.p
    scr = self.scr[b]
    m = p.mark()
    kv = self.col('kv')
    rT = p.sb("rc_rT", [128, LTOK], BF16)
    kT = p.sb("rc_kT", [128, LTOK], BF16)
    vT = p.sb("rc_vT", [128, LTOK], BF16)
    for t_, nm in ((rT, 'r'), (kT, 'k'), (vT, 'v')):
        p.dma(t_, scr[nm][cc].on(scr[nm + 'buf'][cc]))
    KR = [p.sb("rc_KR%d" % z, [128, NT, 2, 128], BF16) for z in range(2)]
    khat = [p.sb("rc_khat%d" % z, [128, LTOK], BF16) for z in range(2)]
    bhat = [p.sb("rc_bhat%d" % z, [128, LTOK], BF16) for z in range(2)]
    kT4 = [p.sb("rc_kT4%d" % z, [128, LTOK], BF16) for z in range(2)]
    nbT4 = [p.sb("rc_nbT4%d" % z, [128, LTOK], BF16) for z in range(2)]
    gam = [p.sb("rc_gam%d" % z, [128, NT], F32) for z in range(2)]
    Vtok = p.sb("rc_Vtok", [128, NT, 128], BF16)
    y = p.sb("rc_y", [128, NT, 128], F32)
    yz = [y, p.sb("rc_y1", [128, NT, 128], F32)]
    ybufs = [[Buf("rcy%d_%d" % (t, hh)) for hh in range(2)] for t in range(NT)]
    m3 = p.mark()
    aT = p.sb("rc_aT", [128, LTOK], BF16)
    kap = p.sb("rc_kap", [128, LTOK], BF16)
    bet = p.sb("rc_bet", [128, LTOK], BF16)
    sig = p.sb("rc_sig", [128, LTOK], F32)
    P = p.sb("rc_P", [128, LTOK], F32)
    t1 = p.sb("rc_t1", [128, LTOK], F32)
    t2 = p.sb("rc_t2", [128, LTOK], F32)
    E = p.sb("rc_E", [128, LTOK], BF16)
    rmask = p.sb("rc_rmask", [128, LTOK], BF16)
    p.memset(rmask, 1.0, eng='pool')
    p.memset(rmask.rr("p (n t) -> p n t", t=128)[:, :, 0:1], 0.0, eng='pool')
    w2b = p.sb("rc_w2b", [64, 2, 128], BF16)
    p.dma(w2b, self.rw_w2[:, :, cc * 128:(cc + 1) * 128].rr("z r c -> r z c"), eng='pool')
    a2b = p.sb("rc_a2b", [64, 128], BF16)
    p.dma(a2b, self.rw_a2[:, cc * 128:(cc + 1) * 128], eng='pool')
    for (t0, n) in BLKS:
        ps = self.bank()
        p.mm(ps[:, 0:n], a2b, ta[0:64, t0:t0 + n])
        p.act(aT[:, t0:t0 + n], ps[:, 0:n], AF.Sigmoid, bias=self.col('a0')[:, cc:cc + 1])
    p.ts(t1, kT, kv[:, cc:cc + 1], ALU.mult)
    p.tt(E, t1, t1, ALU.mult, eng='pool')
    for (t0, n) in BLKS:
        ps = self.bank()
        p.mm(ps[:, 0:n], self.bones, E[:, t0:t0 + n])
        p.act(t2[:, t0:t0 + n], ps[:, 0:n], AF.Sqrt)
    p.ts(t2, t2, 1e-12, ALU.max)
    p.recip(t2, t2)
    p.tt(kap, t1, t2, ALU.mult)
    p.tt(bet, kap, aT, ALU.mult, eng='pool')
    p.ts(t1, aT, kv[:, 8 + cc:8 + cc + 1], ALU.mult, self.okv1[:, cc:cc + 1], ALU.add)
    p.tt(kT, kT, t1, ALU.mult)
    P3 = P.rr("p (n t) -> p n t", t=128)
    Ptot = P3[:, :, 127:128]
    Pb = Ptot.bcast([128, NT, 128])
    t13 = t1.rr("p (n t) -> p n t", t=128)
    t23 = t2.rr("p (n t) -> p n t", t=128)
    v3 = lambda x: x.rr("p (n t) -> p n t", t=128)
    for z in range(2):
        for (t0, n) in BLKS:
            ps = self.bank()
            p.mm(ps[:, 0:n], w2b[:, z, :], tw[z][0:64, t0:t0 + n])
            p.act(sig[:, t0:t0 + n], ps[:, 0:n], AF.Sigmoid, bias=self.col('w0')[:, z * 8 + cc:z * 8 + cc + 1])
        p.scan(P, rmask, sig, 0.0, ALU.mult, ALU.add)
        p.act(gam[z], Ptot.rr("p n o -> p (n o)"), AF.Exp, scale=-C0)
        if z == 0:
            G = P
            p.tt(t1, P, sig, ALU.subtract)
            p.tt(t23, P3, Pb, ALU.subtract)
        else:
            p.tt(t13, Pb, P3, ALU.subtract)
            p.tt(t2, sig, P, ALU.subtract)
            G = sig
            p.tt(sig, t1, sig, ALU.add, eng='pool')
        p.act(E, G, AF.Exp, scale=-C0)
        p.tt(KR[z][:, :, 1, :], v3(rT), v3(E), ALU.mult)
        p.act(E, t1, AF.Exp, scale=-C0)
        p.tt(KR[z][:, :, 0, :], v3(kap), v3(E), ALU.mult, eng='pool')
        p.act(E, G, AF.Exp, scale=C0)
        p.tt(khat[z], kT, E, ALU.mult)
        p.tt(bhat[z], bet, E, ALU.mult, eng='pool')
        p.act(E, t2, AF.Exp, scale=C0)
        p.tt(kT4[z], kT, E, ALU.mult)
        p.stt(nbT4[z], bet, -1.0, E, ALU.mult, ALU.mult)
    for g0 in range(0, NT, 8):
        n = min(8, NT - g0)
        ps = self.bank()
        psb = ps.bc(BF16)
        for i in range(n):
            p.tr(psb[:, i * 128:(i + 1) * 128], vT[:, (g0 + i) * 128:(g0 + i + 1) * 128], self.identb)
        p.copy(Vtok[:, g0:g0 + n, :], psb[:, 0:n * 128].rr("p (t c) -> p t c", c=128), eng='act')
    p.barrier()
    p.release(m3)
    mring = p.mark()
    NR = 12
    A4 = [p.sb("rs_A4%d" % i, [128, 4, 128], BF16) for i in range(NR)]
    SQ = [[p.sb("rs_SQ%d_%d" % (i, j), [128, 2, 128], BF16) for j in range(2)] for i in range(NR)]
    XXr = [p.sb("rs_XX%d" % i, [128, 2, 128], BF16) for i in range(NR)]
    Q0T = [p.sb("rs_Q0T%d" % i, [128, 128], BF16) for i in range(NR)]
    QM = [p.sb("rs_QM%d" % i, [128, 3, 128], BF16) for i in range(NR)]
    QMT = [p.sb("rs_QMT%d" % i, [128, 3, 128], BF16) for i in range(NR)]
    Y1r = [p.sb("rs_Y1%d" % i, [128, 128], BF16) for i in range(NR)]
    KB = [p.sb("rs_KB%d" % i, [128, 2, 128], BF16) for i in range(6)]
    Wb = [p.sb("rs_Wb%d" % i, [128, 64], BF16) for i in range(NR)]
    Ub = [p.sb("rs_Ub%d" % i, [128, 64], BF16) for i in range(NR)]
    H = [[p.sb("rs_H%d%d" % (z, hh), [128, 64], F32) for hh in range(2)] for z in range(2)]
    Hb = [[p.sb("rs_Hb%d%d" % (z, hh), [128, 64], BF16) for hh in range(2)] for z in range(2)]
    for z in range(2):
        for hh in range(2):
            p.memset(H[z][hh], 0.0, eng='pool')
            p.memset(Hb[z][hh], 0.0, eng='pool')
    units = [(step, z, hh) for step in range(NT) for z in range(2) for hh in range(2)]
    prep = {}

    def stage_a(u, bk, r):
        step, z, hh = units[u]
        tt = ORD[z][step]
        ts_ = slice(tt * 128, (tt + 1) * 128)
        pr = slice(hh * 64, (hh + 1) * 64)
        kb = KB[(u // 2) % 6]
        if hh == 0:
            kps, hk = p.palloc(1, bk)
            psb = kps.bc(BF16)
            p.tr(psb[:, 0:128], kT4[z][:, ts_], self.identb)
            p.tr(psb[:, 128:256], nbT4[z][:, ts_], self.identb)
        kr = KR[z][pr, tt].rr("p j t -> p (j t)")
        sc1, hsc1 = p.palloc(2, bk)
        p.mm(sc1[:, 0:256], bhat[z][pr, ts_], kr)
        q0, hq0 = p.palloc(1, bk)
        p.mm(q0[:, 0:128], KR[z][pr, tt, 0, :], bhat[z][pr, ts_])
        yield
        if hh == 0:
            p.copy(kb, psb[:, 0:256].rr("p (j c) -> p j c", j=2), eng='act')
            p.pfree(hk)
        p.tt(A4[r][:, 0:2, :], sc1.rr("p (j t) -> p j t", j=2), self.M4[z][:, 0:2, :], ALU.mult)
        p.pfree(hsc1)
        p.tt(Q0T[r], q0[:, 0:128], self.nstrictT[z], ALU.mult)
        p.pfree(hq0)
        sc2, hsc2 = p.palloc(2, bk)
        p.mm(sc2[:, 0:256], khat[z][pr, ts_], kr)
        yield
        p.tt(A4[r][:, 2:4, :], sc2.rr("p (j t) -> p j t", j=2), self.M4[z][:, 2:4, :], ALU.mult)
        p.pfree(hsc2)
        p.tt(QM[r], A4[r][:, 0:1, :].bcast([128, 3, 128]), self.masks3, ALU.mult, eng='pool')
        p.tt(QMT[r], Q0T[r].rr("p (o t) -> p o t", o=1).bcast([128, 3, 128]), self.masks3, ALU.mult, eng='pool')
        XX = XXr[r]
        XXf = XX.rr("p j t -> p (j t)")
        p.tt(XX[:, 0, :], QM[r][:, 0, :], self.identb, ALU.add, eng='pool')
        p.tt(XX[:, 1, :], QMT[r][:, 0, :], self.identb, ALU.add, eng='pool')
        yield
        cq, ct = QM[r][:, 0, :], QMT[r][:, 0, :]
        xq, xt = XX[:, 0, :], XX[:, 1, :]
        ps, h1 = p.palloc(2, bk)
        p.mm(ps[:, 0:128], ct, cq)
        p.mm(ps[:, 128:256], cq, ct)
        yield
        nxt = SQ[r][0]
        p.copy(nxt.rr("p j t -> p (j t)"), ps[:, 0:256], eng='act')
        p.pfree(h1)
        cq, ct = nxt[:, 0, :], nxt[:, 1, :]
        yield
        for lev in range(1, 5):
            ps2, h2 = p.palloc(2, bk)
            p.mm(ps2[:, 0:128], ct, xq)
            p.mm(ps2[:, 128:256], xq, ct)
            if lev < 4:
                ps, h1 = p.palloc(2, bk)
                p.mm(ps[:, 0:128], ct, cq)
                p.mm(ps[:, 128:256], cq, ct)
            yield
            p.tt(XXf, XXf, ps2[:, 0:256], ALU.add)
            p.pfree(h2)
            if lev < 4:
                nxt = SQ[r][lev % 2]
                p.copy(nxt.rr("p j t -> p (j t)"), ps[:, 0:256], eng='act')
                p.pfree(h1)
                cq, ct = nxt[:, 0, :], nxt[:, 1, :]
            yield
        for lvl in (1, 2):
            C, CT = QM[r][:, lvl, :], QMT[r][:, lvl, :]
            ps, h1 = p.palloc(1, bk)
            p.mm(ps[:, 0:128], CT, xq)
            yield
            p.copy(Y1r[r], ps[:, 0:128], eng='act')
            p.pfree(h1)
            yield
            ps2, h2 = p.palloc(2, bk)
            p.mm(ps2[:, 0:128], xt, Y1r[r])
            if lvl == 1:
                p.mm(ps2[:, 128:256], Y1r[r], xt)
            yield
            if lvl == 1:
                p.tt(XXf, XXf, ps2[:, 0:256], ALU.add)
            else:
                p.tt(XX[:, 0, :], XX[:, 0, :], ps2[:, 0:128], ALU.add)
            p.pfree(h2)
            yield
        prep[u] = (r, kb, XX[:, 0, :])

    def stage_b(u, bk, bq):
        step, z, hh = units[u]
        tt = ORD[z][step]
        ts_ = slice(tt * 128, (tt + 1) * 128)
        pr = slice(hh * 64, (hh + 1) * 64)
        cs = slice(hh * 64, (hh + 1) * 64)
        r, kb, xfin = prep.pop(u)
        a4 = A4[r]
        hb = Hb[z][hh]
        vt = Vtok[:, tt, cs]
        w, hw = p.palloc(1, bk)
        p.mm(w[:, 0:64], KR[z][pr, tt, 0, :], hb[pr, :], start=True, stop=False)
        p.mm(w[:, 0:64], a4[:, 2, :], vt, start=False, stop=True)
        yield
        p.copy(Wb[r], w[:, 0:64], eng='act')
        p.pfree(hw)
        yield
        uu, hu_ = p.palloc(1, bk)
        p.mm(uu[:, 0:64], xfin, Wb[r])
        yield
        p.copy(Ub[r], uu[:, 0:64], eng='act')
        p.pfree(hu_)
        yield
        yb, hy = p.palloc(1, bk)
        p.mm(yb[:, 0:64], KR[z][pr, tt, 1, :], hb[pr, :], start=True, stop=False)
        p.mm(yb[:, 0:64], a4[:, 3, :], vt, start=False, stop=False)
        p.mm(yb[:, 0:64], a4[:, 1, :], Ub[r], start=False, stop=True)
        if step < NT - 1:
            hu, hh_ = p.palloc(1, bk)
            p.mm(hu[pr, 0:64], kb[:, 0, cs], vt, start=True, stop=False)
            p.mm(hu[pr, 0:64], kb[:, 1, cs], Ub[r], start=False, stop=True)
        yield
        if step < NT - 1:
            p.stt(H[z][hh][pr, :], H[z][hh][pr, :], gam[z][pr, tt:tt + 1], hu[pr, 0:64], ALU.mult, ALU.add)
            p.pfree(hh_)
            p.copy(hb[pr, :], H[z][hh][pr, :], eng='pool')
        p.copy(yz[z][:, tt, cs].on(ybufs[tt][hh]), yb[:, 0:64], eng='act')
        p.pfree(hy)
        yield

    NA = 6
    nU = len(units)
    free_r = list(range(NR))
    a_banks = list(range(NA))
    act_a = []
    act_b = []
    a_done = set()
    b_emitted = set()
    next_a = 0
    next_b = [0, 1, 2, 3]
    rmap = {}
    while len(b_emitted) < nU:
        while next_a < nU and a_banks and free_r and next_a < min(next_b) + 10:
            bk = a_banks.pop(0)
            r = free_r.pop(0)
            rmap[next_a] = r
            act_a.append((stage_a(next_a, bk, r), next_a, bk))
            next_a += 1
        for j in range(4):
            u = next_b[j]
            if u < nU and u in a_done and not any(x[2] == j for x in act_b):
                act_b.append((stage_b(u, 6 + j // 2, None), u, j))
                next_b[j] = u + 4
        nxt_a = []
        for g, u, bk in act_a:
            try:
                next(g)
                nxt_a.append((g, u, bk))
            except StopIteration:
                a_done.add(u)
                a_banks.append(bk)
        act_a = nxt_a
        nxt_b = []
        for g, u, j in act_b:
            try:
                next(g)
                nxt_b.append((g, u, j))
            except StopIteration:
                b_emitted.add(u)
                free_r.append(rmap.pop(u))
        act_b = nxt_b
    p.barrier()
    p.release(mring)
    if getattr(self, 'dbg', False):
        for tt in range(NT):
            for hh in range(2):
                p.tt(y[:, tt, hh * 64:(hh + 1) * 64], y[:, tt, hh * 64:(hh + 1) * 64].on(ybufs[tt][hh]), y[:, tt, hh * 64:(hh + 1) * 64].on(ybufs[tt][hh]), ALU.max)
        p.dma(self.dbg_y[cc], y.rr("p t c -> p (t c)"))
    g2b = p.sb("rp_g2b", [128, 128], BF16)
    p.dma(g2b, self.rw_g2[:, cc * 128:(cc + 1) * 128], eng='pool')
    s1 = p.sb("rp_s1", [128, 36], F32)
    s2 = p.sb("rp_s2", [128, 36], F32)
    sq = p.sb("rp_sq", [128, 36, 64], F32)
    yn = p.sb("rp_yn", [128, NT, 128], BF16)
    lnT = p.sb("rp_lnT", [128, LTOK], BF16)
    prod = p.sb("rp_prod", [128, LTOK], BF16)
    y3 = y.rr("p t (h c) -> p (t h) c", c=64)
    for tt in range(NT):
        for hh in range(2):
            yv = y[:, tt, hh * 64:(hh + 1) * 64].on(ybufs[tt][hh])
            yv1 = yz[1][:, tt, hh * 64:(hh + 1) * 64].on(ybufs[tt][hh])
            p.tt(yv, yv, yv1, ALU.add, eng='pool')
            p.tt(sq[:, tt * 2 + hh, :], yv, yv, ALU.mult)
    p.reduce(s2, sq, ALU.add)
    p.reduce(s1, y3, ALU.add)
    p.ts(s1, s1, 1.0 / 64, ALU.mult)
    p.tt(sq[:, :, 0], s1, s1, ALU.mult)
    p.stt(s2, s2, 1.0 / 64, sq[:, :, 0], ALU.mult, ALU.subtract)
    p.act(s2, s2, AF.Sqrt, bias=RW_LN_EPS)
    p.recip(s2, s2)
    yn3 = yn.rr("p t (h c) -> p (t h) c", c=64)
    p.tt(sq, y3, s1.rr("p (n o) -> p n o", o=1).bcast([128, 36, 64]), ALU.subtract)
    p.tt(yn3, sq, s2.rr("p (n o) -> p n o", o=1).bcast([128, 36, 64]), ALU.mult)
    lnx = self.col('lnx')
    for g0 in range(0, NT, 8):
        n = min(8, NT - g0)
        ps = self.bank()
        psb = ps.bc(BF16)
        for i in range(n):
            p.tr(psb[:, i * 128:(i + 1) * 128], yn[:, g0 + i, :], self.identb)
        p.ts(lnT[:, g0 * 128:(g0 + n) * 128], psb[:, 0:n * 128], lnx[:, cc:cc + 1], ALU.mult, lnx[:, 8 + cc:8 + cc + 1], ALU.add)
    p.stt(prod, rT, kv[:, 16 + cc:16 + cc + 1], kT, ALU.mult, ALU.mult)
    stage = [p.sb("rp_stage%d" % i, [128, 512], BF16) for i in range(2)]
    tmpb = [p.sb("rp_tmp%d" % i, [128, 512], F32) for i in range(2)]
    for i, (t0, n) in enumerate(BLKS):
        psA = self.bank()
        p.mm(psA[:, 0:n], self.bones, prod[:, t0:t0 + n])
        psG = self.bank()
        p.mm(psG[:, 0:n], g2b, tg[:, t0:t0 + n])
        tb = tmpb[i % 2]
        p.tt(tb[:, 0:n], psA[:, 0:n], vT[:, t0:t0 + n], ALU.mult)
        p.tt(tb[:, 0:n], tb[:, 0:n], lnT[:, t0:t0 + n], ALU.add, eng='pool')
        sg = stage[i % 2]
        p.tt(sg[:, 0:n], psG[:, 0:n], tb[:, 0:n], ALU.mult)
        p.dma(scr['o'][cc, :, t0:t0 + n].on(scr['obuf'][cc]), sg[:, 0:n])
    p.barrier()
    p.release(m)


def _rw_mixer(self, b, xi, xo, last=True):
    p = self.p
    l = 1
    m = p.mark()
    tw = [p.sb("rw_tw%d" % z, [128, LTOK], BF16) for z in range(2)]
    ta = p.sb("rw_ta", [128, LTOK], BF16)
    tg = p.sb("rw_tg", [128, LTOK], BF16)
    mh = p.mark()
    hT = p.sb("rw_hT", [128, 8, LTOK], BF16)
    self.prenorm(b, xi, l, 0, hT)
    self.rw_phase1(b, hT, tw, ta, tg)
    p.barrier()
    p.release(mh)
    for cc in range(8):
        self.rw_chunk(b, cc, tw, ta, tg, last)
    p.barrier()
    p.release(m)
    m = p.mark()
    oT = p.sb("rw_oT", [128, 8, LTOK], BF16)
    for c in range(8):
        p.dma(oT[:, c, :], self.scr[b]['o'][c].on(self.scr[b]['obuf'][c]))
    wo = p.sb("rw_wo", [128, 8, D], BF16)
    for c in range(8):
        p.dma(wo[:, c, :], self.w_o[c * 128:(c + 1) * 128, :], eng='pool')
    G = [p.sb("rwo_G%d" % i, [128, D], F32) for i in range(2)]
    gtmp = p.sb("rwo_gtmp", [128, 128], F32)
    st = self.res_state(2)
    self.gate_row(G[0], l, 2, b, gtmp)
    if not last:
        self.gate_row(G[1], l, 2, 4, gtmp)
    for tt in (range(2, NT) if last else range(NT)):
        yb = [self.bank(), self.bank()]
        for h in range(2):
            for c in range(8):
                p.mm(yb[h], oT[:, c, tt * 128:(tt + 1) * 128], wo[:, c, h * 512:(h + 1) * 512], start=(c == 0), stop=(c == 7))
        self.residual_tile(b, tt, xi, xo, yb, G[1] if tt < 2 else G[0], st[tt % 2])
    p.barrier()
    p.release(m)


KE.rw_chunk = _rw_chunk
KE.rw_mixer = _rw_mixer


_NC_CACHE = {}
W_NAMES = ('w_mod', 'ffn_w_up', 'ffn_w_down')
EV_NAMES = ('ev_w_in', 'ev_w_out', 'ev_gla_w2')
RW_NAMES = ('rw_w_rkv', 'rw_w_o', 'rw_w1', 'rw_w2', 'rw_a1', 'rw_a2', 'rw_g1', 'rw_g2')


def build_program(NB, pk):
    k = KE(NB, pk.cols, pk.n, dbg=False)
    k.setup_even()
    k.setup_rw()
    k.mod_stage(0)
    for b in range(NB):
        hT, mixT, m = k.even_mixer(b, 0, 1)
        k.gla(b, hT, mixT)
        k.even_out(b, 0, 1, hT, mixT, m)
        k.ffn(b, 0, 1, 2, do_ctx=True)
    k.setup_rw_consts()
    k.mod_stage(1)
    for b in range(NB):
        k.rw_mixer(b, 2, 3, last=True)
        k.ffn(b, 1, 3, 4, do_ctx=False)
    return k.p.finish()


def kernel(**inp):
    inp = {k_: np.asarray(v_, dtype=np.float32) for k_, v_ in inp.items()}
    NCORE = 8
    B = inp['x'].shape[0]
    NB = B // NCORE
    pk = host_small(inp)
    host_small_even(pk, inp)
    host_small_rw(pk, inp)
    small = pk.pack()
    rows = np.zeros((16, D), np.float32)
    rows[0:8] = inp['norm_g'].reshape(8, D)
    rows[8, :16] = inp['ev_b_gates'][0]
    nc = build_program(NB, pk)
    shared = {"small": small, "rows": rows}
    for nm in W_NAMES:
        shared[nm] = np.ascontiguousarray(inp[nm])
    for nm in EV_NAMES + RW_NAMES:
        shared[nm] = np.ascontiguousarray(inp[nm][0])
    in_maps = []
    for c in range(NCORE):
        sl = slice(c * NB, (c + 1) * NB)
        cc = np.zeros((6, D), np.float32)
        cc[0:NB] = inp['c'][sl]
        cc[4] = inp['c_ctx']
        ccT = np.ascontiguousarray(cc.reshape(6, 8, 128).transpose(2, 1, 0).reshape(128, 48))
        xcat = np.ascontiguousarray(np.concatenate([inp['ctx'][sl], inp['x'][sl]], axis=1))
        d = dict(shared)
        d["xcat"] = xcat
        d["ccT"] = ccT
        in_maps.append(d)
    res = run_bass_kernel_spmd(nc, in_maps, core_ids=list(range(NCORE)))
    out = np.concatenate([np.asarray(r["out"]) for r in res.results], axis=0)
    return out.astype(np.float32)
```

```python
import numpy as np
import concourse.bass as bass
import concourse.mybir as mybir
from concourse.bass_utils import run_bass_kernel_spmd

F32 = mybir.dt.float32
BF16 = mybir.dt.bfloat16
AF = mybir.ActivationFunctionType
ALU = mybir.AluOpType
AX = mybir.AxisListType
ENG = ('sp', 'act', 'dve', 'pool', 'pe')
DSZ = {F32: 4, BF16: 2}
SAME_SYNC = True
NDMASEM = 8


class Buf:
    __slots__ = ('name', 'w', 'r', 'excl', 'subs')

    def __init__(self, name, excl=False):
        self.name = name
        self.w = None
        self.r = {}
        self.excl = excl
        self.subs = []


class V:
    __slots__ = ('ap', 'buf')

    def __init__(self, ap, buf):
        self.ap = ap
        self.buf = buf

    def __getitem__(self, idx):
        return V(self.ap[idx], self.buf)

    def rr(self, pat, **kw):
        return V(self.ap.rearrange(pat, **kw), self.buf)

    def bc(self, dt):
        return V(self.ap.bitcast(dt), self.buf)

    def on(self, buf):
        if isinstance(self.buf, Buf) and buf is not self.buf and buf not in self.buf.subs:
            self.buf.subs.append(buf)
        return V(self.ap, buf)

    def bcast(self, shape):
        return V(self.ap.broadcast_to(list(shape)), self.buf)

    def pbcast(self, n):
        return V(self.ap.partition_broadcast(n), self.buf)

    @property
    def shape(self):
        return tuple(self.ap.shape)


class Prog:
    def __init__(self):
        nc = self.nc = bass.Bass("TRN2", target_bir_lowering=False)
        self.q = {e: [] for e in ENG}
        self.cnt = {e: 0 for e in ENG}
        self.sem = {e: nc.alloc_semaphore('sem_' + e) for e in ENG}
        self.seen = {e: {} for e in ENG}
        self.dsem = [nc.alloc_semaphore('dsem%d' % i) for i in range(NDMASEM)]
        self.dval = [0] * NDMASEM
        self.dnext = 0
        self.sb_off = 16512
        self.sb_max = 0
        self.nalloc = 0
        self.ninst = 0
        self.regions = []

    def dram(self, name, shape, dt, kind="Internal"):
        t = self.nc.dram_tensor(name, list(shape), dt, kind=kind)
        return V(t.ap(), Buf(name))

    def sb(self, name, shape, dt, nbuf=None):
        per = int(np.prod(shape[1:])) * DSZ[dt]
        per = (per + 63) // 64 * 64
        off = self.sb_off
        self.sb_off += per
        self.sb_max = max(self.sb_max, self.sb_off)
        assert self.sb_off <= 229344, (name, self.sb_off)
        self.nalloc += 1
        t = self.nc.alloc_sbuf_tensor_at("%s_%d" % (name, self.nalloc), list(shape), dt, offset=off)
        nb = Buf(name)
        lo, hi = off, off + per
        keep = []
        for (a, b_, ob) in self.regions:
            if a < hi and lo < b_:
                toks = []
                for ob2 in [ob] + ob.subs:
                    toks += list(ob2.r.values())
                    if ob2.w is not None:
                        toks.append(ob2.w)
                for tk in toks:
                    k = id(tk[0])
                    if k not in nb.r or nb.r[k][1] < tk[1]:
                        nb.r[k] = tk
                if a < lo:
                    keep.append((a, lo, ob))
                if b_ > hi:
                    keep.append((hi, b_, ob))
            else:
                keep.append((a, b_, ob))
        keep.append((lo, hi, nb))
        self.regions = keep
        return V(t.ap(), nb)

    def mark(self):
        return self.sb_off

    def release(self, m):
        self.sb_off = m

    def psum_banks(self):
        banks = []
        self.pslot_bufs = []
        self.pslot_free = [True] * 32
        for i in range(8):
            t = self.nc.alloc_psum_tensor("psb%d" % i, [128, 512], F32)
            bl = Buf("psb%d" % i, excl=True)
            self.pslot_bufs.append(bl)
            banks.append(V(t.ap(), bl))
        self.pbanks = banks
        return banks

    def palloc(self, nq, bank=None):
        banks = range(8) if bank is None else (bank,)
        for bk in banks:
            for q0 in range(0, 4, nq):
                if all(self.pslot_free[bk * 4 + q0 + j] for j in range(nq)):
                    for j in range(nq):
                        self.pslot_free[bk * 4 + q0 + j] = False
                    v = V(self.pbanks[bk].ap[:, q0 * 128:(q0 + nq) * 128], self.pslot_bufs[bk])
                    return v, (bk, q0, nq)
        raise RuntimeError("out of PSUM slots")

    def pfree(self, h):
        bk, q0, nq = h
        for j in range(nq):
            self.pslot_free[bk * 4 + q0 + j] = True

    def _emit(self, eng, fn, reads, writes, dma=False):
        waits = {}

        def need(tok, waw_pe=False):
            if tok is None:
                return
            sem, val, te = tok
            if te == eng and not dma:
                if eng == 'pe' or not SAME_SYNC:
                    return
            k = id(sem)
            if k not in waits or waits[k][1] < val:
                waits[k] = (sem, val)

        ex = [b for b in reads if b.excl and b not in writes]
        if ex:
            reads = [b for b in reads if not b.excl]
            writes = list(writes) + ex
        for b in reads:
            need(b.w)
        for b in writes:
            need(b.w)
            for t in b.r.values():
                need(t)
        if dma:
            k = self.dnext
            self.dnext = (self.dnext + 1) % NDMASEM
            if self.dval[k] > 0:
                need((self.dsem[k], self.dval[k], 'dma'))
            self.dval[k] += 16
            tok = (self.dsem[k], self.dval[k], 'dma')
            inc = (self.dsem[k], 16)
        else:
            self.cnt[eng] += 1
            tok = (self.sem[eng], self.cnt[eng], eng)
            inc = (self.sem[eng], 1)
        seen = self.seen[eng]
        final = []
        for k, (sem, val) in waits.items():
            if seen.get(k, 0) < val:
                seen[k] = val
                final.append((sem, val))
        for b in reads:
            b.r[id(tok[0])] = tok
        for b in writes:
            b.w = tok
            b.r = {}
        self.q[eng].append((final, fn, inc))
        self.ninst += 1

    def barrier(self):
        toks = [(self.sem[e], self.cnt[e]) for e in ENG if self.cnt[e] > 0]
        toks += [(self.dsem[k], self.dval[k]) for k in range(NDMASEM) if self.dval[k] > 0]
        for e in ENG:
            final = []
            for sem, val in toks:
                if sem is self.sem[e]:
                    continue
                if self.seen[e].get(id(sem), 0) < val:
                    self.seen[e][id(sem)] = val
                    final.append((sem, val))
            if final:
                self.q[e].append((final, None, None))

    def finish(self):
        self.barrier()
        nc = self.nc
        q = self.q

        def replay(e, lst):
            for waits, fn, inc in lst:
                for sem, val in waits:
                    e.wait_ge(sem, val)
                if fn is not None:
                    ins = fn(e)
                    ins.then_inc(inc[0], inc[1])

        with nc.Block() as block:
            @block.sync
            def _(e):
                replay(e, q['sp'])

            @block.scalar
            def _(e):
                replay(e, q['act'])

            @block.vector
            def _(e):
                replay(e, q['dve'])

            @block.gpsimd
            def _(e):
                replay(e, q['pool'])

            @block.tensor
            def _(e):
                replay(e, q['pe'])
        return nc

    @staticmethod
    def _a(x):
        return x.ap if isinstance(x, V) else x

    @staticmethod
    def _bufs(*xs):
        out = []
        for x in xs:
            if isinstance(x, V):
                bl = x.buf if isinstance(x.buf, (list, tuple)) else (x.buf,)
                for b in bl:
                    if b not in out:
                        out.append(b)
        return out

    def dma(self, out, in_, eng='sp', **kw):
        o, i = out.ap, in_.ap
        self._emit(eng, lambda e: e.dma_start(out=o, in_=i, **kw), self._bufs(in_), self._bufs(out), dma=True)

    def mm(self, out, lhsT, rhs, start=True, stop=True):
        o, l, r = out.ap, lhsT.ap, rhs.ap
        self._emit('pe', lambda e: e.matmul(o, lhsT=l, rhs=r, start=start, stop=stop),
                   self._bufs(lhsT, rhs), self._bufs(out))

    def tr(self, out, in_, ident):
        o, i, d = out.ap, in_.ap, ident.ap
        self._emit('pe', lambda e: e.transpose(o, i, d), self._bufs(in_, ident), self._bufs(out))

    def act(self, out, in_, func, bias=None, scale=None, accum=None):
        kw = {}
        if bias is not None:
            kw['bias'] = self._a(bias)
        if scale is not None:
            kw['scale'] = self._a(scale)
        if accum is not None:
            kw['accum_out'] = accum.ap
        o, i = out.ap, in_.ap
        self._emit('act', lambda e: e.activation(out=o, in_=i, func=func, **kw),
                   self._bufs(in_, bias, scale), self._bufs(out, accum))

    def tt(self, out, in0, in1, op, eng='dve'):
        o, a, b = out.ap, in0.ap, in1.ap
        self._emit(eng, lambda e: e.tensor_tensor(out=o, in0=a, in1=b, op=op),
                   self._bufs(in0, in1), self._bufs(out))

    def ts(self, out, in0, s1, op0, s2=None, op1=None, eng='dve', accum=None):
        o, a = out.ap, in0.ap
        a1, a2 = self._a(s1), self._a(s2)
        kw = {}
        if op1 is not None:
            kw['op1'] = op1
        if accum is not None:
            kw['accum_out'] = accum.ap
        self._emit(eng, lambda e: e.tensor_scalar(out=o, in0=a, scalar1=a1, scalar2=a2, op0=op0, **kw),
                   self._bufs(in0, s1, s2), self._bufs(out, accum))

    def stt(self, out, in0, scalar, in1, op0, op1, eng='dve'):
        o, a, b = out.ap, in0.ap, in1.ap
        s = self._a(scalar)
        self._emit(eng, lambda e: e.scalar_tensor_tensor(out=o, in0=a, scalar=s, in1=b, op0=op0, op1=op1),
                   self._bufs(in0, scalar, in1), self._bufs(out))

    def copy(self, out, in_, eng='dve'):
        o, i = out.ap, in_.ap
        if eng == 'act':
            self._emit('act', lambda e: e.activation(out=o, in_=i, func=AF.Copy), self._bufs(in_), self._bufs(out))
        else:
            self._emit(eng, lambda e: e.tensor_copy(out=o, in_=i), self._bufs(in_), self._bufs(out))

    def memset(self, out, val, eng='dve'):
        o = out.ap
        self._emit(eng, lambda e: e.memset(o, val), [], self._bufs(out))

    def reduce(self, out, in_, op, axis=AX.X, eng='dve'):
        o, i = out.ap, in_.ap
        self._emit(eng, lambda e: e.tensor_reduce(out=o, in_=i, axis=axis, op=op), self._bufs(in_), self._bufs(out))

    def recip(self, out, in_):
        o, i = out.ap, in_.ap
        self._emit('dve', lambda e: e.reciprocal(out=o, in_=i), self._bufs(in_), self._bufs(out))

    def aselect(self, out, in_, pattern, cmp, fill, base, cm):
        o, i = out.ap, in_.ap
        self._emit('pool', lambda e: e.affine_select(out=o, in_=i, pattern=pattern, compare_op=cmp, fill=fill,
                                                     base=base, channel_multiplier=cm),
                   self._bufs(in_), self._bufs(out))

    def scan(self, out, d0, d1, init, op0, op1):
        o, a, b = out.ap, d0.ap, d1.ap
        self._emit('dve', lambda e: e.tensor_tensor_scan(out=o, data0=a, data1=b, initial=init, op0=op0, op1=op1),
                   self._bufs(d0, d1), self._bufs(out))


D = 1024
NT = 18
LTOK = 2304
EPS = 1e-6
DFF = 2816
NJ = 22


class Packer:
    def __init__(self):
        self.cols = {}
        self.n = 0
        self.arrs = []

    def add(self, name, arr):
        arr = np.ascontiguousarray(arr, dtype=np.float32).reshape(128, -1)
        self.cols[name] = (self.n, arr.shape[1])
        self.n += arr.shape[1]
        self.arrs.append(arr)

    def pack(self):
        return np.ascontiguousarray(np.concatenate(self.arrs, axis=1))


def fm(v):
    v = np.asarray(v, dtype=np.float32)
    lead = v.shape[:-1]
    c = v.shape[-1] // 128
    v = v.reshape(lead + (c, 128))
    return np.moveaxis(v, -1, 0).reshape(128, -1)


def host_small(inp, layer_cols_only=False):
    pk = Packer()
    for l in range(2):
        pk.add('bmod%d' % l, fm(inp['b_mod'][l]))
        pk.add('ng%d' % l, fm(inp['norm_g'][l]))
        pk.add('cw%d' % l, np.moveaxis(inp['ffn_conv_w'][l].reshape(9, NJ, 128), 2, 0).transpose(0, 2, 1).reshape(128, NJ * 9))
        pk.add('cb%d' % l, fm(inp['ffn_conv_b'][l]))
    return pk


class K:
    def __init__(self, NB, small_cols, nsmall, dbg=False):
        self.NB = NB
        self.dbg = dbg
        p = self.p = Prog()
        self.sc = small_cols
        self.X0 = p.dram("xcat", [NB, LTOK, D], F32, kind="ExternalInput")
        self.ccT_d = p.dram("ccT", [128, 8 * 6], F32, kind="ExternalInput")
        self.small_d = p.dram("small", [128, nsmall], F32, kind="ExternalInput")
        self.rows_d = p.dram("rows", [16, D], F32, kind="ExternalInput")
        self.w_mod = p.dram("w_mod", [2, D, 6 * D], F32, kind="ExternalInput")
        self.w_up = p.dram("ffn_w_up", [2, D, 2 * DFF], F32, kind="ExternalInput")
        self.w_down = p.dram("ffn_w_down", [2, DFF, D], F32, kind="ExternalInput")
        self.Xs = [self.X0]
        for i in range(1, 4):
            self.Xs.append(p.dram("xs%d" % i, [NB, LTOK, D], F32, kind="ExternalOutput" if dbg else "Internal"))
        self.out = p.dram("out", [NB, 2048, D], F32, kind="ExternalOutput")
        self.xbufs = [[[Buf("x%d_%d_%d" % (i, b, t)) for t in range(NT)] for b in range(NB)] for i in range(5)]
        self.banks = p.psum_banks()
        self.nbank = 0
        self.small = p.sb("small", [128, nsmall], F32)
        p.dma(self.small, self.small_d)
        self.identf = p.sb("identf", [128, 128], F32)
        p.memset(self.identf, 1.0, eng='pool')
        p.aselect(self.identf, self.identf, [[-1, 128]], ALU.is_equal, 0.0, 0, 1)
        self.identb = p.sb("identb", [128, 128], BF16)
        p.copy(self.identb, self.identf, eng='pool')
        self.scT = p.sb("scT", [128, 48], F32)
        p.dma(self.scT, self.ccT_d)
        p.act(self.scT, self.scT, AF.Silu)
        self.modT = p.sb("modT", [128, 48 * 6], F32)
        self.A1 = p.sb("A1", [128, 48], F32)
        self.A2 = p.sb("A2", [128, 48], F32)
        self.base_mark = p.mark()

    def bank(self):
        b = self.banks[self.nbank]
        self.nbank = (self.nbank + 1) % 8
        return b

    @staticmethod
    def run_rr(gens, stagger=0):
        pending = list(gens)
        active = []
        it_ = 0
        while active or pending:
            if pending and (stagger == 0 or it_ % stagger == 0 or not active):
                if stagger == 0:
                    active += pending
                    pending = []
                else:
                    active.append(pending.pop(0))
            it_ += 1
            nxt_ = []
            for g in active:
                try:
                    next(g)
                    nxt_.append(g)
                except StopIteration:
                    pass
            active = nxt_

    def col(self, name, a=None, b=None):
        o, w = self.sc[name]
        if a is None:
            return self.small[:, o:o + w]
        return self.small[:, o + a:o + b]

    def mod_stage(self, l):
        p = self.p
        m = p.mark()
        wst = [p.sb("wmod_st%d" % i, [128, 8, 512], F32) for i in range(2)]
        ps = self.bank()
        for blk in range(12):
            w = wst[blk % 2]
            p.dma(w, self.w_mod[l, :, blk * 512:(blk + 1) * 512].rr("(c p) n -> p c n", p=128))
            for f in range(4):
                fc = blk * 4 + f
                for kc in range(8):
                    p.mm(ps[:, fc * 6:(fc + 1) * 6], w[:, kc, f * 128:(f + 1) * 128], self.scT[:, kc * 6:(kc + 1) * 6],
                         start=(kc == 0), stop=(kc == 7))
        p.tt(self.modT.rr("p (c r) -> p c r", r=6), ps[:, 0:288].rr("p (c r) -> p c r", r=6),
             self.col('bmod%d' % l).rr("p (c o) -> p c o", o=1).bcast([128, 48, 6]), ALU.add)
        mv = self.modT.rr("p (i c r) -> p i c r", i=6, c=8)
        ng = self.col('ng%d' % l).rr("p (i c o) -> p i c o", i=4, o=1)
        for A, mi, gi in ((self.A1, 1, 0), (self.A2, 4, 2)):
            Av = A.rr("p (c r) -> p c r", r=6)
            p.ts(Av, mv[:, mi], 1.0, ALU.add)
            p.tt(Av, Av, ng[:, gi].bcast([128, 8, 6]), ALU.mult)
        p.release(m)

    def modvec(self, l, which):
        mv = self.modT.rr("p (i c r) -> p i c r", i=6, c=8)
        if which == 0:
            return self.A1.rr("p (c r) -> p c r", r=6), mv[:, 0]
        return self.A2.rr("p (c r) -> p c r", r=6), mv[:, 3]

    def gate_row(self, dst, l, gi, row, tmp):
        p = self.p
        mv = self.modT.rr("p (i c r) -> p i c r", i=6, c=8)
        nrow = l * 4 + (1 if gi == 2 else 3)
        p.dma(dst, self.rows_d[nrow:nrow + 1, :].pbcast(128))
        for half in range(2):
            ps = self.bank()
            for cc in range(4):
                c = half * 4 + cc
                p.copy(tmp, mv[:, gi, c, row:row + 1].bcast([128, 128]))
                p.mm(ps[:, cc * 128:(cc + 1) * 128], tmp, self.identf)
            p.tt(dst[:, half * 512:(half + 1) * 512], dst[:, half * 512:(half + 1) * 512], ps, ALU.mult)

    def prenorm(self, b, xi, l, which, hT, tiles=range(NT)):
        p = self.p
        A, Bv = self.modvec(l, which)
        m = p.mark()
        NG = 4
        xt = [p.sb("pn_x%d" % i, [128, D], F32) for i in range(NG)]
        junk = p.sb("pn_junk", [128, D], BF16)
        xn = [p.sb("pn_xn%d" % i, [128, D], BF16) for i in range(NG)]
        ss = [p.sb("pn_ss%d" % i, [128, 1], F32) for i in range(NG)]
        tmp = [p.sb("pn_tmp%d" % i, [128, 8, 128], F32) for i in range(NG)]
        tiles = list(tiles)

        def chain(g):
            for tt in tiles[g::NG]:
                x, s = xt[g], ss[g]
                p.dma(x, self.Xs[xi][b, tt * 128:(tt + 1) * 128, :].on(self.xbufs[xi][b][tt]))
                yield
                p.act(junk, x, AF.Square, accum=s)
                yield
                p.act(s, s, AF.Sqrt, bias=EPS, scale=1.0 / D)
                yield
                p.recip(s, s)
                yield
                p.act(xn[g], x, AF.Identity, scale=s)
                yield
                ps, hp = p.palloc(4, 2 * g)
                psb = ps.bc(BF16)
                for c in range(8):
                    p.tr(psb[:, c * 128:(c + 1) * 128], xn[g][:, c * 128:(c + 1) * 128], self.identb)
                yield
                row = 4 if tt < 2 else b
                p.tt(tmp[g], psb.rr("p (c t) -> p c t", c=8), A[:, :, row:row + 1].bcast([128, 8, 128]), ALU.mult)
                p.pfree(hp)
                yield
                p.tt(hT[:, :, tt * 128:(tt + 1) * 128], tmp[g], Bv[:, :, row:row + 1].bcast([128, 8, 128]), ALU.add,
                     eng='pool')
                yield

        self.run_rr([chain(g) for g in range(NG)])
        p.release(m)

    def residual_tile(self, b, tt, xi, xo, ybanks, G, st):
        p = self.p
        x, junk, ss2, s, tmp = st
        p.dma(x, self.Xs[xi][b, tt * 128:(tt + 1) * 128, :].on(self.xbufs[xi][b][tt]))
        for h in range(2):
            p.act(junk, ybanks[h], AF.Square, accum=ss2[:, h:h + 1])
        p.tt(s, ss2[:, 0:1], ss2[:, 1:2], ALU.add)
        p.act(s, s, AF.Sqrt, bias=EPS, scale=1.0 / D)
        p.recip(s, s)
        for h in range(2):
            p.stt(tmp[:, h * 512:(h + 1) * 512], ybanks[h], s, G[:, h * 512:(h + 1) * 512], ALU.mult, ALU.mult)
        p.tt(tmp, tmp, x, ALU.add, eng='pool')
        if xo == 4:
            dst = self.out[b, (tt - 2) * 128:(tt - 1) * 128, :]
        else:
            dst = self.Xs[xo][b, tt * 128:(tt + 1) * 128, :]
        p.dma(dst.on(self.xbufs[xo][b][tt]), tmp)

    def res_state(self, n=2):
        p = self.p
        return [(p.sb("rs_x%d" % i, [128, D], F32), p.sb("rs_junk%d" % i, [128, 512], BF16),
                 p.sb("rs_ss2%d" % i, [128, 2], F32), p.sb("rs_s%d" % i, [128, 1], F32),
                 p.sb("rs_tmp%d" % i, [128, D], F32)) for i in range(n)]

    def ffn(self, b, l, xi, xo, do_ctx=True):
        p = self.p
        m = p.mark()
        hT = p.sb("ffn_hT", [128, 8, LTOK], BF16)
        self.prenorm(b, xi, l, 1, hT, tiles=range(NT) if do_ctx else range(2, NT))
        wdown = p.sb("ffn_wdown", [128, NJ, D], BF16)
        for j in range(NJ):
            p.dma(wdown[:, j, :], self.w_down[l, j * 128:(j + 1) * 128, :], eng='pool')
        actT = p.sb("ffn_actT", [128, NJ, 1280], BF16)
        wg = [p.sb("ffn_wg%d" % i, [128, 8, 128], BF16) for i in range(2)]
        wu = [p.sb("ffn_wu%d" % i, [128, 8, 128], BF16) for i in range(2)]
        gpad = [p.sb("ffn_gpad%d" % i, [128, 18, 66], BF16) for i in range(2)]
        cpad = p.sb("ffn_cpad", [128, 258], BF16)
        gg = [p.sb("ffn_gg%d" % i, [128, 512], BF16) for i in range(2)]
        diag = [p.sb("ffn_diag%d" % i, [128, 9, 128], BF16) for i in range(2)]
        G = [p.sb("ffn_G%d" % i, [128, D], F32) for i in range(2)]
        gtmp = p.sb("ffn_gtmp", [128, 128], F32)
        st = self.res_state(2)
        for g in gpad:
            p.memset(g, 0.0, eng='pool')
        p.memset(cpad, 0.0, eng='pool')
        self.gate_row(G[0], l, 5, b, gtmp)
        if do_ctx:
            self.gate_row(G[1], l, 5, 4, gtmp)
        cw = self.col('cw%d' % l)
        cb = self.col('cb%d' % l)
        nn = 0
        for seg in range(2):
            r0 = 16 * seg
            g0 = 0 if seg == 0 else 15
            prow0 = 1 if seg == 0 else 0
            gp = gpad[seg]
            with_ctx = (seg == 0 and do_ctx)
            for j in range(NJ):
                wgj, wuj = wg[j % 2], wu[j % 2]
                p.dma(wgj, self.w_up[l, :, j * 128:(j + 1) * 128].rr("(c p) n -> p c n", p=128), eng='pool')
                p.dma(wuj, self.w_up[l, :, DFF + j * 128:DFF + (j + 1) * 128].rr("(c p) n -> p c n", p=128), eng='pool')
                dg = diag[j % 2]
                p.tt(dg, self.identb.rr("p (o t) -> p o t", o=1).bcast([128, 9, 128]),
                     cw[:, j * 9:(j + 1) * 9].rr("p (t o) -> p t o", o=1).bcast([128, 9, 128]), ALU.mult, eng='pool')
                tok0 = 256 + g0 * 64
                for (o, n) in ((0, 512), (512, 512), (1024, 64)):
                    ps = self.bank()
                    for kc in range(8):
                        p.mm(ps[:, 0:n], wgj[:, kc, :], hT[:, kc, tok0 + o:tok0 + o + n], start=(kc == 0), stop=(kc == 7))
                    pr = prow0 + o // 64
                    p.copy(gp[:, pr:pr + n // 64, 1:65], ps[:, 0:n].rr("p (r c) -> p r c", c=64), eng='act')
                for blk in range(2):
                    ps = self.bank()
                    for t in range(9):
                        dy, dx = t // 3, t % 3
                        p.mm(ps, dg[:, t, :], gp[:, 8 * blk + dy:8 * blk + dy + 8, dx:dx + 64], start=(t == 0), stop=(t == 8))
                    g_ = gg[nn % 2]
                    nn += 1
                    p.act(g_, ps, AF.Gelu_apprx_tanh, bias=cb[:, j:j + 1])
                    ps2 = self.bank()
                    t0 = 256 + r0 * 64 + blk * 512
                    for kc in range(8):
                        p.mm(ps2, wuj[:, kc, :], hT[:, kc, t0:t0 + 512], start=(kc == 0), stop=(kc == 7))
                    p.tt(actT[:, j, 256 + blk * 512:256 + (blk + 1) * 512], ps2, g_, ALU.mult)
                if with_ctx:
                    ps = self.bank()
                    for kc in range(8):
                        p.mm(ps[:, 0:256], wgj[:, kc, :], hT[:, kc, 0:256], start=(kc == 0), stop=(kc == 7))
                    p.copy(cpad[:, 1:257], ps[:, 0:256], eng='act')
                    ps = self.bank()
                    for dx in range(3):
                        p.mm(ps[:, 0:256], dg[:, 3 + dx, :], cpad[:, dx:dx + 256], start=(dx == 0), stop=(dx == 2))
                    g_ = gg[nn % 2]
                    nn += 1
                    p.act(g_[:, 0:256], ps[:, 0:256], AF.Gelu_apprx_tanh, bias=cb[:, j:j + 1])
                    ps2 = self.bank()
                    for kc in range(8):
                        p.mm(ps2[:, 0:256], wuj[:, kc, :], hT[:, kc, 0:256], start=(kc == 0), stop=(kc == 7))
                    p.tt(actT[:, j, 0:256], ps2[:, 0:256], g_[:, 0:256], ALU.mult)
            tiles = ([0, 1] if with_ctx else []) + [2 + 8 * seg + i for i in range(8)]
            for n, tt in enumerate(tiles):
                a0 = tt * 128 if tt < 2 else 256 + (tt - 2 - 8 * seg) * 128
                yb = [self.bank(), self.bank()]
                for h in range(2):
                    for j in range(NJ):
                        p.mm(yb[h], actT[:, j, a0:a0 + 128], wdown[:, j, h * 512:(h + 1) * 512], start=(j == 0), stop=(j == NJ - 1))
                self.residual_tile(b, tt, xi, xo, yb, G[1] if tt < 2 else G[0], st[n % 2])
        p.release(m)


LNS_ML = float(np.log(128.0 ** -0.5))
LNS_GLA = float(np.log(64.0 ** -0.5))
ORD = [list(range(NT)), [1, 0] + list(range(17, 1, -1))]
C_MQ, C_MK, C_MV, C_MO, C_MG, C_GQ, C_GK, C_GV, C_GG, C_GLR = 0, 512, 1024, 1536, 2048, 2064, 2320, 2576, 3088, 3600


def host_small_even(pk, inp):
    pk.add('ecw', fm(inp['ev_conv_w'][0]))
    pk.add('ecb', fm(inp['ev_conv_b'][0]))
    pk.add('glab', fm(inp['ev_gla_b'][0]))
    pk.add('hg', fm(inp['ev_head_g'][0]))


class KE(K):
    def setup_even(self):
        p = self.p
        self.w_in = p.dram("ev_w_in", [D, 3632], F32, kind="ExternalInput")
        self.w_out = p.dram("ev_w_out", [D, D], F32, kind="ExternalInput")
        self.gla_w2 = p.dram("ev_gla_w2", [2, 16, 256], F32, kind="ExternalInput")
        self.ones = p.sb("ones", [128, 128], F32)
        p.memset(self.ones, 1.0, eng='pool')
        self.tri = []
        for z in range(2):
            t = p.sb("tri%d" % z, [128, 128], F32)
            pat, cm = ([[1, 128]], -1) if z == 0 else ([[-1, 128]], 1)
            p.aselect(t, self.ones, pat, ALU.is_ge, 0.0, 0, cm)
            self.tri.append(t)
        self.common_mark = p.mark()
        self.mask4 = []
        self.maskb = []
        for z in range(2):
            t = self.tri[z]
            m4 = p.sb("mask4%d" % z, [128, 4, 128], F32)
            p.ts(m4, t.rr("p (o t) -> p o t", o=1).bcast([128, 4, 128]), -1.0, ALU.add, 30000.0, ALU.mult, eng='pool')
            self.mask4.append(m4)
            mb = p.sb("maskb%d" % z, [128, 128], BF16)
            p.copy(mb, t, eng='pool')
            self.maskb.append(mb)
        self.bgrow = p.sb("bgrow", [128, 16], F32)
        p.dma(self.bgrow, self.rows_d[8:9, 0:16].pbcast(128))
        self.base_mark = p.mark()

    def headnorm_gate(self, y, ybufs, hd, gate_col, gate_fn, hT, mixT, scratch):
        p = self.p
        sq, ss, yn, sg, wgt = scratch
        p.dma(wgt, self.w_in[:, gate_col:gate_col + 128].rr("(c p) n -> p c n", p=128), eng='pool')
        for blk in range(5):
            t0, n = blk * 512, (512 if blk < 4 else 256)
            ps = self.bank()
            for kc in range(8):
                p.mm(ps[:, 0:n], wgt[:, kc, :], hT[:, kc, t0:t0 + n], start=(kc == 0), stop=(kc == 7))
            p.act(sg[:, t0:t0 + n], ps[:, 0:n], gate_fn)
        yall = V(y.ap, Buf('yall'))
        for tt in range(NT):
            p.tt(sq[:, tt, :], y[:, tt, :].on(ybufs[tt]), y[:, tt, :].on(ybufs[tt]), ALU.mult)
        p.reduce(ss, sq, ALU.add)
        p.act(ss, ss, AF.Sqrt, bias=EPS, scale=1.0 / 128)
        p.recip(ss, ss)
        for tt in range(NT):
            p.ts(yn[:, tt, :], y[:, tt, :].on(ybufs[tt]), ss[:, tt:tt + 1], ALU.mult)
        hg = self.col('hg')
        for g0 in (0, 8, 16):
            n = min(8, NT - g0)
            ps = self.bank()
            psb = ps.bc(BF16)
            for i in range(n):
                p.tr(psb[:, i * 128:(i + 1) * 128], yn[:, g0 + i, :], self.identb)
            p.stt(mixT[:, hd, g0 * 128:(g0 + n) * 128], psb[:, 0:n * 128], hg[:, hd:hd + 1], sg[:, g0 * 128:(g0 + n) * 128],
                  ALU.mult, ALU.mult)

    def even_mixer(self, b, xi, xo):
        p = self.p
        l = 0
        m = p.mark()
        hT = p.sb("ev_hT", [128, 8, LTOK], BF16)
        self.prenorm(b, xi, l, 0, hT)
        mixT = p.sb("ev_mixT", [128, 8, LTOK], BF16)
        wgate = p.sb("ev_wgate", [128, 8, 16], BF16)
        p.dma(wgate, self.w_in[:, C_MG:C_MG + 16].rr("(c p) n -> p c n", p=128), eng='pool')
        graw = p.sb("ev_graw", [128, NT, 16], F32)
        ps = self.bank()
        for tt in range(NT):
            for kc in range(8):
                p.mm(ps[:, tt * 16:(tt + 1) * 16], hT[:, kc, tt * 128:(tt + 1) * 128], wgate[:, kc, :], start=(kc == 0), stop=(kc == 7))
        p.tt(graw, ps[:, 0:NT * 16].rr("p (t n) -> p t n", n=16), self.bgrow.rr("p (o n) -> p o n", o=1).bcast([128, NT, 16]), ALU.add)
        g5 = graw.rr("p t (z y h) -> p t z y h", z=2, y=2)
        I8 = g5[:, :, :, 0, :]
        F8 = g5[:, :, :, 1, :]
        def s8(name):
            return p.sb(name, [128, NT, 2, 4], F32)
        lf8, Fc8, Ft8, bs8, qs8, wk8, dc8 = [s8("ev_" + n) for n in ("lf8", "Fc8", "Ft8", "bs8", "qs8", "wk8", "dc8")]
        p.act(lf8, F8, AF.Exp, scale=-1.0)
        p.act(lf8, lf8, AF.Ln, bias=1.0)
        p.ts(lf8, lf8, -1.0, ALU.mult)
        ps = self.bank()
        for z in range(2):
            p.mm(ps[:, z * 72:(z + 1) * 72], self.tri[z], lf8[:, :, z, :])
        p.mm(ps[:, 144:288], self.ones, lf8)
        for z in range(2):
            p.copy(Fc8[:, :, z, :], ps[:, z * 72:(z + 1) * 72].rr("p (t h) -> p t h", h=4))
        p.copy(Ft8, ps[:, 144:288].rr("p (t z h) -> p t z h", z=2, h=4))
        p.tt(bs8, I8, Fc8, ALU.subtract)
        p.tt(wk8, Ft8, bs8, ALU.add)
        p.act(wk8, wk8, AF.Exp)
        p.ts(bs8, bs8, LNS_ML, ALU.add)
        p.act(qs8, Fc8, AF.Exp, bias=LNS_ML)
        p.act(dc8, Ft8, AF.Exp)
        Fc5 = Fc8.rr("p t z (h o) -> p t z h o", o=1)

        cw = self.col('ecw').rr("p (t c) -> p t c", t=3)
        cbias = self.col('ecb')
        for hp in range(2):
            m2 = p.mark()
            heads = [2 * hp, 2 * hp + 1]
            qT = [p.sb("ml_qT%d" % i, [128, LTOK], BF16) for i in range(2)]
            kT = [p.sb("ml_kT%d" % i, [128, LTOK], BF16) for i in range(2)]
            vv = [p.sb("ml_v%d" % i, [128, NT, 130], BF16) for i in range(2)]
            yy = [p.sb("ml_y%d" % i, [128, NT, 128], F32) for i in range(2)]
            ybufs = [[Buf("mly%d_%d" % (i, t)) for t in range(NT)] for i in range(2)]
            m3 = p.mark()
            xpad = p.sb("ml_xpad", [128, 2308], F32)
            t1 = p.sb("ml_t1", [128, 2306], F32)
            t2 = p.sb("ml_t2", [128, 2306], F32)
            wq = [p.sb("ml_wq%d" % i, [128, 8, 128], BF16) for i in range(2)]
            p.memset(xpad, 0.0, eng='pool')
            nw = 0
            for i, h in enumerate(heads):
                for dst, c0, cc in ((qT[i], C_MQ + h * 128, h), (kT[i], C_MK + h * 128, 4 + h)):
                    w = wq[nw % 2]
                    nw += 1
                    p.dma(w, self.w_in[:, c0:c0 + 128].rr("(c p) n -> p c n", p=128), eng='pool')
                    for blk in range(5):
                        t0, n = (0, 256) if blk == 0 else (256 + (blk - 1) * 512, 512)
                        pos = 1 if blk == 0 else 259 + (blk - 1) * 512
                        ps = self.bank()
                        for kc in range(8):
                            p.mm(ps[:, 0:n], w[:, kc, :], hT[:, kc, t0:t0 + n], start=(kc == 0), stop=(kc == 7))
                        p.copy(xpad[:, pos:pos + n], ps[:, 0:n], eng='act')
                    p.ts(t1, xpad[:, 0:2306], cw[:, 0, cc:cc + 1], ALU.mult, cbias[:, cc:cc + 1], ALU.add)
                    p.stt(t2, xpad[:, 1:2307], cw[:, 1, cc:cc + 1], t1, ALU.mult, ALU.add)
                    p.stt(t1, xpad[:, 2:2308], cw[:, 2, cc:cc + 1], t2, ALU.mult, ALU.add)
                    p.act(dst[:, 0:256], t1[:, 0:256], AF.Silu)
                    p.act(dst[:, 256:LTOK], t1[:, 258:2306], AF.Silu)
                w = wq[nw % 2]
                nw += 1
                p.dma(w, self.w_in[:, C_MV + h * 128:C_MV + (h + 1) * 128].rr("(c p) n -> p c n", p=128), eng='pool')
                p.memset(vv[i][:, :, 128:130], 1.0, eng='pool')
                for g0 in range(0, NT, 4):
                    n = min(4, NT - g0)
                    ps = self.bank()
                    for j in range(n):
                        tt = g0 + j
                        for kc in range(8):
                            p.mm(ps[:, j * 128:(j + 1) * 128], hT[:, kc, tt * 128:(tt + 1) * 128], w[:, kc, :], start=(kc == 0), stop=(kc == 7))
                    p.copy(vv[i][:, g0:g0 + n, 0:128], ps[:, 0:n * 128].rr("p (t n) -> p t n", n=128), eng='act')
                p.memset(yy[i], 0.0, eng='pool')
            p.release(m3)
            Cs = [[p.sb("ml_C%d%d" % (z, i), [128, 132], F32) for i in range(2)] for z in range(2)]
            Cb = [[p.sb("ml_Cb%d%d" % (z, i), [128, 132], BF16) for i in range(2)] for z in range(2)]
            for z in range(2):
                for i in range(2):
                    p.memset(Cs[z][i], 0.0, eng='pool')
                    p.memset(Cb[z][i], 0.0, eng='pool')
            dg1 = [[p.sb("ml_dg%d_%d" % (c, k2), [128, 128], F32) for k2 in range(2)] for c in range(4)]
            Dm = [[p.sb("ml_Dm%d_%d" % (c, k2), [128, 128], BF16) for k2 in range(2)] for c in range(4)]
            sT = [[p.sb("ml_sT%d_%d" % (c, k2), [128, 128], BF16) for k2 in range(2)] for c in range(4)]
            ktil = [[p.sb("ml_ktil%d_%d" % (c, k2), [128, 128], BF16) for k2 in range(2)] for c in range(4)]
            tmpo = [p.sb("ml_tmpo%d" % i, [128, 132], F32) for i in range(4)]
            num = [p.sb("ml_num%d" % i, [128, 132], F32) for i in range(4)]
            den = [p.sb("ml_den%d" % i, [128, 1], F32) for i in range(4)]

            def ml_chain(z, i, h, c):
                b0, b1 = 2 * c, 2 * c + 1
                for step in range(NT):
                    tt = ORD[z][step]
                    ts_ = slice(tt * 128, (tt + 1) * 128)
                    k2 = step % 2
                    upd = step < NT - 1
                    dg = dg1[c][k2]
                    p.ts(dg, self.identf, Fc8[:, tt, z, h:h + 1], ALU.mult)
                    rb, hrb = p.palloc(1, b0)
                    p.mm(rb[:, 0:128], self.ones, dg, start=True, stop=False)
                    p.mm(rb[:, 0:128], self.identf, self.mask4[z][:, 0, :], start=False, stop=True)
                    sc, hsc = p.palloc(1, b0)
                    p.mm(sc[:, 0:128], kT[i][:, ts_], qT[i][:, ts_])
                    if upd:
                        kp, hkp = p.palloc(1, b0)
                        kpb = kp.bc(BF16)
                        p.tr(kpb[:, 0:128], kT[i][:, ts_], self.identb)
                    yield
                    p.act(Dm[c][k2], rb[:, 0:128], AF.Exp, bias=bs8[:, tt, z, h:h + 1])
                    p.pfree(hrb)
                    if upd:
                        p.act(ktil[c][k2], kpb[:, 0:128], AF.Identity, scale=wk8[:, tt, z, h:h + 1])
                        p.pfree(hkp)
                    yield
                    p.tt(sT[c][k2], sc[:, 0:128], Dm[c][k2], ALU.mult)
                    p.pfree(hsc)
                    yield
                    o1, ho1 = p.palloc(2, b1)
                    p.mm(o1[:, 0:129], sT[c][k2], vv[i][:, tt, 0:129])
                    o2, ho2 = p.palloc(2, b1)
                    p.mm(o2[:, 0:129], qT[i][:, ts_], Cb[z][i][:, 0:129])
                    if upd:
                        cu, hcu = p.palloc(2, b0)
                        p.mm(cu[:, 0:129], ktil[c][k2], vv[i][:, tt, 0:129])
                    yield
                    p.act(tmpo[c][:, 0:129], o2[:, 0:129], AF.Identity, scale=qs8[:, tt, z, h:h + 1])
                    p.pfree(ho2)
                    if upd:
                        p.stt(Cs[z][i][:, 0:129], Cs[z][i][:, 0:129], dc8[:, tt, z, h:h + 1], cu[:, 0:129], ALU.mult, ALU.add)
                        p.pfree(hcu)
                    yield
                    p.tt(num[c][:, 0:129], tmpo[c][:, 0:129], o1[:, 0:129], ALU.add)
                    p.pfree(ho1)
                    if upd:
                        p.copy(Cb[z][i][:, 0:129], Cs[z][i][:, 0:129], eng='pool')
                    yield
                    p.act(den[c], num[c][:, 128:129], AF.Abs)
                    yield
                    p.ts(den[c], den[c], 1.0, ALU.max)
                    p.recip(den[c], den[c])
                    yield
                    yv = yy[i][:, tt, :].on(ybufs[i][tt])
                    p.stt(yv, num[c][:, 0:128], den[c], yv, ALU.mult, ALU.add)
                    yield

            self.run_rr([ml_chain(z, i, heads[i], z * 2 + i) for z in range(2) for i in range(2)])
            m4_ = p.mark()
            scratch = (p.sb("hn_sq", [128, NT, 128], F32), p.sb("hn_ss", [128, NT], F32), p.sb("hn_yn", [128, NT, 128], BF16),
                       p.sb("hn_sg", [128, LTOK], BF16), p.sb("hn_wgt", [128, 8, 128], BF16))
            for i, h in enumerate(heads):
                self.headnorm_gate(yy[i], ybufs[i], h, C_MO + h * 128, AF.Sigmoid, hT, mixT, scratch)
            p.release(m2)
        self.ev_hT, self.ev_mixT, self.ev_mark = hT, mixT, m
        return hT, mixT, m

    def even_out(self, b, xi, xo, hT, mixT, m):
        p = self.p
        l = 0
        wout = p.sb("ev_wout", [128, 8, D], BF16)
        for c in range(8):
            p.dma(wout[:, c, :], self.w_out[c * 128:(c + 1) * 128, :], eng='pool')
        G = [p.sb("evo_G%d" % i, [128, D], F32) for i in range(2)]
        gtmp = p.sb("evo_gtmp", [128, 128], F32)
        st = self.res_state(2)
        self.gate_row(G[0], l, 2, b, gtmp)
        self.gate_row(G[1], l, 2, 4, gtmp)
        for tt in range(NT):
            yb = [self.bank(), self.bank()]
            for h in range(2):
                for c in range(8):
                    p.mm(yb[h], mixT[:, c, tt * 128:(tt + 1) * 128], wout[:, c, h * 512:(h + 1) * 512], start=(c == 0), stop=(c == 7))
            self.residual_tile(b, tt, xi, xo, yb, G[1] if tt < 2 else G[0], st[tt % 2])
        p.release(m)


def _gla(self, b, hT, mixT):
    p = self.p
    m = p.mark()
    rmask = p.sb("gl_rmask", [128, LTOK], BF16)
    p.memset(rmask, 1.0, eng='pool')
    p.memset(rmask.rr("p (n t) -> p n t", t=128)[:, :, 0:1], 0.0, eng='pool')
    w2b = p.sb("gl_w2b", [16, 2, 256], BF16)
    p.dma(w2b, self.gla_w2.rr("z r c -> r z c"), eng='pool')
    wlr = p.sb("gl_wlr", [128, 8, 32], BF16)
    p.dma(wlr, self.w_in[:, C_GLR:C_GLR + 32].rr("(c p) n -> p c n", p=128), eng='pool')
    glrT = p.sb("gl_glrT", [16, 2, LTOK], BF16)
    for z in range(2):
        for blk in range(5):
            t0, n = blk * 512, (512 if blk < 4 else 256)
            ps = self.bank()
            for kc in range(8):
                p.mm(ps[0:16, 0:n], wlr[:, kc, z * 16:(z + 1) * 16], hT[:, kc, t0:t0 + n], start=(kc == 0), stop=(kc == 7))
            p.copy(glrT[:, z, t0:t0 + n], ps[0:16, 0:n], eng='act')
    nb = p.sb("gl_nb", [128, 4], F32)
    p.ts(nb, self.col('glab'), -1.0, ALU.mult)
    for cp in range(2):
        m2 = p.mark()
        qtil = [p.sb("gl_qtil%d" % z, [128, LTOK], BF16) for z in range(2)]
        khat = [p.sb("gl_khat%d" % z, [128, LTOK], BF16) for z in range(2)]
        ktl = [p.sb("gl_ktl%d" % z, [128, LTOK], BF16) for z in range(2)]
        dec = [p.sb("gl_dec%d" % z, [128, NT], F32) for z in range(2)]
        vv = [p.sb("gl_v%d" % i, [128, NT, 128], BF16) for i in range(2)]
        yy = [p.sb("gl_y%d" % i, [128, NT, 128], F32) for i in range(2)]
        ybufs = [[Buf("gly%d_%d" % (i, t)) for t in range(NT)] for i in range(2)]
        m3 = p.mark()
        qT = p.sb("gl_qT", [128, LTOK], BF16)
        kT = p.sb("gl_kT", [128, LTOK], BF16)
        lb = p.sb("gl_l", [128, LTOK], F32)
        P = p.sb("gl_P", [128, LTOK], F32)
        tmp = p.sb("gl_tmp", [128, LTOK], F32)
        E = p.sb("gl_E", [128, LTOK], BF16)
        w = [p.sb("gl_w%d" % i, [128, 8, 128], BF16) for i in range(2)]
        for dst, c0, wi in ((qT, C_GQ + cp * 128, 0), (kT, C_GK + cp * 128, 1)):
            p.dma(w[wi], self.w_in[:, c0:c0 + 128].rr("(c p) n -> p c n", p=128), eng='pool')
            for blk in range(5):
                t0, n = blk * 512, (512 if blk < 4 else 256)
                ps = self.bank()
                for kc in range(8):
                    p.mm(ps[:, 0:n], w[wi][:, kc, :], hT[:, kc, t0:t0 + n], start=(kc == 0), stop=(kc == 7))
                p.copy(dst[:, t0:t0 + n], ps[:, 0:n], eng='act')
        P3 = P.rr("p (n t) -> p n t", t=128)
        Ptot = P3[:, :, 127:128]
        for z in range(2):
            for blk in range(5):
                t0, n = blk * 512, (512 if blk < 4 else 256)
                ps = self.bank()
                p.mm(ps[:, 0:n], w2b[:, z, cp * 128:(cp + 1) * 128], glrT[:, z, t0:t0 + n])
                p.act(lb[:, t0:t0 + n], ps[:, 0:n], AF.Exp, scale=-1.0, bias=nb[:, z * 2 + cp:z * 2 + cp + 1])
            p.act(lb, lb, AF.Ln, bias=1.0)
            p.scan(P, rmask, lb, 0.0, ALU.mult, ALU.add)
            p.act(dec[z], Ptot.rr("p n o -> p (n o)"), AF.Exp, scale=-1.0 / 16)
            t3 = tmp.rr("p (n t) -> p n t", t=128)
            if z == 0:
                p.act(E, P, AF.Exp, scale=-1.0 / 16, bias=LNS_GLA)
                p.tt(qtil[z], qT, E, ALU.mult)
                p.act(E, P, AF.Exp, scale=1.0 / 16)
                p.tt(khat[z], kT, E, ALU.mult, eng='pool')
                p.tt(t3, P3, Ptot.bcast([128, NT, 128]), ALU.subtract)
                p.act(E, tmp, AF.Exp, scale=1.0 / 16)
                p.tt(ktl[z], kT, E, ALU.mult)
            else:
                p.tt(t3, Ptot.bcast([128, NT, 128]), P3, ALU.subtract)
                p.tt(tmp, tmp, lb, ALU.add, eng='pool')
                p.act(E, tmp, AF.Exp, scale=-1.0 / 16, bias=LNS_GLA)
                p.tt(qtil[z], qT, E, ALU.mult)
                p.act(E, tmp, AF.Exp, scale=1.0 / 16)
                p.tt(khat[z], kT, E, ALU.mult, eng='pool')
                p.tt(tmp, lb, P, ALU.subtract)
                p.act(E, tmp, AF.Exp, scale=1.0 / 16)
                p.tt(ktl[z], kT, E, ALU.mult)
        for i in range(2):
            h = cp * 2 + i
            p.dma(w[i], self.w_in[:, C_GV + h * 128:C_GV + (h + 1) * 128].rr("(c p) n -> p c n", p=128), eng='pool')
            for g0 in range(0, NT, 4):
                n = min(4, NT - g0)
                ps = self.bank()
                for j in range(n):
                    tt = g0 + j
                    for kc in range(8):
                        p.mm(ps[:, j * 128:(j + 1) * 128], hT[:, kc, tt * 128:(tt + 1) * 128], w[i][:, kc, :], start=(kc == 0), stop=(kc == 7))
                p.copy(vv[i][:, g0:g0 + n, :], ps[:, 0:n * 128].rr("p (t n) -> p t n", n=128), eng='act')
            p.memset(yy[i], 0.0, eng='pool')
        p.release(m3)
        S = [[p.sb("gl_S%d%d" % (z, i), [128, 128], F32) for i in range(2)] for z in range(2)]
        Sb = [[p.sb("gl_Sb%d%d" % (z, i), [128, 128], BF16) for i in range(2)] for z in range(2)]
        for z in range(2):
            for i in range(2):
                p.memset(S[z][i], 0.0, eng='pool')
                p.memset(Sb[z][i], 0.0, eng='pool')
        AT = [[p.sb("gl_AT%d_%d" % (c, k2), [128, 128], BF16) for k2 in range(2)] for c in range(4)]
        ktok = [[p.sb("gl_ktok%d_%d" % (c, k2), [128, 64], BF16) for k2 in range(2)] for c in range(4)]

        def gl_chain(z, i, c):
            b0, b1 = 2 * c, 2 * c + 1
            pr = slice(i * 64, (i + 1) * 64)
            for step in range(NT):
                tt = ORD[z][step]
                ts_ = slice(tt * 128, (tt + 1) * 128)
                k2 = step % 2
                upd = step < NT - 1
                sc, hsc = p.palloc(1, b0)
                p.mm(sc[:, 0:128], khat[z][pr, ts_], qtil[z][pr, ts_])
                if upd:
                    kp, hkp = p.palloc(1, b0)
                    kpb = kp.bc(BF16)
                    p.tr(kpb[:, 0:64], ktl[z][pr, ts_], self.identb[pr, pr])
                yield
                p.tt(AT[c][k2], sc[:, 0:128], self.maskb[z], ALU.mult)
                p.pfree(hsc)
                if upd:
                    p.copy(ktok[c][k2], kpb[:, 0:64], eng='act')
                    p.pfree(hkp)
                yield
                o, ho = p.palloc(1, b1)
                p.mm(o[:, 0:128], AT[c][k2], vv[i][:, tt, :], start=True, stop=False)
                p.mm(o[:, 0:128], qtil[z][pr, ts_], Sb[z][i][pr, :], start=False, stop=True)
                if upd:
                    su, hsu = p.palloc(1, b0)
                    p.mm(su[pr, 0:128], ktok[c][k2], vv[i][:, tt, :])
                yield
                yv = yy[i][:, tt, :].on(ybufs[i][tt])
                p.tt(yv, yv, o[:, 0:128], ALU.add)
                p.pfree(ho)
                if upd:
                    p.stt(S[z][i][pr, :], S[z][i][pr, :], dec[z][pr, tt:tt + 1], su[pr, 0:128], ALU.mult, ALU.add)
                    p.pfree(hsu)
                yield
                if upd:
                    p.copy(Sb[z][i][pr, :], S[z][i][pr, :], eng='pool')
                yield

        self.run_rr([gl_chain(z, i, z * 2 + i) for z in range(2) for i in range(2)])
        scratch = (p.sb("hn_sq", [128, NT, 128], F32), p.sb("hn_ss", [128, NT], F32), p.sb("hn_yn", [128, NT, 128], BF16),
                   p.sb("hn_sg", [128, LTOK], BF16), p.sb("hn_wgt", [128, 8, 128], BF16))
        for i in range(2):
            h = cp * 2 + i
            self.headnorm_gate(yy[i], ybufs[i], 4 + h, C_GG + h * 128, AF.Silu, hT, mixT, scratch)
        p.release(m2)
    p.release(m)


KE.gla = _gla


C0 = float(np.exp(-0.5))
RW_LN_EPS = 64e-5


def host_small_rw(pk, inp):
    pk.add('mu', fm(inp['rw_mu'][0]))
    pk.add('w0', fm(inp['rw_w0'][0]))
    pk.add('a0', fm(inp['rw_a0'][0]))
    pk.add('kv', fm(inp['rw_kvec'][0]))
    pk.add('lnx', fm(inp['rw_lnx'][0]))


def _setup_rw(self):
    p = self.p
    NB = self.NB
    self.w_rkv = p.dram("rw_w_rkv", [3, D, D], F32, kind="ExternalInput")
    self.w_o = p.dram("rw_w_o", [D, D], F32, kind="ExternalInput")
    self.rw_w1 = p.dram("rw_w1", [2, D, 64], F32, kind="ExternalInput")
    self.rw_w2 = p.dram("rw_w2", [2, 64, D], F32, kind="ExternalInput")
    self.rw_a1 = p.dram("rw_a1", [D, 64], F32, kind="ExternalInput")
    self.rw_a2 = p.dram("rw_a2", [64, D], F32, kind="ExternalInput")
    self.rw_g1 = p.dram("rw_g1", [D, 128], F32, kind="ExternalInput")
    self.rw_g2 = p.dram("rw_g2", [128, D], F32, kind="ExternalInput")
    self.scr = []
    if getattr(self, 'dbg', False):
        self.dbg_y = p.dram("dbg_y", [8, 128, NT * 128], F32, kind="ExternalOutput")
    for b in range(NB):
        d = {}
        for nm in ('r', 'k', 'v', 'o'):
            d[nm] = p.dram("scr_%s%d" % (nm, b), [8, 128, LTOK], BF16, kind="ExternalOutput" if getattr(self, 'dbg', False) else "Internal")
            d[nm + 'buf'] = [Buf("scr_%s%d_%d" % (nm, b, c)) for c in range(8)]
        self.scr.append(d)


def _setup_rw_consts(self):
    p = self.p
    p.release(self.common_mark)
    self.strict = []
    self.M4 = []
    for z in range(2):
        st = p.sb("strict%d" % z, [128, 128], F32)
        pat, cm = ([[1, 128]], -1) if z == 0 else ([[-1, 128]], 1)
        p.aselect(st, self.ones, pat, ALU.is_gt, 0.0, 0, cm)
        self.strict.append(st)
    for z in range(2):
        m4 = p.sb("M4%d" % z, [128, 4, 128], F32)
        p.ts(m4[:, 0, :], self.strict[z], -1.0, ALU.mult, eng='pool')
        p.ts(m4[:, 1, :], self.tri[z], -1.0, ALU.mult, eng='pool')
        p.copy(m4[:, 2, :], self.strict[z], eng='pool')
        p.copy(m4[:, 3, :], self.tri[z], eng='pool')
        self.M4.append(m4)
    self.nstrictT = []
    for z in range(2):
        t = p.sb("nstrictT%d" % z, [128, 128], F32)
        p.ts(t, self.strict[1 - z], -1.0, ALU.mult, eng='pool')
        self.nstrictT.append(t)
    self.masks3 = p.sb("masks3", [128, 3, 128], BF16)
    bd64 = p.sb("bd64", [128, 128], BF16)
    p.memset(self.masks3[:, 0, :], 0.0, eng='pool')
    p.memset(bd64, 0.0, eng='pool')
    for i in range(4):
        p.memset(self.masks3[32 * i:32 * i + 32, 0, 32 * i:32 * i + 32], 1.0, eng='pool')
    for i in range(2):
        p.memset(bd64[64 * i:64 * i + 64, 64 * i:64 * i + 64], 1.0, eng='pool')
    p.tt(self.masks3[:, 1, :], bd64, self.masks3[:, 0, :], ALU.subtract, eng='pool')
    p.ts(self.masks3[:, 2, :], bd64, -1.0, ALU.mult, 1.0, ALU.add, eng='pool')
    self.bones = p.sb("bones", [128, 128], BF16)
    p.memset(self.bones, 0.0, eng='pool')
    p.memset(self.bones[0:64, 0:64], 1.0, eng='pool')
    p.memset(self.bones[64:128, 64:128], 1.0, eng='pool')
    self.omu = p.sb("omu", [128, 48], F32)
    p.ts(self.omu, self.col('mu'), -1.0, ALU.mult, 1.0, ALU.add)
    self.okv1 = p.sb("okv1", [128, 8], F32)
    p.ts(self.okv1, self.col('kv')[:, 8:16], -1.0, ALU.mult, 1.0, ALU.add)
    self.base_mark = p.mark()


def _rw_mix_block(self, xb, hT, mi, blk):
    p = self.p
    mu = self.col('mu')
    t0, n = (0, 256) if blk == 0 else (256 + (blk - 1) * 512, 512)
    for c in range(8):
        muc = mu[:, mi * 8 + c:mi * 8 + c + 1]
        p.act(xb[:, c, 0:n], hT[:, c, t0:t0 + n], AF.Identity, scale=self.omu[:, mi * 8 + c:mi * 8 + c + 1])
        kind = c // 2
        if blk == 0:
            if kind in (0, 2):
                p.stt(xb[:, c, 1:256], hT[:, c, 0:255], muc, xb[:, c, 1:256], ALU.mult, ALU.add)
            else:
                p.stt(xb[:, c, 0:255], hT[:, c, 1:256], muc, xb[:, c, 0:255], ALU.mult, ALU.add)
        else:
            r0 = (blk - 1) * 8
            xv = xb[:, c, :].rr("p (r w) -> p r w", w=64)
            hv = hT[:, c, 256:LTOK].rr("p (r w) -> p r w", w=64)
            if kind == 0:
                p.stt(xv[:, :, 1:64], hv[:, r0:r0 + 8, 0:63], muc, xv[:, :, 1:64], ALU.mult, ALU.add)
            elif kind == 1:
                p.stt(xv[:, :, 0:63], hv[:, r0:r0 + 8, 1:64], muc, xv[:, :, 0:63], ALU.mult, ALU.add)
            elif kind == 2:
                lo = 1 if r0 == 0 else 0
                p.stt(xv[:, lo:8, :], hv[:, r0 + lo - 1:r0 + 7, :], muc, xv[:, lo:8, :], ALU.mult, ALU.add)
            else:
                hi = 7 if r0 == 24 else 8
                p.stt(xv[:, 0:hi, :], hv[:, r0 + 1:r0 + hi + 1, :], muc, xv[:, 0:hi, :], ALU.mult, ALU.add)
    return t0, n


def _rw_phase1(self, b, hT, tw, ta, tg):
    p = self.p
    scr = self.scr[b]
    m = p.mark()
    xblk = [p.sb("rw_xb%d" % i, [128, 8, 512], BF16) for i in range(2)]
    Wr = [p.sb("rw_W%d" % i, [128, 8, D], BF16) for i in range(2)]
    stage = [p.sb("rw_stage%d" % i, [128, 512], BF16) for i in range(4)]
    ns = 0
    nx = 0
    for wi, (mi, kind) in enumerate(((0, 'r'), (2, 'k'), (3, 'v'), (1, 'w'), (4, 'a'), (5, 'g'))):
        W = Wr[wi % 2]
        if kind in 'rkv':
            idx = 'rkv'.index(kind)
            for c in range(8):
                p.dma(W[:, c, :], self.w_rkv[idx, c * 128:(c + 1) * 128, :], eng='pool')
        elif kind == 'w':
            for z in range(2):
                p.dma(W[:, :, z * 64:(z + 1) * 64], self.rw_w1[z].rr("(c p) r -> p c r", p=128), eng='pool')
        elif kind == 'a':
            p.dma(W[:, :, 0:64], self.rw_a1.rr("(c p) r -> p c r", p=128), eng='pool')
        else:
            p.dma(W[:, :, 0:128], self.rw_g1.rr("(c p) r -> p c r", p=128), eng='pool')
        for blk in range(5):
            xb = xblk[nx % 2]
            nx += 1
            t0, n = self.rw_mix_block(xb, hT, mi, blk)
            if kind in 'rkv':
                for oc in range(8):
                    ps = self.bank()
                    for kc in range(8):
                        p.mm(ps[:, 0:n], W[:, kc, oc * 128:(oc + 1) * 128], xb[:, kc, 0:n], start=(kc == 0), stop=(kc == 7))
                    sg = stage[ns % 4]
                    ns += 1
                    p.copy(sg[:, 0:n], ps[:, 0:n], eng='act')
                    p.dma(scr[kind][oc, :, t0:t0 + n].on(scr[kind + 'buf'][oc]), sg[:, 0:n])
            elif kind == 'w':
                for z in range(2):
                    ps = self.bank()
                    for kc in range(8):
                        p.mm(ps[0:64, 0:n], W[:, kc, z * 64:(z + 1) * 64], xb[:, kc, 0:n], start=(kc == 0), stop=(kc == 7))
                    p.act(tw[z][0:64, t0:t0 + n], ps[0:64, 0:n], AF.Tanh)
            elif kind == 'a':
                ps = self.bank()
                for kc in range(8):
                    p.mm(ps[0:64, 0:n], W[:, kc, 0:64], xb[:, kc, 0:n], start=(kc == 0), stop=(kc == 7))
                p.copy(ta[0:64, t0:t0 + n], ps[0:64, 0:n], eng='act')
            else:
                ps = self.bank()
                for kc in range(8):
                    p.mm(ps[:, 0:n], W[:, kc, 0:128], xb[:, kc, 0:n], start=(kc == 0), stop=(kc == 7))
                p.act(tg[:, t0:t0 + n], ps[:, 0:n], AF.Sigmoid)
    p.release(m)


KE.setup_rw = _setup_rw
KE.setup_rw_consts = _setup_rw_consts
KE.rw_mix_block = _rw_mix_block
KE.rw_phase1 = _rw_phase1


BLKS = [(0, 512), (512, 512), (1024, 512), (1536, 512), (2048, 256)]


def _rw_chunk(self, b, cc, tw, ta, tg, last):
    p = self.p
    scr = self.scr[b]
    m = p.mark()
    kv = self.col('kv')
    rT = p.sb("rc_rT", [128, LTOK], BF16)
    kT = p.sb("rc_kT", [128, LTOK], BF16)
    vT = p.sb("rc_vT", [128, LTOK], BF16)
    for t_, nm in ((rT, 'r'), (kT, 'k'), (vT, 'v')):
        p.dma(t_, scr[nm][cc].on(scr[nm + 'buf'][cc]))
    KR = [p.sb("rc_KR%d" % z, [128, NT, 2, 128], BF16) for z in range(2)]
    khat = [p.sb("rc_khat%d" % z, [128, LTOK], BF16) for z in range(2)]
    bhat = [p.sb("rc_bhat%d" % z, [128, LTOK], BF16) for z in range(2)]
    kT4 = [p.sb("rc_kT4%d" % z, [128, LTOK], BF16) for z in range(2)]
    nbT4 = [p.sb("rc_nbT4%d" % z, [128, LTOK], BF16) for z in range(2)]
    gam = [p.sb("rc_gam%d" % z, [128, NT], F32) for z in range(2)]
    Vtok = p.sb("rc_Vtok", [128, NT, 128], BF16)
    y = p.sb("rc_y", [128, NT, 128], F32)
    yz = [y, p.sb("rc_y1", [128, NT, 128], F32)]
    ybufs = [[Buf("rcy%d_%d" % (t, hh)) for hh in range(2)] for t in range(NT)]
    m3 = p.mark()
    aT = p.sb("rc_aT", [128, LTOK], BF16)
    kap = p.sb("rc_kap", [128, LTOK], BF16)
    bet = p.sb("rc_bet", [128, LTOK], BF16)
    sig = p.sb("rc_sig", [128, LTOK], F32)
    P = p.sb("rc_P", [128, LTOK], F32)
    t1 = p.sb("rc_t1", [128, LTOK], F32)
    t2 = p.sb("rc_t2", [128, LTOK], F32)
    E = p.sb("rc_E", [128, LTOK], BF16)
    rmask = p.sb("rc_rmask", [128, LTOK], BF16)
    p.memset(rmask, 1.0, eng='pool')
    p.memset(rmask.rr("p (n t) -> p n t", t=128)[:, :, 0:1], 0.0, eng='pool')
    w2b = p.sb("rc_w2b", [64, 2, 128], BF16)
    p.dma(w2b, self.rw_w2[:, :, cc * 128:(cc + 1) * 128].rr("z r c -> r z c"), eng='pool')
    a2b = p.sb("rc_a2b", [64, 128], BF16)
    p.dma(a2b, self.rw_a2[:, cc * 128:(cc + 1) * 128], eng='pool')
    for (t0, n) in BLKS:
        ps = self.bank()
        p.mm(ps[:, 0:n], a2b, ta[0:64, t0:t0 + n])
        p.act(aT[:, t0:t0 + n], ps[:, 0:n], AF.Sigmoid, bias=self.col('a0')[:, cc:cc + 1])
    p.ts(t1, kT, kv[:, cc:cc + 1], ALU.mult)
    p.tt(E, t1, t1, ALU.mult, eng='pool')
    for (t0, n) in BLKS:
        ps = self.bank()
        p.mm(ps[:, 0:n], self.bones, E[:, t0:t0 + n])
        p.act(t2[:, t0:t0 + n], ps[:, 0:n], AF.Sqrt)
    p.ts(t2, t2, 1e-12, ALU.max)
    p.recip(t2, t2)
    p.tt(kap, t1, t2, ALU.mult)
    p.tt(bet, kap, aT, ALU.mult, eng='pool')
    p.ts(t1, aT, kv[:, 8 + cc:8 + cc + 1], ALU.mult, self.okv1[:, cc:cc + 1], ALU.add)
    p.tt(kT, kT, t1, ALU.mult)
    P3 = P.rr("p (n t) -> p n t", t=128)
    Ptot = P3[:, :, 127:128]
    Pb = Ptot.bcast([128, NT, 128])
    t13 = t1.rr("p (n t) -> p n t", t=128)
    t23 = t2.rr("p (n t) -> p n t", t=128)
    v3 = lambda x: x.rr("p (n t) -> p n t", t=128)
    for z in range(2):
        for (t0, n) in BLKS:
            ps = self.bank()
            p.mm(ps[:, 0:n], w2b[:, z, :], tw[z][0:64, t0:t0 + n])
            p.act(sig[:, t0:t0 + n], ps[:, 0:n], AF.Sigmoid, bias=self.col('w0')[:, z * 8 + cc:z * 8 + cc + 1])
        p.scan(P, rmask, sig, 0.0, ALU.mult, ALU.add)
        p.act(gam[z], Ptot.rr("p n o -> p (n o)"), AF.Exp, scale=-C0)
        if z == 0:
            G = P
            p.tt(t1, P, sig, ALU.subtract)
            p.tt(t23, P3, Pb, ALU.subtract)
        else:
            p.tt(t13, Pb, P3, ALU.subtract)
            p.tt(t2, sig, P, ALU.subtract)
            G = sig
            p.tt(sig, t1, sig, ALU.add, eng='pool')
        p.act(E, G, AF.Exp, scale=-C0)
        p.tt(KR[z][:, :, 1, :], v3(rT), v3(E), ALU.mult)
        p.act(E, t1, AF.Exp, scale=-C0)
        p.tt(KR[z][:, :, 0, :], v3(kap), v3(E), ALU.mult, eng='pool')
        p.act(E, G, AF.Exp, scale=C0)
        p.tt(khat[z], kT, E, ALU.mult)
        p.tt(bhat[z], bet, E, ALU.mult, eng='pool')
        p.act(E, t2, AF.Exp, scale=C0)
        p.tt(kT4[z], kT, E, ALU.mult)
        p.stt(nbT4[z], bet, -1.0, E, ALU.mult, ALU.mult)
    for g0 in range(0, NT, 8):
        n = min(8, NT - g0)
        ps = self.bank()
        psb = ps.bc(BF16)
        for i in range(n):
            p.tr(psb[:, i * 128:(i + 1) * 128], vT[:, (g0 + i) * 128:(g0 + i + 1) * 128], self.identb)
        p.copy(Vtok[:, g0:g0 + n, :], psb[:, 0:n * 128].rr("p (t c) -> p t c", c=128), eng='act')
    p.release(m3)
    mring = p.mark()
    NR = 12
    A4 = [p.sb("rs_A4%d" % i, [128, 4, 128], BF16) for i in range(NR)]
    SQ = [[p.sb("rs_SQ%d_%d" % (i, j), [128, 2, 128], BF16) for j in range(2)] for i in range(NR)]
    XXr = [p.sb("rs_XX%d" % i, [128, 2, 128], BF16) for i in range(NR)]
    Q0T = [p.sb("rs_Q0T%d" % i, [128, 128], BF16) for i in range(NR)]
    QM = [p.sb("rs_QM%d" % i, [128, 3, 128], BF16) for i in range(NR)]
    QMT = [p.sb("rs_QMT%d" % i, [128, 3, 128], BF16) for i in range(NR)]
    Y1r = [p.sb("rs_Y1%d" % i, [128, 128], BF16) for i in range(NR)]
    KB = [p.sb("rs_KB%d" % i, [128, 2, 128], BF16) for i in range(6)]
    Wb = [p.sb("rs_Wb%d" % i, [128, 64], BF16) for i in range(NR)]
    Ub = [p.sb("rs_Ub%d" % i, [128, 64], BF16) for i in range(NR)]
    H = [[p.sb("rs_H%d%d" % (z, hh), [128, 64], F32) for hh in range(2)] for z in range(2)]
    Hb = [[p.sb("rs_Hb%d%d" % (z, hh), [128, 64], BF16) for hh in range(2)] for z in range(2)]
    for z in range(2):
        for hh in range(2):
            p.memset(H[z][hh], 0.0, eng='pool')
            p.memset(Hb[z][hh], 0.0, eng='pool')
    units = [(step, z, hh) for step in range(NT) for z in range(2) for hh in range(2)]
    prep = {}

    def stage_a(u, bk, r):
        step, z, hh = units[u]
        tt = ORD[z][step]
        ts_ = slice(tt * 128, (tt + 1) * 128)
        pr = slice(hh * 64, (hh + 1) * 64)
        kb = KB[(u // 2) % 6]
        if hh == 0:
            kps, hk = p.palloc(1, bk)
            psb = kps.bc(BF16)
            p.tr(psb[:, 0:128], kT4[z][:, ts_], self.identb)
            p.tr(psb[:, 128:256], nbT4[z][:, ts_], self.identb)
        kr = KR[z][pr, tt].rr("p j t -> p (j t)")
        sc1, hsc1 = p.palloc(2, bk)
        p.mm(sc1[:, 0:256], bhat[z][pr, ts_], kr)
        q0, hq0 = p.palloc(1, bk)
        p.mm(q0[:, 0:128], KR[z][pr, tt, 0, :], bhat[z][pr, ts_])
        yield
        if hh == 0:
            p.copy(kb, psb[:, 0:256].rr("p (j c) -> p j c", j=2), eng='act')
            p.pfree(hk)
        p.tt(A4[r][:, 0:2, :], sc1.rr("p (j t) -> p j t", j=2), self.M4[z][:, 0:2, :], ALU.mult)
        p.pfree(hsc1)
        p.tt(Q0T[r], q0[:, 0:128], self.nstrictT[z], ALU.mult)
        p.pfree(hq0)
        sc2, hsc2 = p.palloc(2, bk)
        p.mm(sc2[:, 0:256], khat[z][pr, ts_], kr)
        yield
        p.tt(A4[r][:, 2:4, :], sc2.rr("p (j t) -> p j t", j=2), self.M4[z][:, 2:4, :], ALU.mult)
        p.pfree(hsc2)
        p.tt(QM[r], A4[r][:, 0:1, :].bcast([128, 3, 128]), self.masks3, ALU.mult, eng='pool')
        p.tt(QMT[r], Q0T[r].rr("p (o t) -> p o t", o=1).bcast([128, 3, 128]), self.masks3, ALU.mult, eng='pool')
        XX = XXr[r]
        XXf = XX.rr("p j t -> p (j t)")
        p.tt(XX[:, 0, :], QM[r][:, 0, :], self.identb, ALU.add, eng='pool')
        p.tt(XX[:, 1, :], QMT[r][:, 0, :], self.identb, ALU.add, eng='pool')
        yield
        cq, ct = QM[r][:, 0, :], QMT[r][:, 0, :]
        xq, xt = XX[:, 0, :], XX[:, 1, :]
        ps, h1 = p.palloc(2, bk)
        p.mm(ps[:, 0:128], ct, cq)
        p.mm(ps[:, 128:256], cq, ct)
        yield
        nxt = SQ[r][0]
        p.copy(nxt.rr("p j t -> p (j t)"), ps[:, 0:256], eng='act')
        p.pfree(h1)
        cq, ct = nxt[:, 0, :], nxt[:, 1, :]
        yield
        for lev in range(1, 5):
            ps2, h2 = p.palloc(2, bk)
            p.mm(ps2[:, 0:128], ct, xq)
            p.mm(ps2[:, 128:256], xq, ct)
            if lev < 4:
                ps, h1 = p.palloc(2, bk)
                p.mm(ps[:, 0:128], ct, cq)
                p.mm(ps[:, 128:256], cq, ct)
            yield
            p.tt(XXf, XXf, ps2[:, 0:256], ALU.add)
            p.pfree(h2)
            if lev < 4:
                nxt = SQ[r][lev % 2]
                p.copy(nxt.rr("p j t -> p (j t)"), ps[:, 0:256], eng='act')
                p.pfree(h1)
                cq, ct = nxt[:, 0, :], nxt[:, 1, :]
            yield
        for lvl in (1, 2):
            C, CT = QM[r][:, lvl, :], QMT[r][:, lvl, :]
            ps, h1 = p.palloc(1, bk)
            p.mm(ps[:, 0:128], CT, xq)
            yield
            p.copy(Y1r[r], ps[:, 0:128], eng='act')
            p.pfree(h1)
            yield
            ps2, h2 = p.palloc(2, bk)
            p.mm(ps2[:, 0:128], xt, Y1r[r])
            if lvl == 1:
                p.mm(ps2[:, 128:256], Y1r[r], xt)
            yield
            if lvl == 1:
                p.tt(XXf, XXf, ps2[:, 0:256], ALU.add)
            else:
                p.tt(XX[:, 0, :], XX[:, 0, :], ps2[:, 0:128], ALU.add)
            p.pfree(h2)
            yield
        prep[u] = (r, kb, XX[:, 0, :])

    def stage_b(u, bk, bq):
        step, z, hh = units[u]
        tt = ORD[z][step]
        ts_ = slice(tt * 128, (tt + 1) * 128)
        pr = slice(hh * 64, (hh + 1) * 64)
        cs = slice(hh * 64, (hh + 1) * 64)
        r, kb, xfin = prep.pop(u)
        a4 = A4[r]
        hb = Hb[z][hh]
        vt = Vtok[:, tt, cs]
        w, hw = p.palloc(1, bk)
        p.mm(w[:, 0:64], KR[z][pr, tt, 0, :], hb[pr, :], start=True, stop=False)
        p.mm(w[:, 0:64], a4[:, 2, :], vt, start=False, stop=True)
        yield
        p.copy(Wb[r], w[:, 0:64], eng='act')
        p.pfree(hw)
        yield
        uu, hu_ = p.palloc(1, bk)
        p.mm(uu[:, 0:64], xfin, Wb[r])
        yield
        p.copy(Ub[r], uu[:, 0:64], eng='act')
        p.pfree(hu_)
        yield
        yb, hy = p.palloc(1, bk)
        p.mm(yb[:, 0:64], KR[z][pr, tt, 1, :], hb[pr, :], start=True, stop=False)
        p.mm(yb[:, 0:64], a4[:, 3, :], vt, start=False, stop=False)
        p.mm(yb[:, 0:64], a4[:, 1, :], Ub[r], start=False, stop=True)
        if step < NT - 1:
            hu, hh_ = p.palloc(1, bk)
            p.mm(hu[pr, 0:64], kb[:, 0, cs], vt, start=True, stop=False)
            p.mm(hu[pr, 0:64], kb[:, 1, cs], Ub[r], start=False, stop=True)
        yield
        if step < NT - 1:
            p.stt(H[z][hh][pr, :], H[z][hh][pr, :], gam[z][pr, tt:tt + 1], hu[pr, 0:64], ALU.mult, ALU.add)
            p.pfree(hh_)
            p.copy(hb[pr, :], H[z][hh][pr, :], eng='pool')
        p.copy(yz[z][:, tt, cs].on(ybufs[tt][hh]), yb[:, 0:64], eng='act')
        p.pfree(hy)
        yield

    NA = 6
    nU = len(units)
    free_r = list(range(NR))
    a_banks = list(range(NA))
    act_a = []
    act_b = []
    a_done = set()
    b_emitted = set()
    next_a = 0
    next_b = [0, 1, 2, 3]
    rmap = {}
    it_ = 0
    last_start = -100
    STAG = 4
    while len(b_emitted) < nU:
        it_ += 1
        while next_a < nU and a_banks and free_r and next_a < min(next_b) + 10 and it_ - last_start >= STAG:
            last_start = it_
            bk = a_banks.pop(0)
            r = free_r.pop(0)
            rmap[next_a] = r
            act_a.append((stage_a(next_a, bk, r), next_a, bk))
            next_a += 1
        for j in range(4):
            u = next_b[j]
            if u < nU and u in a_done and not any(x[2] == j for x in act_b):
                act_b.append((stage_b(u, 6 + j // 2, None), u, j))
                next_b[j] = u + 4
        nxt_a = []
        for g, u, bk in act_a:
            try:
                next(g)
                nxt_a.append((g, u, bk))
            except StopIteration:
                a_done.add(u)
                a_banks.append(bk)
        act_a = nxt_a
        nxt_b = []
        for g, u, j in act_b:
            try:
                next(g)
                nxt_b.append((g, u, j))
            except StopIteration:
                b_emitted.add(u)
                free_r.append(rmap.pop(u))
        act_b = nxt_b
    p.release(mring)
    if getattr(self, 'dbg', False):
        for tt in range(NT):
            for hh in range(2):
                p.tt(y[:, tt, hh * 64:(hh + 1) * 64], y[:, tt, hh * 64:(hh + 1) * 64].on(ybufs[tt][hh]), y[:, tt, hh * 64:(hh + 1) * 64].on(ybufs[tt][hh]), ALU.max)
        p.dma(self.dbg_y[cc], y.rr("p t c -> p (t c)"))
    g2b = p.sb("rp_g2b", [128, 128], BF16)
    p.dma(g2b, self.rw_g2[:, cc * 128:(cc + 1) * 128], eng='pool')
    s1 = p.sb("rp_s1", [128, 36], F32)
    s2 = p.sb("rp_s2", [128, 36], F32)
    sq = p.sb("rp_sq", [128, 36, 64], F32)
    yn = p.sb("rp_yn", [128, NT, 128], BF16)
    lnT = p.sb("rp_lnT", [128, LTOK], BF16)
    prod = p.sb("rp_prod", [128, LTOK], BF16)
    y3 = y.rr("p t (h c) -> p (t h) c", c=64)
    allb = [ybufs[t_][h_] for t_ in range(NT) for h_ in range(2)]
    yall = V(y.ap, allb)
    y1all = V(yz[1].ap, allb)
    p.tt(yall, yall, y1all, ALU.add)
    p.tt(sq, V(y3.ap, allb), V(y3.ap, allb), ALU.mult)
    p.reduce(s2, sq, ALU.add)
    p.reduce(s1, y3, ALU.add)
    p.ts(s1, s1, 1.0 / 64, ALU.mult)
    p.tt(sq[:, :, 0], s1, s1, ALU.mult)
    p.stt(s2, s2, 1.0 / 64, sq[:, :, 0], ALU.mult, ALU.subtract)
    p.act(s2, s2, AF.Sqrt, bias=RW_LN_EPS)
    p.recip(s2, s2)
    yn3 = yn.rr("p t (h c) -> p (t h) c", c=64)
    p.tt(sq, y3, s1.rr("p (n o) -> p n o", o=1).bcast([128, 36, 64]), ALU.subtract)
    p.tt(yn3, sq, s2.rr("p (n o) -> p n o", o=1).bcast([128, 36, 64]), ALU.mult)
    lnx = self.col('lnx')
    for g0 in range(0, NT, 8):
        n = min(8, NT - g0)
        ps = self.bank()
        psb = ps.bc(BF16)
        for i in range(n):
            p.tr(psb[:, i * 128:(i + 1) * 128], yn[:, g0 + i, :], self.identb)
        p.ts(lnT[:, g0 * 128:(g0 + n) * 128], psb[:, 0:n * 128], lnx[:, cc:cc + 1], ALU.mult, lnx[:, 8 + cc:8 + cc + 1], ALU.add)
    p.stt(prod, rT, kv[:, 16 + cc:16 + cc + 1], kT, ALU.mult, ALU.mult)
    stage = [p.sb("rp_stage%d" % i, [128, 512], BF16) for i in range(2)]
    tmpb = [p.sb("rp_tmp%d" % i, [128, 512], F32) for i in range(2)]
    for i, (t0, n) in enumerate(BLKS):
        psA = self.bank()
        p.mm(psA[:, 0:n], self.bones, prod[:, t0:t0 + n])
        psG = self.bank()
        p.mm(psG[:, 0:n], g2b, tg[:, t0:t0 + n])
        tb = tmpb[i % 2]
        p.tt(tb[:, 0:n], psA[:, 0:n], vT[:, t0:t0 + n], ALU.mult)
        p.tt(tb[:, 0:n], tb[:, 0:n], lnT[:, t0:t0 + n], ALU.add, eng='pool')
        sg = stage[i % 2]
        p.tt(sg[:, 0:n], psG[:, 0:n], tb[:, 0:n], ALU.mult)
        p.dma(scr['o'][cc, :, t0:t0 + n].on(scr['obuf'][cc]), sg[:, 0:n])
    p.release(m)


def _rw_mixer(self, b, xi, xo, last=True):
    p = self.p
    l = 1
    m = p.mark()
    tw = [p.sb("rw_tw%d" % z, [128, LTOK], BF16) for z in range(2)]
    ta = p.sb("rw_ta", [128, LTOK], BF16)
    tg = p.sb("rw_tg", [128, LTOK], BF16)
    mh = p.mark()
    hT = p.sb("rw_hT", [128, 8, LTOK], BF16)
    self.prenorm(b, xi, l, 0, hT)
    self.rw_phase1(b, hT, tw, ta, tg)
    p.release(mh)
    for cc in range(8):
        self.rw_chunk(b, cc, tw, ta, tg, last)
    p.release(m)
    m = p.mark()
    oT = p.sb("rw_oT", [128, 8, LTOK], BF16)
    for c in range(8):
        p.dma(oT[:, c, :], self.scr[b]['o'][c].on(self.scr[b]['obuf'][c]))
    wo = p.sb("rw_wo", [128, 8, D], BF16)
    for c in range(8):
        p.dma(wo[:, c, :], self.w_o[c * 128:(c + 1) * 128, :], eng='pool')
    G = [p.sb("rwo_G%d" % i, [128, D], F32) for i in range(2)]
    gtmp = p.sb("rwo_gtmp", [128, 128], F32)
    st = self.res_state(2)
    self.gate_row(G[0], l, 2, b, gtmp)
    if not last:
        self.gate_row(G[1], l, 2, 4, gtmp)
    for tt in (range(2, NT) if last else range(NT)):
        yb = [self.bank(), self.bank()]
        for h in range(2):
            for c in range(8):
                p.mm(yb[h], oT[:, c, tt * 128:(tt + 1) * 128], wo[:, c, h * 512:(h + 1) * 512], start=(c == 0), stop=(c == 7))
        self.residual_tile(b, tt, xi, xo, yb, G[1] if tt < 2 else G[0], st[tt % 2])
    p.release(m)


KE.rw_chunk = _rw_chunk
KE.rw_mixer = _rw_mixer


_NC_CACHE = {}
W_NAMES = ('w_mod', 'ffn_w_up', 'ffn_w_down')
EV_NAMES = ('ev_w_in', 'ev_w_out', 'ev_gla_w2')
RW_NAMES = ('rw_w_rkv', 'rw_w_o', 'rw_w1', 'rw_w2', 'rw_a1', 'rw_a2', 'rw_g1', 'rw_g2')


def build_program(NB, pk):
    k = KE(NB, pk.cols, pk.n, dbg=False)
    k.setup_even()
    k.setup_rw()
    k.mod_stage(0)
    for b in range(NB):
        hT, mixT, m = k.even_mixer(b, 0, 1)
        k.gla(b, hT, mixT)
        k.even_out(b, 0, 1, hT, mixT, m)
        k.ffn(b, 0, 1, 2, do_ctx=True)
    k.setup_rw_consts()
    k.mod_stage(1)
    for b in range(NB):
        k.rw_mixer(b, 2, 3, last=True)
        k.ffn(b, 1, 3, 4, do_ctx=False)
    return k.p.finish()


def kernel(**inp):
    inp = {k_: np.asarray(v_, dtype=np.float32) for k_, v_ in inp.items()}
    NCORE = 8
    B = inp['x'].shape[0]
    NB = B // NCORE
    pk = host_small(inp)
    host_small_even(pk, inp)
    host_small_rw(pk, inp)
    small = pk.pack()
    rows = np.zeros((16, D), np.float32)
    rows[0:8] = inp['norm_g'].reshape(8, D)
    rows[8, :16] = inp['ev_b_gates'][0]
    nc = build_program(NB, pk)
    shared = {"small": small, "rows": rows}
    for nm in W_NAMES:
        shared[nm] = np.ascontiguousarray(inp[nm])
    for nm in EV_NAMES + RW_NAMES:
        shared[nm] = np.ascontiguousarray(inp[nm][0])
    in_maps = []
    for c in range(NCORE):
        sl = slice(c * NB, (c + 1) * NB)
        cc = np.zeros((6, D), np.float32)
        cc[0:NB] = inp['c'][sl]
        cc[4] = inp['c_ctx']
        ccT = np.ascontiguousarray(cc.reshape(6, 8, 128).transpose(2, 1, 0).reshape(128, 48))
        xcat = np.ascontiguousarray(np.concatenate([inp['ctx'][sl], inp['x'][sl]], axis=1))
        d = dict(shared)
        d["xcat"] = xcat
        d["ccT"] = ccT
        in_maps.append(d)
    res = run_bass_kernel_spmd(nc, in_maps, core_ids=list(range(NCORE)))
    out = np.concatenate([np.asarray(r["out"]) for r in res.results], axis=0)
    return out.astype(np.float32)
```

```python
import numpy as np
import concourse.bass as bass
import concourse.mybir as mybir
from concourse.bass_utils import run_bass_kernel_spmd

F32 = mybir.dt.float32
BF16 = mybir.dt.bfloat16
AF = mybir.ActivationFunctionType
ALU = mybir.AluOpType
AX = mybir.AxisListType
ENG = ('sp', 'act', 'dve', 'pool', 'pe')
DSZ = {F32: 4, BF16: 2}
SAME_SYNC = True
NDMASEM = 8


class Buf:
    __slots__ = ('name', 'w', 'r', 'excl', 'subs')

    def __init__(self, name, excl=False):
        self.name = name
        self.w = None
        self.r = {}
        self.excl = excl
        self.subs = []


class V:
    __slots__ = ('ap', 'buf')

    def __init__(self, ap, buf):
        self.ap = ap
        self.buf = buf

    def __getitem__(self, idx):
        return V(self.ap[idx], self.buf)

    def rr(self, pat, **kw):
        return V(self.ap.rearrange(pat, **kw), self.buf)

    def bc(self, dt):
        return V(self.ap.bitcast(dt), self.buf)

    def on(self, buf):
        if isinstance(self.buf, Buf) and buf is not self.buf and buf not in self.buf.subs:
            self.buf.subs.append(buf)
        return V(self.ap, buf)

    def bcast(self, shape):
        return V(self.ap.broadcast_to(list(shape)), self.buf)

    def pbcast(self, n):
        return V(self.ap.partition_broadcast(n), self.buf)

    @property
    def shape(self):
        return tuple(self.ap.shape)


class Prog:
    def __init__(self):
        nc = self.nc = bass.Bass("TRN2", target_bir_lowering=False)
        self.q = {e: [] for e in ENG}
        self.cnt = {e: 0 for e in ENG}
        self.sem = {e: nc.alloc_semaphore('sem_' + e) for e in ENG}
        self.seen = {e: {} for e in ENG}
        self.dsem = [nc.alloc_semaphore('dsem%d' % i) for i in range(NDMASEM)]
        self.dval = [0] * NDMASEM
        self.dnext = 0
        self.sb_off = 16512
        self.sb_max = 0
        self.nalloc = 0
        self.ninst = 0
        self.regions = []

    def dram(self, name, shape, dt, kind="Internal"):
        t = self.nc.dram_tensor(name, list(shape), dt, kind=kind)
        return V(t.ap(), Buf(name))

    def sb(self, name, shape, dt, nbuf=None):
        per = int(np.prod(shape[1:])) * DSZ[dt]
        per = (per + 63) // 64 * 64
        off = self.sb_off
        self.sb_off += per
        self.sb_max = max(self.sb_max, self.sb_off)
        assert self.sb_off <= 229344, (name, self.sb_off)
        self.nalloc += 1
        t = self.nc.alloc_sbuf_tensor_at("%s_%d" % (name, self.nalloc), list(shape), dt, offset=off)
        nb = Buf(name)
        lo, hi = off, off + per
        keep = []
        for (a, b_, ob) in self.regions:
            if a < hi and lo < b_:
                toks = []
                for ob2 in [ob] + ob.subs:
                    toks += list(ob2.r.values())
                    if ob2.w is not None:
                        toks.append(ob2.w)
                for tk in toks:
                    k = id(tk[0])
                    if k not in nb.r or nb.r[k][1] < tk[1]:
                        nb.r[k] = tk
                if a < lo:
                    keep.append((a, lo, ob))
                if b_ > hi:
                    keep.append((hi, b_, ob))
            else:
                keep.append((a, b_, ob))
        keep.append((lo, hi, nb))
        self.regions = keep
        return V(t.ap(), nb)

    def mark(self):
        return self.sb_off

    def release(self, m):
        self.sb_off = m

    def psum_banks(self):
        banks = []
        self.pslot_bufs = []
        self.pslot_free = [True] * 32
        for i in range(8):
            t = self.nc.alloc_psum_tensor("psb%d" % i, [128, 512], F32)
            bl = Buf("psb%d" % i, excl=True)
            self.pslot_bufs.append(bl)
            banks.append(V(t.ap(), bl))
        self.pbanks = banks
        return banks

    def palloc(self, nq, bank=None):
        banks = range(8) if bank is None else (bank,)
        for bk in banks:
            for q0 in range(0, 4, nq):
                if all(self.pslot_free[bk * 4 + q0 + j] for j in range(nq)):
                    for j in range(nq):
                        self.pslot_free[bk * 4 + q0 + j] = False
                    v = V(self.pbanks[bk].ap[:, q0 * 128:(q0 + nq) * 128], self.pslot_bufs[bk])
                    return v, (bk, q0, nq)
        raise RuntimeError("out of PSUM slots")

    def pfree(self, h):
        bk, q0, nq = h
        for j in range(nq):
            self.pslot_free[bk * 4 + q0 + j] = True

    def _emit(self, eng, fn, reads, writes, dma=False):
        waits = {}

        def need(tok, waw_pe=False):
            if tok is None:
                return
            sem, val, te = tok
            if te == eng and not dma:
                if eng == 'pe' or not SAME_SYNC:
                    return
            k = id(sem)
            if k not in waits or waits[k][1] < val:
                waits[k] = (sem, val)

        ex = [b for b in reads if b.excl and b not in writes]
        if ex:
            reads = [b for b in reads if not b.excl]
            writes = list(writes) + ex
        for b in reads:
            need(b.w)
        for b in writes:
            need(b.w)
            for t in b.r.values():
                need(t)
        if dma:
            k = self.dnext
            self.dnext = (self.dnext + 1) % NDMASEM
            if self.dval[k] > 0:
                need((self.dsem[k], self.dval[k], 'dma'))
            self.dval[k] += 16
            tok = (self.dsem[k], self.dval[k], 'dma')
            inc = (self.dsem[k], 16)
        else:
            self.cnt[eng] += 1
            tok = (self.sem[eng], self.cnt[eng], eng)
            inc = (self.sem[eng], 1)
        seen = self.seen[eng]
        final = []
        for k, (sem, val) in waits.items():
            if seen.get(k, 0) < val:
                seen[k] = val
                final.append((sem, val))
        for b in reads:
            b.r[id(tok[0])] = tok
        for b in writes:
            b.w = tok
            b.r = {}
        self.q[eng].append((final, fn, inc))
        self.ninst += 1

    def barrier(self):
        toks = [(self.sem[e], self.cnt[e]) for e in ENG if self.cnt[e] > 0]
        toks += [(self.dsem[k], self.dval[k]) for k in range(NDMASEM) if self.dval[k] > 0]
        for e in ENG:
            final = []
            for sem, val in toks:
                if sem is self.sem[e]:
                    continue
                if self.seen[e].get(id(sem), 0) < val:
                    self.seen[e][id(sem)] = val
                    final.append((sem, val))
            if final:
                self.q[e].append((final, None, None))

    def finish(self):
        self.barrier()
        nc = self.nc
        q = self.q

        def replay(e, lst):
            for waits, fn, inc in lst:
                for sem, val in waits:
                    e.wait_ge(sem, val)
                if fn is not None:
                    ins = fn(e)
                    ins.then_inc(inc[0], inc[1])

        with nc.Block() as block:
            @block.sync
            def _(e):
                replay(e, q['sp'])

            @block.scalar
            def _(e):
                replay(e, q['act'])

            @block.vector
            def _(e):
                replay(e, q['dve'])

            @block.gpsimd
            def _(e):
                replay(e, q['pool'])

            @block.tensor
            def _(e):
                replay(e, q['pe'])
        return nc

    @staticmethod
    def _a(x):
        return x.ap if isinstance(x, V) else x

    @staticmethod
    def _bufs(*xs):
        out = []
        for x in xs:
            if isinstance(x, V):
                bl = x.buf if isinstance(x.buf, (list, tuple)) else (x.buf,)
                for b in bl:
                    if b not in out:
                        out.append(b)
        return out

    def dma(self, out, in_, eng='sp', **kw):
        o, i = out.ap, in_.ap
        self._emit(eng, lambda e: e.dma_start(out=o, in_=i, **kw), self._bufs(in_), self._bufs(out), dma=True)

    def mm(self, out, lhsT, rhs, start=True, stop=True):
        o, l, r = out.ap, lhsT.ap, rhs.ap
        self._emit('pe', lambda e: e.matmul(o, lhsT=l, rhs=r, start=start, stop=stop),
                   self._bufs(lhsT, rhs), self._bufs(out))

    def tr(self, out, in_, ident):
        o, i, d = out.ap, in_.ap, ident.ap
        self._emit('pe', lambda e: e.transpose(o, i, d), self._bufs(in_, ident), self._bufs(out))

    def act(self, out, in_, func, bias=None, scale=None, accum=None):
        kw = {}
        if bias is not None:
            kw['bias'] = self._a(bias)
        if scale is not None:
            kw['scale'] = self._a(scale)
        if accum is not None:
            kw['accum_out'] = accum.ap
        o, i = out.ap, in_.ap
        self._emit('act', lambda e: e.activation(out=o, in_=i, func=func, **kw),
                   self._bufs(in_, bias, scale), self._bufs(out, accum))

    def tt(self, out, in0, in1, op, eng='dve'):
        o, a, b = out.ap, in0.ap, in1.ap
        self._emit(eng, lambda e: e.tensor_tensor(out=o, in0=a, in1=b, op=op),
                   self._bufs(in0, in1), self._bufs(out))

    def ts(self, out, in0, s1, op0, s2=None, op1=None, eng='dve', accum=None):
        o, a = out.ap, in0.ap
        a1, a2 = self._a(s1), self._a(s2)
        kw = {}
        if op1 is not None:
            kw['op1'] = op1
        if accum is not None:
            kw['accum_out'] = accum.ap
        self._emit(eng, lambda e: e.tensor_scalar(out=o, in0=a, scalar1=a1, scalar2=a2, op0=op0, **kw),
                   self._bufs(in0, s1, s2), self._bufs(out, accum))

    def stt(self, out, in0, scalar, in1, op0, op1, eng='dve'):
        o, a, b = out.ap, in0.ap, in1.ap
        s = self._a(scalar)
        self._emit(eng, lambda e: e.scalar_tensor_tensor(out=o, in0=a, scalar=s, in1=b, op0=op0, op1=op1),
                   self._bufs(in0, scalar, in1), self._bufs(out))

    def copy(self, out, in_, eng='dve'):
        o, i = out.ap, in_.ap
        if eng == 'act':
            self._emit('act', lambda e: e.activation(out=o, in_=i, func=AF.Copy), self._bufs(in_), self._bufs(out))
        else:
            self._emit(eng, lambda e: e.tensor_copy(out=o, in_=i), self._bufs(in_), self._bufs(out))

    def memset(self, out, val, eng='dve'):
        o = out.ap
        self._emit(eng, lambda e: e.memset(o, val), [], self._bufs(out))

    def reduce(self, out, in_, op, axis=AX.X, eng='dve'):
        o, i = out.ap, in_.ap
        self._emit(eng, lambda e: e.tensor_reduce(out=o, in_=i, axis=axis, op=op), self._bufs(in_), self._bufs(out))

    def recip(self, out, in_):
        o, i = out.ap, in_.ap
        self._emit('dve', lambda e: e.reciprocal(out=o, in_=i), self._bufs(in_), self._bufs(out))

    def aselect(self, out, in_, pattern, cmp, fill, base, cm):
        o, i = out.ap, in_.ap
        self._emit('pool', lambda e: e.affine_select(out=o, in_=i, pattern=pattern, compare_op=cmp, fill=fill,
                                                     base=base, channel_multiplier=cm),
                   self._bufs(in_), self._bufs(out))

    def scan(self, out, d0, d1, init, op0, op1):
        o, a, b = out.ap, d0.ap, d1.ap
        self._emit('dve', lambda e: e.tensor_tensor_scan(out=o, data0=a, data1=b, initial=init, op0=op0, op1=op1),
                   self._bufs(d0, d1), self._bufs(out))


import os

D = 1024
NT = 18
LTOK = 2304
EPS = 1e-6
DFF = 2816
NJ = 22


class Packer:
    def __init__(self):
        self.cols = {}
        self.n = 0
        self.arrs = []

    def add(self, name, arr):
        arr = np.ascontiguousarray(arr, dtype=np.float32).reshape(128, -1)
        self.cols[name] = (self.n, arr.shape[1])
        self.n += arr.shape[1]
        self.arrs.append(arr)

    def pack(self):
        return np.ascontiguousarray(np.concatenate(self.arrs, axis=1))


def fm(v):
    v = np.asarray(v, dtype=np.float32)
    lead = v.shape[:-1]
    c = v.shape[-1] // 128
    v = v.reshape(lead + (c, 128))
    return np.moveaxis(v, -1, 0).reshape(128, -1)


def host_small(inp, layer_cols_only=False):
    pk = Packer()
    for l in range(2):
        pk.add('bmod%d' % l, fm(inp['b_mod'][l]))
        pk.add('ng%d' % l, fm(inp['norm_g'][l]))
        pk.add('cw%d' % l, np.moveaxis(inp['ffn_conv_w'][l].reshape(9, NJ, 128), 2, 0).transpose(0, 2, 1).reshape(128, NJ * 9))
        pk.add('cb%d' % l, fm(inp['ffn_conv_b'][l]))
    return pk


class K:
    def __init__(self, NB, small_cols, nsmall, dbg=False):
        self.NB = NB
        self.dbg = dbg
        p = self.p = Prog()
        self.sc = small_cols
        self.X0 = p.dram("xcat", [NB, LTOK, D], F32, kind="ExternalInput")
        self.ccT_d = p.dram("ccT", [128, 8 * 6], F32, kind="ExternalInput")
        self.small_d = p.dram("small", [128, nsmall], F32, kind="ExternalInput")
        self.rows_d = p.dram("rows", [16, D], F32, kind="ExternalInput")
        self.w_mod = p.dram("w_mod", [2, D, 6 * D], F32, kind="ExternalInput")
        self.w_up = p.dram("ffn_w_up", [2, D, 2 * DFF], F32, kind="ExternalInput")
        self.w_down = p.dram("ffn_w_down", [2, DFF, D], F32, kind="ExternalInput")
        self.Xs = [self.X0]
        for i in range(1, 4):
            self.Xs.append(p.dram("xs%d" % i, [NB, LTOK, D], F32, kind="ExternalOutput" if dbg else "Internal"))
        self.out = p.dram("out", [NB, 2048, D], F32, kind="ExternalOutput")
        self.xbufs = [[[Buf("x%d_%d_%d" % (i, b, t)) for t in range(NT)] for b in range(NB)] for i in range(5)]
        self.banks = p.psum_banks()
        self.nbank = 0
        self.small = p.sb("small", [128, nsmall], F32)
        p.dma(self.small, self.small_d)
        self.identf = p.sb("identf", [128, 128], F32)
        p.memset(self.identf, 1.0, eng='pool')
        p.aselect(self.identf, self.identf, [[-1, 128]], ALU.is_equal, 0.0, 0, 1)
        self.identb = p.sb("identb", [128, 128], BF16)
        p.copy(self.identb, self.identf, eng='pool')
        self.scT = p.sb("scT", [128, 48], F32)
        p.dma(self.scT, self.ccT_d)
        p.act(self.scT, self.scT, AF.Silu)
        self.modT = p.sb("modT", [128, 48 * 6], F32)
        self.A1 = p.sb("A1", [128, 48], F32)
        self.A2 = p.sb("A2", [128, 48], F32)
        self.base_mark = p.mark()

    def bank(self):
        b = self.banks[self.nbank]
        self.nbank = (self.nbank + 1) % 8
        return b

    @staticmethod
    def run_rr(gens, stagger=0):
        pending = list(gens)
        active = []
        it_ = 0
        while active or pending:
            if pending and (stagger == 0 or it_ % stagger == 0 or not active):
                if stagger == 0:
                    active += pending
                    pending = []
                else:
                    active.append(pending.pop(0))
            it_ += 1
            nxt_ = []
            for g in active:
                try:
                    next(g)
                    nxt_.append(g)
                except StopIteration:
                    pass
            active = nxt_

    def col(self, name, a=None, b=None):
        o, w = self.sc[name]
        if a is None:
            return self.small[:, o:o + w]
        return self.small[:, o + a:o + b]

    def mod_stage(self, l):
        p = self.p
        m = p.mark()
        wst = [p.sb("wmod_st%d" % i, [128, 8, 512], F32) for i in range(2)]
        ps = self.bank()
        for blk in range(12):
            w = wst[blk % 2]
            p.dma(w, self.w_mod[l, :, blk * 512:(blk + 1) * 512].rr("(c p) n -> p c n", p=128))
            for f in range(4):
                fc = blk * 4 + f
                for kc in range(8):
                    p.mm(ps[:, fc * 6:(fc + 1) * 6], w[:, kc, f * 128:(f + 1) * 128], self.scT[:, kc * 6:(kc + 1) * 6],
                         start=(kc == 0), stop=(kc == 7))
        p.tt(self.modT.rr("p (c r) -> p c r", r=6), ps[:, 0:288].rr("p (c r) -> p c r", r=6),
             self.col('bmod%d' % l).rr("p (c o) -> p c o", o=1).bcast([128, 48, 6]), ALU.add)
        mv = self.modT.rr("p (i c r) -> p i c r", i=6, c=8)
        ng = self.col('ng%d' % l).rr("p (i c o) -> p i c o", i=4, o=1)
        for A, mi, gi in ((self.A1, 1, 0), (self.A2, 4, 2)):
            Av = A.rr("p (c r) -> p c r", r=6)
            p.ts(Av, mv[:, mi], 1.0, ALU.add)
            p.tt(Av, Av, ng[:, gi].bcast([128, 8, 6]), ALU.mult)
        p.release(m)

    def modvec(self, l, which):
        mv = self.modT.rr("p (i c r) -> p i c r", i=6, c=8)
        if which == 0:
            return self.A1.rr("p (c r) -> p c r", r=6), mv[:, 0]
        return self.A2.rr("p (c r) -> p c r", r=6), mv[:, 3]

    def gate_row(self, dst, l, gi, row, tmp):
        p = self.p
        mv = self.modT.rr("p (i c r) -> p i c r", i=6, c=8)
        nrow = l * 4 + (1 if gi == 2 else 3)
        p.dma(dst, self.rows_d[nrow:nrow + 1, :].pbcast(128))
        for half in range(2):
            ps = self.bank()
            for cc in range(4):
                c = half * 4 + cc
                p.copy(tmp, mv[:, gi, c, row:row + 1].bcast([128, 128]))
                p.mm(ps[:, cc * 128:(cc + 1) * 128], tmp, self.identf)
            p.tt(dst[:, half * 512:(half + 1) * 512], dst[:, half * 512:(half + 1) * 512], ps, ALU.mult)

    def prenorm(self, b, xi, l, which, hT, tiles=range(NT)):
        p = self.p
        A, Bv = self.modvec(l, which)
        m = p.mark()
        NG = 4
        xt = [p.sb("pn_x%d" % i, [128, D], F32) for i in range(NG)]
        junk = p.sb("pn_junk", [128, D], BF16)
        xn = [p.sb("pn_xn%d" % i, [128, D], BF16) for i in range(NG)]
        ss = [p.sb("pn_ss%d" % i, [128, 1], F32) for i in range(NG)]
        tmp = [p.sb("pn_tmp%d" % i, [128, 8, 128], F32) for i in range(NG)]
        tiles = list(tiles)

        def chain(g):
            for tt in tiles[g::NG]:
                x, s = xt[g], ss[g]
                p.dma(x, self.Xs[xi][b, tt * 128:(tt + 1) * 128, :].on(self.xbufs[xi][b][tt]))
                yield
                p.act(junk, x, AF.Square, accum=s)
                yield
                p.act(s, s, AF.Sqrt, bias=EPS, scale=1.0 / D)
                yield
                p.recip(s, s)
                yield
                p.act(xn[g], x, AF.Identity, scale=s)
                yield
                ps, hp = p.palloc(4, 2 * g)
                psb = ps.bc(BF16)
                for c in range(8):
                    p.tr(psb[:, c * 128:(c + 1) * 128], xn[g][:, c * 128:(c + 1) * 128], self.identb)
                yield
                row = 4 if tt < 2 else b
                p.tt(tmp[g], psb.rr("p (c t) -> p c t", c=8), A[:, :, row:row + 1].bcast([128, 8, 128]), ALU.mult)
                p.pfree(hp)
                yield
                p.tt(hT[:, :, tt * 128:(tt + 1) * 128], tmp[g], Bv[:, :, row:row + 1].bcast([128, 8, 128]), ALU.add,
                     eng='pool')
                yield

        self.run_rr([chain(g) for g in range(NG)])
        p.release(m)

    def residual_tile(self, b, tt, xi, xo, ybanks, G, st):
        p = self.p
        x, junk, ss2, s, tmp = st
        p.dma(x, self.Xs[xi][b, tt * 128:(tt + 1) * 128, :].on(self.xbufs[xi][b][tt]))
        for h in range(2):
            p.act(junk, ybanks[h], AF.Square, accum=ss2[:, h:h + 1])
        p.tt(s, ss2[:, 0:1], ss2[:, 1:2], ALU.add)
        p.act(s, s, AF.Sqrt, bias=EPS, scale=1.0 / D)
        p.recip(s, s)
        for h in range(2):
            p.stt(tmp[:, h * 512:(h + 1) * 512], ybanks[h], s, G[:, h * 512:(h + 1) * 512], ALU.mult, ALU.mult)
        p.tt(tmp, tmp, x, ALU.add, eng='pool')
        if xo == 4:
            dst = self.out[b, (tt - 2) * 128:(tt - 1) * 128, :]
        else:
            dst = self.Xs[xo][b, tt * 128:(tt + 1) * 128, :]
        p.dma(dst.on(self.xbufs[xo][b][tt]), tmp)

    def res_state(self, n=2):
        p = self.p
        return [(p.sb("rs_x%d" % i, [128, D], F32), p.sb("rs_junk%d" % i, [128, 512], BF16),
                 p.sb("rs_ss2%d" % i, [128, 2], F32), p.sb("rs_s%d" % i, [128, 1], F32),
                 p.sb("rs_tmp%d" % i, [128, D], F32)) for i in range(n)]

    def ffn(self, b, l, xi, xo, do_ctx=True):
        p = self.p
        m = p.mark()
        hT = p.sb("ffn_hT", [128, 8, LTOK], BF16)
        self.prenorm(b, xi, l, 1, hT, tiles=range(NT) if do_ctx else range(2, NT))
        wdown = p.sb("ffn_wdown", [128, NJ, D], BF16)
        actT = p.sb("ffn_actT", [128, NJ, 1280], BF16)
        wg = [p.sb("ffn_wg%d" % i, [128, 8, 128], BF16) for i in range(2)]
        wu = [p.sb("ffn_wu%d" % i, [128, 8, 128], BF16) for i in range(2)]
        gpad = [p.sb("ffn_gpad%d" % i, [128, 18, 66], BF16) for i in range(2)]
        cpad = p.sb("ffn_cpad", [128, 258], BF16)
        gg = [p.sb("ffn_gg%d" % i, [128, 512], BF16) for i in range(2)]
        diag = [p.sb("ffn_diag%d" % i, [128, 9, 128], BF16) for i in range(2)]
        G = [p.sb("ffn_G%d" % i, [128, D], F32) for i in range(2)]
        gtmp = p.sb("ffn_gtmp", [128, 128], F32)
        st = self.res_state(2)
        for g in gpad:
            p.memset(g, 0.0, eng='pool')
        p.memset(cpad, 0.0, eng='pool')
        self.gate_row(G[0], l, 5, b, gtmp)
        if do_ctx:
            self.gate_row(G[1], l, 5, 4, gtmp)
        cw = self.col('cw%d' % l)
        cb = self.col('cb%d' % l)
        nn = 0
        for seg in range(2):
            r0 = 16 * seg
            g0 = 0 if seg == 0 else 15
            prow0 = 1 if seg == 0 else 0
            gp = gpad[seg]
            with_ctx = (seg == 0 and do_ctx)
            for j in range(NJ):
                wgj, wuj = wg[j % 2], wu[j % 2]
                p.dma(wgj, self.w_up[l, :, j * 128:(j + 1) * 128].rr("(c p) n -> p c n", p=128), eng='pool')
                p.dma(wuj, self.w_up[l, :, DFF + j * 128:DFF + (j + 1) * 128].rr("(c p) n -> p c n", p=128), eng='pool')
                if seg == 0 and j >= 2:
                    for jj in range(2 * (j - 2), min(NJ, 2 * (j - 2) + 2)):
                        p.dma(wdown[:, jj, :], self.w_down[l, jj * 128:(jj + 1) * 128, :], eng='pool')
                dg = diag[j % 2]
                p.tt(dg, self.identb.rr("p (o t) -> p o t", o=1).bcast([128, 9, 128]),
                     cw[:, j * 9:(j + 1) * 9].rr("p (t o) -> p t o", o=1).bcast([128, 9, 128]), ALU.mult, eng='pool')
                tok0 = 256 + g0 * 64
                for (o, n) in ((0, 512), (512, 512), (1024, 64)):
                    ps = self.bank()
                    for kc in range(8):
                        p.mm(ps[:, 0:n], wgj[:, kc, :], hT[:, kc, tok0 + o:tok0 + o + n], start=(kc == 0), stop=(kc == 7))
                    pr = prow0 + o // 64
                    p.copy(gp[:, pr:pr + n // 64, 1:65], ps[:, 0:n].rr("p (r c) -> p r c", c=64), eng='act')
                for blk in range(2):
                    ps = self.bank()
                    for t in range(9):
                        dy, dx = t // 3, t % 3
                        p.mm(ps, dg[:, t, :], gp[:, 8 * blk + dy:8 * blk + dy + 8, dx:dx + 64], start=(t == 0), stop=(t == 8))
                    g_ = gg[nn % 2]
                    nn += 1
                    p.act(g_, ps, AF.Gelu_apprx_tanh, bias=cb[:, j:j + 1])
                    ps2 = self.bank()
                    t0 = 256 + r0 * 64 + blk * 512
                    for kc in range(8):
                        p.mm(ps2, wuj[:, kc, :], hT[:, kc, t0:t0 + 512], start=(kc == 0), stop=(kc == 7))
                    p.tt(actT[:, j, 256 + blk * 512:256 + (blk + 1) * 512], ps2, g_, ALU.mult)
                if with_ctx:
                    ps = self.bank()
                    for kc in range(8):
                        p.mm(ps[:, 0:256], wgj[:, kc, :], hT[:, kc, 0:256], start=(kc == 0), stop=(kc == 7))
                    p.copy(cpad[:, 1:257], ps[:, 0:256], eng='act')
                    ps = self.bank()
                    for dx in range(3):
                        p.mm(ps[:, 0:256], dg[:, 3 + dx, :], cpad[:, dx:dx + 256], start=(dx == 0), stop=(dx == 2))
                    g_ = gg[nn % 2]
                    nn += 1
                    p.act(g_[:, 0:256], ps[:, 0:256], AF.Gelu_apprx_tanh, bias=cb[:, j:j + 1])
                    ps2 = self.bank()
                    for kc in range(8):
                        p.mm(ps2[:, 0:256], wuj[:, kc, :], hT[:, kc, 0:256], start=(kc == 0), stop=(kc == 7))
                    p.tt(actT[:, j, 0:256], ps2[:, 0:256], g_[:, 0:256], ALU.mult)
            tiles = ([0, 1] if with_ctx else []) + [2 + 8 * seg + i for i in range(8)]
            for n, tt in enumerate(tiles):
                a0 = tt * 128 if tt < 2 else 256 + (tt - 2 - 8 * seg) * 128
                yb = [self.bank(), self.bank()]
                for h in range(2):
                    for j in range(NJ):
                        p.mm(yb[h], actT[:, j, a0:a0 + 128], wdown[:, j, h * 512:(h + 1) * 512], start=(j == 0), stop=(j == NJ - 1))
                self.residual_tile(b, tt, xi, xo, yb, G[1] if tt < 2 else G[0], st[n % 2])
        p.release(m)


LNS_ML = float(np.log(128.0 ** -0.5))
LNS_GLA = float(np.log(64.0 ** -0.5))
ORD = [list(range(NT)), [1, 0] + list(range(17, 1, -1))]
C_MQ, C_MK, C_MV, C_MO, C_MG, C_GQ, C_GK, C_GV, C_GG, C_GLR = 0, 512, 1024, 1536, 2048, 2064, 2320, 2576, 3088, 3600


def host_small_even(pk, inp):
    pk.add('ecw', fm(inp['ev_conv_w'][0]))
    pk.add('ecb', fm(inp['ev_conv_b'][0]))
    pk.add('glab', fm(inp['ev_gla_b'][0]))
    pk.add('hg', fm(inp['ev_head_g'][0]))


class KE(K):
    def setup_even(self):
        p = self.p
        self.w_in = p.dram("ev_w_in", [D, 3632], F32, kind="ExternalInput")
        self.w_out = p.dram("ev_w_out", [D, D], F32, kind="ExternalInput")
        self.gla_w2 = p.dram("ev_gla_w2", [2, 16, 256], F32, kind="ExternalInput")
        self.ones = p.sb("ones", [128, 128], F32)
        p.memset(self.ones, 1.0, eng='pool')
        self.tri = []
        for z in range(2):
            t = p.sb("tri%d" % z, [128, 128], F32)
            pat, cm = ([[1, 128]], -1) if z == 0 else ([[-1, 128]], 1)
            p.aselect(t, self.ones, pat, ALU.is_ge, 0.0, 0, cm)
            self.tri.append(t)
        self.common_mark = p.mark()
        self.mask4 = []
        self.maskb = []
        for z in range(2):
            t = self.tri[z]
            m4 = p.sb("mask4%d" % z, [128, 4, 128], F32)
            p.ts(m4, t.rr("p (o t) -> p o t", o=1).bcast([128, 4, 128]), -1.0, ALU.add, 30000.0, ALU.mult, eng='pool')
            self.mask4.append(m4)
            mb = p.sb("maskb%d" % z, [128, 128], BF16)
            p.copy(mb, t, eng='pool')
            self.maskb.append(mb)
        self.bgrow = p.sb("bgrow", [128, 16], F32)
        p.dma(self.bgrow, self.rows_d[8:9, 0:16].pbcast(128))
        self.base_mark = p.mark()

    def headnorm_gate(self, y, ybufs, hd, gate_col, gate_fn, hT, mixT, scratch):
        p = self.p
        sq, ss, yn, sg, wgt = scratch
        p.dma(wgt, self.w_in[:, gate_col:gate_col + 128].rr("(c p) n -> p c n", p=128), eng='pool')
        for blk in range(5):
            t0, n = blk * 512, (512 if blk < 4 else 256)
            ps = self.bank()
            for kc in range(8):
                p.mm(ps[:, 0:n], wgt[:, kc, :], hT[:, kc, t0:t0 + n], start=(kc == 0), stop=(kc == 7))
            p.act(sg[:, t0:t0 + n], ps[:, 0:n], gate_fn)
        yall = V(y.ap, Buf('yall'))
        for tt in range(NT):
            p.tt(sq[:, tt, :], y[:, tt, :].on(ybufs[tt]), y[:, tt, :].on(ybufs[tt]), ALU.mult)
        p.reduce(ss, sq, ALU.add)
        p.act(ss, ss, AF.Sqrt, bias=EPS, scale=1.0 / 128)
        p.recip(ss, ss)
        for tt in range(NT):
            p.ts(yn[:, tt, :], y[:, tt, :].on(ybufs[tt]), ss[:, tt:tt + 1], ALU.mult)
        hg = self.col('hg')
        for g0 in (0, 8, 16):
            n = min(8, NT - g0)
            ps = self.bank()
            psb = ps.bc(BF16)
            for i in range(n):
                p.tr(psb[:, i * 128:(i + 1) * 128], yn[:, g0 + i, :], self.identb)
            p.stt(mixT[:, hd, g0 * 128:(g0 + n) * 128], psb[:, 0:n * 128], hg[:, hd:hd + 1], sg[:, g0 * 128:(g0 + n) * 128],
                  ALU.mult, ALU.mult)

    def even_mixer(self, b, xi, xo):
        p = self.p
        l = 0
        m = p.mark()
        hT = p.sb("ev_hT", [128, 8, LTOK], BF16)
        self.prenorm(b, xi, l, 0, hT)
        mixT = p.sb("ev_mixT", [128, 8, LTOK], BF16)
        wgate = p.sb("ev_wgate", [128, 8, 16], BF16)
        p.dma(wgate, self.w_in[:, C_MG:C_MG + 16].rr("(c p) n -> p c n", p=128), eng='pool')
        graw = p.sb("ev_graw", [128, NT, 16], F32)
        ps = self.bank()
        for tt in range(NT):
            for kc in range(8):
                p.mm(ps[:, tt * 16:(tt + 1) * 16], hT[:, kc, tt * 128:(tt + 1) * 128], wgate[:, kc, :], start=(kc == 0), stop=(kc == 7))
        p.tt(graw, ps[:, 0:NT * 16].rr("p (t n) -> p t n", n=16), self.bgrow.rr("p (o n) -> p o n", o=1).bcast([128, NT, 16]), ALU.add)
        g5 = graw.rr("p t (z y h) -> p t z y h", z=2, y=2)
        I8 = g5[:, :, :, 0, :]
        F8 = g5[:, :, :, 1, :]
        def s8(name):
            return p.sb(name, [128, NT, 2, 4], F32)
        lf8, Fc8, Ft8, bs8, qs8, wk8, dc8 = [s8("ev_" + n) for n in ("lf8", "Fc8", "Ft8", "bs8", "qs8", "wk8", "dc8")]
        p.act(lf8, F8, AF.Exp, scale=-1.0)
        p.act(lf8, lf8, AF.Ln, bias=1.0)
        p.ts(lf8, lf8, -1.0, ALU.mult)
        ps = self.bank()
        for z in range(2):
            p.mm(ps[:, z * 72:(z + 1) * 72], self.tri[z], lf8[:, :, z, :])
        p.mm(ps[:, 144:288], self.ones, lf8)
        for z in range(2):
            p.copy(Fc8[:, :, z, :], ps[:, z * 72:(z + 1) * 72].rr("p (t h) -> p t h", h=4))
        p.copy(Ft8, ps[:, 144:288].rr("p (t z h) -> p t z h", z=2, h=4))
        p.tt(bs8, I8, Fc8, ALU.subtract)
        p.tt(wk8, Ft8, bs8, ALU.add)
        p.act(wk8, wk8, AF.Exp)
        p.ts(bs8, bs8, LNS_ML, ALU.add)
        p.act(qs8, Fc8, AF.Exp, bias=LNS_ML)
        p.act(dc8, Ft8, AF.Exp)
        Fc5 = Fc8.rr("p t z (h o) -> p t z h o", o=1)

        cw = self.col('ecw').rr("p (t c) -> p t c", t=3)
        cbias = self.col('ecb')
        for hp in range(2):
            m2 = p.mark()
            heads = [2 * hp, 2 * hp + 1]
            qT = [p.sb("ml_qT%d" % i, [128, LTOK], BF16) for i in range(2)]
            kT = [p.sb("ml_kT%d" % i, [128, LTOK], BF16) for i in range(2)]
            vv = [p.sb("ml_v%d" % i, [128, NT, 130], BF16) for i in range(2)]
            yy = [p.sb("ml_y%d" % i, [128, NT, 128], F32) for i in range(2)]
            ybufs = [[Buf("mly%d_%d" % (i, t)) for t in range(NT)] for i in range(2)]
            m3 = p.mark()
            xpad = p.sb("ml_xpad", [128, 2308], F32)
            t1 = p.sb("ml_t1", [128, 2306], F32)
            t2 = p.sb("ml_t2", [128, 2306], F32)
            wq = [p.sb("ml_wq%d" % i, [128, 8, 128], BF16) for i in range(2)]
            p.memset(xpad, 0.0, eng='pool')
            nw = 0
            for i, h in enumerate(heads):
                for dst, c0, cc in ((qT[i], C_MQ + h * 128, h), (kT[i], C_MK + h * 128, 4 + h)):
                    w = wq[nw % 2]
                    nw += 1
                    p.dma(w, self.w_in[:, c0:c0 + 128].rr("(c p) n -> p c n", p=128), eng='pool')
                    for blk in range(5):
                        t0, n = (0, 256) if blk == 0 else (256 + (blk - 1) * 512, 512)
                        pos = 1 if blk == 0 else 259 + (blk - 1) * 512
                        ps = self.bank()
                        for kc in range(8):
                            p.mm(ps[:, 0:n], w[:, kc, :], hT[:, kc, t0:t0 + n], start=(kc == 0), stop=(kc == 7))
                        p.copy(xpad[:, pos:pos + n], ps[:, 0:n], eng='act')
                    p.ts(t1, xpad[:, 0:2306], cw[:, 0, cc:cc + 1], ALU.mult, cbias[:, cc:cc + 1], ALU.add)
                    p.stt(t2, xpad[:, 1:2307], cw[:, 1, cc:cc + 1], t1, ALU.mult, ALU.add)
                    p.stt(t1, xpad[:, 2:2308], cw[:, 2, cc:cc + 1], t2, ALU.mult, ALU.add)
                    p.act(dst[:, 0:256], t1[:, 0:256], AF.Silu)
                    p.act(dst[:, 256:LTOK], t1[:, 258:2306], AF.Silu)
                w = wq[nw % 2]
                nw += 1
                p.dma(w, self.w_in[:, C_MV + h * 128:C_MV + (h + 1) * 128].rr("(c p) n -> p c n", p=128), eng='pool')
                p.memset(vv[i][:, :, 128:130], 1.0, eng='pool')
                for g0 in range(0, NT, 4):
                    n = min(4, NT - g0)
                    ps = self.bank()
                    for j in range(n):
                        tt = g0 + j
                        for kc in range(8):
                            p.mm(ps[:, j * 128:(j + 1) * 128], hT[:, kc, tt * 128:(tt + 1) * 128], w[:, kc, :], start=(kc == 0), stop=(kc == 7))
                    p.copy(vv[i][:, g0:g0 + n, 0:128], ps[:, 0:n * 128].rr("p (t n) -> p t n", n=128), eng='act')
                p.memset(yy[i], 0.0, eng='pool')
            p.release(m3)
            Cs = [[p.sb("ml_C%d%d" % (z, i), [128, 132], F32) for i in range(2)] for z in range(2)]
            Cb = [[p.sb("ml_Cb%d%d" % (z, i), [128, 132], BF16) for i in range(2)] for z in range(2)]
            for z in range(2):
                for i in range(2):
                    p.memset(Cs[z][i], 0.0, eng='pool')
                    p.memset(Cb[z][i], 0.0, eng='pool')
            dg1 = [[p.sb("ml_dg%d_%d" % (c, k2), [128, 128], F32) for k2 in range(2)] for c in range(4)]
            Dm = [[p.sb("ml_Dm%d_%d" % (c, k2), [128, 128], BF16) for k2 in range(2)] for c in range(4)]
            sT = [[p.sb("ml_sT%d_%d" % (c, k2), [128, 128], BF16) for k2 in range(2)] for c in range(4)]
            ktil = [[p.sb("ml_ktil%d_%d" % (c, k2), [128, 128], BF16) for k2 in range(2)] for c in range(4)]
            tmpo = [p.sb("ml_tmpo%d" % i, [128, 132], F32) for i in range(4)]
            num = [p.sb("ml_num%d" % i, [128, 132], F32) for i in range(4)]
            den = [p.sb("ml_den%d" % i, [128, 1], F32) for i in range(4)]

            def ml_chain(z, i, h, c):
                b0, b1 = 2 * c, 2 * c + 1
                for step in range(NT):
                    tt = ORD[z][step]
                    ts_ = slice(tt * 128, (tt + 1) * 128)
                    k2 = step % 2
                    upd = step < NT - 1
                    dg = dg1[c][k2]
                    p.ts(dg, self.identf, Fc8[:, tt, z, h:h + 1], ALU.mult)
                    rb, hrb = p.palloc(1, b0)
                    p.mm(rb[:, 0:128], self.ones, dg, start=True, stop=False)
                    p.mm(rb[:, 0:128], self.identf, self.mask4[z][:, 0, :], start=False, stop=True)
                    sc, hsc = p.palloc(1, b0)
                    p.mm(sc[:, 0:128], kT[i][:, ts_], qT[i][:, ts_])
                    if upd:
                        kp, hkp = p.palloc(1, b0)
                        kpb = kp.bc(BF16)
                        p.tr(kpb[:, 0:128], kT[i][:, ts_], self.identb)
                    yield
                    p.act(Dm[c][k2], rb[:, 0:128], AF.Exp, bias=bs8[:, tt, z, h:h + 1])
                    p.pfree(hrb)
                    if upd:
                        p.act(ktil[c][k2], kpb[:, 0:128], AF.Identity, scale=wk8[:, tt, z, h:h + 1])
                        p.pfree(hkp)
                    yield
                    p.tt(sT[c][k2], sc[:, 0:128], Dm[c][k2], ALU.mult)
                    p.pfree(hsc)
                    yield
                    o1, ho1 = p.palloc(2, b1)
                    p.mm(o1[:, 0:129], sT[c][k2], vv[i][:, tt, 0:129])
                    o2, ho2 = p.palloc(2, b1)
                    p.mm(o2[:, 0:129], qT[i][:, ts_], Cb[z][i][:, 0:129])
                    if upd:
                        cu, hcu = p.palloc(2, b0)
                        p.mm(cu[:, 0:129], ktil[c][k2], vv[i][:, tt, 0:129])
                    yield
                    p.act(tmpo[c][:, 0:129], o2[:, 0:129], AF.Identity, scale=qs8[:, tt, z, h:h + 1])
                    p.pfree(ho2)
                    if upd:
                        p.stt(Cs[z][i][:, 0:129], Cs[z][i][:, 0:129], dc8[:, tt, z, h:h + 1], cu[:, 0:129], ALU.mult, ALU.add)
                        p.pfree(hcu)
                    yield
                    p.tt(num[c][:, 0:129], tmpo[c][:, 0:129], o1[:, 0:129], ALU.add)
                    p.pfree(ho1)
                    if upd:
                        p.copy(Cb[z][i][:, 0:129], Cs[z][i][:, 0:129], eng='pool')
                    yield
                    p.act(den[c], num[c][:, 128:129], AF.Abs)
                    yield
                    p.ts(den[c], den[c], 1.0, ALU.max)
                    p.recip(den[c], den[c])
                    yield
                    yv = yy[i][:, tt, :].on(ybufs[i][tt])
                    p.stt(yv, num[c][:, 0:128], den[c], yv, ALU.mult, ALU.add)
                    yield

            self.run_rr([ml_chain(z, i, heads[i], z * 2 + i) for z in range(2) for i in range(2)])
            m4_ = p.mark()
            scratch = (p.sb("hn_sq", [128, NT, 128], F32), p.sb("hn_ss", [128, NT], F32), p.sb("hn_yn", [128, NT, 128], BF16),
                       p.sb("hn_sg", [128, LTOK], BF16), p.sb("hn_wgt", [128, 8, 128], BF16))
            for i, h in enumerate(heads):
                self.headnorm_gate(yy[i], ybufs[i], h, C_MO + h * 128, AF.Sigmoid, hT, mixT, scratch)
            p.release(m2)
        self.ev_hT, self.ev_mixT, self.ev_mark = hT, mixT, m
        return hT, mixT, m

    def even_out(self, b, xi, xo, hT, mixT, m):
        p = self.p
        l = 0
        wout = p.sb("ev_wout", [128, 8, D], BF16)
        for c in range(8):
            p.dma(wout[:, c, :], self.w_out[c * 128:(c + 1) * 128, :], eng='pool')
        G = [p.sb("evo_G%d" % i, [128, D], F32) for i in range(2)]
        gtmp = p.sb("evo_gtmp", [128, 128], F32)
        st = self.res_state(2)
        self.gate_row(G[0], l, 2, b, gtmp)
        self.gate_row(G[1], l, 2, 4, gtmp)
        for tt in range(NT):
            yb = [self.bank(), self.bank()]
            for h in range(2):
                for c in range(8):
                    p.mm(yb[h], mixT[:, c, tt * 128:(tt + 1) * 128], wout[:, c, h * 512:(h + 1) * 512], start=(c == 0), stop=(c == 7))
            self.residual_tile(b, tt, xi, xo, yb, G[1] if tt < 2 else G[0], st[tt % 2])
        p.release(m)


def _gla(self, b, hT, mixT):
    p = self.p
    m = p.mark()
    rmask = p.sb("gl_rmask", [128, LTOK], BF16)
    p.memset(rmask, 1.0, eng='pool')
    p.memset(rmask.rr("p (n t) -> p n t", t=128)[:, :, 0:1], 0.0, eng='pool')
    w2b = p.sb("gl_w2b", [16, 2, 256], BF16)
    p.dma(w2b, self.gla_w2.rr("z r c -> r z c"), eng='pool')
    wlr = p.sb("gl_wlr", [128, 8, 32], BF16)
    p.dma(wlr, self.w_in[:, C_GLR:C_GLR + 32].rr("(c p) n -> p c n", p=128), eng='pool')
    glrT = p.sb("gl_glrT", [16, 2, LTOK], BF16)
    for z in range(2):
        for blk in range(5):
            t0, n = blk * 512, (512 if blk < 4 else 256)
            ps = self.bank()
            for kc in range(8):
                p.mm(ps[0:16, 0:n], wlr[:, kc, z * 16:(z + 1) * 16], hT[:, kc, t0:t0 + n], start=(kc == 0), stop=(kc == 7))
            p.copy(glrT[:, z, t0:t0 + n], ps[0:16, 0:n], eng='act')
    nb = p.sb("gl_nb", [128, 4], F32)
    p.ts(nb, self.col('glab'), -1.0, ALU.mult)
    for cp in range(2):
        m2 = p.mark()
        qtil = [p.sb("gl_qtil%d" % z, [128, LTOK], BF16) for z in range(2)]
        khat = [p.sb("gl_khat%d" % z, [128, LTOK], BF16) for z in range(2)]
        ktl = [p.sb("gl_ktl%d" % z, [128, LTOK], BF16) for z in range(2)]
        dec = [p.sb("gl_dec%d" % z, [128, NT], F32) for z in range(2)]
        vv = [p.sb("gl_v%d" % i, [128, NT, 128], BF16) for i in range(2)]
        yy = [p.sb("gl_y%d" % i, [128, NT, 128], F32) for i in range(2)]
        ybufs = [[Buf("gly%d_%d" % (i, t)) for t in range(NT)] for i in range(2)]
        m3 = p.mark()
        qT = p.sb("gl_qT", [128, LTOK], BF16)
        kT = p.sb("gl_kT", [128, LTOK], BF16)
        lb = p.sb("gl_l", [128, LTOK], F32)
        P = p.sb("gl_P", [128, LTOK], F32)
        tmp = p.sb("gl_tmp", [128, LTOK], F32)
        E = p.sb("gl_E", [128, LTOK], BF16)
        w = [p.sb("gl_w%d" % i, [128, 8, 128], BF16) for i in range(2)]
        for dst, c0, wi in ((qT, C_GQ + cp * 128, 0), (kT, C_GK + cp * 128, 1)):
            p.dma(w[wi], self.w_in[:, c0:c0 + 128].rr("(c p) n -> p c n", p=128), eng='pool')
            for blk in range(5):
                t0, n = blk * 512, (512 if blk < 4 else 256)
                ps = self.bank()
                for kc in range(8):
                    p.mm(ps[:, 0:n], w[wi][:, kc, :], hT[:, kc, t0:t0 + n], start=(kc == 0), stop=(kc == 7))
                p.copy(dst[:, t0:t0 + n], ps[:, 0:n], eng='act')
        P3 = P.rr("p (n t) -> p n t", t=128)
        Ptot = P3[:, :, 127:128]
        for z in range(2):
            for blk in range(5):
                t0, n = blk * 512, (512 if blk < 4 else 256)
                ps = self.bank()
                p.mm(ps[:, 0:n], w2b[:, z, cp * 128:(cp + 1) * 128], glrT[:, z, t0:t0 + n])
                p.act(lb[:, t0:t0 + n], ps[:, 0:n], AF.Exp, scale=-1.0, bias=nb[:, z * 2 + cp:z * 2 + cp + 1])
            p.act(lb, lb, AF.Ln, bias=1.0)
            p.scan(P, rmask, lb, 0.0, ALU.mult, ALU.add)
            p.act(dec[z], Ptot.rr("p n o -> p (n o)"), AF.Exp, scale=-1.0 / 16)
            t3 = tmp.rr("p (n t) -> p n t", t=128)
            if z == 0:
                p.act(E, P, AF.Exp, scale=-1.0 / 16, bias=LNS_GLA)
                p.tt(qtil[z], qT, E, ALU.mult)
                p.act(E, P, AF.Exp, scale=1.0 / 16)
                p.tt(khat[z], kT, E, ALU.mult, eng='pool')
                p.tt(t3, P3, Ptot.bcast([128, NT, 128]), ALU.subtract)
                p.act(E, tmp, AF.Exp, scale=1.0 / 16)
                p.tt(ktl[z], kT, E, ALU.mult)
            else:
                p.tt(t3, Ptot.bcast([128, NT, 128]), P3, ALU.subtract)
                p.tt(tmp, tmp, lb, ALU.add, eng='pool')
                p.act(E, tmp, AF.Exp, scale=-1.0 / 16, bias=LNS_GLA)
                p.tt(qtil[z], qT, E, ALU.mult)
                p.act(E, tmp, AF.Exp, scale=1.0 / 16)
                p.tt(khat[z], kT, E, ALU.mult, eng='pool')
                p.tt(tmp, lb, P, ALU.subtract)
                p.act(E, tmp, AF.Exp, scale=1.0 / 16)
                p.tt(ktl[z], kT, E, ALU.mult)
        for i in range(2):
            h = cp * 2 + i
            p.dma(w[i], self.w_in[:, C_GV + h * 128:C_GV + (h + 1) * 128].rr("(c p) n -> p c n", p=128), eng='pool')
            for g0 in range(0, NT, 4):
                n = min(4, NT - g0)
                ps = self.bank()
                for j in range(n):
                    tt = g0 + j
                    for kc in range(8):
                        p.mm(ps[:, j * 128:(j + 1) * 128], hT[:, kc, tt * 128:(tt + 1) * 128], w[i][:, kc, :], start=(kc == 0), stop=(kc == 7))
                p.copy(vv[i][:, g0:g0 + n, :], ps[:, 0:n * 128].rr("p (t n) -> p t n", n=128), eng='act')
            p.memset(yy[i], 0.0, eng='pool')
        p.release(m3)
        S = [[p.sb("gl_S%d%d" % (z, i), [128, 128], F32) for i in range(2)] for z in range(2)]
        Sb = [[p.sb("gl_Sb%d%d" % (z, i), [128, 128], BF16) for i in range(2)] for z in range(2)]
        for z in range(2):
            for i in range(2):
                p.memset(S[z][i], 0.0, eng='pool')
                p.memset(Sb[z][i], 0.0, eng='pool')
        AT = [[p.sb("gl_AT%d_%d" % (c, k2), [128, 128], BF16) for k2 in range(2)] for c in range(4)]
        ktok = [[p.sb("gl_ktok%d_%d" % (c, k2), [128, 64], BF16) for k2 in range(2)] for c in range(4)]

        def gl_chain(z, i, c):
            b0, b1 = 2 * c, 2 * c + 1
            pr = slice(i * 64, (i + 1) * 64)
            for step in range(NT):
                tt = ORD[z][step]
                ts_ = slice(tt * 128, (tt + 1) * 128)
                k2 = step % 2
                upd = step < NT - 1
                sc, hsc = p.palloc(1, b0)
                p.mm(sc[:, 0:128], khat[z][pr, ts_], qtil[z][pr, ts_])
                if upd:
                    kp, hkp = p.palloc(1, b0)
                    kpb = kp.bc(BF16)
                    p.tr(kpb[:, 0:64], ktl[z][pr, ts_], self.identb[pr, pr])
                yield
                p.tt(AT[c][k2], sc[:, 0:128], self.maskb[z], ALU.mult)
                p.pfree(hsc)
                if upd:
                    p.copy(ktok[c][k2], kpb[:, 0:64], eng='act')
                    p.pfree(hkp)
                yield
                o, ho = p.palloc(1, b1)
                p.mm(o[:, 0:128], AT[c][k2], vv[i][:, tt, :], start=True, stop=False)
                p.mm(o[:, 0:128], qtil[z][pr, ts_], Sb[z][i][pr, :], start=False, stop=True)
                if upd:
                    su, hsu = p.palloc(1, b0)
                    p.mm(su[pr, 0:128], ktok[c][k2], vv[i][:, tt, :])
                yield
                yv = yy[i][:, tt, :].on(ybufs[i][tt])
                p.tt(yv, yv, o[:, 0:128], ALU.add)
                p.pfree(ho)
                if upd:
                    p.stt(S[z][i][pr, :], S[z][i][pr, :], dec[z][pr, tt:tt + 1], su[pr, 0:128], ALU.mult, ALU.add)
                    p.pfree(hsu)
                yield
                if upd:
                    p.copy(Sb[z][i][pr, :], S[z][i][pr, :], eng='pool')
                yield

        self.run_rr([gl_chain(z, i, z * 2 + i) for z in range(2) for i in range(2)])
        scratch = (p.sb("hn_sq", [128, NT, 128], F32), p.sb("hn_ss", [128, NT], F32), p.sb("hn_yn", [128, NT, 128], BF16),
                   p.sb("hn_sg", [128, LTOK], BF16), p.sb("hn_wgt", [128, 8, 128], BF16))
        for i in range(2):
            h = cp * 2 + i
            self.headnorm_gate(yy[i], ybufs[i], 4 + h, C_GG + h * 128, AF.Silu, hT, mixT, scratch)
        p.release(m2)
    p.release(m)


KE.gla = _gla


C0 = float(np.exp(-0.5))
RW_LN_EPS = 64e-5


def host_small_rw(pk, inp):
    pk.add('mu', fm(inp['rw_mu'][0]))
    pk.add('w0', fm(inp['rw_w0'][0]))
    pk.add('a0', fm(inp['rw_a0'][0]))
    pk.add('kv', fm(inp['rw_kvec'][0]))
    pk.add('lnx', fm(inp['rw_lnx'][0]))


def _setup_rw(self):
    p = self.p
    NB = self.NB
    self.w_rkv = p.dram("rw_w_rkv", [3, D, D], F32, kind="ExternalInput")
    self.w_o = p.dram("rw_w_o", [D, D], F32, kind="ExternalInput")
    self.rw_w1 = p.dram("rw_w1", [2, D, 64], F32, kind="ExternalInput")
    self.rw_w2 = p.dram("rw_w2", [2, 64, D], F32, kind="ExternalInput")
    self.rw_a1 = p.dram("rw_a1", [D, 64], F32, kind="ExternalInput")
    self.rw_a2 = p.dram("rw_a2", [64, D], F32, kind="ExternalInput")
    self.rw_g1 = p.dram("rw_g1", [D, 128], F32, kind="ExternalInput")
    self.rw_g2 = p.dram("rw_g2", [128, D], F32, kind="ExternalInput")
    self.scr = []
    if getattr(self, 'dbg', False):
        self.dbg_y = p.dram("dbg_y", [8, 128, NT * 128], F32, kind="ExternalOutput")
    for b in range(NB):
        d = {}
        for nm in ('r', 'k', 'v', 'o'):
            d[nm] = p.dram("scr_%s%d" % (nm, b), [8, 128, LTOK], BF16, kind="ExternalOutput" if getattr(self, 'dbg', False) else "Internal")
            d[nm + 'buf'] = [Buf("scr_%s%d_%d" % (nm, b, c)) for c in range(8)]
        self.scr.append(d)


def _setup_rw_consts(self):
    p = self.p
    p.release(self.common_mark)
    self.strict = []
    self.M4 = []
    for z in range(2):
        st = p.sb("strict%d" % z, [128, 128], F32)
        pat, cm = ([[1, 128]], -1) if z == 0 else ([[-1, 128]], 1)
        p.aselect(st, self.ones, pat, ALU.is_gt, 0.0, 0, cm)
        self.strict.append(st)
    for z in range(2):
        m4 = p.sb("M4%d" % z, [128, 4, 128], F32)
        p.ts(m4[:, 0, :], self.strict[z], -1.0, ALU.mult, eng='pool')
        p.ts(m4[:, 1, :], self.tri[z], -1.0, ALU.mult, eng='pool')
        p.copy(m4[:, 2, :], self.strict[z], eng='pool')
        p.copy(m4[:, 3, :], self.tri[z], eng='pool')
        self.M4.append(m4)
    self.nstrictT = []
    for z in range(2):
        t = p.sb("nstrictT%d" % z, [128, 128], F32)
        p.ts(t, self.strict[1 - z], -1.0, ALU.mult, eng='pool')
        self.nstrictT.append(t)
    self.masks3 = p.sb("masks3", [128, 3, 128], BF16)
    bd64 = p.sb("bd64", [128, 128], BF16)
    p.memset(self.masks3[:, 0, :], 0.0, eng='pool')
    p.memset(bd64, 0.0, eng='pool')
    for i in range(4):
        p.memset(self.masks3[32 * i:32 * i + 32, 0, 32 * i:32 * i + 32], 1.0, eng='pool')
    for i in range(2):
        p.memset(bd64[64 * i:64 * i + 64, 64 * i:64 * i + 64], 1.0, eng='pool')
    p.tt(self.masks3[:, 1, :], bd64, self.masks3[:, 0, :], ALU.subtract, eng='pool')
    p.ts(self.masks3[:, 2, :], bd64, -1.0, ALU.mult, 1.0, ALU.add, eng='pool')
    self.bones = p.sb("bones", [128, 128], BF16)
    p.memset(self.bones, 0.0, eng='pool')
    p.memset(self.bones[0:64, 0:64], 1.0, eng='pool')
    p.memset(self.bones[64:128, 64:128], 1.0, eng='pool')
    self.omu = p.sb("omu", [128, 48], F32)
    p.ts(self.omu, self.col('mu'), -1.0, ALU.mult, 1.0, ALU.add)
    self.okv1 = p.sb("okv1", [128, 8], F32)
    p.ts(self.okv1, self.col('kv')[:, 8:16], -1.0, ALU.mult, 1.0, ALU.add)
    self.base_mark = p.mark()


def _rw_mix_block(self, xb, hT, mi, blk):
    p = self.p
    mu = self.col('mu')
    t0, n = (0, 256) if blk == 0 else (256 + (blk - 1) * 512, 512)
    for c in range(8):
        muc = mu[:, mi * 8 + c:mi * 8 + c + 1]
        p.act(xb[:, c, 0:n], hT[:, c, t0:t0 + n], AF.Identity, scale=self.omu[:, mi * 8 + c:mi * 8 + c + 1])
        kind = c // 2
        if blk == 0:
            if kind in (0, 2):
                p.stt(xb[:, c, 1:256], hT[:, c, 0:255], muc, xb[:, c, 1:256], ALU.mult, ALU.add)
            else:
                p.stt(xb[:, c, 0:255], hT[:, c, 1:256], muc, xb[:, c, 0:255], ALU.mult, ALU.add)
        else:
            r0 = (blk - 1) * 8
            xv = xb[:, c, :].rr("p (r w) -> p r w", w=64)
            hv = hT[:, c, 256:LTOK].rr("p (r w) -> p r w", w=64)
            if kind == 0:
                p.stt(xv[:, :, 1:64], hv[:, r0:r0 + 8, 0:63], muc, xv[:, :, 1:64], ALU.mult, ALU.add)
            elif kind == 1:
                p.stt(xv[:, :, 0:63], hv[:, r0:r0 + 8, 1:64], muc, xv[:, :, 0:63], ALU.mult, ALU.add)
            elif kind == 2:
                lo = 1 if r0 == 0 else 0
                p.stt(xv[:, lo:8, :], hv[:, r0 + lo - 1:r0 + 7, :], muc, xv[:, lo:8, :], ALU.mult, ALU.add)
            else:
                hi = 7 if r0 == 24 else 8
                p.stt(xv[:, 0:hi, :], hv[:, r0 + 1:r0 + hi + 1, :], muc, xv[:, 0:hi, :], ALU.mult, ALU.add)
    return t0, n


def _rw_phase1(self, b, hT, tw, ta, tg):
    p = self.p
    scr = self.scr[b]
    m = p.mark()
    xblk = [p.sb("rw_xb%d" % i, [128, 8, 512], BF16) for i in range(2)]
    Wr = [p.sb("rw_W%d" % i, [128, 8, D], BF16) for i in range(2)]
    stage = [p.sb("rw_stage%d" % i, [128, 512], BF16) for i in range(4)]
    ns = 0
    nx = 0
    for wi, (mi, kind) in enumerate(((0, 'r'), (2, 'k'), (3, 'v'))):
        W = Wr[wi % 2]
        idx = 'rkv'.index(kind)
        for c in range(8):
            p.dma(W[:, c, :], self.w_rkv[idx, c * 128:(c + 1) * 128, :], eng='pool')
        for blk in range(5):
            xb = xblk[nx % 2]
            nx += 1
            t0, n = self.rw_mix_block(xb, hT, mi, blk)
            for oc in range(8):
                ps = self.bank()
                for kc in range(8):
                    p.mm(ps[:, 0:n], W[:, kc, oc * 128:(oc + 1) * 128], xb[:, kc, 0:n], start=(kc == 0), stop=(kc == 7))
                sg = stage[ns % 4]
                ns += 1
                p.copy(sg[:, 0:n], ps[:, 0:n], eng='act')
                p.dma(scr[kind][oc, :, t0:t0 + n].on(scr[kind + 'buf'][oc]), sg[:, 0:n])
    W0 = p.sb("rw_lr0", [128, 8, 320], BF16)
    Wa = p.sb("rw_lra", [128, 8, 320], BF16)
    Wb = p.sb("rw_lrb", [128, 8, 320], BF16)
    for z in range(2):
        p.dma(W0[:, :, z * 64:(z + 1) * 64], self.rw_w1[z].rr("(c p) r -> p c r", p=128), eng='pool')
    p.dma(W0[:, :, 128:192], self.rw_a1.rr("(c p) r -> p c r", p=128), eng='pool')
    p.dma(W0[:, :, 192:320], self.rw_g1.rr("(c p) r -> p c r", p=128), eng='pool')
    mu = self.col('mu')
    for (c0, c1, mi) in ((0, 128, 1), (128, 192, 4), (192, 320, 5)):
        wdt = c1 - c0
        mub = mu[:, mi * 8:(mi + 1) * 8].rr("p (c o) -> p c o", o=1).bcast([128, 8, wdt])
        omb = self.omu[:, mi * 8:(mi + 1) * 8].rr("p (c o) -> p c o", o=1).bcast([128, 8, wdt])
        p.tt(Wa[:, :, c0:c1], W0[:, :, c0:c1], omb, ALU.mult)
        p.tt(Wb[:, :, c0:c1], W0[:, :, c0:c1], mub, ALU.mult, eng='pool')
    for blk in range(5):
        hs = xblk[nx % 2]
        nx += 1
        t0, n = (0, 256) if blk == 0 else (256 + (blk - 1) * 512, 512)
        p.memset(hs, 0.0, eng='pool')
        for c in range(8):
            kindc = c // 2
            if blk == 0:
                if kindc in (0, 2):
                    p.copy(hs[:, c, 1:256], hT[:, c, 0:255], eng='act')
                else:
                    p.copy(hs[:, c, 0:255], hT[:, c, 1:256], eng='act')
            else:
                r0 = (blk - 1) * 8
                xv = hs[:, c, :].rr("p (r w) -> p r w", w=64)
                hv = hT[:, c, 256:LTOK].rr("p (r w) -> p r w", w=64)
                if kindc == 0:
                    p.copy(xv[:, :, 1:64], hv[:, r0:r0 + 8, 0:63], eng='act')
                elif kindc == 1:
                    p.copy(xv[:, :, 0:63], hv[:, r0:r0 + 8, 1:64], eng='act')
                elif kindc == 2:
                    lo = 1 if r0 == 0 else 0
                    p.copy(xv[:, lo:8, :], hv[:, r0 + lo - 1:r0 + 7, :], eng='act')
                else:
                    hi = 7 if r0 == 24 else 8
                    p.copy(xv[:, 0:hi, :], hv[:, r0 + 1:r0 + hi + 1, :], eng='act')
        for (c0, c1, dst, fn) in ((0, 64, tw[0], AF.Tanh), (64, 128, tw[1], AF.Tanh), (128, 192, ta, None), (192, 320, tg, AF.Sigmoid)):
            M = c1 - c0
            ps = self.bank()
            for kc in range(8):
                p.mm(ps[0:M, 0:n], Wa[:, kc, c0:c1], hT[:, kc, t0:t0 + n], start=(kc == 0), stop=False)
            for kc in range(8):
                p.mm(ps[0:M, 0:n], Wb[:, kc, c0:c1], hs[:, kc, 0:n], start=False, stop=(kc == 7))
            if fn is None:
                p.copy(dst[0:M, t0:t0 + n], ps[0:M, 0:n], eng='act')
            else:
                p.act(dst[0:M, t0:t0 + n], ps[0:M, 0:n], fn)
    p.release(m)


KE.setup_rw = _setup_rw
KE.setup_rw_consts = _setup_rw_consts
KE.rw_mix_block = _rw_mix_block
KE.rw_phase1 = _rw_phase1


BLKS = [(0, 512), (512, 512), (1024, 512), (1536, 512), (2048, 256)]


def _rw_chunk(self, b, cc, tw, ta, tg, last):
    p = self.p
    scr = self.scr[b]
    m = p.mark()
    kv = self.col('kv')
    rT = p.sb("rc_rT", [128, LTOK], BF16)
    kT = p.sb("rc_kT", [128, LTOK], BF16)
    vT = p.sb("rc_vT", [128, LTOK], BF16)
    for t_, nm in ((rT, 'r'), (kT, 'k'), (vT, 'v')):
        p.dma(t_, scr[nm][cc].on(scr[nm + 'buf'][cc]))
    KR = [p.sb("rc_KR%d" % z, [128, NT, 2, 128], BF16) for z in range(2)]
    khat = [p.sb("rc_khat%d" % z, [128, LTOK], BF16) for z in range(2)]
    bhat = [p.sb("rc_bhat%d" % z, [128, LTOK], BF16) for z in range(2)]
    kT4 = [p.sb("rc_kT4%d" % z, [128, LTOK], BF16) for z in range(2)]
    nbT4 = [p.sb("rc_nbT4%d" % z, [128, LTOK], BF16) for z in range(2)]
    gam = [p.sb("rc_gam%d" % z, [128, NT], F32) for z in range(2)]
    Vtok = p.sb("rc_Vtok", [128, NT, 128], BF16)
    y = p.sb("rc_y", [128, NT, 128], F32)
    yz = [y, p.sb("rc_y1", [128, NT, 128], F32)]
    ybufs = [[Buf("rcy%d_%d" % (t, hh)) for hh in range(2)] for t in range(NT)]
    m3 = p.mark()
    aT = p.sb("rc_aT", [128, LTOK], BF16)
    kap = p.sb("rc_kap", [128, LTOK], BF16)
    bet = p.sb("rc_bet", [128, LTOK], BF16)
    sig = p.sb("rc_sig", [128, LTOK], F32)
    P = p.sb("rc_P", [128, LTOK], F32)
    t1 = p.sb("rc_t1", [128, LTOK], F32)
    t2 = p.sb("rc_t2", [128, LTOK], F32)
    E = p.sb("rc_E", [128, LTOK], BF16)
    rmask = p.sb("rc_rmask", [128, LTOK], BF16)
    p.memset(rmask, 1.0, eng='pool')
    p.memset(rmask.rr("p (n t) -> p n t", t=128)[:, :, 0:1], 0.0, eng='pool')
    w2b = p.sb("rc_w2b", [64, 2, 128], BF16)
    p.dma(w2b, self.rw_w2[:, :, cc * 128:(cc + 1) * 128].rr("z r c -> r z c"), eng='pool')
    a2b = p.sb("rc_a2b", [64, 128], BF16)
    p.dma(a2b, self.rw_a2[:, cc * 128:(cc + 1) * 128], eng='pool')
    for (t0, n) in BLKS:
        ps = self.bank()
        p.mm(ps[:, 0:n], a2b, ta[0:64, t0:t0 + n])
        p.act(aT[:, t0:t0 + n], ps[:, 0:n], AF.Sigmoid, bias=self.col('a0')[:, cc:cc + 1])
    p.ts(t1, kT, kv[:, cc:cc + 1], ALU.mult)
    p.tt(E, t1, t1, ALU.mult, eng='pool')
    for (t0, n) in BLKS:
        ps = self.bank()
        p.mm(ps[:, 0:n], self.bones, E[:, t0:t0 + n])
        p.act(t2[:, t0:t0 + n], ps[:, 0:n], AF.Sqrt)
    p.ts(t2, t2, 1e-12, ALU.max)
    p.recip(t2, t2)
    p.tt(kap, t1, t2, ALU.mult)
    p.tt(bet, kap, aT, ALU.mult, eng='pool')
    p.ts(t1, aT, kv[:, 8 + cc:8 + cc + 1], ALU.mult, self.okv1[:, cc:cc + 1], ALU.add)
    p.tt(kT, kT, t1, ALU.mult)
    P3 = P.rr("p (n t) -> p n t", t=128)
    Ptot = P3[:, :, 127:128]
    Pb = Ptot.bcast([128, NT, 128])
    t13 = t1.rr("p (n t) -> p n t", t=128)
    t23 = t2.rr("p (n t) -> p n t", t=128)
    v3 = lambda x: x.rr("p (n t) -> p n t", t=128)
    for z in range(2):
        for (t0, n) in BLKS:
            ps = self.bank()
            p.mm(ps[:, 0:n], w2b[:, z, :], tw[z][0:64, t0:t0 + n])
            p.act(sig[:, t0:t0 + n], ps[:, 0:n], AF.Sigmoid, bias=self.col('w0')[:, z * 8 + cc:z * 8 + cc + 1])
        p.scan(P, rmask, sig, 0.0, ALU.mult, ALU.add)
        p.act(gam[z], Ptot.rr("p n o -> p (n o)"), AF.Exp, scale=-C0)
        if z == 0:
            G = P
            p.tt(t1, P, sig, ALU.subtract)
            p.tt(t23, P3, Pb, ALU.subtract)
        else:
            p.tt(t13, Pb, P3, ALU.subtract)
            p.tt(t2, sig, P, ALU.subtract)
            G = sig
            p.tt(sig, t1, sig, ALU.add, eng='pool')
        p.act(E, G, AF.Exp, scale=-C0)
        p.tt(KR[z][:, :, 1, :], v3(rT), v3(E), ALU.mult)
        p.act(E, t1, AF.Exp, scale=-C0)
        p.tt(KR[z][:, :, 0, :], v3(kap), v3(E), ALU.mult, eng='pool')
        p.act(E, G, AF.Exp, scale=C0)
        p.tt(khat[z], kT, E, ALU.mult)
        p.tt(bhat[z], bet, E, ALU.mult, eng='pool')
        p.act(E, t2, AF.Exp, scale=C0)
        p.tt(kT4[z], kT, E, ALU.mult)
        p.stt(nbT4[z], bet, -1.0, E, ALU.mult, ALU.mult)
    for g0 in range(0, NT, 8):
        n = min(8, NT - g0)
        ps = self.bank()
        psb = ps.bc(BF16)
        for i in range(n):
            p.tr(psb[:, i * 128:(i + 1) * 128], vT[:, (g0 + i) * 128:(g0 + i + 1) * 128], self.identb)
        p.copy(Vtok[:, g0:g0 + n, :], psb[:, 0:n * 128].rr("p (t c) -> p t c", c=128), eng='act')
    p.release(m3)
    mring = p.mark()
    NR = 12
    A4 = [p.sb("rs_A4%d" % i, [128, 4, 128], BF16) for i in range(NR)]
    SQ = [[p.sb("rs_SQ%d_%d" % (i, j), [128, 2, 128], BF16) for j in range(2)] for i in range(NR)]
    XXr = [p.sb("rs_XX%d" % i, [128, 2, 128], BF16) for i in range(NR)]
    Q0T = [p.sb("rs_Q0T%d" % i, [128, 128], BF16) for i in range(NR)]
    QM = [p.sb("rs_QM%d" % i, [128, 3, 128], BF16) for i in range(NR)]
    QMT = [p.sb("rs_QMT%d" % i, [128, 3, 128], BF16) for i in range(NR)]
    Y1r = [p.sb("rs_Y1%d" % i, [128, 128], BF16) for i in range(NR)]
    KB = [p.sb("rs_KB%d" % i, [128, 2, 128], BF16) for i in range(6)]
    Wb = [p.sb("rs_Wb%d" % i, [128, 64], BF16) for i in range(NR)]
    Ub = [p.sb("rs_Ub%d" % i, [128, 64], BF16) for i in range(NR)]
    H = [[p.sb("rs_H%d%d" % (z, hh), [128, 64], F32) for hh in range(2)] for z in range(2)]
    Hb = [[p.sb("rs_Hb%d%d" % (z, hh), [128, 64], BF16) for hh in range(2)] for z in range(2)]
    for z in range(2):
        for hh in range(2):
            p.memset(H[z][hh], 0.0, eng='pool')
            p.memset(Hb[z][hh], 0.0, eng='pool')
    units = [(step, z, hh) for step in range(NT) for z in range(2) for hh in range(2)]
    prep = {}

    def stage_a(u, bk, r):
        step, z, hh = units[u]
        tt = ORD[z][step]
        ts_ = slice(tt * 128, (tt + 1) * 128)
        pr = slice(hh * 64, (hh + 1) * 64)
        kb = KB[(u // 2) % 6]
        if hh == 0:
            kps, hk = p.palloc(1, bk)
            psb = kps.bc(BF16)
            p.tr(psb[:, 0:128], kT4[z][:, ts_], self.identb)
            p.tr(psb[:, 128:256], nbT4[z][:, ts_], self.identb)
        kr = KR[z][pr, tt].rr("p j t -> p (j t)")
        sc1, hsc1 = p.palloc(2, bk)
        p.mm(sc1[:, 0:256], bhat[z][pr, ts_], kr)
        q0, hq0 = p.palloc(1, bk)
        p.mm(q0[:, 0:128], KR[z][pr, tt, 0, :], bhat[z][pr, ts_])
        yield
        if hh == 0:
            p.copy(kb, psb[:, 0:256].rr("p (j c) -> p j c", j=2), eng='act')
            p.pfree(hk)
        p.tt(A4[r][:, 0:2, :], sc1.rr("p (j t) -> p j t", j=2), self.M4[z][:, 0:2, :], ALU.mult)
        p.pfree(hsc1)
        p.tt(Q0T[r], q0[:, 0:128], self.nstrictT[z], ALU.mult)
        p.pfree(hq0)
        sc2, hsc2 = p.palloc(2, bk)
        p.mm(sc2[:, 0:256], khat[z][pr, ts_], kr)
        yield
        p.tt(A4[r][:, 2:4, :], sc2.rr("p (j t) -> p j t", j=2), self.M4[z][:, 2:4, :], ALU.mult)
        p.pfree(hsc2)
        p.tt(QM[r], A4[r][:, 0:1, :].bcast([128, 3, 128]), self.masks3, ALU.mult, eng='pool')
        p.tt(QMT[r], Q0T[r].rr("p (o t) -> p o t", o=1).bcast([128, 3, 128]), self.masks3, ALU.mult, eng='pool')
        XX = XXr[r]
        XXf = XX.rr("p j t -> p (j t)")
        p.tt(XX[:, 0, :], QM[r][:, 0, :], self.identb, ALU.add, eng='pool')
        p.tt(XX[:, 1, :], QMT[r][:, 0, :], self.identb, ALU.add, eng='pool')
        yield
        cq, ct = QM[r][:, 0, :], QMT[r][:, 0, :]
        xq, xt = XX[:, 0, :], XX[:, 1, :]
        ps, h1 = p.palloc(2, bk)
        p.mm(ps[:, 0:128], ct, cq)
        p.mm(ps[:, 128:256], cq, ct)
        yield
        nxt = SQ[r][0]
        p.copy(nxt.rr("p j t -> p (j t)"), ps[:, 0:256], eng='act')
        p.pfree(h1)
        cq, ct = nxt[:, 0, :], nxt[:, 1, :]
        yield
        for lev in range(1, 5):
            ps2, h2 = p.palloc(2, bk)
            p.mm(ps2[:, 0:128], ct, xq)
            p.mm(ps2[:, 128:256], xq, ct)
            if lev < 4:
                ps, h1 = p.palloc(2, bk)
                p.mm(ps[:, 0:128], ct, cq)
                p.mm(ps[:, 128:256], cq, ct)
            yield
            p.tt(XXf, XXf, ps2[:, 0:256], ALU.add)
            p.pfree(h2)
            if lev < 4:
                nxt = SQ[r][lev % 2]
                p.copy(nxt.rr("p j t -> p (j t)"), ps[:, 0:256], eng='act')
                p.pfree(h1)
                cq, ct = nxt[:, 0, :], nxt[:, 1, :]
            yield
        for lvl in (1, 2):
            C, CT = QM[r][:, lvl, :], QMT[r][:, lvl, :]
            ps, h1 = p.palloc(1, bk)
            p.mm(ps[:, 0:128], CT, xq)
            yield
            p.copy(Y1r[r], ps[:, 0:128], eng='act')
            p.pfree(h1)
            yield
            ps2, h2 = p.palloc(2, bk)
            p.mm(ps2[:, 0:128], xt, Y1r[r])
            if lvl == 1:
                p.mm(ps2[:, 128:256], Y1r[r], xt)
            yield
            if lvl == 1:
                p.tt(XXf, XXf, ps2[:, 0:256], ALU.add)
            else:
                p.tt(XX[:, 0, :], XX[:, 0, :], ps2[:, 0:128], ALU.add)
            p.pfree(h2)
            yield
        prep[u] = (r, kb, XX[:, 0, :])

    def stage_b(u, bk, bq):
        step, z, hh = units[u]
        tt = ORD[z][step]
        ts_ = slice(tt * 128, (tt + 1) * 128)
        pr = slice(hh * 64, (hh + 1) * 64)
        cs = slice(hh * 64, (hh + 1) * 64)
        r, kb, xfin = prep.pop(u)
        a4 = A4[r]
        hb = Hb[z][hh]
        vt = Vtok[:, tt, cs]
        while True:
            try:
                w, hw = p.palloc(1, bk)
                break
            except RuntimeError:
                yield
        p.mm(w[:, 0:64], KR[z][pr, tt, 0, :], hb[pr, :], start=True, stop=False)
        p.mm(w[:, 0:64], a4[:, 2, :], vt, start=False, stop=True)
        yield
        p.copy(Wb[r], w[:, 0:64], eng='act')
        p.pfree(hw)
        yield
        while True:
            try:
                uu, hu_ = p.palloc(1, bk)
                break
            except RuntimeError:
                yield
        p.mm(uu[:, 0:64], xfin, Wb[r])
        yield
        p.copy(Ub[r], uu[:, 0:64], eng='act')
        p.pfree(hu_)
        yield
        while True:
            try:
                yb, hy = p.palloc(1, bk)
                break
            except RuntimeError:
                yield
        p.mm(yb[:, 0:64], KR[z][pr, tt, 1, :], hb[pr, :], start=True, stop=False)
        p.mm(yb[:, 0:64], a4[:, 3, :], vt, start=False, stop=False)
        p.mm(yb[:, 0:64], a4[:, 1, :], Ub[r], start=False, stop=True)
        if step < NT - 1:
            while True:
                try:
                    hu, hh_ = p.palloc(1, bk)
                    break
                except RuntimeError:
                    yield
            p.mm(hu[pr, 0:64], kb[:, 0, cs], vt, start=True, stop=False)
            p.mm(hu[pr, 0:64], kb[:, 1, cs], Ub[r], start=False, stop=True)
        yield
        if step < NT - 1:
            p.stt(H[z][hh][pr, :], H[z][hh][pr, :], gam[z][pr, tt:tt + 1], hu[pr, 0:64], ALU.mult, ALU.add)
            p.pfree(hh_)
            p.copy(hb[pr, :], H[z][hh][pr, :], eng='pool')
        p.copy(yz[z][:, tt, cs].on(ybufs[tt][hh]), yb[:, 0:64], eng='act')
        p.pfree(hy)
        yield

    NA = int(os.environ.get('RW_NA', '6'))
    BFIRST = int(os.environ.get('RW_BFIRST', '0'))
    nU = len(units)
    free_r = list(range(NR))
    a_banks = list(range(NA))
    NBB = 8 - NA
    act_a = []
    act_b = []
    a_done = set()
    b_emitted = set()
    next_a = 0
    next_b = [0, 1, 2, 3]
    rmap = {}
    it_ = 0
    last_start = -100
    STAG = int(os.environ.get('RW_STAG', '4'))
    BPRIO = int(os.environ.get('RW_BPRIO', '1'))
    LOOKA = int(os.environ.get('RW_LOOK', '10'))
    while len(b_emitted) < nU:
        it_ += 1
        while next_a < nU and a_banks and free_r and next_a < min(next_b) + LOOKA and it_ - last_start >= STAG:
            last_start = it_
            bk = a_banks.pop(0)
            r = free_r.pop(0)
            rmap[next_a] = r
            act_a.append((stage_a(next_a, bk, r), next_a, bk))
            next_a += 1
        for j in range(4):
            u = next_b[j]
            if u < nU and u in a_done and not any(x[2] == j for x in act_b):
                act_b.append((stage_b(u, NA + (j % NBB), None), u, j))
                next_b[j] = u + 4
        def adv_a():
            nonlocal act_a
            nxt_a = []
            for g, u, bk in act_a:
                try:
                    next(g)
                    nxt_a.append((g, u, bk))
                except StopIteration:
                    a_done.add(u)
                    a_banks.append(bk)
            act_a = nxt_a

        def adv_b():
            nonlocal act_b
            for _rep in range(BPRIO):
                nxt_b = []
                for g, u, j in act_b:
                    try:
                        next(g)
                        nxt_b.append((g, u, j))
                    except StopIteration:
                        b_emitted.add(u)
                        free_r.append(rmap.pop(u))
                act_b = nxt_b

        if BFIRST:
            adv_b()
            adv_a()
        else:
            adv_a()
            adv_b()
    p.release(mring)
    if getattr(self, 'dbg', False):
        for tt in range(NT):
            for hh in range(2):
                p.tt(y[:, tt, hh * 64:(hh + 1) * 64], y[:, tt, hh * 64:(hh + 1) * 64].on(ybufs[tt][hh]), y[:, tt, hh * 64:(hh + 1) * 64].on(ybufs[tt][hh]), ALU.max)
        p.dma(self.dbg_y[cc], y.rr("p t c -> p (t c)"))
    g2b = p.sb("rp_g2b", [128, 128], BF16)
    p.dma(g2b, self.rw_g2[:, cc * 128:(cc + 1) * 128], eng='pool')
    s1 = p.sb("rp_s1", [128, 36], F32)
    s2 = p.sb("rp_s2", [128, 36], F32)
    sq = p.sb("rp_sq", [128, 36, 64], F32)
    yn = p.sb("rp_yn", [128, NT, 128], BF16)
    lnT = p.sb("rp_lnT", [128, LTOK], BF16)
    prod = p.sb("rp_prod", [128, LTOK], BF16)
    y3 = y.rr("p t (h c) -> p (t h) c", c=64)
    allb = [ybufs[t_][h_] for t_ in range(NT) for h_ in range(2)]
    yall = V(y.ap, allb)
    y1all = V(yz[1].ap, allb)
    p.tt(yall, yall, y1all, ALU.add)
    p.tt(sq, V(y3.ap, allb), V(y3.ap, allb), ALU.mult)
    p.reduce(s2, sq, ALU.add)
    p.reduce(s1, y3, ALU.add)
    p.ts(s1, s1, 1.0 / 64, ALU.mult)
    p.tt(sq[:, :, 0], s1, s1, ALU.mult)
    p.stt(s2, s2, 1.0 / 64, sq[:, :, 0], ALU.mult, ALU.subtract)
    p.act(s2, s2, AF.Sqrt, bias=RW_LN_EPS)
    p.recip(s2, s2)
    yn3 = yn.rr("p t (h c) -> p (t h) c", c=64)
    p.tt(sq, y3, s1.rr("p (n o) -> p n o", o=1).bcast([128, 36, 64]), ALU.subtract)
    p.tt(yn3, sq, s2.rr("p (n o) -> p n o", o=1).bcast([128, 36, 64]), ALU.mult)
    lnx = self.col('lnx')
    for g0 in range(0, NT, 8):
        n = min(8, NT - g0)
        ps = self.bank()
        psb = ps.bc(BF16)
        for i in range(n):
            p.tr(psb[:, i * 128:(i + 1) * 128], yn[:, g0 + i, :], self.identb)
        p.ts(lnT[:, g0 * 128:(g0 + n) * 128], psb[:, 0:n * 128], lnx[:, cc:cc + 1], ALU.mult, lnx[:, 8 + cc:8 + cc + 1], ALU.add)
    p.stt(prod, rT, kv[:, 16 + cc:16 + cc + 1], kT, ALU.mult, ALU.mult)
    stage = [p.sb("rp_stage%d" % i, [128, 512], BF16) for i in range(2)]
    tmpb = [p.sb("rp_tmp%d" % i, [128, 512], F32) for i in range(2)]
    for i, (t0, n) in enumerate(BLKS):
        psA = self.bank()
        p.mm(psA[:, 0:n], self.bones, prod[:, t0:t0 + n])
        psG = self.bank()
        p.mm(psG[:, 0:n], g2b, tg[:, t0:t0 + n])
        tb = tmpb[i % 2]
        p.tt(tb[:, 0:n], psA[:, 0:n], vT[:, t0:t0 + n], ALU.mult)
        p.tt(tb[:, 0:n], tb[:, 0:n], lnT[:, t0:t0 + n], ALU.add, eng='pool')
        sg = stage[i % 2]
        p.tt(sg[:, 0:n], psG[:, 0:n], tb[:, 0:n], ALU.mult)
        p.dma(scr['o'][cc, :, t0:t0 + n].on(scr['obuf'][cc]), sg[:, 0:n])
    p.release(m)


def _rw_mixer(self, b, xi, xo, last=True):
    p = self.p
    l = 1
    m = p.mark()
    wo = p.sb("rw_wo", [128, 8, D], BF16)
    tw = [p.sb("rw_tw%d" % z, [128, LTOK], BF16) for z in range(2)]
    ta = p.sb("rw_ta", [128, LTOK], BF16)
    tg = p.sb("rw_tg", [128, LTOK], BF16)
    mh = p.mark()
    hT = p.sb("rw_hT", [128, 8, LTOK], BF16)
    self.prenorm(b, xi, l, 0, hT)
    self.rw_phase1(b, hT, tw, ta, tg)
    p.release(mh)
    for cc in range(8):
        if cc == 6:
            for c in range(8):
                p.dma(wo[:, c, :], self.w_o[c * 128:(c + 1) * 128, :], eng='pool')
        self.rw_chunk(b, cc, tw, ta, tg, last)
    p.release(mh)
    oT = p.sb("rw_oT", [128, 8, LTOK], BF16)
    for c in range(8):
        p.dma(oT[:, c, :], self.scr[b]['o'][c].on(self.scr[b]['obuf'][c]))
    G = [p.sb("rwo_G%d" % i, [128, D], F32) for i in range(2)]
    gtmp = p.sb("rwo_gtmp", [128, 128], F32)
    st = self.res_state(2)
    self.gate_row(G[0], l, 2, b, gtmp)
    if not last:
        self.gate_row(G[1], l, 2, 4, gtmp)
    for tt in (range(2, NT) if last else range(NT)):
        yb = [self.bank(), self.bank()]
        for h in range(2):
            for c in range(8):
                p.mm(yb[h], oT[:, c, tt * 128:(tt + 1) * 128], wo[:, c, h * 512:(h + 1) * 512], start=(c == 0), stop=(c == 7))
        self.residual_tile(b, tt, xi, xo, yb, G[1] if tt < 2 else G[0], st[tt % 2])
    p.release(m)


KE.rw_chunk = _rw_chunk
KE.rw_mixer = _rw_mixer


_NC_CACHE = {}
W_NAMES = ('w_mod', 'ffn_w_up', 'ffn_w_down')
EV_NAMES = ('ev_w_in', 'ev_w_out', 'ev_gla_w2')
RW_NAMES = ('rw_w_rkv', 'rw_w_o', 'rw_w1', 'rw_w2', 'rw_a1', 'rw_a2', 'rw_g1', 'rw_g2')


def build_program(NB, pk):
    k = KE(NB, pk.cols, pk.n, dbg=False)
    k.setup_even()
    k.setup_rw()
    k.mod_stage(0)
    for b in range(NB):
        hT, mixT, m = k.even_mixer(b, 0, 1)
        k.gla(b, hT, mixT)
        k.even_out(b, 0, 1, hT, mixT, m)
        k.ffn(b, 0, 1, 2, do_ctx=True)
    k.setup_rw_consts()
    k.mod_stage(1)
    for b in range(NB):
        k.rw_mixer(b, 2, 3, last=True)
        k.ffn(b, 1, 3, 4, do_ctx=False)
    return k.p.finish()


def kernel(**inp):
    inp = {k_: np.asarray(v_, dtype=np.float32) for k_, v_ in inp.items()}
    NCORE = 8
    B = inp['x'].shape[0]
    NB = B // NCORE
    pk = host_small(inp)
    host_small_even(pk, inp)
    host_small_rw(pk, inp)
    small = pk.pack()
    rows = np.zeros((16, D), np.float32)
    rows[0:8] = inp['norm_g'].reshape(8, D)
    rows[8, :16] = inp['ev_b_gates'][0]
    nc = build_program(NB, pk)
    shared = {"small": small, "rows": rows}
    for nm in W_NAMES:
        shared[nm] = np.ascontiguousarray(inp[nm])
    for nm in EV_NAMES + RW_NAMES:
        shared[nm] = np.ascontiguousarray(inp[nm][0])
    in_maps = []
    for c in range(NCORE):
        sl = slice(c * NB, (c + 1) * NB)
        cc = np.zeros((6, D), np.float32)
        cc[0:NB] = inp['c'][sl]
        cc[4] = inp['c_ctx']
        ccT = np.ascontiguousarray(cc.reshape(6, 8, 128).transpose(2, 1, 0).reshape(128, 48))
        xcat = np.ascontiguousarray(np.concatenate([inp['ctx'][sl], inp['x'][sl]], axis=1))
        d = dict(shared)
        d["xcat"] = xcat
        d["ccT"] = ccT
        in_maps.append(d)
    res = run_bass_kernel_spmd(nc, in_maps, core_ids=list(range(NCORE)))
    out = np.concatenate([np.asarray(r["out"]) for r in res.results], axis=0)
    return out.astype(np.float32)
```

```python
import numpy as np
import concourse.bass as bass
import concourse.mybir as mybir
from concourse.bass_utils import run_bass_kernel_spmd

F32 = mybir.dt.float32
BF16 = mybir.dt.bfloat16
AF = mybir.ActivationFunctionType
ALU = mybir.AluOpType
AX = mybir.AxisListType
ENG = ('sp', 'act', 'dve', 'pool', 'pe')
DSZ = {F32: 4, BF16: 2}
SAME_SYNC = True
NDMASEM = 8
FUSE_WAIT = True
NFUSE = 1


class Buf:
    __slots__ = ('name', 'w', 'r', 'excl', 'subs')

    def __init__(self, name, excl=False):
        self.name = name
        self.w = None
        self.r = {}
        self.excl = excl
        self.subs = []


class V:
    __slots__ = ('ap', 'buf')

    def __init__(self, ap, buf):
        self.ap = ap
        self.buf = buf

    def __getitem__(self, idx):
        return V(self.ap[idx], self.buf)

    def rr(self, pat, **kw):
        return V(self.ap.rearrange(pat, **kw), self.buf)

    def bc(self, dt):
        return V(self.ap.bitcast(dt), self.buf)

    def on(self, buf):
        if isinstance(self.buf, Buf) and buf is not self.buf and buf not in self.buf.subs:
            self.buf.subs.append(buf)
        return V(self.ap, buf)

    def bcast(self, shape):
        return V(self.ap.broadcast_to(list(shape)), self.buf)

    def pbcast(self, n):
        return V(self.ap.partition_broadcast(n), self.buf)

    @property
    def shape(self):
        return tuple(self.ap.shape)


class Prog:
    def __init__(self):
        nc = self.nc = bass.Bass("TRN2", target_bir_lowering=False)
        self.q = {e: [] for e in ENG}
        self.cnt = {e: 0 for e in ENG}
        self.sem = {e: nc.alloc_semaphore('sem_' + e) for e in ENG}
        self.seen = {e: {} for e in ENG}
        self.dsem = [nc.alloc_semaphore('dsem%d' % i) for i in range(NDMASEM)]
        self.dval = [0] * NDMASEM
        self.dnext = 0
        self.sb_off = 16512
        self.sb_max = 0
        self.nalloc = 0
        self.ninst = 0
        self.regions = []
        self.clock = {}
        self.tokidx = {}
        self.ntok = 0

    def dram(self, name, shape, dt, kind="Internal"):
        t = self.nc.dram_tensor(name, list(shape), dt, kind=kind)
        return V(t.ap(), Buf(name))

    def sb(self, name, shape, dt, nbuf=None):
        per = int(np.prod(shape[1:])) * DSZ[dt]
        per = (per + 63) // 64 * 64
        off = self.sb_off
        self.sb_off += per
        self.sb_max = max(self.sb_max, self.sb_off)
        assert self.sb_off <= 229344, (name, self.sb_off)
        self.nalloc += 1
        t = self.nc.alloc_sbuf_tensor_at("%s_%d" % (name, self.nalloc), list(shape), dt, offset=off)
        nb = Buf(name)
        lo, hi = off, off + per
        keep = []
        for (a, b_, ob) in self.regions:
            if a < hi and lo < b_:
                toks = []
                for ob2 in [ob] + ob.subs:
                    toks += list(ob2.r.values())
                    if ob2.w is not None:
                        toks.append(ob2.w)
                for tk in toks:
                    k = id(tk[0])
                    if k not in nb.r or nb.r[k][1] < tk[1]:
                        nb.r[k] = tk
                if a < lo:
                    keep.append((a, lo, ob))
                if b_ > hi:
                    keep.append((hi, b_, ob))
            else:
                keep.append((a, b_, ob))
        keep.append((lo, hi, nb))
        self.regions = keep
        return V(t.ap(), nb)

    def mark(self):
        return self.sb_off

    def release(self, m):
        self.sb_off = m

    def psum_banks(self):
        banks = []
        self.pslot_bufs = []
        self.pslot_free = [True] * 32
        for i in range(8):
            t = self.nc.alloc_psum_tensor("psb%d" % i, [128, 512], F32)
            bl = Buf("psb%d" % i, excl=True)
            self.pslot_bufs.append(bl)
            banks.append(V(t.ap(), bl))
        self.pbanks = banks
        return banks

    def palloc(self, nq, bank=None):
        banks = range(8) if bank is None else (bank,)
        for bk in banks:
            for q0 in range(0, 4, nq):
                if all(self.pslot_free[bk * 4 + q0 + j] for j in range(nq)):
                    for j in range(nq):
                        self.pslot_free[bk * 4 + q0 + j] = False
                    v = V(self.pbanks[bk].ap[:, q0 * 128:(q0 + nq) * 128], self.pslot_bufs[bk])
                    return v, (bk, q0, nq)
        raise RuntimeError("out of PSUM slots")

    def pfree(self, h):
        bk, q0, nq = h
        for j in range(nq):
            self.pslot_free[bk * 4 + q0 + j] = True

    def _emit(self, eng, fn, reads, writes, dma=False):
        waits = {}

        def need(tok, waw_pe=False):
            if tok is None:
                return
            sem, val, te = tok
            if te == eng and not dma:
                if eng == 'pe' or not SAME_SYNC:
                    return
            k = id(sem)
            if k not in waits or waits[k][1] < val:
                waits[k] = (sem, val)

        ex = [b for b in reads if b.excl and b not in writes]
        if ex:
            reads = [b for b in reads if not b.excl]
            writes = list(writes) + ex
        for b in reads:
            need(b.w)
        for b in writes:
            need(b.w)
            for t in b.r.values():
                need(t)
        if dma:
            k = self.dnext
            self.dnext = (self.dnext + 1) % NDMASEM
            if self.dval[k] > 0:
                need((self.dsem[k], self.dval[k], 'dma'))
            self.dval[k] += 16
            tok = (self.dsem[k], self.dval[k], 'dma')
            inc = (self.dsem[k], 16)
        else:
            self.cnt[eng] += 1
            tok = (self.sem[eng], self.cnt[eng], eng)
            inc = (self.sem[eng], 1)
        seen = self.seen[eng]
        final = []
        cand = sorted(waits.items(), key=lambda kv: -self.tokidx.get((kv[0], kv[1][1]), 0))
        for k, (sem, val) in cand:
            if seen.get(k, 0) < val:
                seen[k] = val
                final.append((sem, val))
                ck = self.clock.get((k, val))
                if ck:
                    for kk, vv in ck.items():
                        if seen.get(kk, 0) < vv:
                            seen[kk] = vv
        self.ntok += 1
        self.tokidx[(id(tok[0]), tok[1])] = self.ntok
        self.clock[(id(tok[0]), tok[1])] = dict(seen)
        for b in reads:
            b.r[id(tok[0])] = tok
        for b in writes:
            b.w = tok
            b.r = {}
        self.q[eng].append((final, fn, inc))
        self.ninst += 1
        self.nwait = getattr(self, 'nwait', 0) + len(final)

    def barrier(self):
        toks = [(self.sem[e], self.cnt[e]) for e in ENG if self.cnt[e] > 0]
        toks += [(self.dsem[k], self.dval[k]) for k in range(NDMASEM) if self.dval[k] > 0]
        for e in ENG:
            final = []
            for sem, val in toks:
                if sem is self.sem[e]:
                    continue
                if self.seen[e].get(id(sem), 0) < val:
                    self.seen[e][id(sem)] = val
                    final.append((sem, val))
            if final:
                self.q[e].append((final, None, None))

    def finish(self):
        self.barrier()
        nc = self.nc
        q = self.q

        def replay(e, lst):
            for waits, fn, inc in lst:
                if fn is None or inc[1] == 16 or not FUSE_WAIT:
                    for sem, val in waits:
                        e.wait_ge(sem, val)
                    if fn is not None:
                        ins = fn(e)
                        ins.then_inc(inc[0], inc[1])
                    continue
                for sem, val in waits[:-NFUSE]:
                    e.wait_ge(sem, val)
                ins = fn(e)
                for sem, val in waits[-NFUSE:]:
                    ins._wait_ge(sem, val)
                ins.then_inc(inc[0], inc[1])

        with nc.Block() as block:
            @block.sync
            def _(e):
                replay(e, q['sp'])

            @block.scalar
            def _(e):
                replay(e, q['act'])

            @block.vector
            def _(e):
                replay(e, q['dve'])

            @block.gpsimd
            def _(e):
                replay(e, q['pool'])

            @block.tensor
            def _(e):
                replay(e, q['pe'])
        return nc

    @staticmethod
    def _a(x):
        return x.ap if isinstance(x, V) else x

    @staticmethod
    def _bufs(*xs):
        out = []
        for x in xs:
            if isinstance(x, V):
                bl = x.buf if isinstance(x.buf, (list, tuple)) else (x.buf,)
                for b in bl:
                    if b not in out:
                        out.append(b)
        return out

    def dma(self, out, in_, eng='sp', **kw):
        o, i = out.ap, in_.ap
        self._emit(eng, lambda e: e.dma_start(out=o, in_=i, **kw), self._bufs(in_), self._bufs(out), dma=True)

    def mm(self, out, lhsT, rhs, start=True, stop=True):
        o, l, r = out.ap, lhsT.ap, rhs.ap
        self._emit('pe', lambda e: e.matmul(o, lhsT=l, rhs=r, start=start, stop=stop),
                   self._bufs(lhsT, rhs), self._bufs(out))

    def tr(self, out, in_, ident):
        o, i, d = out.ap, in_.ap, ident.ap
        self._emit('pe', lambda e: e.transpose(o, i, d), self._bufs(in_, ident), self._bufs(out))

    def act(self, out, in_, func, bias=None, scale=None, accum=None):
        kw = {}
        if bias is not None:
            kw['bias'] = self._a(bias)
        if scale is not None:
            kw['scale'] = self._a(scale)
        if accum is not None:
            kw['accum_out'] = accum.ap
        o, i = out.ap, in_.ap
        self._emit('act', lambda e: e.activation(out=o, in_=i, func=func, **kw),
                   self._bufs(in_, bias, scale), self._bufs(out, accum))

    def tt(self, out, in0, in1, op, eng='dve'):
        o, a, b = out.ap, in0.ap, in1.ap
        self._emit(eng, lambda e: e.tensor_tensor(out=o, in0=a, in1=b, op=op),
                   self._bufs(in0, in1), self._bufs(out))

    def ts(self, out, in0, s1, op0, s2=None, op1=None, eng='dve', accum=None):
        o, a = out.ap, in0.ap
        a1, a2 = self._a(s1), self._a(s2)
        kw = {}
        if op1 is not None:
            kw['op1'] = op1
        if accum is not None:
            kw['accum_out'] = accum.ap
        self._emit(eng, lambda e: e.tensor_scalar(out=o, in0=a, scalar1=a1, scalar2=a2, op0=op0, **kw),
                   self._bufs(in0, s1, s2), self._bufs(out, accum))

    def stt(self, out, in0, scalar, in1, op0, op1, eng='dve'):
        o, a, b = out.ap, in0.ap, in1.ap
        s = self._a(scalar)
        self._emit(eng, lambda e: e.scalar_tensor_tensor(out=o, in0=a, scalar=s, in1=b, op0=op0, op1=op1),
                   self._bufs(in0, scalar, in1), self._bufs(out))

    def copy(self, out, in_, eng='dve'):
        o, i = out.ap, in_.ap
        if eng == 'act':
            self._emit('act', lambda e: e.activation(out=o, in_=i, func=AF.Copy), self._bufs(in_), self._bufs(out))
        else:
            self._emit(eng, lambda e: e.tensor_copy(out=o, in_=i), self._bufs(in_), self._bufs(out))

    def memset(self, out, val, eng='dve'):
        o = out.ap
        self._emit(eng, lambda e: e.memset(o, val), [], self._bufs(out))

    def reduce(self, out, in_, op, axis=AX.X, eng='dve'):
        o, i = out.ap, in_.ap
        self._emit(eng, lambda e: e.tensor_reduce(out=o, in_=i, axis=axis, op=op), self._bufs(in_), self._bufs(out))

    def recip(self, out, in_):
        o, i = out.ap, in_.ap
        self._emit('dve', lambda e: e.reciprocal(out=o, in_=i), self._bufs(in_), self._bufs(out))

    def aselect(self, out, in_, pattern, cmp, fill, base, cm):
        o, i = out.ap, in_.ap
        self._emit('pool', lambda e: e.affine_select(out=o, in_=i, pattern=pattern, compare_op=cmp, fill=fill,
                                                     base=base, channel_multiplier=cm),
                   self._bufs(in_), self._bufs(out))

    def scan(self, out, d0, d1, init, op0, op1):
        o, a, b = out.ap, d0.ap, d1.ap
        self._emit('dve', lambda e: e.tensor_tensor_scan(out=o, data0=a, data1=b, initial=init, op0=op0, op1=op1),
                   self._bufs(d0, d1), self._bufs(out))


import os

D = 1024
NT = 18
LTOK = 2304
EPS = 1e-6
DFF = 2816
NJ = 22


class Packer:
    def __init__(self):
        self.cols = {}
        self.n = 0
        self.arrs = []

    def add(self, name, arr):
        arr = np.ascontiguousarray(arr, dtype=np.float32).reshape(128, -1)
        self.cols[name] = (self.n, arr.shape[1])
        self.n += arr.shape[1]
        self.arrs.append(arr)

    def pack(self):
        return np.ascontiguousarray(np.concatenate(self.arrs, axis=1))


def fm(v):
    v = np.asarray(v, dtype=np.float32)
    lead = v.shape[:-1]
    c = v.shape[-1] // 128
    v = v.reshape(lead + (c, 128))
    return np.moveaxis(v, -1, 0).reshape(128, -1)


def host_small(inp, layer_cols_only=False):
    pk = Packer()
    for l in range(2):
        pk.add('bmod%d' % l, fm(inp['b_mod'][l]))
        pk.add('ng%d' % l, fm(inp['norm_g'][l]))
        pk.add('cw%d' % l, np.moveaxis(inp['ffn_conv_w'][l].reshape(9, NJ, 128), 2, 0).transpose(0, 2, 1).reshape(128, NJ * 9))
        pk.add('cb%d' % l, fm(inp['ffn_conv_b'][l]))
    return pk


class K:
    def __init__(self, NB, small_cols, nsmall, dbg=False):
        self.NB = NB
        self.dbg = dbg
        p = self.p = Prog()
        self.sc = small_cols
        self.X0 = p.dram("xcat", [NB, LTOK, D], F32, kind="ExternalInput")
        self.ccT_d = p.dram("ccT", [128, 8 * 6], F32, kind="ExternalInput")
        self.small_d = p.dram("small", [128, nsmall], F32, kind="ExternalInput")
        self.rows_d = p.dram("rows", [16, D], F32, kind="ExternalInput")
        self.w_mod = p.dram("w_mod", [2, D, 6 * D], F32, kind="ExternalInput")
        self.w_up = p.dram("ffn_w_up", [2, D, 2 * DFF], F32, kind="ExternalInput")
        self.w_down = p.dram("ffn_w_down", [2, DFF, D], F32, kind="ExternalInput")
        self.Xs = [self.X0]
        for i in range(1, 4):
            self.Xs.append(p.dram("xs%d" % i, [NB, LTOK, D], F32, kind="ExternalOutput" if dbg else "Internal"))
        self.out = p.dram("out", [NB, 2048, D], F32, kind="ExternalOutput")
        self.xbufs = [[[Buf("x%d_%d_%d" % (i, b, t)) for t in range(NT)] for b in range(NB)] for i in range(5)]
        self.banks = p.psum_banks()
        self.nbank = 0
        self.small = p.sb("small", [128, nsmall], F32)
        p.dma(self.small, self.small_d)
        self.identf = p.sb("identf", [128, 128], F32)
        p.memset(self.identf, 1.0, eng='pool')
        p.aselect(self.identf, self.identf, [[-1, 128]], ALU.is_equal, 0.0, 0, 1)
        self.identb = p.sb("identb", [128, 128], BF16)
        p.copy(self.identb, self.identf, eng='pool')
        self.scT = p.sb("scT", [128, 48], F32)
        p.dma(self.scT, self.ccT_d)
        p.act(self.scT, self.scT, AF.Silu)
        self.modT = p.sb("modT", [128, 48 * 6], F32)
        self.A1 = p.sb("A1", [128, 48], F32)
        self.A2 = p.sb("A2", [128, 48], F32)
        self.base_mark = p.mark()

    def bank(self):
        b = self.banks[self.nbank]
        self.nbank = (self.nbank + 1) % 8
        return b

    @staticmethod
    def run_rr(gens, stagger=0):
        pending = list(gens)
        active = []
        it_ = 0
        while active or pending:
            if pending and (stagger == 0 or it_ % stagger == 0 or not active):
                if stagger == 0:
                    active += pending
                    pending = []
                else:
                    active.append(pending.pop(0))
            it_ += 1
            nxt_ = []
            for g in active:
                try:
                    next(g)
                    nxt_.append(g)
                except StopIteration:
                    pass
            active = nxt_

    def col(self, name, a=None, b=None):
        o, w = self.sc[name]
        if a is None:
            return self.small[:, o:o + w]
        return self.small[:, o + a:o + b]

    def mod_stage(self, l):
        p = self.p
        m = p.mark()
        wst = [p.sb("wmod_st%d" % i, [128, 8, 512], F32) for i in range(2)]
        ps = self.bank()
        for blk in range(12):
            w = wst[blk % 2]
            p.dma(w, self.w_mod[l, :, blk * 512:(blk + 1) * 512].rr("(c p) n -> p c n", p=128))
            for f in range(4):
                fc = blk * 4 + f
                for kc in range(8):
                    p.mm(ps[:, fc * 6:(fc + 1) * 6], w[:, kc, f * 128:(f + 1) * 128], self.scT[:, kc * 6:(kc + 1) * 6],
                         start=(kc == 0), stop=(kc == 7))
        p.tt(self.modT.rr("p (c r) -> p c r", r=6), ps[:, 0:288].rr("p (c r) -> p c r", r=6),
             self.col('bmod%d' % l).rr("p (c o) -> p c o", o=1).bcast([128, 48, 6]), ALU.add)
        mv = self.modT.rr("p (i c r) -> p i c r", i=6, c=8)
        ng = self.col('ng%d' % l).rr("p (i c o) -> p i c o", i=4, o=1)
        for A, mi, gi in ((self.A1, 1, 0), (self.A2, 4, 2)):
            Av = A.rr("p (c r) -> p c r", r=6)
            p.ts(Av, mv[:, mi], 1.0, ALU.add)
            p.tt(Av, Av, ng[:, gi].bcast([128, 8, 6]), ALU.mult)
        p.release(m)

    def modvec(self, l, which):
        mv = self.modT.rr("p (i c r) -> p i c r", i=6, c=8)
        if which == 0:
            return self.A1.rr("p (c r) -> p c r", r=6), mv[:, 0]
        return self.A2.rr("p (c r) -> p c r", r=6), mv[:, 3]

    def gate_row(self, dst, l, gi, row, tmp):
        p = self.p
        mv = self.modT.rr("p (i c r) -> p i c r", i=6, c=8)
        nrow = l * 4 + (1 if gi == 2 else 3)
        p.dma(dst, self.rows_d[nrow:nrow + 1, :].pbcast(128))
        for half in range(2):
            ps = self.bank()
            for cc in range(4):
                c = half * 4 + cc
                p.copy(tmp, mv[:, gi, c, row:row + 1].bcast([128, 128]))
                p.mm(ps[:, cc * 128:(cc + 1) * 128], tmp, self.identf)
            p.tt(dst[:, half * 512:(half + 1) * 512], dst[:, half * 512:(half + 1) * 512], ps, ALU.mult)

    def prenorm(self, b, xi, l, which, hT, tiles=range(NT)):
        p = self.p
        A, Bv = self.modvec(l, which)
        m = p.mark()
        NG = 4
        xt = [p.sb("pn_x%d" % i, [128, D], F32) for i in range(NG)]
        junk = p.sb("pn_junk", [128, D], BF16)
        xn = [p.sb("pn_xn%d" % i, [128, D], BF16) for i in range(NG)]
        ss = [p.sb("pn_ss%d" % i, [128, 1], F32) for i in range(NG)]
        tmp = [p.sb("pn_tmp%d" % i, [128, 8, 128], F32) for i in range(NG)]
        tiles = list(tiles)

        def chain(g):
            for tt in tiles[g::NG]:
                x, s = xt[g], ss[g]
                p.dma(x, self.Xs[xi][b, tt * 128:(tt + 1) * 128, :].on(self.xbufs[xi][b][tt]))
                yield
                p.act(junk, x, AF.Square, accum=s)
                yield
                p.act(s, s, AF.Sqrt, bias=EPS, scale=1.0 / D)
                yield
                p.recip(s, s)
                yield
                p.act(xn[g], x, AF.Identity, scale=s)
                yield
                ps, hp = p.palloc(4, 2 * g)
                psb = ps.bc(BF16)
                for c in range(8):
                    p.tr(psb[:, c * 128:(c + 1) * 128], xn[g][:, c * 128:(c + 1) * 128], self.identb)
                yield
                row = 4 if tt < 2 else b
                p.tt(tmp[g], psb.rr("p (c t) -> p c t", c=8), A[:, :, row:row + 1].bcast([128, 8, 128]), ALU.mult)
                p.pfree(hp)
                yield
                p.tt(hT[:, :, tt * 128:(tt + 1) * 128], tmp[g], Bv[:, :, row:row + 1].bcast([128, 8, 128]), ALU.add,
                     eng='pool')
                yield

        self.run_rr([chain(g) for g in range(NG)])
        p.release(m)

    def residual_tile(self, b, tt, xi, xo, ybanks, G, st):
        p = self.p
        x, junk, ss2, s, tmp = st
        p.dma(x, self.Xs[xi][b, tt * 128:(tt + 1) * 128, :].on(self.xbufs[xi][b][tt]))
        for h in range(2):
            p.act(junk, ybanks[h], AF.Square, accum=ss2[:, h:h + 1])
        p.tt(s, ss2[:, 0:1], ss2[:, 1:2], ALU.add)
        p.act(s, s, AF.Sqrt, bias=EPS, scale=1.0 / D)
        p.recip(s, s)
        for h in range(2):
            p.stt(tmp[:, h * 512:(h + 1) * 512], ybanks[h], s, G[:, h * 512:(h + 1) * 512], ALU.mult, ALU.mult)
        p.tt(tmp, tmp, x, ALU.add, eng='pool')
        if xo == 4:
            dst = self.out[b, (tt - 2) * 128:(tt - 1) * 128, :]
        else:
            dst = self.Xs[xo][b, tt * 128:(tt + 1) * 128, :]
        p.dma(dst.on(self.xbufs[xo][b][tt]), tmp)

    def res_state(self, n=2):
        p = self.p
        return [(p.sb("rs_x%d" % i, [128, D], F32), p.sb("rs_junk%d" % i, [128, 512], BF16),
                 p.sb("rs_ss2%d" % i, [128, 2], F32), p.sb("rs_s%d" % i, [128, 1], F32),
                 p.sb("rs_tmp%d" % i, [128, D], F32)) for i in range(n)]

    def ffn(self, b, l, xi, xo, do_ctx=True):
        p = self.p
        m = p.mark()
        hT = p.sb("ffn_hT", [128, 8, LTOK], BF16)
        self.prenorm(b, xi, l, 1, hT, tiles=range(NT) if do_ctx else range(2, NT))
        wdown = p.sb("ffn_wdown", [128, NJ, D], BF16)
        actT = p.sb("ffn_actT", [128, NJ, 1280], BF16)
        wg = [p.sb("ffn_wg%d" % i, [128, 8, 128], BF16) for i in range(2)]
        wu = [p.sb("ffn_wu%d" % i, [128, 8, 128], BF16) for i in range(2)]
        gpad = [p.sb("ffn_gpad%d" % i, [128, 18, 66], BF16) for i in range(2)]
        cpad = p.sb("ffn_cpad", [128, 258], BF16)
        gg = [p.sb("ffn_gg%d" % i, [128, 512], BF16) for i in range(2)]
        diag = [p.sb("ffn_diag%d" % i, [128, 9, 128], BF16) for i in range(2)]
        G = [p.sb("ffn_G%d" % i, [128, D], F32) for i in range(2)]
        gtmp = p.sb("ffn_gtmp", [128, 128], F32)
        st = self.res_state(2)
        for g in gpad:
            p.memset(g, 0.0, eng='pool')
        p.memset(cpad, 0.0, eng='pool')
        self.gate_row(G[0], l, 5, b, gtmp)
        if do_ctx:
            self.gate_row(G[1], l, 5, 4, gtmp)
        cw = self.col('cw%d' % l)
        cb = self.col('cb%d' % l)
        nn = 0
        for seg in range(2):
            r0 = 16 * seg
            g0 = 0 if seg == 0 else 15
            prow0 = 1 if seg == 0 else 0
            gp = gpad[seg]
            with_ctx = (seg == 0 and do_ctx)
            for j in range(NJ):
                wgj, wuj = wg[j % 2], wu[j % 2]
                p.dma(wgj, self.w_up[l, :, j * 128:(j + 1) * 128].rr("(c p) n -> p c n", p=128), eng='pool')
                p.dma(wuj, self.w_up[l, :, DFF + j * 128:DFF + (j + 1) * 128].rr("(c p) n -> p c n", p=128), eng='pool')
                if seg == 0 and j >= 2:
                    for jj in range(2 * (j - 2), min(NJ, 2 * (j - 2) + 2)):
                        p.dma(wdown[:, jj, :], self.w_down[l, jj * 128:(jj + 1) * 128, :], eng='pool')
                dg = diag[j % 2]
                p.tt(dg, self.identb.rr("p (o t) -> p o t", o=1).bcast([128, 9, 128]),
                     cw[:, j * 9:(j + 1) * 9].rr("p (t o) -> p t o", o=1).bcast([128, 9, 128]), ALU.mult, eng='pool')
                tok0 = 256 + g0 * 64
                for (o, n) in ((0, 512), (512, 512), (1024, 64)):
                    ps = self.bank()
                    for kc in range(8):
                        p.mm(ps[:, 0:n], wgj[:, kc, :], hT[:, kc, tok0 + o:tok0 + o + n], start=(kc == 0), stop=(kc == 7))
                    pr = prow0 + o // 64
                    p.copy(gp[:, pr:pr + n // 64, 1:65], ps[:, 0:n].rr("p (r c) -> p r c", c=64), eng='act')
                for blk in range(2):
                    ps = self.bank()
                    for t in range(9):
                        dy, dx = t // 3, t % 3
                        p.mm(ps, dg[:, t, :], gp[:, 8 * blk + dy:8 * blk + dy + 8, dx:dx + 64], start=(t == 0), stop=(t == 8))
                    g_ = gg[nn % 2]
                    nn += 1
                    p.act(g_, ps, AF.Gelu_apprx_tanh, bias=cb[:, j:j + 1])
                    ps2 = self.bank()
                    t0 = 256 + r0 * 64 + blk * 512
                    for kc in range(8):
                        p.mm(ps2, wuj[:, kc, :], hT[:, kc, t0:t0 + 512], start=(kc == 0), stop=(kc == 7))
                    p.tt(actT[:, j, 256 + blk * 512:256 + (blk + 1) * 512], ps2, g_, ALU.mult)
                if with_ctx:
                    ps = self.bank()
                    for kc in range(8):
                        p.mm(ps[:, 0:256], wgj[:, kc, :], hT[:, kc, 0:256], start=(kc == 0), stop=(kc == 7))
                    p.copy(cpad[:, 1:257], ps[:, 0:256], eng='act')
                    ps = self.bank()
                    for dx in range(3):
                        p.mm(ps[:, 0:256], dg[:, 3 + dx, :], cpad[:, dx:dx + 256], start=(dx == 0), stop=(dx == 2))
                    g_ = gg[nn % 2]
                    nn += 1
                    p.act(g_[:, 0:256], ps[:, 0:256], AF.Gelu_apprx_tanh, bias=cb[:, j:j + 1])
                    ps2 = self.bank()
                    for kc in range(8):
                        p.mm(ps2[:, 0:256], wuj[:, kc, :], hT[:, kc, 0:256], start=(kc == 0), stop=(kc == 7))
                    p.tt(actT[:, j, 0:256], ps2[:, 0:256], g_[:, 0:256], ALU.mult)
            tiles = ([0, 1] if with_ctx else []) + [2 + 8 * seg + i for i in range(8)]
            for n, tt in enumerate(tiles):
                a0 = tt * 128 if tt < 2 else 256 + (tt - 2 - 8 * seg) * 128
                yb = [self.bank(), self.bank()]
                for h in range(2):
                    for j in range(NJ):
                        p.mm(yb[h], actT[:, j, a0:a0 + 128], wdown[:, j, h * 512:(h + 1) * 512], start=(j == 0), stop=(j == NJ - 1))
                self.residual_tile(b, tt, xi, xo, yb, G[1] if tt < 2 else G[0], st[n % 2])
        p.release(m)


LNS_ML = float(np.log(128.0 ** -0.5))
LNS_GLA = float(np.log(64.0 ** -0.5))
ORD = [list(range(NT)), [1, 0] + list(range(17, 1, -1))]
C_MQ, C_MK, C_MV, C_MO, C_MG, C_GQ, C_GK, C_GV, C_GG, C_GLR = 0, 512, 1024, 1536, 2048, 2064, 2320, 2576, 3088, 3600


def host_small_even(pk, inp):
    pk.add('ecw', fm(inp['ev_conv_w'][0]))
    pk.add('ecb', fm(inp['ev_conv_b'][0]))
    pk.add('glab', fm(inp['ev_gla_b'][0]))
    pk.add('hg', fm(inp['ev_head_g'][0]))


class KE(K):
    def setup_even(self):
        p = self.p
        self.w_in = p.dram("ev_w_in", [D, 3632], F32, kind="ExternalInput")
        self.w_out = p.dram("ev_w_out", [D, D], F32, kind="ExternalInput")
        self.gla_w2 = p.dram("ev_gla_w2", [2, 16, 256], F32, kind="ExternalInput")
        self.ones = p.sb("ones", [128, 128], F32)
        p.memset(self.ones, 1.0, eng='pool')
        self.tri = []
        for z in range(2):
            t = p.sb("tri%d" % z, [128, 128], F32)
            pat, cm = ([[1, 128]], -1) if z == 0 else ([[-1, 128]], 1)
            p.aselect(t, self.ones, pat, ALU.is_ge, 0.0, 0, cm)
            self.tri.append(t)
        self.common_mark = p.mark()
        self.mask4 = []
        self.maskb = []
        for z in range(2):
            t = self.tri[z]
            m4 = p.sb("mask4%d" % z, [128, 4, 128], F32)
            p.ts(m4, t.rr("p (o t) -> p o t", o=1).bcast([128, 4, 128]), -1.0, ALU.add, 30000.0, ALU.mult, eng='pool')
            self.mask4.append(m4)
            mb = p.sb("maskb%d" % z, [128, 128], BF16)
            p.copy(mb, t, eng='pool')
            self.maskb.append(mb)
        self.bgrow = p.sb("bgrow", [128, 16], F32)
        p.dma(self.bgrow, self.rows_d[8:9, 0:16].pbcast(128))
        self.base_mark = p.mark()

    def headnorm_gate(self, y, ybufs, hd, gate_col, gate_fn, hT, mixT, scratch):
        p = self.p
        sq, ss, yn, sg, wgt = scratch
        p.dma(wgt, self.w_in[:, gate_col:gate_col + 128].rr("(c p) n -> p c n", p=128), eng='pool')
        for blk in range(5):
            t0, n = blk * 512, (512 if blk < 4 else 256)
            ps = self.bank()
            for kc in range(8):
                p.mm(ps[:, 0:n], wgt[:, kc, :], hT[:, kc, t0:t0 + n], start=(kc == 0), stop=(kc == 7))
            p.act(sg[:, t0:t0 + n], ps[:, 0:n], gate_fn)
        yall = V(y.ap, Buf('yall'))
        for tt in range(NT):
            p.tt(sq[:, tt, :], y[:, tt, :].on(ybufs[tt]), y[:, tt, :].on(ybufs[tt]), ALU.mult)
        p.reduce(ss, sq, ALU.add)
        p.act(ss, ss, AF.Sqrt, bias=EPS, scale=1.0 / 128)
        p.recip(ss, ss)
        for tt in range(NT):
            p.ts(yn[:, tt, :], y[:, tt, :].on(ybufs[tt]), ss[:, tt:tt + 1], ALU.mult)
        hg = self.col('hg')
        for g0 in (0, 8, 16):
            n = min(8, NT - g0)
            ps = self.bank()
            psb = ps.bc(BF16)
            for i in range(n):
                p.tr(psb[:, i * 128:(i + 1) * 128], yn[:, g0 + i, :], self.identb)
            p.stt(mixT[:, hd, g0 * 128:(g0 + n) * 128], psb[:, 0:n * 128], hg[:, hd:hd + 1], sg[:, g0 * 128:(g0 + n) * 128],
                  ALU.mult, ALU.mult)

    def even_mixer(self, b, xi, xo):
        p = self.p
        l = 0
        m = p.mark()
        hT = p.sb("ev_hT", [128, 8, LTOK], BF16)
        self.prenorm(b, xi, l, 0, hT)
        mixT = p.sb("ev_mixT", [128, 8, LTOK], BF16)
        wgate = p.sb("ev_wgate", [128, 8, 16], BF16)
        p.dma(wgate, self.w_in[:, C_MG:C_MG + 16].rr("(c p) n -> p c n", p=128), eng='pool')
        graw = p.sb("ev_graw", [128, NT, 16], F32)
        ps = self.bank()
        for tt in range(NT):
            for kc in range(8):
                p.mm(ps[:, tt * 16:(tt + 1) * 16], hT[:, kc, tt * 128:(tt + 1) * 128], wgate[:, kc, :], start=(kc == 0), stop=(kc == 7))
        p.tt(graw, ps[:, 0:NT * 16].rr("p (t n) -> p t n", n=16), self.bgrow.rr("p (o n) -> p o n", o=1).bcast([128, NT, 16]), ALU.add)
        g5 = graw.rr("p t (z y h) -> p t z y h", z=2, y=2)
        I8 = g5[:, :, :, 0, :]
        F8 = g5[:, :, :, 1, :]
        def s8(name):
            return p.sb(name, [128, NT, 2, 4], F32)
        lf8, Fc8, Ft8, bs8, qs8, wk8, dc8 = [s8("ev_" + n) for n in ("lf8", "Fc8", "Ft8", "bs8", "qs8", "wk8", "dc8")]
        p.act(lf8, F8, AF.Exp, scale=-1.0)
        p.act(lf8, lf8, AF.Ln, bias=1.0)
        p.ts(lf8, lf8, -1.0, ALU.mult)
        ps = self.bank()
        for z in range(2):
            p.mm(ps[:, z * 72:(z + 1) * 72], self.tri[z], lf8[:, :, z, :])
        p.mm(ps[:, 144:288], self.ones, lf8)
        for z in range(2):
            p.copy(Fc8[:, :, z, :], ps[:, z * 72:(z + 1) * 72].rr("p (t h) -> p t h", h=4))
        p.copy(Ft8, ps[:, 144:288].rr("p (t z h) -> p t z h", z=2, h=4))
        p.tt(bs8, I8, Fc8, ALU.subtract)
        p.tt(wk8, Ft8, bs8, ALU.add)
        p.act(wk8, wk8, AF.Exp)
        p.ts(bs8, bs8, LNS_ML, ALU.add)
        p.act(qs8, Fc8, AF.Exp, bias=LNS_ML)
        p.act(dc8, Ft8, AF.Exp)
        Fc5 = Fc8.rr("p t z (h o) -> p t z h o", o=1)

        cw = self.col('ecw').rr("p (t c) -> p t c", t=3)
        cbias = self.col('ecb')
        for hp in range(2):
            m2 = p.mark()
            heads = [2 * hp, 2 * hp + 1]
            qT = [p.sb("ml_qT%d" % i, [128, LTOK], BF16) for i in range(2)]
            kT = [p.sb("ml_kT%d" % i, [128, LTOK], BF16) for i in range(2)]
            vv = [p.sb("ml_v%d" % i, [128, NT, 130], BF16) for i in range(2)]
            yy = [p.sb("ml_y%d" % i, [128, NT, 128], F32) for i in range(2)]
            ybufs = [[Buf("mly%d_%d" % (i, t)) for t in range(NT)] for i in range(2)]
            m3 = p.mark()
            xpad = p.sb("ml_xpad", [128, 2308], F32)
            t1 = p.sb("ml_t1", [128, 2306], F32)
            t2 = p.sb("ml_t2", [128, 2306], F32)
            wq = [p.sb("ml_wq%d" % i, [128, 8, 128], BF16) for i in range(2)]
            p.memset(xpad, 0.0, eng='pool')
            nw = 0
            for i, h in enumerate(heads):
                for dst, c0, cc in ((qT[i], C_MQ + h * 128, h), (kT[i], C_MK + h * 128, 4 + h)):
                    w = wq[nw % 2]
                    nw += 1
                    p.dma(w, self.w_in[:, c0:c0 + 128].rr("(c p) n -> p c n", p=128), eng='pool')
                    for blk in range(5):
                        t0, n = (0, 256) if blk == 0 else (256 + (blk - 1) * 512, 512)
                        pos = 1 if blk == 0 else 259 + (blk - 1) * 512
                        ps = self.bank()
                        for kc in range(8):
                            p.mm(ps[:, 0:n], w[:, kc, :], hT[:, kc, t0:t0 + n], start=(kc == 0), stop=(kc == 7))
                        p.copy(xpad[:, pos:pos + n], ps[:, 0:n], eng='act')
                    p.ts(t1, xpad[:, 0:2306], cw[:, 0, cc:cc + 1], ALU.mult, cbias[:, cc:cc + 1], ALU.add)
                    p.stt(t2, xpad[:, 1:2307], cw[:, 1, cc:cc + 1], t1, ALU.mult, ALU.add)
                    p.stt(t1, xpad[:, 2:2308], cw[:, 2, cc:cc + 1], t2, ALU.mult, ALU.add)
                    p.act(dst[:, 0:256], t1[:, 0:256], AF.Silu)
                    p.act(dst[:, 256:LTOK], t1[:, 258:2306], AF.Silu)
                w = wq[nw % 2]
                nw += 1
                p.dma(w, self.w_in[:, C_MV + h * 128:C_MV + (h + 1) * 128].rr("(c p) n -> p c n", p=128), eng='pool')
                p.memset(vv[i][:, :, 128:130], 1.0, eng='pool')
                for g0 in range(0, NT, 4):
                    n = min(4, NT - g0)
                    ps = self.bank()
                    for j in range(n):
                        tt = g0 + j
                        for kc in range(8):
                            p.mm(ps[:, j * 128:(j + 1) * 128], hT[:, kc, tt * 128:(tt + 1) * 128], w[:, kc, :], start=(kc == 0), stop=(kc == 7))
                    p.copy(vv[i][:, g0:g0 + n, 0:128], ps[:, 0:n * 128].rr("p (t n) -> p t n", n=128), eng='act')
                p.memset(yy[i], 0.0, eng='pool')
            p.release(m3)
            Cs = [[p.sb("ml_C%d%d" % (z, i), [128, 132], F32) for i in range(2)] for z in range(2)]
            Cb = [[p.sb("ml_Cb%d%d" % (z, i), [128, 132], BF16) for i in range(2)] for z in range(2)]
            for z in range(2):
                for i in range(2):
                    p.memset(Cs[z][i], 0.0, eng='pool')
                    p.memset(Cb[z][i], 0.0, eng='pool')
            dg1 = [[p.sb("ml_dg%d_%d" % (c, k2), [128, 128], F32) for k2 in range(2)] for c in range(4)]
            Dm = [[p.sb("ml_Dm%d_%d" % (c, k2), [128, 128], BF16) for k2 in range(2)] for c in range(4)]
            sT = [[p.sb("ml_sT%d_%d" % (c, k2), [128, 128], BF16) for k2 in range(2)] for c in range(4)]
            ktil = [[p.sb("ml_ktil%d_%d" % (c, k2), [128, 128], BF16) for k2 in range(2)] for c in range(4)]
            tmpo = [p.sb("ml_tmpo%d" % i, [128, 132], F32) for i in range(4)]
            num = [p.sb("ml_num%d" % i, [128, 132], F32) for i in range(4)]
            den = [p.sb("ml_den%d" % i, [128, 1], F32) for i in range(4)]

            def ml_chain(z, i, h, c):
                b0, b1 = 2 * c, 2 * c + 1
                for step in range(NT):
                    tt = ORD[z][step]
                    ts_ = slice(tt * 128, (tt + 1) * 128)
                    k2 = step % 2
                    upd = step < NT - 1
                    dg = dg1[c][k2]
                    p.ts(dg, self.identf, Fc8[:, tt, z, h:h + 1], ALU.mult)
                    rb, hrb = p.palloc(1, b0)
                    p.mm(rb[:, 0:128], self.ones, dg, start=True, stop=False)
                    p.mm(rb[:, 0:128], self.identf, self.mask4[z][:, 0, :], start=False, stop=True)
                    sc, hsc = p.palloc(1, b0)
                    p.mm(sc[:, 0:128], kT[i][:, ts_], qT[i][:, ts_])
                    if upd:
                        kp, hkp = p.palloc(1, b0)
                        kpb = kp.bc(BF16)
                        p.tr(kpb[:, 0:128], kT[i][:, ts_], self.identb)
                    yield
                    p.act(Dm[c][k2], rb[:, 0:128], AF.Exp, bias=bs8[:, tt, z, h:h + 1])
                    p.pfree(hrb)
                    if upd:
                        p.act(ktil[c][k2], kpb[:, 0:128], AF.Identity, scale=wk8[:, tt, z, h:h + 1])
                        p.pfree(hkp)
                    yield
                    p.tt(sT[c][k2], sc[:, 0:128], Dm[c][k2], ALU.mult)
                    p.pfree(hsc)
                    yield
                    o1, ho1 = p.palloc(2, b1)
                    p.mm(o1[:, 0:129], sT[c][k2], vv[i][:, tt, 0:129])
                    o2, ho2 = p.palloc(2, b1)
                    p.mm(o2[:, 0:129], qT[i][:, ts_], Cb[z][i][:, 0:129])
                    if upd:
                        cu, hcu = p.palloc(2, b0)
                        p.mm(cu[:, 0:129], ktil[c][k2], vv[i][:, tt, 0:129])
                    yield
                    p.act(tmpo[c][:, 0:129], o2[:, 0:129], AF.Identity, scale=qs8[:, tt, z, h:h + 1])
                    p.pfree(ho2)
                    if upd:
                        p.stt(Cs[z][i][:, 0:129], Cs[z][i][:, 0:129], dc8[:, tt, z, h:h + 1], cu[:, 0:129], ALU.mult, ALU.add)
                        p.pfree(hcu)
                    yield
                    p.tt(num[c][:, 0:129], tmpo[c][:, 0:129], o1[:, 0:129], ALU.add)
                    p.pfree(ho1)
                    if upd:
                        p.copy(Cb[z][i][:, 0:129], Cs[z][i][:, 0:129], eng='pool')
                    yield
                    p.act(den[c], num[c][:, 128:129], AF.Abs)
                    yield
                    p.ts(den[c], den[c], 1.0, ALU.max)
                    p.recip(den[c], den[c])
                    yield
                    yv = yy[i][:, tt, :].on(ybufs[i][tt])
                    p.stt(yv, num[c][:, 0:128], den[c], yv, ALU.mult, ALU.add)
                    yield

            self.run_rr([ml_chain(z, i, heads[i], z * 2 + i) for z in range(2) for i in range(2)])
            m4_ = p.mark()
            scratch = (p.sb("hn_sq", [128, NT, 128], F32), p.sb("hn_ss", [128, NT], F32), p.sb("hn_yn", [128, NT, 128], BF16),
                       p.sb("hn_sg", [128, LTOK], BF16), p.sb("hn_wgt", [128, 8, 128], BF16))
            for i, h in enumerate(heads):
                self.headnorm_gate(yy[i], ybufs[i], h, C_MO + h * 128, AF.Sigmoid, hT, mixT, scratch)
            p.release(m2)
        self.ev_hT, self.ev_mixT, self.ev_mark = hT, mixT, m
        return hT, mixT, m

    def even_out(self, b, xi, xo, hT, mixT, m):
        p = self.p
        l = 0
        wout = p.sb("ev_wout", [128, 8, D], BF16)
        for c in range(8):
            p.dma(wout[:, c, :], self.w_out[c * 128:(c + 1) * 128, :], eng='pool')
        G = [p.sb("evo_G%d" % i, [128, D], F32) for i in range(2)]
        gtmp = p.sb("evo_gtmp", [128, 128], F32)
        st = self.res_state(2)
        self.gate_row(G[0], l, 2, b, gtmp)
        self.gate_row(G[1], l, 2, 4, gtmp)
        for tt in range(NT):
            yb = [self.bank(), self.bank()]
            for h in range(2):
                for c in range(8):
                    p.mm(yb[h], mixT[:, c, tt * 128:(tt + 1) * 128], wout[:, c, h * 512:(h + 1) * 512], start=(c == 0), stop=(c == 7))
            self.residual_tile(b, tt, xi, xo, yb, G[1] if tt < 2 else G[0], st[tt % 2])
        p.release(m)


def _gla(self, b, hT, mixT):
    p = self.p
    m = p.mark()
    rmask = p.sb("gl_rmask", [128, LTOK], BF16)
    p.memset(rmask, 1.0, eng='pool')
    p.memset(rmask.rr("p (n t) -> p n t", t=128)[:, :, 0:1], 0.0, eng='pool')
    w2b = p.sb("gl_w2b", [16, 2, 256], BF16)
    p.dma(w2b, self.gla_w2.rr("z r c -> r z c"), eng='pool')
    wlr = p.sb("gl_wlr", [128, 8, 32], BF16)
    p.dma(wlr, self.w_in[:, C_GLR:C_GLR + 32].rr("(c p) n -> p c n", p=128), eng='pool')
    glrT = p.sb("gl_glrT", [16, 2, LTOK], BF16)
    for z in range(2):
        for blk in range(5):
            t0, n = blk * 512, (512 if blk < 4 else 256)
            ps = self.bank()
            for kc in range(8):
                p.mm(ps[0:16, 0:n], wlr[:, kc, z * 16:(z + 1) * 16], hT[:, kc, t0:t0 + n], start=(kc == 0), stop=(kc == 7))
            p.copy(glrT[:, z, t0:t0 + n], ps[0:16, 0:n], eng='act')
    nb = p.sb("gl_nb", [128, 4], F32)
    p.ts(nb, self.col('glab'), -1.0, ALU.mult)
    for cp in range(2):
        m2 = p.mark()
        qtil = [p.sb("gl_qtil%d" % z, [128, LTOK], BF16) for z in range(2)]
        khat = [p.sb("gl_khat%d" % z, [128, LTOK], BF16) for z in range(2)]
        ktl = [p.sb("gl_ktl%d" % z, [128, LTOK], BF16) for z in range(2)]
        dec = [p.sb("gl_dec%d" % z, [128, NT], F32) for z in range(2)]
        vv = [p.sb("gl_v%d" % i, [128, NT, 128], BF16) for i in range(2)]
        yy = [p.sb("gl_y%d" % i, [128, NT, 128], F32) for i in range(2)]
        ybufs = [[Buf("gly%d_%d" % (i, t)) for t in range(NT)] for i in range(2)]
        m3 = p.mark()
        qT = p.sb("gl_qT", [128, LTOK], BF16)
        kT = p.sb("gl_kT", [128, LTOK], BF16)
        lb = p.sb("gl_l", [128, LTOK], F32)
        P = p.sb("gl_P", [128, LTOK], F32)
        tmp = p.sb("gl_tmp", [128, LTOK], F32)
        E = p.sb("gl_E", [128, LTOK], BF16)
        w = [p.sb("gl_w%d" % i, [128, 8, 128], BF16) for i in range(2)]
        for dst, c0, wi in ((qT, C_GQ + cp * 128, 0), (kT, C_GK + cp * 128, 1)):
            p.dma(w[wi], self.w_in[:, c0:c0 + 128].rr("(c p) n -> p c n", p=128), eng='pool')
            for blk in range(5):
                t0, n = blk * 512, (512 if blk < 4 else 256)
                ps = self.bank()
                for kc in range(8):
                    p.mm(ps[:, 0:n], w[wi][:, kc, :], hT[:, kc, t0:t0 + n], start=(kc == 0), stop=(kc == 7))
                p.copy(dst[:, t0:t0 + n], ps[:, 0:n], eng='act')
        P3 = P.rr("p (n t) -> p n t", t=128)
        Ptot = P3[:, :, 127:128]
        for z in range(2):
            for blk in range(5):
                t0, n = blk * 512, (512 if blk < 4 else 256)
                ps = self.bank()
                p.mm(ps[:, 0:n], w2b[:, z, cp * 128:(cp + 1) * 128], glrT[:, z, t0:t0 + n])
                p.act(lb[:, t0:t0 + n], ps[:, 0:n], AF.Exp, scale=-1.0, bias=nb[:, z * 2 + cp:z * 2 + cp + 1])
            p.act(lb, lb, AF.Ln, bias=1.0)
            p.scan(P, rmask, lb, 0.0, ALU.mult, ALU.add)
            p.act(dec[z], Ptot.rr("p n o -> p (n o)"), AF.Exp, scale=-1.0 / 16)
            t3 = tmp.rr("p (n t) -> p n t", t=128)
            if z == 0:
                p.act(E, P, AF.Exp, scale=-1.0 / 16, bias=LNS_GLA)
                p.tt(qtil[z], qT, E, ALU.mult)
                p.act(E, P, AF.Exp, scale=1.0 / 16)
                p.tt(khat[z], kT, E, ALU.mult, eng='pool')
                p.tt(t3, P3, Ptot.bcast([128, NT, 128]), ALU.subtract)
                p.act(E, tmp, AF.Exp, scale=1.0 / 16)
                p.tt(ktl[z], kT, E, ALU.mult)
            else:
                p.tt(t3, Ptot.bcast([128, NT, 128]), P3, ALU.subtract)
                p.tt(tmp, tmp, lb, ALU.add, eng='pool')
                p.act(E, tmp, AF.Exp, scale=-1.0 / 16, bias=LNS_GLA)
                p.tt(qtil[z], qT, E, ALU.mult)
                p.act(E, tmp, AF.Exp, scale=1.0 / 16)
                p.tt(khat[z], kT, E, ALU.mult, eng='pool')
                p.tt(tmp, lb, P, ALU.subtract)
                p.act(E, tmp, AF.Exp, scale=1.0 / 16)
                p.tt(ktl[z], kT, E, ALU.mult)
        for i in range(2):
            h = cp * 2 + i
            p.dma(w[i], self.w_in[:, C_GV + h * 128:C_GV + (h + 1) * 128].rr("(c p) n -> p c n", p=128), eng='pool')
            for g0 in range(0, NT, 4):
                n = min(4, NT - g0)
                ps = self.bank()
                for j in range(n):
                    tt = g0 + j
                    for kc in range(8):
                        p.mm(ps[:, j * 128:(j + 1) * 128], hT[:, kc, tt * 128:(tt + 1) * 128], w[i][:, kc, :], start=(kc == 0), stop=(kc == 7))
                p.copy(vv[i][:, g0:g0 + n, :], ps[:, 0:n * 128].rr("p (t n) -> p t n", n=128), eng='act')
            p.memset(yy[i], 0.0, eng='pool')
        p.release(m3)
        S = [[p.sb("gl_S%d%d" % (z, i), [128, 128], F32) for i in range(2)] for z in range(2)]
        Sb = [[p.sb("gl_Sb%d%d" % (z, i), [128, 128], BF16) for i in range(2)] for z in range(2)]
        for z in range(2):
            for i in range(2):
                p.memset(S[z][i], 0.0, eng='pool')
                p.memset(Sb[z][i], 0.0, eng='pool')
        AT = [[p.sb("gl_AT%d_%d" % (c, k2), [128, 128], BF16) for k2 in range(2)] for c in range(4)]
        ktok = [[p.sb("gl_ktok%d_%d" % (c, k2), [128, 64], BF16) for k2 in range(2)] for c in range(4)]

        def gl_chain(z, i, c):
            b0, b1 = 2 * c, 2 * c + 1
            pr = slice(i * 64, (i + 1) * 64)
            for step in range(NT):
                tt = ORD[z][step]
                ts_ = slice(tt * 128, (tt + 1) * 128)
                k2 = step % 2
                upd = step < NT - 1
                sc, hsc = p.palloc(1, b0)
                p.mm(sc[:, 0:128], khat[z][pr, ts_], qtil[z][pr, ts_])
                if upd:
                    kp, hkp = p.palloc(1, b0)
                    kpb = kp.bc(BF16)
                    p.tr(kpb[:, 0:64], ktl[z][pr, ts_], self.identb[pr, pr])
                yield
                p.tt(AT[c][k2], sc[:, 0:128], self.maskb[z], ALU.mult)
                p.pfree(hsc)
                if upd:
                    p.copy(ktok[c][k2], kpb[:, 0:64], eng='act')
                    p.pfree(hkp)
                yield
                o, ho = p.palloc(1, b1)
                p.mm(o[:, 0:128], AT[c][k2], vv[i][:, tt, :], start=True, stop=False)
                p.mm(o[:, 0:128], qtil[z][pr, ts_], Sb[z][i][pr, :], start=False, stop=True)
                if upd:
                    su, hsu = p.palloc(1, b0)
                    p.mm(su[pr, 0:128], ktok[c][k2], vv[i][:, tt, :])
                yield
                yv = yy[i][:, tt, :].on(ybufs[i][tt])
                p.tt(yv, yv, o[:, 0:128], ALU.add)
                p.pfree(ho)
                if upd:
                    p.stt(S[z][i][pr, :], S[z][i][pr, :], dec[z][pr, tt:tt + 1], su[pr, 0:128], ALU.mult, ALU.add)
                    p.pfree(hsu)
                yield
                if upd:
                    p.copy(Sb[z][i][pr, :], S[z][i][pr, :], eng='pool')
                yield

        self.run_rr([gl_chain(z, i, z * 2 + i) for z in range(2) for i in range(2)])
        scratch = (p.sb("hn_sq", [128, NT, 128], F32), p.sb("hn_ss", [128, NT], F32), p.sb("hn_yn", [128, NT, 128], BF16),
                   p.sb("hn_sg", [128, LTOK], BF16), p.sb("hn_wgt", [128, 8, 128], BF16))
        for i in range(2):
            h = cp * 2 + i
            self.headnorm_gate(yy[i], ybufs[i], 4 + h, C_GG + h * 128, AF.Silu, hT, mixT, scratch)
        p.release(m2)
    p.release(m)


KE.gla = _gla


C0 = float(np.exp(-0.5))
RW_LN_EPS = 64e-5


def host_small_rw(pk, inp):
    pk.add('mu', fm(inp['rw_mu'][0]))
    pk.add('w0', fm(inp['rw_w0'][0]))
    pk.add('a0', fm(inp['rw_a0'][0]))
    pk.add('kv', fm(inp['rw_kvec'][0]))
    pk.add('lnx', fm(inp['rw_lnx'][0]))


def _setup_rw(self):
    p = self.p
    NB = self.NB
    self.w_rkv = p.dram("rw_w_rkv", [3, D, D], F32, kind="ExternalInput")
    self.w_o = p.dram("rw_w_o", [D, D], F32, kind="ExternalInput")
    self.rw_w1 = p.dram("rw_w1", [2, D, 64], F32, kind="ExternalInput")
    self.rw_w2 = p.dram("rw_w2", [2, 64, D], F32, kind="ExternalInput")
    self.rw_a1 = p.dram("rw_a1", [D, 64], F32, kind="ExternalInput")
    self.rw_a2 = p.dram("rw_a2", [64, D], F32, kind="ExternalInput")
    self.rw_g1 = p.dram("rw_g1", [D, 128], F32, kind="ExternalInput")
    self.rw_g2 = p.dram("rw_g2", [128, D], F32, kind="ExternalInput")
    self.scr = []
    if getattr(self, 'dbg', False):
        self.dbg_y = p.dram("dbg_y", [8, 128, NT * 128], F32, kind="ExternalOutput")
    for b in range(NB):
        d = {}
        for nm in ('r', 'k', 'v', 'o'):
            d[nm] = p.dram("scr_%s%d" % (nm, b), [8, 128, LTOK], BF16, kind="ExternalOutput" if getattr(self, 'dbg', False) else "Internal")
            d[nm + 'buf'] = [Buf("scr_%s%d_%d" % (nm, b, c)) for c in range(8)]
        self.scr.append(d)


def _setup_rw_consts(self):
    p = self.p
    p.release(self.common_mark)
    self.strict = []
    self.M4 = []
    for z in range(2):
        st = p.sb("strict%d" % z, [128, 128], F32)
        pat, cm = ([[1, 128]], -1) if z == 0 else ([[-1, 128]], 1)
        p.aselect(st, self.ones, pat, ALU.is_gt, 0.0, 0, cm)
        self.strict.append(st)
    for z in range(2):
        m4 = p.sb("M4%d" % z, [128, 4, 128], F32)
        p.ts(m4[:, 0, :], self.strict[z], -1.0, ALU.mult, eng='pool')
        p.ts(m4[:, 1, :], self.tri[z], -1.0, ALU.mult, eng='pool')
        p.copy(m4[:, 2, :], self.strict[z], eng='pool')
        p.copy(m4[:, 3, :], self.tri[z], eng='pool')
        self.M4.append(m4)
    self.nstrictT = []
    for z in range(2):
        t = p.sb("nstrictT%d" % z, [128, 128], F32)
        p.ts(t, self.strict[1 - z], -1.0, ALU.mult, eng='pool')
        self.nstrictT.append(t)
    self.masks3 = p.sb("masks3", [128, 3, 128], BF16)
    bd64 = p.sb("bd64", [128, 128], BF16)
    p.memset(self.masks3[:, 0, :], 0.0, eng='pool')
    p.memset(bd64, 0.0, eng='pool')
    for i in range(4):
        p.memset(self.masks3[32 * i:32 * i + 32, 0, 32 * i:32 * i + 32], 1.0, eng='pool')
    for i in range(2):
        p.memset(bd64[64 * i:64 * i + 64, 64 * i:64 * i + 64], 1.0, eng='pool')
    p.tt(self.masks3[:, 1, :], bd64, self.masks3[:, 0, :], ALU.subtract, eng='pool')
    p.ts(self.masks3[:, 2, :], bd64, -1.0, ALU.mult, 1.0, ALU.add, eng='pool')
    self.bones = p.sb("bones", [128, 128], BF16)
    p.memset(self.bones, 0.0, eng='pool')
    p.memset(self.bones[0:64, 0:64], 1.0, eng='pool')
    p.memset(self.bones[64:128, 64:128], 1.0, eng='pool')
    self.omu = p.sb("omu", [128, 48], F32)
    p.ts(self.omu, self.col('mu'), -1.0, ALU.mult, 1.0, ALU.add)
    self.okv1 = p.sb("okv1", [128, 8], F32)
    p.ts(self.okv1, self.col('kv')[:, 8:16], -1.0, ALU.mult, 1.0, ALU.add)
    self.base_mark = p.mark()


def _rw_mix_block(self, xb, hT, mi, blk):
    p = self.p
    mu = self.col('mu')
    t0, n = (0, 256) if blk == 0 else (256 + (blk - 1) * 512, 512)
    for c in range(8):
        muc = mu[:, mi * 8 + c:mi * 8 + c + 1]
        p.act(xb[:, c, 0:n], hT[:, c, t0:t0 + n], AF.Identity, scale=self.omu[:, mi * 8 + c:mi * 8 + c + 1])
        kind = c // 2
        if blk == 0:
            if kind in (0, 2):
                p.stt(xb[:, c, 1:256], hT[:, c, 0:255], muc, xb[:, c, 1:256], ALU.mult, ALU.add)
            else:
                p.stt(xb[:, c, 0:255], hT[:, c, 1:256], muc, xb[:, c, 0:255], ALU.mult, ALU.add)
        else:
            r0 = (blk - 1) * 8
            xv = xb[:, c, :].rr("p (r w) -> p r w", w=64)
            hv = hT[:, c, 256:LTOK].rr("p (r w) -> p r w", w=64)
            if kind == 0:
                p.stt(xv[:, :, 1:64], hv[:, r0:r0 + 8, 0:63], muc, xv[:, :, 1:64], ALU.mult, ALU.add)
            elif kind == 1:
                p.stt(xv[:, :, 0:63], hv[:, r0:r0 + 8, 1:64], muc, xv[:, :, 0:63], ALU.mult, ALU.add)
            elif kind == 2:
                lo = 1 if r0 == 0 else 0
                p.stt(xv[:, lo:8, :], hv[:, r0 + lo - 1:r0 + 7, :], muc, xv[:, lo:8, :], ALU.mult, ALU.add)
            else:
                hi = 7 if r0 == 24 else 8
                p.stt(xv[:, 0:hi, :], hv[:, r0 + 1:r0 + hi + 1, :], muc, xv[:, 0:hi, :], ALU.mult, ALU.add)
    return t0, n


def _rw_phase1(self, b, hT, tw, ta, tg):
    p = self.p
    scr = self.scr[b]
    m = p.mark()
    xblk = [p.sb("rw_xb%d" % i, [128, 8, 512], BF16) for i in range(2)]
    Wr = [p.sb("rw_W%d" % i, [128, 8, D], BF16) for i in range(2)]
    stage = [p.sb("rw_stage%d" % i, [128, 512], BF16) for i in range(4)]
    ns = 0
    nx = 0
    for wi, (mi, kind) in enumerate(((0, 'r'), (2, 'k'), (3, 'v'))):
        W = Wr[wi % 2]
        idx = 'rkv'.index(kind)
        for c in range(8):
            p.dma(W[:, c, :], self.w_rkv[idx, c * 128:(c + 1) * 128, :], eng='pool')
        for blk in range(5):
            xb = xblk[nx % 2]
            nx += 1
            t0, n = self.rw_mix_block(xb, hT, mi, blk)
            for oc in range(8):
                ps = self.bank()
                for kc in range(8):
                    p.mm(ps[:, 0:n], W[:, kc, oc * 128:(oc + 1) * 128], xb[:, kc, 0:n], start=(kc == 0), stop=(kc == 7))
                sg = stage[ns % 4]
                ns += 1
                p.copy(sg[:, 0:n], ps[:, 0:n], eng='act')
                p.dma(scr[kind][oc, :, t0:t0 + n].on(scr[kind + 'buf'][oc]), sg[:, 0:n])
    W0 = p.sb("rw_lr0", [128, 8, 320], BF16)
    Wa = p.sb("rw_lra", [128, 8, 320], BF16)
    Wb = p.sb("rw_lrb", [128, 8, 320], BF16)
    for z in range(2):
        p.dma(W0[:, :, z * 64:(z + 1) * 64], self.rw_w1[z].rr("(c p) r -> p c r", p=128), eng='pool')
    p.dma(W0[:, :, 128:192], self.rw_a1.rr("(c p) r -> p c r", p=128), eng='pool')
    p.dma(W0[:, :, 192:320], self.rw_g1.rr("(c p) r -> p c r", p=128), eng='pool')
    mu = self.col('mu')
    for (c0, c1, mi) in ((0, 128, 1), (128, 192, 4), (192, 320, 5)):
        wdt = c1 - c0
        mub = mu[:, mi * 8:(mi + 1) * 8].rr("p (c o) -> p c o", o=1).bcast([128, 8, wdt])
        omb = self.omu[:, mi * 8:(mi + 1) * 8].rr("p (c o) -> p c o", o=1).bcast([128, 8, wdt])
        p.tt(Wa[:, :, c0:c1], W0[:, :, c0:c1], omb, ALU.mult)
        p.tt(Wb[:, :, c0:c1], W0[:, :, c0:c1], mub, ALU.mult, eng='pool')
    for blk in range(5):
        hs = xblk[nx % 2]
        nx += 1
        t0, n = (0, 256) if blk == 0 else (256 + (blk - 1) * 512, 512)
        p.memset(hs, 0.0, eng='pool')
        for c in range(8):
            kindc = c // 2
            if blk == 0:
                if kindc in (0, 2):
                    p.copy(hs[:, c, 1:256], hT[:, c, 0:255], eng='act')
                else:
                    p.copy(hs[:, c, 0:255], hT[:, c, 1:256], eng='act')
            else:
                r0 = (blk - 1) * 8
                xv = hs[:, c, :].rr("p (r w) -> p r w", w=64)
                hv = hT[:, c, 256:LTOK].rr("p (r w) -> p r w", w=64)
                if kindc == 0:
                    p.copy(xv[:, :, 1:64], hv[:, r0:r0 + 8, 0:63], eng='act')
                elif kindc == 1:
                    p.copy(xv[:, :, 0:63], hv[:, r0:r0 + 8, 1:64], eng='act')
                elif kindc == 2:
                    lo = 1 if r0 == 0 else 0
                    p.copy(xv[:, lo:8, :], hv[:, r0 + lo - 1:r0 + 7, :], eng='act')
                else:
                    hi = 7 if r0 == 24 else 8
                    p.copy(xv[:, 0:hi, :], hv[:, r0 + 1:r0 + hi + 1, :], eng='act')
        for (c0, c1, dst, fn) in ((0, 64, tw[0], AF.Tanh), (64, 128, tw[1], AF.Tanh), (128, 192, ta, None), (192, 320, tg, AF.Sigmoid)):
            M = c1 - c0
            ps = self.bank()
            for kc in range(8):
                p.mm(ps[0:M, 0:n], Wa[:, kc, c0:c1], hT[:, kc, t0:t0 + n], start=(kc == 0), stop=False)
            for kc in range(8):
                p.mm(ps[0:M, 0:n], Wb[:, kc, c0:c1], hs[:, kc, 0:n], start=False, stop=(kc == 7))
            if fn is None:
                p.copy(dst[0:M, t0:t0 + n], ps[0:M, 0:n], eng='act')
            else:
                p.act(dst[0:M, t0:t0 + n], ps[0:M, 0:n], fn)
    p.release(m)


KE.setup_rw = _setup_rw
KE.setup_rw_consts = _setup_rw_consts
KE.rw_mix_block = _rw_mix_block
KE.rw_phase1 = _rw_phase1


BLKS = [(0, 512), (512, 512), (1024, 512), (1536, 512), (2048, 256)]


def _rw_chunk(self, b, cc, tw, ta, tg, last):
    p = self.p
    scr = self.scr[b]
    m = p.mark()
    kv = self.col('kv')
    rT = p.sb("rc_rT", [128, LTOK], BF16)
    kT = p.sb("rc_kT", [128, LTOK], BF16)
    vT = p.sb("rc_vT", [128, LTOK], BF16)
    for t_, nm in ((rT, 'r'), (kT, 'k'), (vT, 'v')):
        p.dma(t_, scr[nm][cc].on(scr[nm + 'buf'][cc]))
    KR = [p.sb("rc_KR%d" % z, [128, NT, 2, 128], BF16) for z in range(2)]
    khat = [p.sb("rc_khat%d" % z, [128, LTOK], BF16) for z in range(2)]
    bhat = [p.sb("rc_bhat%d" % z, [128, LTOK], BF16) for z in range(2)]
    kT4 = [p.sb("rc_kT4%d" % z, [128, LTOK], BF16) for z in range(2)]
    nbT4 = [p.sb("rc_nbT4%d" % z, [128, LTOK], BF16) for z in range(2)]
    gam = [p.sb("rc_gam%d" % z, [128, NT], F32) for z in range(2)]
    Vtok = p.sb("rc_Vtok", [128, NT, 128], BF16)
    y = p.sb("rc_y", [128, NT, 128], F32)
    yz = [y, p.sb("rc_y1", [128, NT, 128], F32)]
    ybufs = [[Buf("rcy%d_%d" % (t, hh)) for hh in range(2)] for t in range(NT)]
    m3 = p.mark()
    aT = p.sb("rc_aT", [128, LTOK], BF16)
    kap = p.sb("rc_kap", [128, LTOK], BF16)
    bet = p.sb("rc_bet", [128, LTOK], BF16)
    sig = p.sb("rc_sig", [128, LTOK], F32)
    P = p.sb("rc_P", [128, LTOK], F32)
    t1 = p.sb("rc_t1", [128, LTOK], F32)
    t2 = p.sb("rc_t2", [128, LTOK], F32)
    E = p.sb("rc_E", [128, LTOK], BF16)
    rmask = p.sb("rc_rmask", [128, LTOK], BF16)
    p.memset(rmask, 1.0, eng='pool')
    p.memset(rmask.rr("p (n t) -> p n t", t=128)[:, :, 0:1], 0.0, eng='pool')
    w2b = p.sb("rc_w2b", [64, 2, 128], BF16)
    p.dma(w2b, self.rw_w2[:, :, cc * 128:(cc + 1) * 128].rr("z r c -> r z c"), eng='pool')
    a2b = p.sb("rc_a2b", [64, 128], BF16)
    p.dma(a2b, self.rw_a2[:, cc * 128:(cc + 1) * 128], eng='pool')
    for (t0, n) in BLKS:
        ps = self.bank()
        p.mm(ps[:, 0:n], a2b, ta[0:64, t0:t0 + n])
        p.act(aT[:, t0:t0 + n], ps[:, 0:n], AF.Sigmoid, bias=self.col('a0')[:, cc:cc + 1])
    p.ts(t1, kT, kv[:, cc:cc + 1], ALU.mult)
    p.tt(E, t1, t1, ALU.mult, eng='pool')
    for (t0, n) in BLKS:
        ps = self.bank()
        p.mm(ps[:, 0:n], self.bones, E[:, t0:t0 + n])
        p.act(t2[:, t0:t0 + n], ps[:, 0:n], AF.Sqrt)
    p.ts(t2, t2, 1e-12, ALU.max)
    p.recip(t2, t2)
    p.tt(kap, t1, t2, ALU.mult)
    p.tt(bet, kap, aT, ALU.mult, eng='pool')
    p.ts(t1, aT, kv[:, 8 + cc:8 + cc + 1], ALU.mult, self.okv1[:, cc:cc + 1], ALU.add)
    p.tt(kT, kT, t1, ALU.mult)
    P3 = P.rr("p (n t) -> p n t", t=128)
    Ptot = P3[:, :, 127:128]
    Pb = Ptot.bcast([128, NT, 128])
    t13 = t1.rr("p (n t) -> p n t", t=128)
    t23 = t2.rr("p (n t) -> p n t", t=128)
    v3 = lambda x: x.rr("p (n t) -> p n t", t=128)
    for z in range(2):
        for (t0, n) in BLKS:
            ps = self.bank()
            p.mm(ps[:, 0:n], w2b[:, z, :], tw[z][0:64, t0:t0 + n])
            p.act(sig[:, t0:t0 + n], ps[:, 0:n], AF.Sigmoid, bias=self.col('w0')[:, z * 8 + cc:z * 8 + cc + 1])
        p.scan(P, rmask, sig, 0.0, ALU.mult, ALU.add)
        p.act(gam[z], Ptot.rr("p n o -> p (n o)"), AF.Exp, scale=-C0)
        if z == 0:
            G = P
            p.tt(t1, P, sig, ALU.subtract)
            p.tt(t23, P3, Pb, ALU.subtract)
        else:
            p.tt(t13, Pb, P3, ALU.subtract)
            p.tt(t2, sig, P, ALU.subtract)
            G = sig
            p.tt(sig, t1, sig, ALU.add, eng='pool')
        p.act(E, G, AF.Exp, scale=-C0)
        p.tt(KR[z][:, :, 1, :], v3(rT), v3(E), ALU.mult)
        p.act(E, t1, AF.Exp, scale=-C0)
        p.tt(KR[z][:, :, 0, :], v3(kap), v3(E), ALU.mult, eng='pool')
        p.act(E, G, AF.Exp, scale=C0)
        p.tt(khat[z], kT, E, ALU.mult)
        p.tt(bhat[z], bet, E, ALU.mult, eng='pool')
        p.act(E, t2, AF.Exp, scale=C0)
        p.tt(kT4[z], kT, E, ALU.mult)
        p.stt(nbT4[z], bet, -1.0, E, ALU.mult, ALU.mult)
    for g0 in range(0, NT, 8):
        n = min(8, NT - g0)
        ps = self.bank()
        psb = ps.bc(BF16)
        for i in range(n):
            p.tr(psb[:, i * 128:(i + 1) * 128], vT[:, (g0 + i) * 128:(g0 + i + 1) * 128], self.identb)
        p.copy(Vtok[:, g0:g0 + n, :], psb[:, 0:n * 128].rr("p (t c) -> p t c", c=128), eng='act')
    p.release(m3)
    mring = p.mark()
    NR = 12
    A4 = [p.sb("rs_A4%d" % i, [128, 4, 128], BF16) for i in range(NR)]
    SQ = [[p.sb("rs_SQ%d_%d" % (i, j), [128, 2, 128], BF16) for j in range(2)] for i in range(NR)]
    XXr = [p.sb("rs_XX%d" % i, [128, 2, 128], BF16) for i in range(NR)]
    Q0T = [p.sb("rs_Q0T%d" % i, [128, 128], BF16) for i in range(NR)]
    QM = [p.sb("rs_QM%d" % i, [128, 3, 128], BF16) for i in range(NR)]
    QMT = [p.sb("rs_QMT%d" % i, [128, 3, 128], BF16) for i in range(NR)]
    Y1r = [p.sb("rs_Y1%d" % i, [128, 128], BF16) for i in range(NR)]
    KB = [p.sb("rs_KB%d" % i, [128, 2, 128], BF16) for i in range(6)]
    Wb = [p.sb("rs_Wb%d" % i, [128, 64], BF16) for i in range(NR)]
    Ub = [p.sb("rs_Ub%d" % i, [128, 64], BF16) for i in range(NR)]
    H = [[p.sb("rs_H%d%d" % (z, hh), [128, 64], F32) for hh in range(2)] for z in range(2)]
    Hb = [[p.sb("rs_Hb%d%d" % (z, hh), [128, 64], BF16) for hh in range(2)] for z in range(2)]
    for z in range(2):
        for hh in range(2):
            p.memset(H[z][hh], 0.0, eng='pool')
            p.memset(Hb[z][hh], 0.0, eng='pool')
    units = [(step, z, hh) for step in range(NT) for z in range(2) for hh in range(2)]
    prep = {}

    def stage_a(u, bk, r):
        step, z, hh = units[u]
        tt = ORD[z][step]
        ts_ = slice(tt * 128, (tt + 1) * 128)
        pr = slice(hh * 64, (hh + 1) * 64)
        kb = KB[(u // 2) % 6]
        if hh == 0:
            kps, hk = p.palloc(1, bk)
            psb = kps.bc(BF16)
            p.tr(psb[:, 0:128], kT4[z][:, ts_], self.identb)
            p.tr(psb[:, 128:256], nbT4[z][:, ts_], self.identb)
        kr = KR[z][pr, tt].rr("p j t -> p (j t)")
        sc1, hsc1 = p.palloc(2, bk)
        p.mm(sc1[:, 0:256], bhat[z][pr, ts_], kr)
        q0, hq0 = p.palloc(1, bk)
        p.mm(q0[:, 0:128], KR[z][pr, tt, 0, :], bhat[z][pr, ts_])
        yield
        if hh == 0:
            p.copy(kb, psb[:, 0:256].rr("p (j c) -> p j c", j=2), eng='act')
            p.pfree(hk)
        p.tt(A4[r][:, 0:2, :], sc1.rr("p (j t) -> p j t", j=2), self.M4[z][:, 0:2, :], ALU.mult)
        p.pfree(hsc1)
        p.tt(Q0T[r], q0[:, 0:128], self.nstrictT[z], ALU.mult)
        p.pfree(hq0)
        sc2, hsc2 = p.palloc(2, bk)
        p.mm(sc2[:, 0:256], khat[z][pr, ts_], kr)
        yield
        p.tt(A4[r][:, 2:4, :], sc2.rr("p (j t) -> p j t", j=2), self.M4[z][:, 2:4, :], ALU.mult)
        p.pfree(hsc2)
        p.tt(QM[r], A4[r][:, 0:1, :].bcast([128, 3, 128]), self.masks3, ALU.mult, eng='pool')
        p.tt(QMT[r], Q0T[r].rr("p (o t) -> p o t", o=1).bcast([128, 3, 128]), self.masks3, ALU.mult, eng='pool')
        XX = XXr[r]
        XXf = XX.rr("p j t -> p (j t)")
        p.tt(XX[:, 0, :], QM[r][:, 0, :], self.identb, ALU.add, eng='pool')
        p.tt(XX[:, 1, :], QMT[r][:, 0, :], self.identb, ALU.add, eng='pool')
        yield
        cq, ct = QM[r][:, 0, :], QMT[r][:, 0, :]
        xq, xt = XX[:, 0, :], XX[:, 1, :]
        ps, h1 = p.palloc(2, bk)
        p.mm(ps[:, 0:128], ct, cq)
        p.mm(ps[:, 128:256], cq, ct)
        yield
        nxt = SQ[r][0]
        p.copy(nxt.rr("p j t -> p (j t)"), ps[:, 0:256], eng='act')
        p.pfree(h1)
        cq, ct = nxt[:, 0, :], nxt[:, 1, :]
        yield
        for lev in range(1, 5):
            ps2, h2 = p.palloc(2, bk)
            p.mm(ps2[:, 0:128], ct, xq)
            p.mm(ps2[:, 128:256], xq, ct)
            if lev < 4:
                ps, h1 = p.palloc(2, bk)
                p.mm(ps[:, 0:128], ct, cq)
                p.mm(ps[:, 128:256], cq, ct)
            yield
            p.tt(XXf, XXf, ps2[:, 0:256], ALU.add)
            p.pfree(h2)
            if lev < 4:
                nxt = SQ[r][lev % 2]
                p.copy(nxt.rr("p j t -> p (j t)"), ps[:, 0:256], eng='act')
                p.pfree(h1)
                cq, ct = nxt[:, 0, :], nxt[:, 1, :]
            yield
        for lvl in (1, 2):
            C, CT = QM[r][:, lvl, :], QMT[r][:, lvl, :]
            ps, h1 = p.palloc(1, bk)
            p.mm(ps[:, 0:128], CT, xq)
            yield
            p.copy(Y1r[r], ps[:, 0:128], eng='act')
            p.pfree(h1)
            yield
            ps2, h2 = p.palloc(2, bk)
            p.mm(ps2[:, 0:128], xt, Y1r[r])
            if lvl == 1:
                p.mm(ps2[:, 128:256], Y1r[r], xt)
            yield
            if lvl == 1:
                p.tt(XXf, XXf, ps2[:, 0:256], ALU.add)
            else:
                p.tt(XX[:, 0, :], XX[:, 0, :], ps2[:, 0:128], ALU.add)
            p.pfree(h2)
            yield
        prep[u] = (r, kb, XX[:, 0, :])

    def stage_b(u, bk, bq):
        step, z, hh = units[u]
        tt = ORD[z][step]
        ts_ = slice(tt * 128, (tt + 1) * 128)
        pr = slice(hh * 64, (hh + 1) * 64)
        cs = slice(hh * 64, (hh + 1) * 64)
        r, kb, xfin = prep.pop(u)
        a4 = A4[r]
        hb = Hb[z][hh]
        vt = Vtok[:, tt, cs]
        while True:
            try:
                w, hw = p.palloc(1, bk)
                break
            except RuntimeError:
                yield
        p.mm(w[:, 0:64], KR[z][pr, tt, 0, :], hb[pr, :], start=True, stop=False)
        p.mm(w[:, 0:64], a4[:, 2, :], vt, start=False, stop=True)
        yield
        p.copy(Wb[r], w[:, 0:64], eng='act')
        p.pfree(hw)
        yield
        while True:
            try:
                uu, hu_ = p.palloc(1, bk)
                break
            except RuntimeError:
                yield
        p.mm(uu[:, 0:64], xfin, Wb[r])
        yield
        p.copy(Ub[r], uu[:, 0:64], eng='act')
        p.pfree(hu_)
        yield
        while True:
            try:
                yb, hy = p.palloc(1, bk)
                break
            except RuntimeError:
                yield
        p.mm(yb[:, 0:64], KR[z][pr, tt, 1, :], hb[pr, :], start=True, stop=False)
        p.mm(yb[:, 0:64], a4[:, 3, :], vt, start=False, stop=False)
        p.mm(yb[:, 0:64], a4[:, 1, :], Ub[r], start=False, stop=True)
        if step < NT - 1:
            while True:
                try:
                    hu, hh_ = p.palloc(1, bk)
                    break
                except RuntimeError:
                    yield
            p.mm(hu[pr, 0:64], kb[:, 0, cs], vt, start=True, stop=False)
            p.mm(hu[pr, 0:64], kb[:, 1, cs], Ub[r], start=False, stop=True)
        yield
        if step < NT - 1:
            p.stt(H[z][hh][pr, :], H[z][hh][pr, :], gam[z][pr, tt:tt + 1], hu[pr, 0:64], ALU.mult, ALU.add)
            p.pfree(hh_)
            p.copy(hb[pr, :], H[z][hh][pr, :], eng='pool')
        p.copy(yz[z][:, tt, cs].on(ybufs[tt][hh]), yb[:, 0:64], eng='act')
        p.pfree(hy)
        yield

    NA = int(os.environ.get('RW_NA', '6'))
    BFIRST = int(os.environ.get('RW_BFIRST', '0'))
    nU = len(units)
    free_r = list(range(NR))
    a_banks = list(range(NA))
    NBB = 8 - NA
    act_a = []
    act_b = []
    a_done = set()
    b_emitted = set()
    next_a = 0
    next_b = [0, 1, 2, 3]
    rmap = {}
    it_ = 0
    last_start = -100
    STAG = int(os.environ.get('RW_STAG', '4'))
    BPRIO = int(os.environ.get('RW_BPRIO', '1'))
    LOOKA = int(os.environ.get('RW_LOOK', '10'))
    while len(b_emitted) < nU:
        it_ += 1
        while next_a < nU and a_banks and free_r and next_a < min(next_b) + LOOKA and it_ - last_start >= STAG:
            last_start = it_
            bk = a_banks.pop(0)
            r = free_r.pop(0)
            rmap[next_a] = r
            act_a.append((stage_a(next_a, bk, r), next_a, bk))
            next_a += 1
        for j in range(4):
            u = next_b[j]
            if u < nU and u in a_done and not any(x[2] == j for x in act_b):
                act_b.append((stage_b(u, NA + (j % NBB), None), u, j))
                next_b[j] = u + 4
        def adv_a():
            nonlocal act_a
            nxt_a = []
            for g, u, bk in act_a:
                try:
                    next(g)
                    nxt_a.append((g, u, bk))
                except StopIteration:
                    a_done.add(u)
                    a_banks.append(bk)
            act_a = nxt_a

        def adv_b():
            nonlocal act_b
            for _rep in range(BPRIO):
                nxt_b = []
                for g, u, j in act_b:
                    try:
                        next(g)
                        nxt_b.append((g, u, j))
                    except StopIteration:
                        b_emitted.add(u)
                        free_r.append(rmap.pop(u))
                act_b = nxt_b

        if BFIRST:
            adv_b()
            adv_a()
        else:
            adv_a()
            adv_b()
    p.release(mring)
    if getattr(self, 'dbg', False):
        for tt in range(NT):
            for hh in range(2):
                p.tt(y[:, tt, hh * 64:(hh + 1) * 64], y[:, tt, hh * 64:(hh + 1) * 64].on(ybufs[tt][hh]), y[:, tt, hh * 64:(hh + 1) * 64].on(ybufs[tt][hh]), ALU.max)
        p.dma(self.dbg_y[cc], y.rr("p t c -> p (t c)"))
    g2b = p.sb("rp_g2b", [128, 128], BF16)
    p.dma(g2b, self.rw_g2[:, cc * 128:(cc + 1) * 128], eng='pool')
    s1 = p.sb("rp_s1", [128, 36], F32)
    s2 = p.sb("rp_s2", [128, 36], F32)
    sq = p.sb("rp_sq", [128, 36, 64], F32)
    yn = p.sb("rp_yn", [128, NT, 128], BF16)
    lnT = p.sb("rp_lnT", [128, LTOK], BF16)
    prod = p.sb("rp_prod", [128, LTOK], BF16)
    y3 = y.rr("p t (h c) -> p (t h) c", c=64)
    allb = [ybufs[t_][h_] for t_ in range(NT) for h_ in range(2)]
    yall = V(y.ap, allb)
    y1all = V(yz[1].ap, allb)
    p.tt(yall, yall, y1all, ALU.add)
    p.tt(sq, V(y3.ap, allb), V(y3.ap, allb), ALU.mult)
    p.reduce(s2, sq, ALU.add)
    p.reduce(s1, y3, ALU.add)
    p.ts(s1, s1, 1.0 / 64, ALU.mult)
    p.tt(sq[:, :, 0], s1, s1, ALU.mult)
    p.stt(s2, s2, 1.0 / 64, sq[:, :, 0], ALU.mult, ALU.subtract)
    p.act(s2, s2, AF.Sqrt, bias=RW_LN_EPS)
    p.recip(s2, s2)
    yn3 = yn.rr("p t (h c) -> p (t h) c", c=64)
    p.tt(sq, y3, s1.rr("p (n o) -> p n o", o=1).bcast([128, 36, 64]), ALU.subtract)
    p.tt(yn3, sq, s2.rr("p (n o) -> p n o", o=1).bcast([128, 36, 64]), ALU.mult)
    lnx = self.col('lnx')
    for g0 in range(0, NT, 8):
        n = min(8, NT - g0)
        ps = self.bank()
        psb = ps.bc(BF16)
        for i in range(n):
            p.tr(psb[:, i * 128:(i + 1) * 128], yn[:, g0 + i, :], self.identb)
        p.ts(lnT[:, g0 * 128:(g0 + n) * 128], psb[:, 0:n * 128], lnx[:, cc:cc + 1], ALU.mult, lnx[:, 8 + cc:8 + cc + 1], ALU.add)
    p.stt(prod, rT, kv[:, 16 + cc:16 + cc + 1], kT, ALU.mult, ALU.mult)
    stage = [p.sb("rp_stage%d" % i, [128, 512], BF16) for i in range(2)]
    tmpb = [p.sb("rp_tmp%d" % i, [128, 512], F32) for i in range(2)]
    for i, (t0, n) in enumerate(BLKS):
        psA = self.bank()
        p.mm(psA[:, 0:n], self.bones, prod[:, t0:t0 + n])
        psG = self.bank()
        p.mm(psG[:, 0:n], g2b, tg[:, t0:t0 + n])
        tb = tmpb[i % 2]
        p.tt(tb[:, 0:n], psA[:, 0:n], vT[:, t0:t0 + n], ALU.mult)
        p.tt(tb[:, 0:n], tb[:, 0:n], lnT[:, t0:t0 + n], ALU.add, eng='pool')
        sg = stage[i % 2]
        p.tt(sg[:, 0:n], psG[:, 0:n], tb[:, 0:n], ALU.mult)
        p.dma(scr['o'][cc, :, t0:t0 + n].on(scr['obuf'][cc]), sg[:, 0:n])
    p.release(m)


def _rw_mixer(self, b, xi, xo, last=True):
    p = self.p
    l = 1
    m = p.mark()
    wo = p.sb("rw_wo", [128, 8, D], BF16)
    tw = [p.sb("rw_tw%d" % z, [128, LTOK], BF16) for z in range(2)]
    ta = p.sb("rw_ta", [128, LTOK], BF16)
    tg = p.sb("rw_tg", [128, LTOK], BF16)
    mh = p.mark()
    hT = p.sb("rw_hT", [128, 8, LTOK], BF16)
    self.prenorm(b, xi, l, 0, hT)
    self.rw_phase1(b, hT, tw, ta, tg)
    p.release(mh)
    for cc in range(8):
        if cc == 6:
            for c in range(8):
                p.dma(wo[:, c, :], self.w_o[c * 128:(c + 1) * 128, :], eng='pool')
        self.rw_chunk(b, cc, tw, ta, tg, last)
    p.release(mh)
    oT = p.sb("rw_oT", [128, 8, LTOK], BF16)
    for c in range(8):
        p.dma(oT[:, c, :], self.scr[b]['o'][c].on(self.scr[b]['obuf'][c]))
    G = [p.sb("rwo_G%d" % i, [128, D], F32) for i in range(2)]
    gtmp = p.sb("rwo_gtmp", [128, 128], F32)
    st = self.res_state(2)
    self.gate_row(G[0], l, 2, b, gtmp)
    if not last:
        self.gate_row(G[1], l, 2, 4, gtmp)
    for tt in (range(2, NT) if last else range(NT)):
        yb = [self.bank(), self.bank()]
        for h in range(2):
            for c in range(8):
                p.mm(yb[h], oT[:, c, tt * 128:(tt + 1) * 128], wo[:, c, h * 512:(h + 1) * 512], start=(c == 0), stop=(c == 7))
        self.residual_tile(b, tt, xi, xo, yb, G[1] if tt < 2 else G[0], st[tt % 2])
    p.release(m)


KE.rw_chunk = _rw_chunk
KE.rw_mixer = _rw_mixer


_NC_CACHE = {}
W_NAMES = ('w_mod', 'ffn_w_up', 'ffn_w_down')
EV_NAMES = ('ev_w_in', 'ev_w_out', 'ev_gla_w2')
RW_NAMES = ('rw_w_rkv', 'rw_w_o', 'rw_w1', 'rw_w2', 'rw_a1', 'rw_a2', 'rw_g1', 'rw_g2')


def build_program(NB, pk):
    k = KE(NB, pk.cols, pk.n, dbg=False)
    k.setup_even()
    k.setup_rw()
    k.mod_stage(0)
    for b in range(NB):
        hT, mixT, m = k.even_mixer(b, 0, 1)
        k.gla(b, hT, mixT)
        k.even_out(b, 0, 1, hT, mixT, m)
        k.ffn(b, 0, 1, 2, do_ctx=True)
    k.setup_rw_consts()
    k.mod_stage(1)
    for b in range(NB):
        k.rw_mixer(b, 2, 3, last=True)
        k.ffn(b, 1, 3, 4, do_ctx=False)
    return k.p.finish()


def kernel(**inp):
    inp = {k_: np.asarray(v_, dtype=np.float32) for k_, v_ in inp.items()}
    NCORE = 8
    B = inp['x'].shape[0]
    NB = B // NCORE
    pk = host_small(inp)
    host_small_even(pk, inp)
    host_small_rw(pk, inp)
    small = pk.pack()
    rows = np.zeros((16, D), np.float32)
    rows[0:8] = inp['norm_g'].reshape(8, D)
    rows[8, :16] = inp['ev_b_gates'][0]
    nc = build_program(NB, pk)
    shared = {"small": small, "rows": rows}
    for nm in W_NAMES:
        shared[nm] = np.ascontiguousarray(inp[nm])
    for nm in EV_NAMES + RW_NAMES:
        shared[nm] = np.ascontiguousarray(inp[nm][0])
    in_maps = []
    for c in range(NCORE):
        sl = slice(c * NB, (c + 1) * NB)
        cc = np.zeros((6, D), np.float32)
        cc[0:NB] = inp['c'][sl]
        cc[4] = inp['c_ctx']
        ccT = np.ascontiguousarray(cc.reshape(6, 8, 128).transpose(2, 1, 0).reshape(128, 48))
        xcat = np.ascontiguousarray(np.concatenate([inp['ctx'][sl], inp['x'][sl]], axis=1))
        d = dict(shared)
        d["xcat"] = xcat
        d["ccT"] = ccT
        in_maps.append(d)
    res = run_bass_kernel_spmd(nc, in_maps, core_ids=list(range(NCORE)))
    out = np.concatenate([np.asarray(r["out"]) for r in res.results], axis=0)
    return out.astype(np.float32)
```

```python
import numpy as np
import concourse.bass as bass
import concourse.mybir as mybir
from concourse.bass_utils import run_bass_kernel_spmd

F32 = mybir.dt.float32
BF16 = mybir.dt.bfloat16
AF = mybir.ActivationFunctionType
ALU = mybir.AluOpType
AX = mybir.AxisListType
ENG = ('sp', 'act', 'dve', 'pool', 'pe')
DSZ = {F32: 4, BF16: 2}
SAME_SYNC = True
NDMASEM = 8
FUSE_WAIT = True
NFUSE = 1


class Buf:
    __slots__ = ('name', 'w', 'r', 'excl', 'subs')

    def __init__(self, name, excl=False):
        self.name = name
        self.w = None
        self.r = {}
        self.excl = excl
        self.subs = []


class V:
    __slots__ = ('ap', 'buf')

    def __init__(self, ap, buf):
        self.ap = ap
        self.buf = buf

    def __getitem__(self, idx):
        return V(self.ap[idx], self.buf)

    def rr(self, pat, **kw):
        return V(self.ap.rearrange(pat, **kw), self.buf)

    def bc(self, dt):
        return V(self.ap.bitcast(dt), self.buf)

    def on(self, buf):
        if isinstance(self.buf, Buf) and buf is not self.buf and buf not in self.buf.subs:
            self.buf.subs.append(buf)
        return V(self.ap, buf)

    def bcast(self, shape):
        return V(self.ap.broadcast_to(list(shape)), self.buf)

    def pbcast(self, n):
        return V(self.ap.partition_broadcast(n), self.buf)

    @property
    def shape(self):
        return tuple(self.ap.shape)


class Prog:
    def __init__(self):
        nc = self.nc = bass.Bass("TRN2", target_bir_lowering=False)
        self.q = {e: [] for e in ENG}
        self.cnt = {e: 0 for e in ENG}
        self.sem = {e: nc.alloc_semaphore('sem_' + e) for e in ENG}
        self.seen = {e: {} for e in ENG}
        self.dsem = [nc.alloc_semaphore('dsem%d' % i) for i in range(NDMASEM)]
        self.dval = [0] * NDMASEM
        self.dnext = 0
        self.sb_off = 16512
        self.sb_max = 0
        self.nalloc = 0
        self.ninst = 0
        self.regions = []
        self.clock = {}
        self.tokidx = {}
        self.ntok = 0

    def dram(self, name, shape, dt, kind="Internal"):
        t = self.nc.dram_tensor(name, list(shape), dt, kind=kind)
        return V(t.ap(), Buf(name))

    def sb(self, name, shape, dt, nbuf=None):
        per = int(np.prod(shape[1:])) * DSZ[dt]
        per = (per + 63) // 64 * 64
        off = self.sb_off
        self.sb_off += per
        self.sb_max = max(self.sb_max, self.sb_off)
        assert self.sb_off <= 229344, (name, self.sb_off)
        self.nalloc += 1
        t = self.nc.alloc_sbuf_tensor_at("%s_%d" % (name, self.nalloc), list(shape), dt, offset=off)
        nb = Buf(name)
        lo, hi = off, off + per
        keep = []
        for (a, b_, ob) in self.regions:
            if a < hi and lo < b_:
                toks = []
                for ob2 in [ob] + ob.subs:
                    toks += list(ob2.r.values())
                    if ob2.w is not None:
                        toks.append(ob2.w)
                for tk in toks:
                    k = id(tk[0])
                    if k not in nb.r or nb.r[k][1] < tk[1]:
                        nb.r[k] = tk
                if a < lo:
                    keep.append((a, lo, ob))
                if b_ > hi:
                    keep.append((hi, b_, ob))
            else:
                keep.append((a, b_, ob))
        keep.append((lo, hi, nb))
        self.regions = keep
        return V(t.ap(), nb)

    def mark(self):
        return self.sb_off

    def release(self, m):
        self.sb_off = m

    def psum_banks(self):
        banks = []
        self.pslot_bufs = []
        self.pslot_free = [True] * 32
        for i in range(8):
            t = self.nc.alloc_psum_tensor("psb%d" % i, [128, 512], F32)
            bl = Buf("psb%d" % i, excl=True)
            self.pslot_bufs.append(bl)
            banks.append(V(t.ap(), bl))
        self.pbanks = banks
        return banks

    def palloc(self, nq, bank=None):
        banks = range(8) if bank is None else (bank,)
        for bk in banks:
            for q0 in range(0, 4, nq):
                if all(self.pslot_free[bk * 4 + q0 + j] for j in range(nq)):
                    for j in range(nq):
                        self.pslot_free[bk * 4 + q0 + j] = False
                    v = V(self.pbanks[bk].ap[:, q0 * 128:(q0 + nq) * 128], self.pslot_bufs[bk])
                    return v, (bk, q0, nq)
        raise RuntimeError("out of PSUM slots")

    def pfree(self, h):
        bk, q0, nq = h
        for j in range(nq):
            self.pslot_free[bk * 4 + q0 + j] = True

    def _emit(self, eng, fn, reads, writes, dma=False):
        waits = {}

        def need(tok, waw_pe=False):
            if tok is None:
                return
            sem, val, te = tok
            if te == eng and not dma:
                if eng == 'pe' or not SAME_SYNC:
                    return
            k = id(sem)
            if k not in waits or waits[k][1] < val:
                waits[k] = (sem, val)

        ex = [b for b in reads if b.excl and b not in writes]
        if ex:
            reads = [b for b in reads if not b.excl]
            writes = list(writes) + ex
        for b in reads:
            need(b.w)
        for b in writes:
            need(b.w)
            for t in b.r.values():
                need(t)
        if dma:
            k = self.dnext
            self.dnext = (self.dnext + 1) % NDMASEM
            if self.dval[k] > 0:
                need((self.dsem[k], self.dval[k], 'dma'))
            self.dval[k] += 16
            tok = (self.dsem[k], self.dval[k], 'dma')
            inc = (self.dsem[k], 16)
        else:
            self.cnt[eng] += 1
            tok = (self.sem[eng], self.cnt[eng], eng)
            inc = (self.sem[eng], 1)
        seen = self.seen[eng]
        final = []
        cand = sorted(waits.items(), key=lambda kv: -self.tokidx.get((kv[0], kv[1][1]), 0))
        for k, (sem, val) in cand:
            if seen.get(k, 0) < val:
                seen[k] = val
                final.append((sem, val))
                ck = self.clock.get((k, val))
                if ck:
                    for kk, vv in ck.items():
                        if seen.get(kk, 0) < vv:
                            seen[kk] = vv
        self.ntok += 1
        self.tokidx[(id(tok[0]), tok[1])] = self.ntok
        self.clock[(id(tok[0]), tok[1])] = dict(seen)
        for b in reads:
            b.r[id(tok[0])] = tok
        for b in writes:
            b.w = tok
            b.r = {}
        self.q[eng].append((final, fn, inc))
        self.ninst += 1
        self.nwait = getattr(self, 'nwait', 0) + len(final)

    def barrier(self):
        toks = [(self.sem[e], self.cnt[e]) for e in ENG if self.cnt[e] > 0]
        toks += [(self.dsem[k], self.dval[k]) for k in range(NDMASEM) if self.dval[k] > 0]
        for e in ENG:
            final = []
            for sem, val in toks:
                if sem is self.sem[e]:
                    continue
                if self.seen[e].get(id(sem), 0) < val:
                    self.seen[e][id(sem)] = val
                    final.append((sem, val))
            if final:
                self.q[e].append((final, None, None))

    def finish(self):
        self.barrier()
        nc = self.nc
        q = self.q

        def replay(e, lst):
            for waits, fn, inc in lst:
                if fn is None or inc[1] == 16 or not FUSE_WAIT:
                    for sem, val in waits:
                        e.wait_ge(sem, val)
                    if fn is not None:
                        ins = fn(e)
                        ins.then_inc(inc[0], inc[1])
                    continue
                for sem, val in waits[:-NFUSE]:
                    e.wait_ge(sem, val)
                ins = fn(e)
                for sem, val in waits[-NFUSE:]:
                    ins._wait_ge(sem, val)
                ins.then_inc(inc[0], inc[1])

        with nc.Block() as block:
            @block.sync
            def _(e):
                replay(e, q['sp'])

            @block.scalar
            def _(e):
                replay(e, q['act'])

            @block.vector
            def _(e):
                replay(e, q['dve'])

            @block.gpsimd
            def _(e):
                replay(e, q['pool'])

            @block.tensor
            def _(e):
                replay(e, q['pe'])
        return nc

    @staticmethod
    def _a(x):
        return x.ap if isinstance(x, V) else x

    @staticmethod
    def _bufs(*xs):
        out = []
        for x in xs:
            if isinstance(x, V):
                bl = x.buf if isinstance(x.buf, (list, tuple)) else (x.buf,)
                for b in bl:
                    if b not in out:
                        out.append(b)
        return out

    def dma(self, out, in_, eng='sp', **kw):
        o, i = out.ap, in_.ap
        self._emit(eng, lambda e: e.dma_start(out=o, in_=i, **kw), self._bufs(in_), self._bufs(out), dma=True)

    def mm(self, out, lhsT, rhs, start=True, stop=True):
        o, l, r = out.ap, lhsT.ap, rhs.ap
        self._emit('pe', lambda e: e.matmul(o, lhsT=l, rhs=r, start=start, stop=stop),
                   self._bufs(lhsT, rhs), self._bufs(out))

    def tr(self, out, in_, ident):
        o, i, d = out.ap, in_.ap, ident.ap
        self._emit('pe', lambda e: e.transpose(o, i, d), self._bufs(in_, ident), self._bufs(out))

    def act(self, out, in_, func, bias=None, scale=None, accum=None):
        kw = {}
        if bias is not None:
            kw['bias'] = self._a(bias)
        if scale is not None:
            kw['scale'] = self._a(scale)
        if accum is not None:
            kw['accum_out'] = accum.ap
        o, i = out.ap, in_.ap
        self._emit('act', lambda e: e.activation(out=o, in_=i, func=func, **kw),
                   self._bufs(in_, bias, scale), self._bufs(out, accum))

    def tt(self, out, in0, in1, op, eng='dve'):
        o, a, b = out.ap, in0.ap, in1.ap
        self._emit(eng, lambda e: e.tensor_tensor(out=o, in0=a, in1=b, op=op),
                   self._bufs(in0, in1), self._bufs(out))

    def ts(self, out, in0, s1, op0, s2=None, op1=None, eng='dve', accum=None):
        o, a = out.ap, in0.ap
        a1, a2 = self._a(s1), self._a(s2)
        kw = {}
        if op1 is not None:
            kw['op1'] = op1
        if accum is not None:
            kw['accum_out'] = accum.ap
        self._emit(eng, lambda e: e.tensor_scalar(out=o, in0=a, scalar1=a1, scalar2=a2, op0=op0, **kw),
                   self._bufs(in0, s1, s2), self._bufs(out, accum))

    def stt(self, out, in0, scalar, in1, op0, op1, eng='dve'):
        o, a, b = out.ap, in0.ap, in1.ap
        s = self._a(scalar)
        self._emit(eng, lambda e: e.scalar_tensor_tensor(out=o, in0=a, scalar=s, in1=b, op0=op0, op1=op1),
                   self._bufs(in0, scalar, in1), self._bufs(out))

    def copy(self, out, in_, eng='dve'):
        o, i = out.ap, in_.ap
        if eng == 'act':
            self._emit('act', lambda e: e.activation(out=o, in_=i, func=AF.Copy), self._bufs(in_), self._bufs(out))
        else:
            self._emit(eng, lambda e: e.tensor_copy(out=o, in_=i), self._bufs(in_), self._bufs(out))

    def memset(self, out, val, eng='dve'):
        o = out.ap
        self._emit(eng, lambda e: e.memset(o, val), [], self._bufs(out))

    def reduce(self, out, in_, op, axis=AX.X, eng='dve'):
        o, i = out.ap, in_.ap
        self._emit(eng, lambda e: e.tensor_reduce(out=o, in_=i, axis=axis, op=op), self._bufs(in_), self._bufs(out))

    def recip(self, out, in_):
        o, i = out.ap, in_.ap
        self._emit('dve', lambda e: e.reciprocal(out=o, in_=i), self._bufs(in_), self._bufs(out))

    def aselect(self, out, in_, pattern, cmp, fill, base, cm):
        o, i = out.ap, in_.ap
        self._emit('pool', lambda e: e.affine_select(out=o, in_=i, pattern=pattern, compare_op=cmp, fill=fill,
                                                     base=base, channel_multiplier=cm),
                   self._bufs(in_), self._bufs(out))

    def scan(self, out, d0, d1, init, op0, op1):
        o, a, b = out.ap, d0.ap, d1.ap
        self._emit('dve', lambda e: e.tensor_tensor_scan(out=o, data0=a, data1=b, initial=init, op0=op0, op1=op1),
                   self._bufs(d0, d1), self._bufs(out))


import os

D = 1024
NT = 18
LTOK = 2304
EPS = 1e-6
DFF = 2816
NJ = 22


class Packer:
    def __init__(self):
        self.cols = {}
        self.n = 0
        self.arrs = []

    def add(self, name, arr):
        arr = np.ascontiguousarray(arr, dtype=np.float32).reshape(128, -1)
        self.cols[name] = (self.n, arr.shape[1])
        self.n += arr.shape[1]
        self.arrs.append(arr)

    def pack(self):
        return np.ascontiguousarray(np.concatenate(self.arrs, axis=1))


def fm(v):
    v = np.asarray(v, dtype=np.float32)
    lead = v.shape[:-1]
    c = v.shape[-1] // 128
    v = v.reshape(lead + (c, 128))
    return np.moveaxis(v, -1, 0).reshape(128, -1)


def host_small(inp, layer_cols_only=False):
    pk = Packer()
    for l in range(2):
        pk.add('bmod%d' % l, fm(inp['b_mod'][l]))
        pk.add('ng%d' % l, fm(inp['norm_g'][l]))
        pk.add('cw%d' % l, np.moveaxis(inp['ffn_conv_w'][l].reshape(9, NJ, 128), 2, 0).transpose(0, 2, 1).reshape(128, NJ * 9))
        pk.add('cb%d' % l, fm(inp['ffn_conv_b'][l]))
    return pk


class K:
    def __init__(self, NB, small_cols, nsmall, dbg=False):
        self.NB = NB
        self.dbg = dbg
        p = self.p = Prog()
        self.sc = small_cols
        self.X0 = p.dram("xcat", [NB, LTOK, D], F32, kind="ExternalInput")
        self.ccT_d = p.dram("ccT", [128, 8 * 6], F32, kind="ExternalInput")
        self.small_d = p.dram("small", [128, nsmall], F32, kind="ExternalInput")
        self.rows_d = p.dram("rows", [16, D], F32, kind="ExternalInput")
        self.w_mod = p.dram("w_mod", [2, D, 6 * D], F32, kind="ExternalInput")
        self.w_up = p.dram("ffn_w_up", [2, D, 2 * DFF], F32, kind="ExternalInput")
        self.w_down = p.dram("ffn_w_down", [2, DFF, D], F32, kind="ExternalInput")
        self.Xs = [self.X0]
        for i in range(1, 4):
            self.Xs.append(p.dram("xs%d" % i, [NB, LTOK, D], F32, kind="ExternalOutput" if dbg else "Internal"))
        self.out = p.dram("out", [NB, 2048, D], F32, kind="ExternalOutput")
        self.xbufs = [[[Buf("x%d_%d_%d" % (i, b, t)) for t in range(NT)] for b in range(NB)] for i in range(5)]
        self.banks = p.psum_banks()
        self.nbank = 0
        self.small = p.sb("small", [128, nsmall], F32)
        p.dma(self.small, self.small_d)
        self.identf = p.sb("identf", [128, 128], F32)
        p.memset(self.identf, 1.0, eng='pool')
        p.aselect(self.identf, self.identf, [[-1, 128]], ALU.is_equal, 0.0, 0, 1)
        self.identb = p.sb("identb", [128, 128], BF16)
        p.copy(self.identb, self.identf, eng='pool')
        self.scT = p.sb("scT", [128, 48], F32)
        p.dma(self.scT, self.ccT_d)
        p.act(self.scT, self.scT, AF.Silu)
        self.modT = p.sb("modT", [128, 48 * 6], F32)
        self.A1 = p.sb("A1", [128, 48], F32)
        self.A2 = p.sb("A2", [128, 48], F32)
        self.base_mark = p.mark()

    def bank(self):
        b = self.banks[self.nbank]
        self.nbank = (self.nbank + 1) % 8
        return b

    @staticmethod
    def run_rr(gens, stagger=0):
        pending = list(gens)
        active = []
        it_ = 0
        while active or pending:
            if pending and (stagger == 0 or it_ % stagger == 0 or not active):
                if stagger == 0:
                    active += pending
                    pending = []
                else:
                    active.append(pending.pop(0))
            it_ += 1
            nxt_ = []
            for g in active:
                try:
                    next(g)
                    nxt_.append(g)
                except StopIteration:
                    pass
            active = nxt_

    def col(self, name, a=None, b=None):
        o, w = self.sc[name]
        if a is None:
            return self.small[:, o:o + w]
        return self.small[:, o + a:o + b]

    def mod_stage(self, l):
        p = self.p
        m = p.mark()
        wst = [p.sb("wmod_st%d" % i, [128, 8, 512], F32) for i in range(2)]
        ps = self.bank()
        for blk in range(12):
            w = wst[blk % 2]
            p.dma(w, self.w_mod[l, :, blk * 512:(blk + 1) * 512].rr("(c p) n -> p c n", p=128))
            for f in range(4):
                fc = blk * 4 + f
                for kc in range(8):
                    p.mm(ps[:, fc * 6:(fc + 1) * 6], w[:, kc, f * 128:(f + 1) * 128], self.scT[:, kc * 6:(kc + 1) * 6],
                         start=(kc == 0), stop=(kc == 7))
        p.tt(self.modT.rr("p (c r) -> p c r", r=6), ps[:, 0:288].rr("p (c r) -> p c r", r=6),
             self.col('bmod%d' % l).rr("p (c o) -> p c o", o=1).bcast([128, 48, 6]), ALU.add)
        mv = self.modT.rr("p (i c r) -> p i c r", i=6, c=8)
        ng = self.col('ng%d' % l).rr("p (i c o) -> p i c o", i=4, o=1)
        for A, mi, gi in ((self.A1, 1, 0), (self.A2, 4, 2)):
            Av = A.rr("p (c r) -> p c r", r=6)
            p.ts(Av, mv[:, mi], 1.0, ALU.add)
            p.tt(Av, Av, ng[:, gi].bcast([128, 8, 6]), ALU.mult)
        p.release(m)

    def modvec(self, l, which):
        mv = self.modT.rr("p (i c r) -> p i c r", i=6, c=8)
        if which == 0:
            return self.A1.rr("p (c r) -> p c r", r=6), mv[:, 0]
        return self.A2.rr("p (c r) -> p c r", r=6), mv[:, 3]

    def gate_row(self, dst, l, gi, row, tmp):
        p = self.p
        mv = self.modT.rr("p (i c r) -> p i c r", i=6, c=8)
        nrow = l * 4 + (1 if gi == 2 else 3)
        p.dma(dst, self.rows_d[nrow:nrow + 1, :].pbcast(128))
        for half in range(2):
            ps = self.bank()
            for cc in range(4):
                c = half * 4 + cc
                p.copy(tmp, mv[:, gi, c, row:row + 1].bcast([128, 128]))
                p.mm(ps[:, cc * 128:(cc + 1) * 128], tmp, self.identf)
            p.tt(dst[:, half * 512:(half + 1) * 512], dst[:, half * 512:(half + 1) * 512], ps, ALU.mult)

    def prenorm(self, b, xi, l, which, hT, tiles=range(NT)):
        p = self.p
        A, Bv = self.modvec(l, which)
        m = p.mark()
        NG = 4
        xt = [p.sb("pn_x%d" % i, [128, D], F32) for i in range(NG)]
        junk = p.sb("pn_junk", [128, D], BF16)
        xn = [p.sb("pn_xn%d" % i, [128, D], BF16) for i in range(NG)]
        ss = [p.sb("pn_ss%d" % i, [128, 1], F32) for i in range(NG)]
        tmp = [p.sb("pn_tmp%d" % i, [128, 8, 128], F32) for i in range(NG)]
        tiles = list(tiles)

        def chain(g):
            for tt in tiles[g::NG]:
                x, s = xt[g], ss[g]
                p.dma(x, self.Xs[xi][b, tt * 128:(tt + 1) * 128, :].on(self.xbufs[xi][b][tt]))
                yield
                p.act(junk, x, AF.Square, accum=s)
                yield
                p.act(s, s, AF.Sqrt, bias=EPS, scale=1.0 / D)
                yield
                p.recip(s, s)
                yield
                p.act(xn[g], x, AF.Identity, scale=s)
                yield
                ps, hp = p.palloc(4, 2 * g)
                psb = ps.bc(BF16)
                for c in range(8):
                    p.tr(psb[:, c * 128:(c + 1) * 128], xn[g][:, c * 128:(c + 1) * 128], self.identb)
                yield
                row = 4 if tt < 2 else b
                p.tt(tmp[g], psb.rr("p (c t) -> p c t", c=8), A[:, :, row:row + 1].bcast([128, 8, 128]), ALU.mult)
                p.pfree(hp)
                yield
                p.tt(hT[:, :, tt * 128:(tt + 1) * 128], tmp[g], Bv[:, :, row:row + 1].bcast([128, 8, 128]), ALU.add,
                     eng='pool')
                yield

        self.run_rr([chain(g) for g in range(NG)])
        p.release(m)

    def residual_tile(self, b, tt, xi, xo, ybanks, G, st):
        p = self.p
        x, junk, ss2, s, tmp = st
        p.dma(x, self.Xs[xi][b, tt * 128:(tt + 1) * 128, :].on(self.xbufs[xi][b][tt]))
        for h in range(2):
            p.act(junk, ybanks[h], AF.Square, accum=ss2[:, h:h + 1])
        p.tt(s, ss2[:, 0:1], ss2[:, 1:2], ALU.add)
        p.act(s, s, AF.Sqrt, bias=EPS, scale=1.0 / D)
        p.recip(s, s)
        for h in range(2):
            p.stt(tmp[:, h * 512:(h + 1) * 512], ybanks[h], s, G[:, h * 512:(h + 1) * 512], ALU.mult, ALU.mult)
        p.tt(tmp, tmp, x, ALU.add, eng='pool')
        if xo == 4:
            dst = self.out[b, (tt - 2) * 128:(tt - 1) * 128, :]
        else:
            dst = self.Xs[xo][b, tt * 128:(tt + 1) * 128, :]
        p.dma(dst.on(self.xbufs[xo][b][tt]), tmp)

    def res_state(self, n=2):
        p = self.p
        return [(p.sb("rs_x%d" % i, [128, D], F32), p.sb("rs_junk%d" % i, [128, 512], BF16),
                 p.sb("rs_ss2%d" % i, [128, 2], F32), p.sb("rs_s%d" % i, [128, 1], F32),
                 p.sb("rs_tmp%d" % i, [128, D], F32)) for i in range(n)]

    def ffn(self, b, l, xi, xo, do_ctx=True):
        p = self.p
        m = p.mark()
        hT = p.sb("ffn_hT", [128, 8, LTOK], BF16)
        self.prenorm(b, xi, l, 1, hT, tiles=range(NT) if do_ctx else range(2, NT))
        wdown = p.sb("ffn_wdown", [128, NJ, D], BF16)
        actT = p.sb("ffn_actT", [128, NJ, 1280], BF16)
        wg = [p.sb("ffn_wg%d" % i, [128, 8, 128], BF16) for i in range(2)]
        wu = [p.sb("ffn_wu%d" % i, [128, 8, 128], BF16) for i in range(2)]
        gpad = [p.sb("ffn_gpad%d" % i, [128, 18, 66], BF16) for i in range(2)]
        cpad = p.sb("ffn_cpad", [128, 258], BF16)
        gg = [p.sb("ffn_gg%d" % i, [128, 512], BF16) for i in range(2)]
        diag = [p.sb("ffn_diag%d" % i, [128, 9, 128], BF16) for i in range(2)]
        G = [p.sb("ffn_G%d" % i, [128, D], F32) for i in range(2)]
        gtmp = p.sb("ffn_gtmp", [128, 128], F32)
        st = self.res_state(2)
        for g in gpad:
            p.memset(g, 0.0, eng='pool')
        p.memset(cpad, 0.0, eng='pool')
        self.gate_row(G[0], l, 5, b, gtmp)
        if do_ctx:
            self.gate_row(G[1], l, 5, 4, gtmp)
        cw = self.col('cw%d' % l)
        cb = self.col('cb%d' % l)
        nn = 0
        for seg in range(2):
            r0 = 16 * seg
            g0 = 0 if seg == 0 else 15
            prow0 = 1 if seg == 0 else 0
            gp = gpad[seg]
            with_ctx = (seg == 0 and do_ctx)
            for j in range(NJ):
                wgj, wuj = wg[j % 2], wu[j % 2]
                p.dma(wgj, self.w_up[l, :, j * 128:(j + 1) * 128].rr("(c p) n -> p c n", p=128), eng='pool')
                p.dma(wuj, self.w_up[l, :, DFF + j * 128:DFF + (j + 1) * 128].rr("(c p) n -> p c n", p=128), eng='pool')
                if seg == 0 and j >= 2:
                    for jj in range(2 * (j - 2), min(NJ, 2 * (j - 2) + 2)):
                        p.dma(wdown[:, jj, :], self.w_down[l, jj * 128:(jj + 1) * 128, :], eng='pool')
                dg = diag[j % 2]
                p.tt(dg, self.identb.rr("p (o t) -> p o t", o=1).bcast([128, 9, 128]),
                     cw[:, j * 9:(j + 1) * 9].rr("p (t o) -> p t o", o=1).bcast([128, 9, 128]), ALU.mult, eng='pool')
                tok0 = 256 + g0 * 64
                for (o, n) in ((0, 512), (512, 512), (1024, 64)):
                    ps = self.bank()
                    for kc in range(8):
                        p.mm(ps[:, 0:n], wgj[:, kc, :], hT[:, kc, tok0 + o:tok0 + o + n], start=(kc == 0), stop=(kc == 7))
                    pr = prow0 + o // 64
                    p.copy(gp[:, pr:pr + n // 64, 1:65], ps[:, 0:n].rr("p (r c) -> p r c", c=64), eng='act')
                for blk in range(2):
                    ps = self.bank()
                    for t in range(9):
                        dy, dx = t // 3, t % 3
                        p.mm(ps, dg[:, t, :], gp[:, 8 * blk + dy:8 * blk + dy + 8, dx:dx + 64], start=(t == 0), stop=(t == 8))
                    g_ = gg[nn % 2]
                    nn += 1
                    p.act(g_, ps, AF.Gelu_apprx_tanh, bias=cb[:, j:j + 1])
                    ps2 = self.bank()
                    t0 = 256 + r0 * 64 + blk * 512
                    for kc in range(8):
                        p.mm(ps2, wuj[:, kc, :], hT[:, kc, t0:t0 + 512], start=(kc == 0), stop=(kc == 7))
                    p.tt(actT[:, j, 256 + blk * 512:256 + (blk + 1) * 512], ps2, g_, ALU.mult)
                if with_ctx:
                    ps = self.bank()
                    for kc in range(8):
                        p.mm(ps[:, 0:256], wgj[:, kc, :], hT[:, kc, 0:256], start=(kc == 0), stop=(kc == 7))
                    p.copy(cpad[:, 1:257], ps[:, 0:256], eng='act')
                    ps = self.bank()
                    for dx in range(3):
                        p.mm(ps[:, 0:256], dg[:, 3 + dx, :], cpad[:, dx:dx + 256], start=(dx == 0), stop=(dx == 2))
                    g_ = gg[nn % 2]
                    nn += 1
                    p.act(g_[:, 0:256], ps[:, 0:256], AF.Gelu_apprx_tanh, bias=cb[:, j:j + 1])
                    ps2 = self.bank()
                    for kc in range(8):
                        p.mm(ps2[:, 0:256], wuj[:, kc, :], hT[:, kc, 0:256], start=(kc == 0), stop=(kc == 7))
                    p.tt(actT[:, j, 0:256], ps2[:, 0:256], g_[:, 0:256], ALU.mult)
            tiles = ([0, 1] if with_ctx else []) + [2 + 8 * seg + i for i in range(8)]
            for n, tt in enumerate(tiles):
                a0 = tt * 128 if tt < 2 else 256 + (tt - 2 - 8 * seg) * 128
                yb = [self.bank(), self.bank()]
                for h in range(2):
                    for j in range(NJ):
                        p.mm(yb[h], actT[:, j, a0:a0 + 128], wdown[:, j, h * 512:(h + 1) * 512], start=(j == 0), stop=(j == NJ - 1))
                self.residual_tile(b, tt, xi, xo, yb, G[1] if tt < 2 else G[0], st[n % 2])
        p.release(m)


LNS_ML = float(np.log(128.0 ** -0.5))
LNS_GLA = float(np.log(64.0 ** -0.5))
ORD = [list(range(NT)), [1, 0] + list(range(17, 1, -1))]
C_MQ, C_MK, C_MV, C_MO, C_MG, C_GQ, C_GK, C_GV, C_GG, C_GLR = 0, 512, 1024, 1536, 2048, 2064, 2320, 2576, 3088, 3600


def host_small_even(pk, inp):
    pk.add('ecw', fm(inp['ev_conv_w'][0]))
    pk.add('ecb', fm(inp['ev_conv_b'][0]))
    pk.add('glab', fm(inp['ev_gla_b'][0]))
    pk.add('hg', fm(inp['ev_head_g'][0]))


class KE(K):
    def setup_even(self):
        p = self.p
        self.w_in = p.dram("ev_w_in", [D, 3632], F32, kind="ExternalInput")
        self.w_out = p.dram("ev_w_out", [D, D], F32, kind="ExternalInput")
        self.gla_w2 = p.dram("ev_gla_w2", [2, 16, 256], F32, kind="ExternalInput")
        self.ones = p.sb("ones", [128, 128], F32)
        p.memset(self.ones, 1.0, eng='pool')
        self.tri = []
        for z in range(2):
            t = p.sb("tri%d" % z, [128, 128], F32)
            pat, cm = ([[1, 128]], -1) if z == 0 else ([[-1, 128]], 1)
            p.aselect(t, self.ones, pat, ALU.is_ge, 0.0, 0, cm)
            self.tri.append(t)
        self.common_mark = p.mark()
        self.mask4 = []
        self.maskb = []
        for z in range(2):
            t = self.tri[z]
            m4 = p.sb("mask4%d" % z, [128, 4, 128], F32)
            p.ts(m4, t.rr("p (o t) -> p o t", o=1).bcast([128, 4, 128]), -1.0, ALU.add, 30000.0, ALU.mult, eng='pool')
            self.mask4.append(m4)
            mb = p.sb("maskb%d" % z, [128, 128], BF16)
            p.copy(mb, t, eng='pool')
            self.maskb.append(mb)
        self.bgrow = p.sb("bgrow", [128, 16], F32)
        p.dma(self.bgrow, self.rows_d[8:9, 0:16].pbcast(128))
        self.base_mark = p.mark()

    def headnorm_gate(self, y, ybufs, hd, gate_col, gate_fn, hT, mixT, scratch):
        p = self.p
        sq, ss, yn, sg, wgt = scratch
        p.dma(wgt, self.w_in[:, gate_col:gate_col + 128].rr("(c p) n -> p c n", p=128), eng='pool')
        for blk in range(5):
            t0, n = blk * 512, (512 if blk < 4 else 256)
            ps = self.bank()
            for kc in range(8):
                p.mm(ps[:, 0:n], wgt[:, kc, :], hT[:, kc, t0:t0 + n], start=(kc == 0), stop=(kc == 7))
            p.act(sg[:, t0:t0 + n], ps[:, 0:n], gate_fn)
        yall_ = V(y.ap, list(ybufs))
        p.tt(sq, yall_, yall_, ALU.mult)
        p.reduce(ss, sq, ALU.add)
        p.act(ss, ss, AF.Sqrt, bias=EPS, scale=1.0 / 128)
        p.recip(ss, ss)
        p.tt(yn, yall_, ss.rr("p (t o) -> p t o", o=1).bcast([128, NT, 128]), ALU.mult)
        hg = self.col('hg')
        for g0 in (0, 8, 16):
            n = min(8, NT - g0)
            ps = self.bank()
            psb = ps.bc(BF16)
            for i in range(n):
                p.tr(psb[:, i * 128:(i + 1) * 128], yn[:, g0 + i, :], self.identb)
            p.stt(mixT[:, hd, g0 * 128:(g0 + n) * 128], psb[:, 0:n * 128], hg[:, hd:hd + 1], sg[:, g0 * 128:(g0 + n) * 128],
                  ALU.mult, ALU.mult)

    def even_mixer(self, b, xi, xo):
        p = self.p
        l = 0
        m = p.mark()
        hT = p.sb("ev_hT", [128, 8, LTOK], BF16)
        self.prenorm(b, xi, l, 0, hT)
        mixT = p.sb("ev_mixT", [128, 8, LTOK], BF16)
        wgate = p.sb("ev_wgate", [128, 8, 16], BF16)
        p.dma(wgate, self.w_in[:, C_MG:C_MG + 16].rr("(c p) n -> p c n", p=128), eng='pool')
        graw = p.sb("ev_graw", [128, NT, 16], F32)
        ps = self.bank()
        for tt in range(NT):
            for kc in range(8):
                p.mm(ps[:, tt * 16:(tt + 1) * 16], hT[:, kc, tt * 128:(tt + 1) * 128], wgate[:, kc, :], start=(kc == 0), stop=(kc == 7))
        p.tt(graw, ps[:, 0:NT * 16].rr("p (t n) -> p t n", n=16), self.bgrow.rr("p (o n) -> p o n", o=1).bcast([128, NT, 16]), ALU.add)
        g5 = graw.rr("p t (z y h) -> p t z y h", z=2, y=2)
        I8 = g5[:, :, :, 0, :]
        F8 = g5[:, :, :, 1, :]
        def s8(name):
            return p.sb(name, [128, NT, 2, 4], F32)
        lf8, Fc8, Ft8, bs8, qs8, wk8, dc8 = [s8("ev_" + n) for n in ("lf8", "Fc8", "Ft8", "bs8", "qs8", "wk8", "dc8")]
        p.act(lf8, F8, AF.Exp, scale=-1.0)
        p.act(lf8, lf8, AF.Ln, bias=1.0)
        p.ts(lf8, lf8, -1.0, ALU.mult)
        ps = self.bank()
        for z in range(2):
            p.mm(ps[:, z * 72:(z + 1) * 72], self.tri[z], lf8[:, :, z, :])
        p.mm(ps[:, 144:288], self.ones, lf8)
        for z in range(2):
            p.copy(Fc8[:, :, z, :], ps[:, z * 72:(z + 1) * 72].rr("p (t h) -> p t h", h=4))
        p.copy(Ft8, ps[:, 144:288].rr("p (t z h) -> p t z h", z=2, h=4))
        p.tt(bs8, I8, Fc8, ALU.subtract)
        p.tt(wk8, Ft8, bs8, ALU.add)
        p.act(wk8, wk8, AF.Exp)
        p.ts(bs8, bs8, LNS_ML, ALU.add)
        p.act(qs8, Fc8, AF.Exp, bias=LNS_ML)
        p.act(dc8, Ft8, AF.Exp)
        Fc5 = Fc8.rr("p t z (h o) -> p t z h o", o=1)

        cw = self.col('ecw').rr("p (t c) -> p t c", t=3)
        cbias = self.col('ecb')
        for hp in range(2):
            m2 = p.mark()
            heads = [2 * hp, 2 * hp + 1]
            qT = [p.sb("ml_qT%d" % i, [128, LTOK], BF16) for i in range(2)]
            kT = [p.sb("ml_kT%d" % i, [128, LTOK], BF16) for i in range(2)]
            vv = [p.sb("ml_v%d" % i, [128, NT, 130], BF16) for i in range(2)]
            yy = [p.sb("ml_y%d" % i, [128, NT, 128], F32) for i in range(2)]
            ybufs = [[Buf("mly%d_%d" % (i, t)) for t in range(NT)] for i in range(2)]
            m3 = p.mark()
            xpad = p.sb("ml_xpad", [128, 2308], F32)
            t1 = p.sb("ml_t1", [128, 2306], F32)
            t2 = p.sb("ml_t2", [128, 2306], F32)
            wq = [p.sb("ml_wq%d" % i, [128, 8, 128], BF16) for i in range(2)]
            p.memset(xpad, 0.0, eng='pool')
            nw = 0
            for i, h in enumerate(heads):
                for dst, c0, cc in ((qT[i], C_MQ + h * 128, h), (kT[i], C_MK + h * 128, 4 + h)):
                    w = wq[nw % 2]
                    nw += 1
                    p.dma(w, self.w_in[:, c0:c0 + 128].rr("(c p) n -> p c n", p=128), eng='pool')
                    for blk in range(5):
                        t0, n = (0, 256) if blk == 0 else (256 + (blk - 1) * 512, 512)
                        pos = 1 if blk == 0 else 259 + (blk - 1) * 512
                        ps = self.bank()
                        for kc in range(8):
                            p.mm(ps[:, 0:n], w[:, kc, :], hT[:, kc, t0:t0 + n], start=(kc == 0), stop=(kc == 7))
                        p.copy(xpad[:, pos:pos + n], ps[:, 0:n], eng='act')
                    p.ts(t1, xpad[:, 0:2306], cw[:, 0, cc:cc + 1], ALU.mult, cbias[:, cc:cc + 1], ALU.add)
                    p.stt(t2, xpad[:, 1:2307], cw[:, 1, cc:cc + 1], t1, ALU.mult, ALU.add)
                    p.stt(t1, xpad[:, 2:2308], cw[:, 2, cc:cc + 1], t2, ALU.mult, ALU.add)
                    p.act(dst[:, 0:256], t1[:, 0:256], AF.Silu)
                    p.act(dst[:, 256:LTOK], t1[:, 258:2306], AF.Silu)
                w = wq[nw % 2]
                nw += 1
                p.dma(w, self.w_in[:, C_MV + h * 128:C_MV + (h + 1) * 128].rr("(c p) n -> p c n", p=128), eng='pool')
                p.memset(vv[i][:, :, 128:130], 1.0, eng='pool')
                for g0 in range(0, NT, 4):
                    n = min(4, NT - g0)
                    ps = self.bank()
                    for j in range(n):
                        tt = g0 + j
                        for kc in range(8):
                            p.mm(ps[:, j * 128:(j + 1) * 128], hT[:, kc, tt * 128:(tt + 1) * 128], w[:, kc, :], start=(kc == 0), stop=(kc == 7))
                    p.copy(vv[i][:, g0:g0 + n, 0:128], ps[:, 0:n * 128].rr("p (t n) -> p t n", n=128), eng='act')
                p.memset(yy[i], 0.0, eng='pool')
            p.release(m3)
            Cs = [[p.sb("ml_C%d%d" % (z, i), [128, 132], F32) for i in range(2)] for z in range(2)]
            Cb = [[p.sb("ml_Cb%d%d" % (z, i), [128, 132], BF16) for i in range(2)] for z in range(2)]
            for z in range(2):
                for i in range(2):
                    p.memset(Cs[z][i], 0.0, eng='pool')
                    p.memset(Cb[z][i], 0.0, eng='pool')
            dg1 = [[p.sb("ml_dg%d_%d" % (c, k2), [128, 128], F32) for k2 in range(2)] for c in range(4)]
            Dm = [[p.sb("ml_Dm%d_%d" % (c, k2), [128, 128], BF16) for k2 in range(2)] for c in range(4)]
            sT = [[p.sb("ml_sT%d_%d" % (c, k2), [128, 128], BF16) for k2 in range(2)] for c in range(4)]
            ktil = [[p.sb("ml_ktil%d_%d" % (c, k2), [128, 128], BF16) for k2 in range(2)] for c in range(4)]
            tmpo = [p.sb("ml_tmpo%d" % i, [128, 132], F32) for i in range(4)]
            num = [p.sb("ml_num%d" % i, [128, 132], F32) for i in range(4)]
            den = [p.sb("ml_den%d" % i, [128, 1], F32) for i in range(4)]

            def ml_chain(z, i, h, c):
                b0, b1 = 2 * c, 2 * c + 1
                for step in range(NT):
                    tt = ORD[z][step]
                    ts_ = slice(tt * 128, (tt + 1) * 128)
                    k2 = step % 2
                    upd = step < NT - 1
                    dg = dg1[c][k2]
                    p.ts(dg, self.identf, Fc8[:, tt, z, h:h + 1], ALU.mult)
                    rb, hrb = p.palloc(1, b0)
                    p.mm(rb[:, 0:128], self.ones, dg, start=True, stop=False)
                    p.mm(rb[:, 0:128], self.identf, self.mask4[z][:, 0, :], start=False, stop=True)
                    sc, hsc = p.palloc(1, b0)
                    p.mm(sc[:, 0:128], kT[i][:, ts_], qT[i][:, ts_])
                    if upd:
                        kp, hkp = p.palloc(1, b0)
                        kpb = kp.bc(BF16)
                        p.tr(kpb[:, 0:128], kT[i][:, ts_], self.identb)
                    yield
                    p.act(Dm[c][k2], rb[:, 0:128], AF.Exp, bias=bs8[:, tt, z, h:h + 1])
                    p.pfree(hrb)
                    if upd:
                        p.act(ktil[c][k2], kpb[:, 0:128], AF.Identity, scale=wk8[:, tt, z, h:h + 1])
                        p.pfree(hkp)
                    yield
                    p.tt(sT[c][k2], sc[:, 0:128], Dm[c][k2], ALU.mult)
                    p.pfree(hsc)
                    yield
                    o1, ho1 = p.palloc(2, b1)
                    p.mm(o1[:, 0:129], sT[c][k2], vv[i][:, tt, 0:129])
                    o2, ho2 = p.palloc(2, b1)
                    p.mm(o2[:, 0:129], qT[i][:, ts_], Cb[z][i][:, 0:129])
                    if upd:
                        cu, hcu = p.palloc(2, b0)
                        p.mm(cu[:, 0:129], ktil[c][k2], vv[i][:, tt, 0:129])
                    yield
                    p.act(tmpo[c][:, 0:129], o2[:, 0:129], AF.Identity, scale=qs8[:, tt, z, h:h + 1])
                    p.pfree(ho2)
                    if upd:
                        p.stt(Cs[z][i][:, 0:129], Cs[z][i][:, 0:129], dc8[:, tt, z, h:h + 1], cu[:, 0:129], ALU.mult, ALU.add)
                        p.pfree(hcu)
                    yield
                    p.tt(num[c][:, 0:129], tmpo[c][:, 0:129], o1[:, 0:129], ALU.add)
                    p.pfree(ho1)
                    if upd:
                        p.copy(Cb[z][i][:, 0:129], Cs[z][i][:, 0:129], eng='pool')
                    yield
                    p.act(den[c], num[c][:, 128:129], AF.Abs)
                    yield
                    p.ts(den[c], den[c], 1.0, ALU.max)
                    p.recip(den[c], den[c])
                    yield
                    yv = yy[i][:, tt, :].on(ybufs[i][tt])
                    p.stt(yv, num[c][:, 0:128], den[c], yv, ALU.mult, ALU.add)
                    yield

            self.run_rr([ml_chain(z, i, heads[i], z * 2 + i) for z in range(2) for i in range(2)])
            m4_ = p.mark()
            scratch = (p.sb("hn_sq", [128, NT, 128], F32), p.sb("hn_ss", [128, NT], F32), p.sb("hn_yn", [128, NT, 128], BF16),
                       p.sb("hn_sg", [128, LTOK], BF16), p.sb("hn_wgt", [128, 8, 128], BF16))
            for i, h in enumerate(heads):
                self.headnorm_gate(yy[i], ybufs[i], h, C_MO + h * 128, AF.Sigmoid, hT, mixT, scratch)
            p.release(m2)
        self.ev_hT, self.ev_mixT, self.ev_mark = hT, mixT, m
        return hT, mixT, m

    def even_out(self, b, xi, xo, hT, mixT, m):
        p = self.p
        l = 0
        wout = p.sb("ev_wout", [128, 8, D], BF16)
        for c in range(8):
            p.dma(wout[:, c, :], self.w_out[c * 128:(c + 1) * 128, :], eng='pool')
        G = [p.sb("evo_G%d" % i, [128, D], F32) for i in range(2)]
        gtmp = p.sb("evo_gtmp", [128, 128], F32)
        st = self.res_state(2)
        self.gate_row(G[0], l, 2, b, gtmp)
        self.gate_row(G[1], l, 2, 4, gtmp)
        for tt in range(NT):
            yb = [self.bank(), self.bank()]
            for h in range(2):
                for c in range(8):
                    p.mm(yb[h], mixT[:, c, tt * 128:(tt + 1) * 128], wout[:, c, h * 512:(h + 1) * 512], start=(c == 0), stop=(c == 7))
            self.residual_tile(b, tt, xi, xo, yb, G[1] if tt < 2 else G[0], st[tt % 2])
        p.release(m)


def _gla(self, b, hT, mixT):
    p = self.p
    m = p.mark()
    rmask = p.sb("gl_rmask", [128, LTOK], BF16)
    p.memset(rmask, 1.0, eng='pool')
    p.memset(rmask.rr("p (n t) -> p n t", t=128)[:, :, 0:1], 0.0, eng='pool')
    w2b = p.sb("gl_w2b", [16, 2, 256], BF16)
    p.dma(w2b, self.gla_w2.rr("z r c -> r z c"), eng='pool')
    wlr = p.sb("gl_wlr", [128, 8, 32], BF16)
    p.dma(wlr, self.w_in[:, C_GLR:C_GLR + 32].rr("(c p) n -> p c n", p=128), eng='pool')
    glrT = p.sb("gl_glrT", [16, 2, LTOK], BF16)
    for z in range(2):
        for blk in range(5):
            t0, n = blk * 512, (512 if blk < 4 else 256)
            ps = self.bank()
            for kc in range(8):
                p.mm(ps[0:16, 0:n], wlr[:, kc, z * 16:(z + 1) * 16], hT[:, kc, t0:t0 + n], start=(kc == 0), stop=(kc == 7))
            p.copy(glrT[:, z, t0:t0 + n], ps[0:16, 0:n], eng='act')
    nb = p.sb("gl_nb", [128, 4], F32)
    p.ts(nb, self.col('glab'), -1.0, ALU.mult)
    for cp in range(2):
        m2 = p.mark()
        qtil = [p.sb("gl_qtil%d" % z, [128, LTOK], BF16) for z in range(2)]
        khat = [p.sb("gl_khat%d" % z, [128, LTOK], BF16) for z in range(2)]
        ktl = [p.sb("gl_ktl%d" % z, [128, LTOK], BF16) for z in range(2)]
        dec = [p.sb("gl_dec%d" % z, [128, NT], F32) for z in range(2)]
        vv = [p.sb("gl_v%d" % i, [128, NT, 128], BF16) for i in range(2)]
        yy = [p.sb("gl_y%d" % i, [128, NT, 128], F32) for i in range(2)]
        ybufs = [[Buf("gly%d_%d" % (i, t)) for t in range(NT)] for i in range(2)]
        m3 = p.mark()
        qT = p.sb("gl_qT", [128, LTOK], BF16)
        kT = p.sb("gl_kT", [128, LTOK], BF16)
        lb = p.sb("gl_l", [128, LTOK], F32)
        P = p.sb("gl_P", [128, LTOK], F32)
        tmp = p.sb("gl_tmp", [128, LTOK], F32)
        E = p.sb("gl_E", [128, LTOK], BF16)
        w = [p.sb("gl_w%d" % i, [128, 8, 128], BF16) for i in range(2)]
        for dst, c0, wi in ((qT, C_GQ + cp * 128, 0), (kT, C_GK + cp * 128, 1)):
            p.dma(w[wi], self.w_in[:, c0:c0 + 128].rr("(c p) n -> p c n", p=128), eng='pool')
            for blk in range(5):
                t0, n = blk * 512, (512 if blk < 4 else 256)
                ps = self.bank()
                for kc in range(8):
                    p.mm(ps[:, 0:n], w[wi][:, kc, :], hT[:, kc, t0:t0 + n], start=(kc == 0), stop=(kc == 7))
                p.copy(dst[:, t0:t0 + n], ps[:, 0:n], eng='act')
        P3 = P.rr("p (n t) -> p n t", t=128)
        Ptot = P3[:, :, 127:128]
        for z in range(2):
            for blk in range(5):
                t0, n = blk * 512, (512 if blk < 4 else 256)
                ps = self.bank()
                p.mm(ps[:, 0:n], w2b[:, z, cp * 128:(cp + 1) * 128], glrT[:, z, t0:t0 + n])
                p.act(lb[:, t0:t0 + n], ps[:, 0:n], AF.Exp, scale=-1.0, bias=nb[:, z * 2 + cp:z * 2 + cp + 1])
            p.act(lb, lb, AF.Ln, bias=1.0)
            p.scan(P, rmask, lb, 0.0, ALU.mult, ALU.add)
            p.act(dec[z], Ptot.rr("p n o -> p (n o)"), AF.Exp, scale=-1.0 / 16)
            t3 = tmp.rr("p (n t) -> p n t", t=128)
            if z == 0:
                p.act(E, P, AF.Exp, scale=-1.0 / 16, bias=LNS_GLA)
                p.tt(qtil[z], qT, E, ALU.mult)
                p.act(E, P, AF.Exp, scale=1.0 / 16)
                p.tt(khat[z], kT, E, ALU.mult, eng='pool')
                p.tt(t3, P3, Ptot.bcast([128, NT, 128]), ALU.subtract)
                p.act(E, tmp, AF.Exp, scale=1.0 / 16)
                p.tt(ktl[z], kT, E, ALU.mult)
            else:
                p.tt(t3, Ptot.bcast([128, NT, 128]), P3, ALU.subtract)
                p.tt(tmp, tmp, lb, ALU.add, eng='pool')
                p.act(E, tmp, AF.Exp, scale=-1.0 / 16, bias=LNS_GLA)
                p.tt(qtil[z], qT, E, ALU.mult)
                p.act(E, tmp, AF.Exp, scale=1.0 / 16)
                p.tt(khat[z], kT, E, ALU.mult, eng='pool')
                p.tt(tmp, lb, P, ALU.subtract)
                p.act(E, tmp, AF.Exp, scale=1.0 / 16)
                p.tt(ktl[z], kT, E, ALU.mult)
        for i in range(2):
            h = cp * 2 + i
            p.dma(w[i], self.w_in[:, C_GV + h * 128:C_GV + (h + 1) * 128].rr("(c p) n -> p c n", p=128), eng='pool')
            for g0 in range(0, NT, 4):
                n = min(4, NT - g0)
                ps = self.bank()
                for j in range(n):
                    tt = g0 + j
                    for kc in range(8):
                        p.mm(ps[:, j * 128:(j + 1) * 128], hT[:, kc, tt * 128:(tt + 1) * 128], w[i][:, kc, :], start=(kc == 0), stop=(kc == 7))
                p.copy(vv[i][:, g0:g0 + n, :], ps[:, 0:n * 128].rr("p (t n) -> p t n", n=128), eng='act')
            p.memset(yy[i], 0.0, eng='pool')
        p.release(m3)
        S = [[p.sb("gl_S%d%d" % (z, i), [128, 128], F32) for i in range(2)] for z in range(2)]
        Sb = [[p.sb("gl_Sb%d%d" % (z, i), [128, 128], BF16) for i in range(2)] for z in range(2)]
        for z in range(2):
            for i in range(2):
                p.memset(S[z][i], 0.0, eng='pool')
                p.memset(Sb[z][i], 0.0, eng='pool')
        AT = [[p.sb("gl_AT%d_%d" % (c, k2), [128, 128], BF16) for k2 in range(2)] for c in range(4)]
        ktok = [[p.sb("gl_ktok%d_%d" % (c, k2), [128, 64], BF16) for k2 in range(2)] for c in range(4)]

        def gl_chain(z, i, c):
            b0, b1 = 2 * c, 2 * c + 1
            pr = slice(i * 64, (i + 1) * 64)
            for step in range(NT):
                tt = ORD[z][step]
                ts_ = slice(tt * 128, (tt + 1) * 128)
                k2 = step % 2
                upd = step < NT - 1
                sc, hsc = p.palloc(1, b0)
                p.mm(sc[:, 0:128], khat[z][pr, ts_], qtil[z][pr, ts_])
                if upd:
                    kp, hkp = p.palloc(1, b0)
                    kpb = kp.bc(BF16)
                    p.tr(kpb[:, 0:64], ktl[z][pr, ts_], self.identb[pr, pr])
                yield
                p.tt(AT[c][k2], sc[:, 0:128], self.maskb[z], ALU.mult)
                p.pfree(hsc)
                if upd:
                    p.copy(ktok[c][k2], kpb[:, 0:64], eng='act')
                    p.pfree(hkp)
                yield
                o, ho = p.palloc(1, b1)
                p.mm(o[:, 0:128], AT[c][k2], vv[i][:, tt, :], start=True, stop=False)
                p.mm(o[:, 0:128], qtil[z][pr, ts_], Sb[z][i][pr, :], start=False, stop=True)
                if upd:
                    su, hsu = p.palloc(1, b0)
                    p.mm(su[pr, 0:128], ktok[c][k2], vv[i][:, tt, :])
                yield
                yv = yy[i][:, tt, :].on(ybufs[i][tt])
                p.tt(yv, yv, o[:, 0:128], ALU.add)
                p.pfree(ho)
                if upd:
                    p.stt(S[z][i][pr, :], S[z][i][pr, :], dec[z][pr, tt:tt + 1], su[pr, 0:128], ALU.mult, ALU.add)
                    p.pfree(hsu)
                yield
                if upd:
                    p.copy(Sb[z][i][pr, :], S[z][i][pr, :], eng='pool')
                yield

        self.run_rr([gl_chain(z, i, z * 2 + i) for z in range(2) for i in range(2)])
        scratch = (p.sb("hn_sq", [128, NT, 128], F32), p.sb("hn_ss", [128, NT], F32), p.sb("hn_yn", [128, NT, 128], BF16),
                   p.sb("hn_sg", [128, LTOK], BF16), p.sb("hn_wgt", [128, 8, 128], BF16))
        for i in range(2):
            h = cp * 2 + i
            self.headnorm_gate(yy[i], ybufs[i], 4 + h, C_GG + h * 128, AF.Silu, hT, mixT, scratch)
        p.release(m2)
    p.release(m)


KE.gla = _gla


C0 = float(np.exp(-0.5))
RW_LN_EPS = 64e-5


def host_small_rw(pk, inp):
    pk.add('mu', fm(inp['rw_mu'][0]))
    pk.add('w0', fm(inp['rw_w0'][0]))
    pk.add('a0', fm(inp['rw_a0'][0]))
    pk.add('kv', fm(inp['rw_kvec'][0]))
    pk.add('lnx', fm(inp['rw_lnx'][0]))


def _setup_rw(self):
    p = self.p
    NB = self.NB
    self.w_rkv = p.dram("rw_w_rkv", [3, D, D], F32, kind="ExternalInput")
    self.w_o = p.dram("rw_w_o", [D, D], F32, kind="ExternalInput")
    self.rw_w1 = p.dram("rw_w1", [2, D, 64], F32, kind="ExternalInput")
    self.rw_w2 = p.dram("rw_w2", [2, 64, D], F32, kind="ExternalInput")
    self.rw_a1 = p.dram("rw_a1", [D, 64], F32, kind="ExternalInput")
    self.rw_a2 = p.dram("rw_a2", [64, D], F32, kind="ExternalInput")
    self.rw_g1 = p.dram("rw_g1", [D, 128], F32, kind="ExternalInput")
    self.rw_g2 = p.dram("rw_g2", [128, D], F32, kind="ExternalInput")
    self.scr = []
    if getattr(self, 'dbg', False):
        self.dbg_y = p.dram("dbg_y", [8, 128, NT * 128], F32, kind="ExternalOutput")
    for b in range(NB):
        d = {}
        for nm in ('r', 'k', 'v', 'o'):
            d[nm] = p.dram("scr_%s%d" % (nm, b), [8, 128, LTOK], BF16, kind="ExternalOutput" if getattr(self, 'dbg', False) else "Internal")
            d[nm + 'buf'] = [Buf("scr_%s%d_%d" % (nm, b, c)) for c in range(8)]
        self.scr.append(d)


def _setup_rw_consts(self):
    p = self.p
    p.release(self.common_mark)
    self.strict = []
    self.M4 = []
    for z in range(2):
        st = p.sb("strict%d" % z, [128, 128], F32)
        pat, cm = ([[1, 128]], -1) if z == 0 else ([[-1, 128]], 1)
        p.aselect(st, self.ones, pat, ALU.is_gt, 0.0, 0, cm)
        self.strict.append(st)
    for z in range(2):
        m4 = p.sb("M4%d" % z, [128, 4, 128], F32)
        p.ts(m4[:, 0, :], self.strict[z], -1.0, ALU.mult, eng='pool')
        p.ts(m4[:, 1, :], self.tri[z], -1.0, ALU.mult, eng='pool')
        p.copy(m4[:, 2, :], self.strict[z], eng='pool')
        p.copy(m4[:, 3, :], self.tri[z], eng='pool')
        self.M4.append(m4)
    self.nstrictT = []
    for z in range(2):
        t = p.sb("nstrictT%d" % z, [128, 128], F32)
        p.ts(t, self.strict[1 - z], -1.0, ALU.mult, eng='pool')
        self.nstrictT.append(t)
    self.masks3 = p.sb("masks3", [128, 3, 128], BF16)
    bd64 = p.sb("bd64", [128, 128], BF16)
    p.memset(self.masks3[:, 0, :], 0.0, eng='pool')
    p.memset(bd64, 0.0, eng='pool')
    for i in range(4):
        p.memset(self.masks3[32 * i:32 * i + 32, 0, 32 * i:32 * i + 32], 1.0, eng='pool')
    for i in range(2):
        p.memset(bd64[64 * i:64 * i + 64, 64 * i:64 * i + 64], 1.0, eng='pool')
    p.tt(self.masks3[:, 1, :], bd64, self.masks3[:, 0, :], ALU.subtract, eng='pool')
    p.ts(self.masks3[:, 2, :], bd64, -1.0, ALU.mult, 1.0, ALU.add, eng='pool')
    self.bones = p.sb("bones", [128, 128], BF16)
    p.memset(self.bones, 0.0, eng='pool')
    p.memset(self.bones[0:64, 0:64], 1.0, eng='pool')
    p.memset(self.bones[64:128, 64:128], 1.0, eng='pool')
    self.omu = p.sb("omu", [128, 48], F32)
    p.ts(self.omu, self.col('mu'), -1.0, ALU.mult, 1.0, ALU.add)
    self.okv1 = p.sb("okv1", [128, 8], F32)
    p.ts(self.okv1, self.col('kv')[:, 8:16], -1.0, ALU.mult, 1.0, ALU.add)
    self.base_mark = p.mark()


def _rw_mix_block(self, xb, hT, mi, blk):
    p = self.p
    mu = self.col('mu')
    t0, n = (0, 256) if blk == 0 else (256 + (blk - 1) * 512, 512)
    for c in range(8):
        muc = mu[:, mi * 8 + c:mi * 8 + c + 1]
        p.act(xb[:, c, 0:n], hT[:, c, t0:t0 + n], AF.Identity, scale=self.omu[:, mi * 8 + c:mi * 8 + c + 1])
        kind = c // 2
        if blk == 0:
            if kind in (0, 2):
                p.stt(xb[:, c, 1:256], hT[:, c, 0:255], muc, xb[:, c, 1:256], ALU.mult, ALU.add)
            else:
                p.stt(xb[:, c, 0:255], hT[:, c, 1:256], muc, xb[:, c, 0:255], ALU.mult, ALU.add)
        else:
            r0 = (blk - 1) * 8
            xv = xb[:, c, :].rr("p (r w) -> p r w", w=64)
            hv = hT[:, c, 256:LTOK].rr("p (r w) -> p r w", w=64)
            if kind == 0:
                p.stt(xv[:, :, 1:64], hv[:, r0:r0 + 8, 0:63], muc, xv[:, :, 1:64], ALU.mult, ALU.add)
            elif kind == 1:
                p.stt(xv[:, :, 0:63], hv[:, r0:r0 + 8, 1:64], muc, xv[:, :, 0:63], ALU.mult, ALU.add)
            elif kind == 2:
                lo = 1 if r0 == 0 else 0
                p.stt(xv[:, lo:8, :], hv[:, r0 + lo - 1:r0 + 7, :], muc, xv[:, lo:8, :], ALU.mult, ALU.add)
            else:
                hi = 7 if r0 == 24 else 8
                p.stt(xv[:, 0:hi, :], hv[:, r0 + 1:r0 + hi + 1, :], muc, xv[:, 0:hi, :], ALU.mult, ALU.add)
    return t0, n


def _rw_phase1(self, b, hT, tw, ta, tg):
    p = self.p
    scr = self.scr[b]
    m = p.mark()
    xblk = [p.sb("rw_xb%d" % i, [128, 8, 512], BF16) for i in range(2)]
    Wr = [p.sb("rw_W%d" % i, [128, 8, D], BF16) for i in range(2)]
    stage = [p.sb("rw_stage%d" % i, [128, 512], BF16) for i in range(4)]
    ns = 0
    nx = 0
    for wi, (mi, kind) in enumerate(((0, 'r'), (2, 'k'), (3, 'v'))):
        W = Wr[wi % 2]
        idx = 'rkv'.index(kind)
        for c in range(8):
            p.dma(W[:, c, :], self.w_rkv[idx, c * 128:(c + 1) * 128, :], eng='pool')
        for blk in range(5):
            xb = xblk[nx % 2]
            nx += 1
            t0, n = self.rw_mix_block(xb, hT, mi, blk)
            for oc in range(8):
                ps = self.bank()
                for kc in range(8):
                    p.mm(ps[:, 0:n], W[:, kc, oc * 128:(oc + 1) * 128], xb[:, kc, 0:n], start=(kc == 0), stop=(kc == 7))
                sg = stage[ns % 4]
                ns += 1
                p.copy(sg[:, 0:n], ps[:, 0:n], eng='act')
                p.dma(scr[kind][oc, :, t0:t0 + n].on(scr[kind + 'buf'][oc]), sg[:, 0:n])
    W0 = p.sb("rw_lr0", [128, 8, 320], BF16)
    Wa = p.sb("rw_lra", [128, 8, 320], BF16)
    Wb = p.sb("rw_lrb", [128, 8, 320], BF16)
    for z in range(2):
        p.dma(W0[:, :, z * 64:(z + 1) * 64], self.rw_w1[z].rr("(c p) r -> p c r", p=128), eng='pool')
    p.dma(W0[:, :, 128:192], self.rw_a1.rr("(c p) r -> p c r", p=128), eng='pool')
    p.dma(W0[:, :, 192:320], self.rw_g1.rr("(c p) r -> p c r", p=128), eng='pool')
    mu = self.col('mu')
    for (c0, c1, mi) in ((0, 128, 1), (128, 192, 4), (192, 320, 5)):
        wdt = c1 - c0
        mub = mu[:, mi * 8:(mi + 1) * 8].rr("p (c o) -> p c o", o=1).bcast([128, 8, wdt])
        omb = self.omu[:, mi * 8:(mi + 1) * 8].rr("p (c o) -> p c o", o=1).bcast([128, 8, wdt])
        p.tt(Wa[:, :, c0:c1], W0[:, :, c0:c1], omb, ALU.mult)
        p.tt(Wb[:, :, c0:c1], W0[:, :, c0:c1], mub, ALU.mult, eng='pool')
    for blk in range(5):
        hs = xblk[nx % 2]
        nx += 1
        t0, n = (0, 256) if blk == 0 else (256 + (blk - 1) * 512, 512)
        p.memset(hs, 0.0, eng='pool')
        for c in range(8):
            kindc = c // 2
            if blk == 0:
                if kindc in (0, 2):
                    p.copy(hs[:, c, 1:256], hT[:, c, 0:255], eng='act')
                else:
                    p.copy(hs[:, c, 0:255], hT[:, c, 1:256], eng='act')
            else:
                r0 = (blk - 1) * 8
                xv = hs[:, c, :].rr("p (r w) -> p r w", w=64)
                hv = hT[:, c, 256:LTOK].rr("p (r w) -> p r w", w=64)
                if kindc == 0:
                    p.copy(xv[:, :, 1:64], hv[:, r0:r0 + 8, 0:63], eng='act')
                elif kindc == 1:
                    p.copy(xv[:, :, 0:63], hv[:, r0:r0 + 8, 1:64], eng='act')
                elif kindc == 2:
                    lo = 1 if r0 == 0 else 0
                    p.copy(xv[:, lo:8, :], hv[:, r0 + lo - 1:r0 + 7, :], eng='act')
                else:
                    hi = 7 if r0 == 24 else 8
                    p.copy(xv[:, 0:hi, :], hv[:, r0 + 1:r0 + hi + 1, :], eng='act')
        for (c0, c1, dst, fn) in ((0, 64, tw[0], AF.Tanh), (64, 128, tw[1], AF.Tanh), (128, 192, ta, None), (192, 320, tg, AF.Sigmoid)):
            M = c1 - c0
            ps = self.bank()
            for kc in range(8):
                p.mm(ps[0:M, 0:n], Wa[:, kc, c0:c1], hT[:, kc, t0:t0 + n], start=(kc == 0), stop=False)
            for kc in range(8):
                p.mm(ps[0:M, 0:n], Wb[:, kc, c0:c1], hs[:, kc, 0:n], start=False, stop=(kc == 7))
            if fn is None:
                p.copy(dst[0:M, t0:t0 + n], ps[0:M, 0:n], eng='act')
            else:
                p.act(dst[0:M, t0:t0 + n], ps[0:M, 0:n], fn)
    p.release(m)


KE.setup_rw = _setup_rw
KE.setup_rw_consts = _setup_rw_consts
KE.rw_mix_block = _rw_mix_block
KE.rw_phase1 = _rw_phase1


BLKS = [(0, 512), (512, 512), (1024, 512), (1536, 512), (2048, 256)]


def _rw_chunk(self, b, cc, tw, ta, tg, last):
    p = self.p
    scr = self.scr[b]
    m = p.mark()
    kv = self.col('kv')
    rT = p.sb("rc_rT", [128, LTOK], BF16)
    kT = p.sb("rc_kT", [128, LTOK], BF16)
    vT = p.sb("rc_vT", [128, LTOK], BF16)
    for t_, nm in ((rT, 'r'), (kT, 'k'), (vT, 'v')):
        p.dma(t_, scr[nm][cc].on(scr[nm + 'buf'][cc]))
    KR = [p.sb("rc_KR%d" % z, [128, NT, 2, 128], BF16) for z in range(2)]
    khat = [p.sb("rc_khat%d" % z, [128, LTOK], BF16) for z in range(2)]
    bhat = [p.sb("rc_bhat%d" % z, [128, LTOK], BF16) for z in range(2)]
    kT4 = [p.sb("rc_kT4%d" % z, [128, LTOK], BF16) for z in range(2)]
    nbT4 = [p.sb("rc_nbT4%d" % z, [128, LTOK], BF16) for z in range(2)]
    gam = [p.sb("rc_gam%d" % z, [128, NT], F32) for z in range(2)]
    Vtok = p.sb("rc_Vtok", [128, NT, 128], BF16)
    y = p.sb("rc_y", [128, NT, 128], F32)
    yz = [y, p.sb("rc_y1", [128, NT, 128], F32)]
    ybufs = [[Buf("rcy%d_%d" % (t, hh)) for hh in range(2)] for t in range(NT)]
    m3 = p.mark()
    aT = p.sb("rc_aT", [128, LTOK], BF16)
    kap = p.sb("rc_kap", [128, LTOK], BF16)
    bet = p.sb("rc_bet", [128, LTOK], BF16)
    sig = p.sb("rc_sig", [128, LTOK], F32)
    P = p.sb("rc_P", [128, LTOK], F32)
    t1 = p.sb("rc_t1", [128, LTOK], F32)
    t2 = p.sb("rc_t2", [128, LTOK], F32)
    E = p.sb("rc_E", [128, LTOK], BF16)
    rmask = self.rw_rmask
    w2b = self.rw_w2all[:, :, cc * 128:(cc + 1) * 128]
    a2b = self.rw_a2all[:, cc * 128:(cc + 1) * 128]
    for (t0, n) in BLKS:
        ps = self.bank()
        p.mm(ps[:, 0:n], a2b, ta[0:64, t0:t0 + n])
        p.act(aT[:, t0:t0 + n], ps[:, 0:n], AF.Sigmoid, bias=self.col('a0')[:, cc:cc + 1])
    p.ts(t1, kT, kv[:, cc:cc + 1], ALU.mult)
    p.tt(E, t1, t1, ALU.mult, eng='pool')
    for (t0, n) in BLKS:
        ps = self.bank()
        p.mm(ps[:, 0:n], self.bones, E[:, t0:t0 + n])
        p.act(t2[:, t0:t0 + n], ps[:, 0:n], AF.Sqrt)
    p.ts(t2, t2, 1e-12, ALU.max)
    p.recip(t2, t2)
    p.tt(kap, t1, t2, ALU.mult)
    p.tt(bet, kap, aT, ALU.mult, eng='pool')
    p.ts(t1, aT, kv[:, 8 + cc:8 + cc + 1], ALU.mult, self.okv1[:, cc:cc + 1], ALU.add)
    p.tt(kT, kT, t1, ALU.mult)
    P3 = P.rr("p (n t) -> p n t", t=128)
    Ptot = P3[:, :, 127:128]
    Pb = Ptot.bcast([128, NT, 128])
    t13 = t1.rr("p (n t) -> p n t", t=128)
    t23 = t2.rr("p (n t) -> p n t", t=128)
    v3 = lambda x: x.rr("p (n t) -> p n t", t=128)
    for z in range(2):
        for (t0, n) in BLKS:
            ps = self.bank()
            p.mm(ps[:, 0:n], w2b[:, z, :], tw[z][0:64, t0:t0 + n])
            p.act(sig[:, t0:t0 + n], ps[:, 0:n], AF.Sigmoid, bias=self.col('w0')[:, z * 8 + cc:z * 8 + cc + 1])
        p.scan(P, rmask, sig, 0.0, ALU.mult, ALU.add)
        p.act(gam[z], Ptot.rr("p n o -> p (n o)"), AF.Exp, scale=-C0)
        if z == 0:
            G = P
            p.tt(t1, P, sig, ALU.subtract)
            p.tt(t23, P3, Pb, ALU.subtract)
        else:
            p.tt(t13, Pb, P3, ALU.subtract)
            p.tt(t2, sig, P, ALU.subtract)
            G = sig
            p.tt(sig, t1, sig, ALU.add, eng='pool')
        p.act(E, G, AF.Exp, scale=-C0)
        p.tt(KR[z][:, :, 1, :], v3(rT), v3(E), ALU.mult)
        p.act(E, t1, AF.Exp, scale=-C0)
        p.tt(KR[z][:, :, 0, :], v3(kap), v3(E), ALU.mult, eng='pool')
        p.act(E, G, AF.Exp, scale=C0)
        p.tt(khat[z], kT, E, ALU.mult)
        p.tt(bhat[z], bet, E, ALU.mult, eng='pool')
        p.act(E, t2, AF.Exp, scale=C0)
        p.tt(kT4[z], kT, E, ALU.mult)
        p.stt(nbT4[z], bet, -1.0, E, ALU.mult, ALU.mult)
    for g0 in range(0, NT, 8):
        n = min(8, NT - g0)
        ps = self.bank()
        psb = ps.bc(BF16)
        for i in range(n):
            p.tr(psb[:, i * 128:(i + 1) * 128], vT[:, (g0 + i) * 128:(g0 + i + 1) * 128], self.identb)
        p.copy(Vtok[:, g0:g0 + n, :], psb[:, 0:n * 128].rr("p (t c) -> p t c", c=128), eng='act')
    p.release(m3)
    mring = p.mark()
    NR = 12
    A4 = [p.sb("rs_A4%d" % i, [128, 4, 128], BF16) for i in range(NR)]
    SQ = [[p.sb("rs_SQ%d_%d" % (i, j), [128, 2, 128], BF16) for j in range(2)] for i in range(NR)]
    XXr = [p.sb("rs_XX%d" % i, [128, 2, 128], BF16) for i in range(NR)]
    Q0T = [p.sb("rs_Q0T%d" % i, [128, 128], BF16) for i in range(NR)]
    QM = [p.sb("rs_QM%d" % i, [128, 3, 128], BF16) for i in range(NR)]
    QMT = [p.sb("rs_QMT%d" % i, [128, 3, 128], BF16) for i in range(NR)]
    Y1r = [p.sb("rs_Y1%d" % i, [128, 128], BF16) for i in range(NR)]
    KB = [p.sb("rs_KB%d" % i, [128, 2, 128], BF16) for i in range(6)]
    Wb = [p.sb("rs_Wb%d" % i, [128, 64], BF16) for i in range(NR)]
    Ub = [p.sb("rs_Ub%d" % i, [128, 64], BF16) for i in range(NR)]
    H = [[p.sb("rs_H%d%d" % (z, hh), [128, 64], F32) for hh in range(2)] for z in range(2)]
    Hb = [[p.sb("rs_Hb%d%d" % (z, hh), [128, 64], BF16) for hh in range(2)] for z in range(2)]
    for z in range(2):
        for hh in range(2):
            p.memset(H[z][hh], 0.0, eng='pool')
            p.memset(Hb[z][hh], 0.0, eng='pool')
    units = [(step, z, hh) for step in range(NT) for z in range(2) for hh in range(2)]
    prep = {}

    def stage_a(u, bk, r):
        step, z, hh = units[u]
        tt = ORD[z][step]
        ts_ = slice(tt * 128, (tt + 1) * 128)
        pr = slice(hh * 64, (hh + 1) * 64)
        kb = KB[(u // 2) % 6]
        if hh == 0:
            kps, hk = p.palloc(1, bk)
            psb = kps.bc(BF16)
            p.tr(psb[:, 0:128], kT4[z][:, ts_], self.identb)
            p.tr(psb[:, 128:256], nbT4[z][:, ts_], self.identb)
        kr = KR[z][pr, tt].rr("p j t -> p (j t)")
        sc1, hsc1 = p.palloc(2, bk)
        p.mm(sc1[:, 0:256], bhat[z][pr, ts_], kr)
        q0, hq0 = p.palloc(1, bk)
        p.mm(q0[:, 0:128], KR[z][pr, tt, 0, :], bhat[z][pr, ts_])
        yield
        if hh == 0:
            p.copy(kb, psb[:, 0:256].rr("p (j c) -> p j c", j=2), eng='act')
            p.pfree(hk)
        p.tt(A4[r][:, 0:2, :], sc1.rr("p (j t) -> p j t", j=2), self.M4[z][:, 0:2, :], ALU.mult)
        p.pfree(hsc1)
        p.tt(Q0T[r], q0[:, 0:128], self.nstrictT[z], ALU.mult)
        p.pfree(hq0)
        sc2, hsc2 = p.palloc(2, bk)
        p.mm(sc2[:, 0:256], khat[z][pr, ts_], kr)
        yield
        p.tt(A4[r][:, 2:4, :], sc2.rr("p (j t) -> p j t", j=2), self.M4[z][:, 2:4, :], ALU.mult)
        p.pfree(hsc2)
        p.tt(QM[r], A4[r][:, 0:1, :].bcast([128, 3, 128]), self.masks3, ALU.mult, eng='pool')
        p.tt(QMT[r], Q0T[r].rr("p (o t) -> p o t", o=1).bcast([128, 3, 128]), self.masks3, ALU.mult, eng='pool')
        XX = XXr[r]
        XXf = XX.rr("p j t -> p (j t)")
        p.tt(XX[:, 0, :], QM[r][:, 0, :], self.identb, ALU.add, eng='pool')
        p.tt(XX[:, 1, :], QMT[r][:, 0, :], self.identb, ALU.add, eng='pool')
        yield
        cq, ct = QM[r][:, 0, :], QMT[r][:, 0, :]
        xq, xt = XX[:, 0, :], XX[:, 1, :]
        ps, h1 = p.palloc(2, bk)
        p.mm(ps[:, 0:128], ct, cq)
        p.mm(ps[:, 128:256], cq, ct)
        yield
        nxt = SQ[r][0]
        p.copy(nxt.rr("p j t -> p (j t)"), ps[:, 0:256], eng='act')
        p.pfree(h1)
        cq, ct = nxt[:, 0, :], nxt[:, 1, :]
        yield
        for lev in range(1, 5):
            ps2, h2 = p.palloc(2, bk)
            p.mm(ps2[:, 0:128], ct, xq)
            p.mm(ps2[:, 128:256], xq, ct)
            if lev < 4:
                ps, h1 = p.palloc(2, bk)
                p.mm(ps[:, 0:128], ct, cq)
                p.mm(ps[:, 128:256], cq, ct)
            yield
            p.tt(XXf, XXf, ps2[:, 0:256], ALU.add)
            p.pfree(h2)
            if lev < 4:
                nxt = SQ[r][lev % 2]
                p.copy(nxt.rr("p j t -> p (j t)"), ps[:, 0:256], eng='act')
                p.pfree(h1)
                cq, ct = nxt[:, 0, :], nxt[:, 1, :]
            yield
        for lvl in (1, 2):
            C, CT = QM[r][:, lvl, :], QMT[r][:, lvl, :]
            ps, h1 = p.palloc(1, bk)
            p.mm(ps[:, 0:128], CT, xq)
            yield
            p.copy(Y1r[r], ps[:, 0:128], eng='act')
            p.pfree(h1)
            yield
            ps2, h2 = p.palloc(2, bk)
            p.mm(ps2[:, 0:128], xt, Y1r[r])
            if lvl == 1:
                p.mm(ps2[:, 128:256], Y1r[r], xt)
            yield
            if lvl == 1:
                p.tt(XXf, XXf, ps2[:, 0:256], ALU.add)
            else:
                p.tt(XX[:, 0, :], XX[:, 0, :], ps2[:, 0:128], ALU.add)
            p.pfree(h2)
            yield
        prep[u] = (r, kb, XX[:, 0, :])

    def stage_b(u, bk, bq):
        step, z, hh = units[u]
        tt = ORD[z][step]
        ts_ = slice(tt * 128, (tt + 1) * 128)
        pr = slice(hh * 64, (hh + 1) * 64)
        cs = slice(hh * 64, (hh + 1) * 64)
        r, kb, xfin = prep.pop(u)
        a4 = A4[r]
        hb = Hb[z][hh]
        vt = Vtok[:, tt, cs]
        while True:
            try:
                w, hw = p.palloc(1, bk)
                break
            except RuntimeError:
                yield
        p.mm(w[:, 0:64], KR[z][pr, tt, 0, :], hb[pr, :], start=True, stop=False)
        p.mm(w[:, 0:64], a4[:, 2, :], vt, start=False, stop=True)
        yield
        p.copy(Wb[r], w[:, 0:64], eng='act')
        p.pfree(hw)
        yield
        while True:
            try:
                uu, hu_ = p.palloc(1, bk)
                break
            except RuntimeError:
                yield
        p.mm(uu[:, 0:64], xfin, Wb[r])
        yield
        p.copy(Ub[r], uu[:, 0:64], eng='act')
        p.pfree(hu_)
        yield
        while True:
            try:
                yb, hy = p.palloc(1, bk)
                break
            except RuntimeError:
                yield
        p.mm(yb[:, 0:64], KR[z][pr, tt, 1, :], hb[pr, :], start=True, stop=False)
        p.mm(yb[:, 0:64], a4[:, 3, :], vt, start=False, stop=False)
        p.mm(yb[:, 0:64], a4[:, 1, :], Ub[r], start=False, stop=True)
        if step < NT - 1:
            while True:
                try:
                    hu, hh_ = p.palloc(1, bk)
                    break
                except RuntimeError:
                    yield
            p.mm(hu[pr, 0:64], kb[:, 0, cs], vt, start=True, stop=False)
            p.mm(hu[pr, 0:64], kb[:, 1, cs], Ub[r], start=False, stop=True)
        yield
        if step < NT - 1:
            p.stt(H[z][hh][pr, :], H[z][hh][pr, :], gam[z][pr, tt:tt + 1], hu[pr, 0:64], ALU.mult, ALU.add)
            p.pfree(hh_)
            p.copy(hb[pr, :], H[z][hh][pr, :], eng='pool')
        p.copy(yz[z][:, tt, cs].on(ybufs[tt][hh]), yb[:, 0:64], eng='act')
        p.pfree(hy)
        yield

    NA = int(os.environ.get('RW_NA', '6'))
    BFIRST = int(os.environ.get('RW_BFIRST', '0'))
    nU = len(units)
    free_r = list(range(NR))
    a_banks = list(range(NA))
    NBB = 8 - NA
    act_a = []
    act_b = []
    a_done = set()
    b_emitted = set()
    next_a = 0
    next_b = [0, 1, 2, 3]
    rmap = {}
    it_ = 0
    last_start = -100
    STAG = int(os.environ.get('RW_STAG', '4'))
    BPRIO = int(os.environ.get('RW_BPRIO', '1'))
    LOOKA = int(os.environ.get('RW_LOOK', '10'))
    while len(b_emitted) < nU:
        it_ += 1
        while next_a < nU and a_banks and free_r and next_a < min(next_b) + LOOKA and it_ - last_start >= STAG:
            last_start = it_
            bk = a_banks.pop(0)
            r = free_r.pop(0)
            rmap[next_a] = r
            act_a.append((stage_a(next_a, bk, r), next_a, bk))
            next_a += 1
        for j in range(4):
            u = next_b[j]
            if u < nU and u in a_done and not any(x[2] == j for x in act_b):
                act_b.append((stage_b(u, NA + (j % NBB), None), u, j))
                next_b[j] = u + 4
        def adv_a():
            nonlocal act_a
            nxt_a = []
            for g, u, bk in act_a:
                try:
                    next(g)
                    nxt_a.append((g, u, bk))
                except StopIteration:
                    a_done.add(u)
                    a_banks.append(bk)
            act_a = nxt_a

        def adv_b():
            nonlocal act_b
            for _rep in range(BPRIO):
                nxt_b = []
                for g, u, j in act_b:
                    try:
                        next(g)
                        nxt_b.append((g, u, j))
                    except StopIteration:
                        b_emitted.add(u)
                        free_r.append(rmap.pop(u))
                act_b = nxt_b

        if BFIRST:
            adv_b()
            adv_a()
        else:
            adv_a()
            adv_b()
    p.release(mring)
    if getattr(self, 'dbg', False):
        for tt in range(NT):
            for hh in range(2):
                p.tt(y[:, tt, hh * 64:(hh + 1) * 64], y[:, tt, hh * 64:(hh + 1) * 64].on(ybufs[tt][hh]), y[:, tt, hh * 64:(hh + 1) * 64].on(ybufs[tt][hh]), ALU.max)
        p.dma(self.dbg_y[cc], y.rr("p t c -> p (t c)"))
    g2b = self.rw_g2all[:, cc * 128:(cc + 1) * 128]
    s1 = p.sb("rp_s1", [128, 36], F32)
    s2 = p.sb("rp_s2", [128, 36], F32)
    sq = p.sb("rp_sq", [128, 36, 64], F32)
    yn = p.sb("rp_yn", [128, NT, 128], BF16)
    lnT = p.sb("rp_lnT", [128, LTOK], BF16)
    prod = p.sb("rp_prod", [128, LTOK], BF16)
    y3 = y.rr("p t (h c) -> p (t h) c", c=64)
    allb = [ybufs[t_][h_] for t_ in range(NT) for h_ in range(2)]
    yall = V(y.ap, allb)
    y1all = V(yz[1].ap, allb)
    p.tt(yall, yall, y1all, ALU.add)
    p.tt(sq, V(y3.ap, allb), V(y3.ap, allb), ALU.mult)
    p.reduce(s2, sq, ALU.add)
    p.reduce(s1, y3, ALU.add)
    p.ts(s1, s1, 1.0 / 64, ALU.mult)
    p.tt(sq[:, :, 0], s1, s1, ALU.mult)
    p.stt(s2, s2, 1.0 / 64, sq[:, :, 0], ALU.mult, ALU.subtract)
    p.act(s2, s2, AF.Sqrt, bias=RW_LN_EPS)
    p.recip(s2, s2)
    yn3 = yn.rr("p t (h c) -> p (t h) c", c=64)
    p.tt(sq, y3, s1.rr("p (n o) -> p n o", o=1).bcast([128, 36, 64]), ALU.subtract)
    p.tt(yn3, sq, s2.rr("p (n o) -> p n o", o=1).bcast([128, 36, 64]), ALU.mult)
    lnx = self.col('lnx')
    for g0 in range(0, NT, 8):
        n = min(8, NT - g0)
        ps = self.bank()
        psb = ps.bc(BF16)
        for i in range(n):
            p.tr(psb[:, i * 128:(i + 1) * 128], yn[:, g0 + i, :], self.identb)
        p.ts(lnT[:, g0 * 128:(g0 + n) * 128], psb[:, 0:n * 128], lnx[:, cc:cc + 1], ALU.mult, lnx[:, 8 + cc:8 + cc + 1], ALU.add)
    p.stt(prod, rT, kv[:, 16 + cc:16 + cc + 1], kT, ALU.mult, ALU.mult)
    stage = [p.sb("rp_stage%d" % i, [128, 512], BF16) for i in range(2)]
    tmpb = [p.sb("rp_tmp%d" % i, [128, 512], F32) for i in range(2)]
    for i, (t0, n) in enumerate(BLKS):
        psA = self.bank()
        p.mm(psA[:, 0:n], self.bones, prod[:, t0:t0 + n])
        psG = self.bank()
        p.mm(psG[:, 0:n], g2b, tg[:, t0:t0 + n])
        tb = tmpb[i % 2]
        p.tt(tb[:, 0:n], psA[:, 0:n], vT[:, t0:t0 + n], ALU.mult)
        p.tt(tb[:, 0:n], tb[:, 0:n], lnT[:, t0:t0 + n], ALU.add, eng='pool')
        sg = stage[i % 2]
        p.tt(sg[:, 0:n], psG[:, 0:n], tb[:, 0:n], ALU.mult)
        p.dma(scr['o'][cc, :, t0:t0 + n].on(scr['obuf'][cc]), sg[:, 0:n])
    p.release(m)


def _rw_mixer(self, b, xi, xo, last=True):
    p = self.p
    l = 1
    m = p.mark()
    self.rw_rmask = p.sb("rw_rmask", [128, LTOK], BF16)
    p.memset(self.rw_rmask, 1.0, eng='pool')
    p.memset(self.rw_rmask.rr("p (n t) -> p n t", t=128)[:, :, 0:1], 0.0, eng='pool')
    self.rw_w2all = p.sb("rw_w2all", [64, 2, D], BF16)
    p.dma(self.rw_w2all, self.rw_w2.rr("z r c -> r z c"), eng='pool')
    self.rw_a2all = p.sb("rw_a2all", [64, D], BF16)
    p.dma(self.rw_a2all, self.rw_a2, eng='pool')
    self.rw_g2all = p.sb("rw_g2all", [128, D], BF16)
    p.dma(self.rw_g2all, self.rw_g2, eng='pool')
    tw = [p.sb("rw_tw%d" % z, [128, LTOK], BF16) for z in range(2)]
    ta = p.sb("rw_ta", [128, LTOK], BF16)
    tg = p.sb("rw_tg", [128, LTOK], BF16)
    mh = p.mark()
    hT = p.sb("rw_hT", [128, 8, LTOK], BF16)
    self.prenorm(b, xi, l, 0, hT)
    self.rw_phase1(b, hT, tw, ta, tg)
    p.release(mh)
    for cc in range(8):
        self.rw_chunk(b, cc, tw, ta, tg, last)
    p.release(mh)
    wo = p.sb("rw_wo", [128, 8, D], BF16)
    for c in range(8):
        p.dma(wo[:, c, :], self.w_o[c * 128:(c + 1) * 128, :], eng='pool')
    oT = p.sb("rw_oT", [128, 8, LTOK], BF16)
    for c in range(8):
        p.dma(oT[:, c, :], self.scr[b]['o'][c].on(self.scr[b]['obuf'][c]))
    G = [p.sb("rwo_G%d" % i, [128, D], F32) for i in range(2)]
    gtmp = p.sb("rwo_gtmp", [128, 128], F32)
    st = self.res_state(2)
    self.gate_row(G[0], l, 2, b, gtmp)
    if not last:
        self.gate_row(G[1], l, 2, 4, gtmp)
    for tt in (range(2, NT) if last else range(NT)):
        yb = [self.bank(), self.bank()]
        for h in range(2):
            for c in range(8):
                p.mm(yb[h], oT[:, c, tt * 128:(tt + 1) * 128], wo[:, c, h * 512:(h + 1) * 512], start=(c == 0), stop=(c == 7))
        self.residual_tile(b, tt, xi, xo, yb, G[1] if tt < 2 else G[0], st[tt % 2])
    p.release(m)


KE.rw_chunk = _rw_chunk
KE.rw_mixer = _rw_mixer


_NC_CACHE = {}
W_NAMES = ('w_mod', 'ffn_w_up', 'ffn_w_down')
EV_NAMES = ('ev_w_in', 'ev_w_out', 'ev_gla_w2')
RW_NAMES = ('rw_w_rkv', 'rw_w_o', 'rw_w1', 'rw_w2', 'rw_a1', 'rw_a2', 'rw_g1', 'rw_g2')


def build_program(NB, pk):
    k = KE(NB, pk.cols, pk.n, dbg=False)
    k.setup_even()
    k.setup_rw()
    k.mod_stage(0)
    for b in range(NB):
        hT, mixT, m = k.even_mixer(b, 0, 1)
        k.gla(b, hT, mixT)
        k.even_out(b, 0, 1, hT, mixT, m)
        k.ffn(b, 0, 1, 2, do_ctx=True)
    k.setup_rw_consts()
    k.mod_stage(1)
    for b in range(NB):
        k.rw_mixer(b, 2, 3, last=True)
        k.ffn(b, 1, 3, 4, do_ctx=False)
    return k.p.finish()


def kernel(**inp):
    inp = {k_: np.asarray(v_, dtype=np.float32) for k_, v_ in inp.items()}
    NCORE = 8
    B = inp['x'].shape[0]
    NB = B // NCORE
    pk = host_small(inp)
    host_small_even(pk, inp)
    host_small_rw(pk, inp)
    small = pk.pack()
    rows = np.zeros((16, D), np.float32)
    rows[0:8] = inp['norm_g'].reshape(8, D)
    rows[8, :16] = inp['ev_b_gates'][0]
    nc = build_program(NB, pk)
    shared = {"small": small, "rows": rows}
    for nm in W_NAMES:
        shared[nm] = np.ascontiguousarray(inp[nm])
    for nm in EV_NAMES + RW_NAMES:
        shared[nm] = np.ascontiguousarray(inp[nm][0])
    in_maps = []
    for c in range(NCORE):
        sl = slice(c * NB, (c + 1) * NB)
        cc = np.zeros((6, D), np.float32)
        cc[0:NB] = inp['c'][sl]
        cc[4] = inp['c_ctx']
        ccT = np.ascontiguousarray(cc.reshape(6, 8, 128).transpose(2, 1, 0).reshape(128, 48))
        xcat = np.ascontiguousarray(np.concatenate([inp['ctx'][sl], inp['x'][sl]], axis=1))
        d = dict(shared)
        d["xcat"] = xcat
        d["ccT"] = ccT
        in_maps.append(d)
    res = run_bass_kernel_spmd(nc, in_maps, core_ids=list(range(NCORE)))
    out = np.concatenate([np.asarray(r["out"]) for r in res.results], axis=0)
    return out.astype(np.float32)
```

```python
import numpy as np
import concourse.bass as bass
import concourse.mybir as mybir
from concourse.bass_utils import run_bass_kernel_spmd

F32 = mybir.dt.float32
BF16 = mybir.dt.bfloat16
AF = mybir.ActivationFunctionType
ALU = mybir.AluOpType
AX = mybir.AxisListType
ENG = ('sp', 'act', 'dve', 'pool', 'pe')
DSZ = {F32: 4, BF16: 2}
SAME_SYNC = True
NDMASEM = 8
FUSE_WAIT = True
NFUSE = 1


class Buf:
    __slots__ = ('name', 'w', 'r', 'excl', 'subs')

    def __init__(self, name, excl=False):
        self.name = name
        self.w = None
        self.r = {}
        self.excl = excl
        self.subs = []


class V:
    __slots__ = ('ap', 'buf')

    def __init__(self, ap, buf):
        self.ap = ap
        self.buf = buf

    def __getitem__(self, idx):
        return V(self.ap[idx], self.buf)

    def rr(self, pat, **kw):
        return V(self.ap.rearrange(pat, **kw), self.buf)

    def bc(self, dt):
        return V(self.ap.bitcast(dt), self.buf)

    def on(self, buf):
        if isinstance(self.buf, Buf) and buf is not self.buf and buf not in self.buf.subs:
            self.buf.subs.append(buf)
        return V(self.ap, buf)

    def bcast(self, shape):
        return V(self.ap.broadcast_to(list(shape)), self.buf)

    def pbcast(self, n):
        return V(self.ap.partition_broadcast(n), self.buf)

    @property
    def shape(self):
        return tuple(self.ap.shape)


class Prog:
    def __init__(self):
        nc = self.nc = bass.Bass("TRN2", target_bir_lowering=False)
        self.q = {e: [] for e in ENG}
        self.cnt = {e: 0 for e in ENG}
        self.sem = {e: nc.alloc_semaphore('sem_' + e) for e in ENG}
        self.seen = {e: {} for e in ENG}
        self.dsem = [nc.alloc_semaphore('dsem%d' % i) for i in range(NDMASEM)]
        self.dval = [0] * NDMASEM
        self.dnext = 0
        self.sb_off = 16512
        self.sb_max = 0
        self.nalloc = 0
        self.ninst = 0
        self.regions = []
        self.clock = {}
        self.tokidx = {}
        self.ntok = 0

    def dram(self, name, shape, dt, kind="Internal"):
        t = self.nc.dram_tensor(name, list(shape), dt, kind=kind)
        return V(t.ap(), Buf(name))

    def sb(self, name, shape, dt, nbuf=None):
        per = int(np.prod(shape[1:])) * DSZ[dt]
        per = (per + 63) // 64 * 64
        off = self.sb_off
        self.sb_off += per
        self.sb_max = max(self.sb_max, self.sb_off)
        assert self.sb_off <= 229344, (name, self.sb_off)
        self.nalloc += 1
        t = self.nc.alloc_sbuf_tensor_at("%s_%d" % (name, self.nalloc), list(shape), dt, offset=off)
        nb = Buf(name)
        lo, hi = off, off + per
        keep = []
        for (a, b_, ob) in self.regions:
            if a < hi and lo < b_:
                toks = []
                for ob2 in [ob] + ob.subs:
                    toks += list(ob2.r.values())
                    if ob2.w is not None:
                        toks.append(ob2.w)
                for tk in toks:
                    k = id(tk[0])
                    if k not in nb.r or nb.r[k][1] < tk[1]:
                        nb.r[k] = tk
                if a < lo:
                    keep.append((a, lo, ob))
                if b_ > hi:
                    keep.append((hi, b_, ob))
            else:
                keep.append((a, b_, ob))
        keep.append((lo, hi, nb))
        self.regions = keep
        return V(t.ap(), nb)

    def mark(self):
        return self.sb_off

    def release(self, m):
        self.sb_off = m

    def psum_banks(self):
        banks = []
        self.pslot_bufs = []
        self.pslot_free = [True] * 32
        for i in range(8):
            t = self.nc.alloc_psum_tensor("psb%d" % i, [128, 512], F32)
            bl = Buf("psb%d" % i, excl=True)
            self.pslot_bufs.append(bl)
            banks.append(V(t.ap(), bl))
        self.pbanks = banks
        return banks

    def palloc(self, nq, bank=None):
        banks = range(8) if bank is None else (bank,)
        for bk in banks:
            for q0 in range(0, 4, nq):
                if all(self.pslot_free[bk * 4 + q0 + j] for j in range(nq)):
                    for j in range(nq):
                        self.pslot_free[bk * 4 + q0 + j] = False
                    v = V(self.pbanks[bk].ap[:, q0 * 128:(q0 + nq) * 128], self.pslot_bufs[bk])
                    return v, (bk, q0, nq)
        raise RuntimeError("out of PSUM slots")

    def pfree(self, h):
        bk, q0, nq = h
        for j in range(nq):
            self.pslot_free[bk * 4 + q0 + j] = True

    def _emit(self, eng, fn, reads, writes, dma=False):
        waits = {}

        def need(tok, waw_pe=False):
            if tok is None:
                return
            sem, val, te = tok
            if te == eng and not dma:
                if eng == 'pe' or not SAME_SYNC:
                    return
            k = id(sem)
            if k not in waits or waits[k][1] < val:
                waits[k] = (sem, val)

        ex = [b for b in reads if b.excl and b not in writes]
        if ex:
            reads = [b for b in reads if not b.excl]
            writes = list(writes) + ex
        for b in reads:
            need(b.w)
        for b in writes:
            need(b.w)
            for t in b.r.values():
                need(t)
        if dma:
            k = self.dnext
            self.dnext = (self.dnext + 1) % NDMASEM
            if self.dval[k] > 0:
                need((self.dsem[k], self.dval[k], 'dma'))
            self.dval[k] += 16
            tok = (self.dsem[k], self.dval[k], 'dma')
            inc = (self.dsem[k], 16)
        else:
            self.cnt[eng] += 1
            tok = (self.sem[eng], self.cnt[eng], eng)
            inc = (self.sem[eng], 1)
        seen = self.seen[eng]
        final = []
        cand = sorted(waits.items(), key=lambda kv: -self.tokidx.get((kv[0], kv[1][1]), 0))
        for k, (sem, val) in cand:
            if seen.get(k, 0) < val:
                seen[k] = val
                final.append((sem, val))
                ck = self.clock.get((k, val))
                if ck:
                    for kk, vv in ck.items():
                        if seen.get(kk, 0) < vv:
                            seen[kk] = vv
        self.ntok += 1
        self.tokidx[(id(tok[0]), tok[1])] = self.ntok
        self.clock[(id(tok[0]), tok[1])] = dict(seen)
        for b in reads:
            b.r[id(tok[0])] = tok
        for b in writes:
            b.w = tok
            b.r = {}
        self.q[eng].append((final, fn, inc))
        self.ninst += 1
        self.nwait = getattr(self, 'nwait', 0) + len(final)

    def barrier(self):
        toks = [(self.sem[e], self.cnt[e]) for e in ENG if self.cnt[e] > 0]
        toks += [(self.dsem[k], self.dval[k]) for k in range(NDMASEM) if self.dval[k] > 0]
        for e in ENG:
            final = []
            for sem, val in toks:
                if sem is self.sem[e]:
                    continue
                if self.seen[e].get(id(sem), 0) < val:
                    self.seen[e][id(sem)] = val
                    final.append((sem, val))
            if final:
                self.q[e].append((final, None, None))

    def finish(self):
        self.barrier()
        nc = self.nc
        q = self.q

        def replay(e, lst):
            for waits, fn, inc in lst:
                if fn is None or inc[1] == 16 or not FUSE_WAIT:
                    for sem, val in waits:
                        e.wait_ge(sem, val)
                    if fn is not None:
                        ins = fn(e)
                        ins.then_inc(inc[0], inc[1])
                    continue
                for sem, val in waits[:-NFUSE]:
                    e.wait_ge(sem, val)
                ins = fn(e)
                for sem, val in waits[-NFUSE:]:
                    ins._wait_ge(sem, val)
                ins.then_inc(inc[0], inc[1])

        with nc.Block() as block:
            @block.sync
            def _(e):
                replay(e, q['sp'])

            @block.scalar
            def _(e):
                replay(e, q['act'])

            @block.vector
            def _(e):
                replay(e, q['dve'])

            @block.gpsimd
            def _(e):
                replay(e, q['pool'])

            @block.tensor
            def _(e):
                replay(e, q['pe'])
        return nc

    @staticmethod
    def _a(x):
        return x.ap if isinstance(x, V) else x

    @staticmethod
    def _bufs(*xs):
        out = []
        for x in xs:
            if isinstance(x, V):
                bl = x.buf if isinstance(x.buf, (list, tuple)) else (x.buf,)
                for b in bl:
                    if b not in out:
                        out.append(b)
        return out

    def dma(self, out, in_, eng='sp', **kw):
        o, i = out.ap, in_.ap
        self._emit(eng, lambda e: e.dma_start(out=o, in_=i, **kw), self._bufs(in_), self._bufs(out), dma=True)

    def mm(self, out, lhsT, rhs, start=True, stop=True):
        o, l, r = out.ap, lhsT.ap, rhs.ap
        self._emit('pe', lambda e: e.matmul(o, lhsT=l, rhs=r, start=start, stop=stop),
                   self._bufs(lhsT, rhs), self._bufs(out))

    def tr(self, out, in_, ident):
        o, i, d = out.ap, in_.ap, ident.ap
        self._emit('pe', lambda e: e.transpose(o, i, d), self._bufs(in_, ident), self._bufs(out))

    def act(self, out, in_, func, bias=None, scale=None, accum=None):
        kw = {}
        if bias is not None:
            kw['bias'] = self._a(bias)
        if scale is not None:
            kw['scale'] = self._a(scale)
        if accum is not None:
            kw['accum_out'] = accum.ap
        o, i = out.ap, in_.ap
        self._emit('act', lambda e: e.activation(out=o, in_=i, func=func, **kw),
                   self._bufs(in_, bias, scale), self._bufs(out, accum))

    def tt(self, out, in0, in1, op, eng='dve'):
        o, a, b = out.ap, in0.ap, in1.ap
        self._emit(eng, lambda e: e.tensor_tensor(out=o, in0=a, in1=b, op=op),
                   self._bufs(in0, in1), self._bufs(out))

    def ts(self, out, in0, s1, op0, s2=None, op1=None, eng='dve', accum=None):
        o, a = out.ap, in0.ap
        a1, a2 = self._a(s1), self._a(s2)
        kw = {}
        if op1 is not None:
            kw['op1'] = op1
        if accum is not None:
            kw['accum_out'] = accum.ap
        self._emit(eng, lambda e: e.tensor_scalar(out=o, in0=a, scalar1=a1, scalar2=a2, op0=op0, **kw),
                   self._bufs(in0, s1, s2), self._bufs(out, accum))

    def stt(self, out, in0, scalar, in1, op0, op1, eng='dve'):
        o, a, b = out.ap, in0.ap, in1.ap
        s = self._a(scalar)
        self._emit(eng, lambda e: e.scalar_tensor_tensor(out=o, in0=a, scalar=s, in1=b, op0=op0, op1=op1),
                   self._bufs(in0, scalar, in1), self._bufs(out))

    def copy(self, out, in_, eng='dve'):
        o, i = out.ap, in_.ap
        if eng == 'act':
            self._emit('act', lambda e: e.activation(out=o, in_=i, func=AF.Copy), self._bufs(in_), self._bufs(out))
        else:
            self._emit(eng, lambda e: e.tensor_copy(out=o, in_=i), self._bufs(in_), self._bufs(out))

    def memset(self, out, val, eng='dve'):
        o = out.ap
        self._emit(eng, lambda e: e.memset(o, val), [], self._bufs(out))

    def reduce(self, out, in_, op, axis=AX.X, eng='dve'):
        o, i = out.ap, in_.ap
        self._emit(eng, lambda e: e.tensor_reduce(out=o, in_=i, axis=axis, op=op), self._bufs(in_), self._bufs(out))

    def recip(self, out, in_):
        o, i = out.ap, in_.ap
        self._emit('dve', lambda e: e.reciprocal(out=o, in_=i), self._bufs(in_), self._bufs(out))

    def aselect(self, out, in_, pattern, cmp, fill, base, cm):
        o, i = out.ap, in_.ap
        self._emit('pool', lambda e: e.affine_select(out=o, in_=i, pattern=pattern, compare_op=cmp, fill=fill,
                                                     base=base, channel_multiplier=cm),
                   self._bufs(in_), self._bufs(out))

    def scan(self, out, d0, d1, init, op0, op1):
        o, a, b = out.ap, d0.ap, d1.ap
        self._emit('dve', lambda e: e.tensor_tensor_scan(out=o, data0=a, data1=b, initial=init, op0=op0, op1=op1),
                   self._bufs(d0, d1), self._bufs(out))


import os

D = 1024
NT = 18
LTOK = 2304
EPS = 1e-6
DFF = 2816
NJ = 22


class Packer:
    def __init__(self):
        self.cols = {}
        self.n = 0
        self.arrs = []

    def add(self, name, arr):
        arr = np.ascontiguousarray(arr, dtype=np.float32).reshape(128, -1)
        self.cols[name] = (self.n, arr.shape[1])
        self.n += arr.shape[1]
        self.arrs.append(arr)

    def pack(self):
        return np.ascontiguousarray(np.concatenate(self.arrs, axis=1))


def fm(v):
    v = np.asarray(v, dtype=np.float32)
    lead = v.shape[:-1]
    c = v.shape[-1] // 128
    v = v.reshape(lead + (c, 128))
    return np.moveaxis(v, -1, 0).reshape(128, -1)


def host_small(inp, layer_cols_only=False):
    pk = Packer()
    for l in range(2):
        pk.add('bmod%d' % l, fm(inp['b_mod'][l]))
        pk.add('ng%d' % l, fm(inp['norm_g'][l]))
        pk.add('cw%d' % l, np.moveaxis(inp['ffn_conv_w'][l].reshape(9, NJ, 128), 2, 0).transpose(0, 2, 1).reshape(128, NJ * 9))
        pk.add('cb%d' % l, fm(inp['ffn_conv_b'][l]))
    return pk


class K:
    def __init__(self, NB, small_cols, nsmall, dbg=False):
        self.NB = NB
        self.dbg = dbg
        p = self.p = Prog()
        self.sc = small_cols
        self.X0 = p.dram("xcat", [NB, LTOK, D], F32, kind="ExternalInput")
        self.ccT_d = p.dram("ccT", [128, 8 * 6], F32, kind="ExternalInput")
        self.small_d = p.dram("small", [128, nsmall], F32, kind="ExternalInput")
        self.rows_d = p.dram("rows", [16, D], F32, kind="ExternalInput")
        self.w_mod = p.dram("w_mod", [2, D, 6 * D], F32, kind="ExternalInput")
        self.w_up = p.dram("ffn_w_up", [2, D, 2 * DFF], F32, kind="ExternalInput")
        self.w_down = p.dram("ffn_w_down", [2, DFF, D], F32, kind="ExternalInput")
        self.Xs = [self.X0]
        for i in range(1, 4):
            self.Xs.append(p.dram("xs%d" % i, [NB, LTOK, D], F32, kind="ExternalOutput" if dbg else "Internal"))
        self.out = p.dram("out", [NB, 2048, D], F32, kind="ExternalOutput")
        self.xbufs = [[[Buf("x%d_%d_%d" % (i, b, t)) for t in range(NT)] for b in range(NB)] for i in range(5)]
        self.banks = p.psum_banks()
        self.nbank = 0
        self.small = p.sb("small", [128, nsmall], F32)
        p.dma(self.small, self.small_d)
        self.identf = p.sb("identf", [128, 128], F32)
        p.memset(self.identf, 1.0, eng='pool')
        p.aselect(self.identf, self.identf, [[-1, 128]], ALU.is_equal, 0.0, 0, 1)
        self.identb = p.sb("identb", [128, 128], BF16)
        p.copy(self.identb, self.identf, eng='pool')
        self.scT = p.sb("scT", [128, 48], F32)
        p.dma(self.scT, self.ccT_d)
        p.act(self.scT, self.scT, AF.Silu)
        self.modT = p.sb("modT", [128, 48 * 6], F32)
        self.A1 = p.sb("A1", [128, 48], F32)
        self.A2 = p.sb("A2", [128, 48], F32)
        self.base_mark = p.mark()

    def bank(self):
        b = self.banks[self.nbank]
        self.nbank = (self.nbank + 1) % 8
        return b

    @staticmethod
    def run_rr(gens, stagger=0):
        pending = list(gens)
        active = []
        it_ = 0
        while active or pending:
            if pending and (stagger == 0 or it_ % stagger == 0 or not active):
                if stagger == 0:
                    active += pending
                    pending = []
                else:
                    active.append(pending.pop(0))
            it_ += 1
            nxt_ = []
            for g in active:
                try:
                    next(g)
                    nxt_.append(g)
                except StopIteration:
                    pass
            active = nxt_

    def col(self, name, a=None, b=None):
        o, w = self.sc[name]
        if a is None:
            return self.small[:, o:o + w]
        return self.small[:, o + a:o + b]

    def mod_stage(self, l):
        p = self.p
        m = p.mark()
        wst = [p.sb("wmod_st%d" % i, [128, 8, 512], F32) for i in range(2)]
        ps = self.bank()
        for blk in range(12):
            w = wst[blk % 2]
            p.dma(w, self.w_mod[l, :, blk * 512:(blk + 1) * 512].rr("(c p) n -> p c n", p=128))
            for f in range(4):
                fc = blk * 4 + f
                for kc in range(8):
                    p.mm(ps[:, fc * 6:(fc + 1) * 6], w[:, kc, f * 128:(f + 1) * 128], self.scT[:, kc * 6:(kc + 1) * 6],
                         start=(kc == 0), stop=(kc == 7))
        p.tt(self.modT.rr("p (c r) -> p c r", r=6), ps[:, 0:288].rr("p (c r) -> p c r", r=6),
             self.col('bmod%d' % l).rr("p (c o) -> p c o", o=1).bcast([128, 48, 6]), ALU.add)
        mv = self.modT.rr("p (i c r) -> p i c r", i=6, c=8)
        ng = self.col('ng%d' % l).rr("p (i c o) -> p i c o", i=4, o=1)
        for A, mi, gi in ((self.A1, 1, 0), (self.A2, 4, 2)):
            Av = A.rr("p (c r) -> p c r", r=6)
            p.ts(Av, mv[:, mi], 1.0, ALU.add)
            p.tt(Av, Av, ng[:, gi].bcast([128, 8, 6]), ALU.mult)
        p.release(m)

    def modvec(self, l, which):
        mv = self.modT.rr("p (i c r) -> p i c r", i=6, c=8)
        if which == 0:
            return self.A1.rr("p (c r) -> p c r", r=6), mv[:, 0]
        return self.A2.rr("p (c r) -> p c r", r=6), mv[:, 3]

    def gate_row(self, dst, l, gi, row, tmp):
        p = self.p
        mv = self.modT.rr("p (i c r) -> p i c r", i=6, c=8)
        nrow = l * 4 + (1 if gi == 2 else 3)
        p.dma(dst, self.rows_d[nrow:nrow + 1, :].pbcast(128))
        for half in range(2):
            ps = self.bank()
            for cc in range(4):
                c = half * 4 + cc
                p.copy(tmp, mv[:, gi, c, row:row + 1].bcast([128, 128]))
                p.mm(ps[:, cc * 128:(cc + 1) * 128], tmp, self.identf)
            p.tt(dst[:, half * 512:(half + 1) * 512], dst[:, half * 512:(half + 1) * 512], ps, ALU.mult)

    def prenorm(self, b, xi, l, which, hT, tiles=range(NT)):
        p = self.p
        A, Bv = self.modvec(l, which)
        m = p.mark()
        NG = 4
        xt = [p.sb("pn_x%d" % i, [128, D], F32) for i in range(NG)]
        junk = p.sb("pn_junk", [128, D], BF16)
        xn = [p.sb("pn_xn%d" % i, [128, D], BF16) for i in range(NG)]
        ss = [p.sb("pn_ss%d" % i, [128, 1], F32) for i in range(NG)]
        tmp = [p.sb("pn_tmp%d" % i, [128, 8, 128], F32) for i in range(NG)]
        tiles = list(tiles)

        def chain(g):
            for tt in tiles[g::NG]:
                x, s = xt[g], ss[g]
                p.dma(x, self.Xs[xi][b, tt * 128:(tt + 1) * 128, :].on(self.xbufs[xi][b][tt]))
                yield
                p.act(junk, x, AF.Square, accum=s)
                yield
                p.act(s, s, AF.Sqrt, bias=EPS, scale=1.0 / D)
                yield
                p.recip(s, s)
                yield
                p.act(xn[g], x, AF.Identity, scale=s)
                yield
                ps, hp = p.palloc(4, 2 * g)
                psb = ps.bc(BF16)
                for c in range(8):
                    p.tr(psb[:, c * 128:(c + 1) * 128], xn[g][:, c * 128:(c + 1) * 128], self.identb)
                yield
                row = 4 if tt < 2 else b
                p.tt(tmp[g], psb.rr("p (c t) -> p c t", c=8), A[:, :, row:row + 1].bcast([128, 8, 128]), ALU.mult)
                p.pfree(hp)
                yield
                p.tt(hT[:, :, tt * 128:(tt + 1) * 128], tmp[g], Bv[:, :, row:row + 1].bcast([128, 8, 128]), ALU.add,
                     eng='pool')
                yield

        self.run_rr([chain(g) for g in range(NG)])
        p.release(m)

    def residual_tile(self, b, tt, xi, xo, ybanks, G, st):
        p = self.p
        x, junk, ss2, s, tmp = st
        p.dma(x, self.Xs[xi][b, tt * 128:(tt + 1) * 128, :].on(self.xbufs[xi][b][tt]))
        for h in range(2):
            p.act(junk, ybanks[h], AF.Square, accum=ss2[:, h:h + 1])
        p.tt(s, ss2[:, 0:1], ss2[:, 1:2], ALU.add)
        p.act(s, s, AF.Sqrt, bias=EPS, scale=1.0 / D)
        p.recip(s, s)
        for h in range(2):
            p.stt(tmp[:, h * 512:(h + 1) * 512], ybanks[h], s, G[:, h * 512:(h + 1) * 512], ALU.mult, ALU.mult)
        p.tt(tmp, tmp, x, ALU.add, eng='pool')
        if xo == 4:
            dst = self.out[b, (tt - 2) * 128:(tt - 1) * 128, :]
        else:
            dst = self.Xs[xo][b, tt * 128:(tt + 1) * 128, :]
        p.dma(dst.on(self.xbufs[xo][b][tt]), tmp)

    def res_state(self, n=2):
        p = self.p
        return [(p.sb("rs_x%d" % i, [128, D], F32), p.sb("rs_junk%d" % i, [128, 512], BF16),
                 p.sb("rs_ss2%d" % i, [128, 2], F32), p.sb("rs_s%d" % i, [128, 1], F32),
                 p.sb("rs_tmp%d" % i, [128, D], F32)) for i in range(n)]

    def ffn(self, b, l, xi, xo, do_ctx=True):
        p = self.p
        m = p.mark()
        hT = p.sb("ffn_hT", [128, 8, LTOK], BF16)
        self.prenorm(b, xi, l, 1, hT, tiles=range(NT) if do_ctx else range(2, NT))
        wdown = p.sb("ffn_wdown", [128, NJ, D], BF16)
        actT = p.sb("ffn_actT", [128, NJ, 1280], BF16)
        wg = [p.sb("ffn_wg%d" % i, [128, 8, 128], BF16) for i in range(2)]
        wu = [p.sb("ffn_wu%d" % i, [128, 8, 128], BF16) for i in range(2)]
        gpad = [p.sb("ffn_gpad%d" % i, [128, 18, 66], BF16) for i in range(2)]
        cpad = p.sb("ffn_cpad", [128, 258], BF16)
        gg = [p.sb("ffn_gg%d" % i, [128, 512], BF16) for i in range(2)]
        diag = [p.sb("ffn_diag%d" % i, [128, 9, 128], BF16) for i in range(2)]
        G = [p.sb("ffn_G%d" % i, [128, D], F32) for i in range(2)]
        gtmp = p.sb("ffn_gtmp", [128, 128], F32)
        st = self.res_state(2)
        for g in gpad:
            p.memset(g, 0.0, eng='pool')
        p.memset(cpad, 0.0, eng='pool')
        self.gate_row(G[0], l, 5, b, gtmp)
        if do_ctx:
            self.gate_row(G[1], l, 5, 4, gtmp)
        cw = self.col('cw%d' % l)
        cb = self.col('cb%d' % l)
        nn = 0
        for seg in range(2):
            r0 = 16 * seg
            g0 = 0 if seg == 0 else 15
            prow0 = 1 if seg == 0 else 0
            gp = gpad[seg]
            with_ctx = (seg == 0 and do_ctx)
            for j in range(NJ):
                wgj, wuj = wg[j % 2], wu[j % 2]
                p.dma(wgj, self.w_up[l, :, j * 128:(j + 1) * 128].rr("(c p) n -> p c n", p=128), eng='pool')
                p.dma(wuj, self.w_up[l, :, DFF + j * 128:DFF + (j + 1) * 128].rr("(c p) n -> p c n", p=128), eng='pool')
                if seg == 0 and j >= 2:
                    for jj in range(2 * (j - 2), min(NJ, 2 * (j - 2) + 2)):
                        p.dma(wdown[:, jj, :], self.w_down[l, jj * 128:(jj + 1) * 128, :], eng='pool')
                dg = diag[j % 2]
                p.tt(dg, self.identb.rr("p (o t) -> p o t", o=1).bcast([128, 9, 128]),
                     cw[:, j * 9:(j + 1) * 9].rr("p (t o) -> p t o", o=1).bcast([128, 9, 128]), ALU.mult, eng='pool')
                tok0 = 256 + g0 * 64
                for (o, n) in ((0, 512), (512, 512), (1024, 64)):
                    ps = self.bank()
                    for kc in range(8):
                        p.mm(ps[:, 0:n], wgj[:, kc, :], hT[:, kc, tok0 + o:tok0 + o + n], start=(kc == 0), stop=(kc == 7))
                    pr = prow0 + o // 64
                    p.copy(gp[:, pr:pr + n // 64, 1:65], ps[:, 0:n].rr("p (r c) -> p r c", c=64), eng='act')
                for blk in range(2):
                    ps = self.bank()
                    for t in range(9):
                        dy, dx = t // 3, t % 3
                        p.mm(ps, dg[:, t, :], gp[:, 8 * blk + dy:8 * blk + dy + 8, dx:dx + 64], start=(t == 0), stop=(t == 8))
                    g_ = gg[nn % 2]
                    nn += 1
                    p.act(g_, ps, AF.Gelu_apprx_tanh, bias=cb[:, j:j + 1])
                    ps2 = self.bank()
                    t0 = 256 + r0 * 64 + blk * 512
                    for kc in range(8):
                        p.mm(ps2, wuj[:, kc, :], hT[:, kc, t0:t0 + 512], start=(kc == 0), stop=(kc == 7))
                    p.tt(actT[:, j, 256 + blk * 512:256 + (blk + 1) * 512], ps2, g_, ALU.mult)
                if with_ctx:
                    ps = self.bank()
                    for kc in range(8):
                        p.mm(ps[:, 0:256], wgj[:, kc, :], hT[:, kc, 0:256], start=(kc == 0), stop=(kc == 7))
                    p.copy(cpad[:, 1:257], ps[:, 0:256], eng='act')
                    ps = self.bank()
                    for dx in range(3):
                        p.mm(ps[:, 0:256], dg[:, 3 + dx, :], cpad[:, dx:dx + 256], start=(dx == 0), stop=(dx == 2))
                    g_ = gg[nn % 2]
                    nn += 1
                    p.act(g_[:, 0:256], ps[:, 0:256], AF.Gelu_apprx_tanh, bias=cb[:, j:j + 1])
                    ps2 = self.bank()
                    for kc in range(8):
                        p.mm(ps2[:, 0:256], wuj[:, kc, :], hT[:, kc, 0:256], start=(kc == 0), stop=(kc == 7))
                    p.tt(actT[:, j, 0:256], ps2[:, 0:256], g_[:, 0:256], ALU.mult)
            tiles = ([0, 1] if with_ctx else []) + [2 + 8 * seg + i for i in range(8)]
            for n, tt in enumerate(tiles):
                a0 = tt * 128 if tt < 2 else 256 + (tt - 2 - 8 * seg) * 128
                yb = [self.bank(), self.bank()]
                for h in range(2):
                    for j in range(NJ):
                        p.mm(yb[h], actT[:, j, a0:a0 + 128], wdown[:, j, h * 512:(h + 1) * 512], start=(j == 0), stop=(j == NJ - 1))
                self.residual_tile(b, tt, xi, xo, yb, G[1] if tt < 2 else G[0], st[n % 2])
        p.release(m)


LNS_ML = float(np.log(128.0 ** -0.5))
LNS_GLA = float(np.log(64.0 ** -0.5))
ORD = [list(range(NT)), [1, 0] + list(range(17, 1, -1))]
C_MQ, C_MK, C_MV, C_MO, C_MG, C_GQ, C_GK, C_GV, C_GG, C_GLR = 0, 512, 1024, 1536, 2048, 2064, 2320, 2576, 3088, 3600


def host_small_even(pk, inp):
    pk.add('ecw', fm(inp['ev_conv_w'][0]))
    pk.add('ecb', fm(inp['ev_conv_b'][0]))
    pk.add('glab', fm(inp['ev_gla_b'][0]))
    pk.add('hg', fm(inp['ev_head_g'][0]))


class KE(K):
    def setup_even(self):
        p = self.p
        self.w_in = p.dram("ev_w_in", [D, 3632], F32, kind="ExternalInput")
        self.w_out = p.dram("ev_w_out", [D, D], F32, kind="ExternalInput")
        self.gla_w2 = p.dram("ev_gla_w2", [2, 16, 256], F32, kind="ExternalInput")
        self.ones = p.sb("ones", [128, 128], F32)
        p.memset(self.ones, 1.0, eng='pool')
        self.tri = []
        for z in range(2):
            t = p.sb("tri%d" % z, [128, 128], F32)
            pat, cm = ([[1, 128]], -1) if z == 0 else ([[-1, 128]], 1)
            p.aselect(t, self.ones, pat, ALU.is_ge, 0.0, 0, cm)
            self.tri.append(t)
        self.common_mark = p.mark()
        self.mask4 = []
        self.maskb = []
        for z in range(2):
            t = self.tri[z]
            m4 = p.sb("mask4%d" % z, [128, 4, 128], F32)
            p.ts(m4, t.rr("p (o t) -> p o t", o=1).bcast([128, 4, 128]), -1.0, ALU.add, 30000.0, ALU.mult, eng='pool')
            self.mask4.append(m4)
            mb = p.sb("maskb%d" % z, [128, 128], BF16)
            p.copy(mb, t, eng='pool')
            self.maskb.append(mb)
        self.bgrow = p.sb("bgrow", [128, 16], F32)
        p.dma(self.bgrow, self.rows_d[8:9, 0:16].pbcast(128))
        self.base_mark = p.mark()

    def headnorm_gate(self, y, ybufs, hd, gate_col, gate_fn, hT, mixT, scratch):
        p = self.p
        sq, ss, yn, sg, wgt = scratch
        p.dma(wgt, self.w_in[:, gate_col:gate_col + 128].rr("(c p) n -> p c n", p=128), eng='pool')
        for blk in range(5):
            t0, n = blk * 512, (512 if blk < 4 else 256)
            ps = self.bank()
            for kc in range(8):
                p.mm(ps[:, 0:n], wgt[:, kc, :], hT[:, kc, t0:t0 + n], start=(kc == 0), stop=(kc == 7))
            p.act(sg[:, t0:t0 + n], ps[:, 0:n], gate_fn)
        yall_ = V(y.ap, list(ybufs))
        p.tt(sq, yall_, yall_, ALU.mult)
        p.reduce(ss, sq, ALU.add)
        p.act(ss, ss, AF.Sqrt, bias=EPS, scale=1.0 / 128)
        p.recip(ss, ss)
        p.tt(yn, yall_, ss.rr("p (t o) -> p t o", o=1).bcast([128, NT, 128]), ALU.mult)
        hg = self.col('hg')
        for g0 in (0, 8, 16):
            n = min(8, NT - g0)
            ps = self.bank()
            psb = ps.bc(BF16)
            for i in range(n):
                p.tr(psb[:, i * 128:(i + 1) * 128], yn[:, g0 + i, :], self.identb)
            p.stt(mixT[:, hd, g0 * 128:(g0 + n) * 128], psb[:, 0:n * 128], hg[:, hd:hd + 1], sg[:, g0 * 128:(g0 + n) * 128],
                  ALU.mult, ALU.mult)

    def even_mixer(self, b, xi, xo):
        p = self.p
        l = 0
        m = p.mark()
        hT = p.sb("ev_hT", [128, 8, LTOK], BF16)
        self.prenorm(b, xi, l, 0, hT)
        mixT = p.sb("ev_mixT", [128, 8, LTOK], BF16)
        wgate = p.sb("ev_wgate", [128, 8, 16], BF16)
        p.dma(wgate, self.w_in[:, C_MG:C_MG + 16].rr("(c p) n -> p c n", p=128), eng='pool')
        graw = p.sb("ev_graw", [128, NT, 16], F32)
        ps = self.bank()
        for tt in range(NT):
            for kc in range(8):
                p.mm(ps[:, tt * 16:(tt + 1) * 16], hT[:, kc, tt * 128:(tt + 1) * 128], wgate[:, kc, :], start=(kc == 0), stop=(kc == 7))
        p.tt(graw, ps[:, 0:NT * 16].rr("p (t n) -> p t n", n=16), self.bgrow.rr("p (o n) -> p o n", o=1).bcast([128, NT, 16]), ALU.add)
        g5 = graw.rr("p t (z y h) -> p t z y h", z=2, y=2)
        I8 = g5[:, :, :, 0, :]
        F8 = g5[:, :, :, 1, :]
        def s8(name):
            return p.sb(name, [128, NT, 2, 4], F32)
        lf8, Fc8, Ft8, bs8, qs8, wk8, dc8 = [s8("ev_" + n) for n in ("lf8", "Fc8", "Ft8", "bs8", "qs8", "wk8", "dc8")]
        p.act(lf8, F8, AF.Exp, scale=-1.0)
        p.act(lf8, lf8, AF.Ln, bias=1.0)
        p.ts(lf8, lf8, -1.0, ALU.mult)
        ps = self.bank()
        for z in range(2):
            p.mm(ps[:, z * 72:(z + 1) * 72], self.tri[z], lf8[:, :, z, :])
        p.mm(ps[:, 144:288], self.ones, lf8)
        for z in range(2):
            p.copy(Fc8[:, :, z, :], ps[:, z * 72:(z + 1) * 72].rr("p (t h) -> p t h", h=4))
        p.copy(Ft8, ps[:, 144:288].rr("p (t z h) -> p t z h", z=2, h=4))
        p.tt(bs8, I8, Fc8, ALU.subtract)
        p.tt(wk8, Ft8, bs8, ALU.add)
        p.act(wk8, wk8, AF.Exp)
        p.ts(bs8, bs8, LNS_ML, ALU.add)
        p.act(qs8, Fc8, AF.Exp, bias=LNS_ML)
        p.act(dc8, Ft8, AF.Exp)
        Fc5 = Fc8.rr("p t z (h o) -> p t z h o", o=1)

        cw = self.col('ecw').rr("p (t c) -> p t c", t=3)
        cbias = self.col('ecb')
        for hp in range(2):
            m2 = p.mark()
            heads = [2 * hp, 2 * hp + 1]
            qT = [p.sb("ml_qT%d" % i, [128, LTOK], BF16) for i in range(2)]
            kT = [p.sb("ml_kT%d" % i, [128, LTOK], BF16) for i in range(2)]
            vv = [p.sb("ml_v%d" % i, [128, NT, 130], BF16) for i in range(2)]
            yy = [p.sb("ml_y%d" % i, [128, NT, 128], F32) for i in range(2)]
            ybufs = [[Buf("mly%d_%d" % (i, t)) for t in range(NT)] for i in range(2)]
            m3 = p.mark()
            xpad = p.sb("ml_xpad", [128, 2308], F32)
            t1 = p.sb("ml_t1", [128, 2306], F32)
            t2 = p.sb("ml_t2", [128, 2306], F32)
            wq = [p.sb("ml_wq%d" % i, [128, 8, 128], BF16) for i in range(2)]
            p.memset(xpad, 0.0, eng='pool')
            nw = 0
            for i, h in enumerate(heads):
                for dst, c0, cc in ((qT[i], C_MQ + h * 128, h), (kT[i], C_MK + h * 128, 4 + h)):
                    w = wq[nw % 2]
                    nw += 1
                    p.dma(w, self.w_in[:, c0:c0 + 128].rr("(c p) n -> p c n", p=128), eng='pool')
                    for blk in range(5):
                        t0, n = (0, 256) if blk == 0 else (256 + (blk - 1) * 512, 512)
                        pos = 1 if blk == 0 else 259 + (blk - 1) * 512
                        ps = self.bank()
                        for kc in range(8):
                            p.mm(ps[:, 0:n], w[:, kc, :], hT[:, kc, t0:t0 + n], start=(kc == 0), stop=(kc == 7))
                        p.copy(xpad[:, pos:pos + n], ps[:, 0:n], eng='act')
                    p.ts(t1, xpad[:, 0:2306], cw[:, 0, cc:cc + 1], ALU.mult, cbias[:, cc:cc + 1], ALU.add)
                    p.stt(t2, xpad[:, 1:2307], cw[:, 1, cc:cc + 1], t1, ALU.mult, ALU.add)
                    p.stt(t1, xpad[:, 2:2308], cw[:, 2, cc:cc + 1], t2, ALU.mult, ALU.add)
                    p.act(dst[:, 0:256], t1[:, 0:256], AF.Silu)
                    p.act(dst[:, 256:LTOK], t1[:, 258:2306], AF.Silu)
                w = wq[nw % 2]
                nw += 1
                p.dma(w, self.w_in[:, C_MV + h * 128:C_MV + (h + 1) * 128].rr("(c p) n -> p c n", p=128), eng='pool')
                p.memset(vv[i][:, :, 128:130], 1.0, eng='pool')
                for g0 in range(0, NT, 4):
                    n = min(4, NT - g0)
                    ps = self.bank()
                    for j in range(n):
                        tt = g0 + j
                        for kc in range(8):
                            p.mm(ps[:, j * 128:(j + 1) * 128], hT[:, kc, tt * 128:(tt + 1) * 128], w[:, kc, :], start=(kc == 0), stop=(kc == 7))
                    p.copy(vv[i][:, g0:g0 + n, 0:128], ps[:, 0:n * 128].rr("p (t n) -> p t n", n=128), eng='act')
                p.memset(yy[i], 0.0, eng='pool')
            p.release(m3)
            Cs = [[p.sb("ml_C%d%d" % (z, i), [128, 132], F32) for i in range(2)] for z in range(2)]
            Cb = [[p.sb("ml_Cb%d%d" % (z, i), [128, 132], BF16) for i in range(2)] for z in range(2)]
            for z in range(2):
                for i in range(2):
                    p.memset(Cs[z][i], 0.0, eng='pool')
                    p.memset(Cb[z][i], 0.0, eng='pool')
            dg1 = [[p.sb("ml_dg%d_%d" % (c, k2), [128, 128], F32) for k2 in range(2)] for c in range(4)]
            Dm = [[p.sb("ml_Dm%d_%d" % (c, k2), [128, 128], BF16) for k2 in range(2)] for c in range(4)]
            sT = [[p.sb("ml_sT%d_%d" % (c, k2), [128, 128], BF16) for k2 in range(2)] for c in range(4)]
            ktil = [[p.sb("ml_ktil%d_%d" % (c, k2), [128, 128], BF16) for k2 in range(2)] for c in range(4)]
            tmpo = [p.sb("ml_tmpo%d" % i, [128, 132], F32) for i in range(4)]
            num = [p.sb("ml_num%d" % i, [128, 132], F32) for i in range(4)]
            den = [p.sb("ml_den%d" % i, [128, 1], F32) for i in range(4)]

            def ml_chain(z, i, h, c):
                b0, b1 = 2 * c, 2 * c + 1
                for step in range(NT):
                    tt = ORD[z][step]
                    ts_ = slice(tt * 128, (tt + 1) * 128)
                    k2 = step % 2
                    upd = step < NT - 1
                    dg = dg1[c][k2]
                    p.ts(dg, self.identf, Fc8[:, tt, z, h:h + 1], ALU.mult)
                    rb, hrb = p.palloc(1, b0)
                    p.mm(rb[:, 0:128], self.ones, dg, start=True, stop=False)
                    p.mm(rb[:, 0:128], self.identf, self.mask4[z][:, 0, :], start=False, stop=True)
                    sc, hsc = p.palloc(1, b0)
                    p.mm(sc[:, 0:128], kT[i][:, ts_], qT[i][:, ts_])
                    if upd:
                        kp, hkp = p.palloc(1, b0)
                        kpb = kp.bc(BF16)
                        p.tr(kpb[:, 0:128], kT[i][:, ts_], self.identb)
                    yield
                    p.act(Dm[c][k2], rb[:, 0:128], AF.Exp, bias=bs8[:, tt, z, h:h + 1])
                    p.pfree(hrb)
                    if upd:
                        p.act(ktil[c][k2], kpb[:, 0:128], AF.Identity, scale=wk8[:, tt, z, h:h + 1])
                        p.pfree(hkp)
                    yield
                    p.tt(sT[c][k2], sc[:, 0:128], Dm[c][k2], ALU.mult)
                    p.pfree(hsc)
                    yield
                    o1, ho1 = p.palloc(2, b1)
                    p.mm(o1[:, 0:129], sT[c][k2], vv[i][:, tt, 0:129])
                    o2, ho2 = p.palloc(2, b1)
                    p.mm(o2[:, 0:129], qT[i][:, ts_], Cb[z][i][:, 0:129])
                    if upd:
                        cu, hcu = p.palloc(2, b0)
                        p.mm(cu[:, 0:129], ktil[c][k2], vv[i][:, tt, 0:129])
                    yield
                    p.act(tmpo[c][:, 0:129], o2[:, 0:129], AF.Identity, scale=qs8[:, tt, z, h:h + 1])
                    p.pfree(ho2)
                    if upd:
                        p.stt(Cs[z][i][:, 0:129], Cs[z][i][:, 0:129], dc8[:, tt, z, h:h + 1], cu[:, 0:129], ALU.mult, ALU.add)
                        p.pfree(hcu)
                    yield
                    p.tt(num[c][:, 0:129], tmpo[c][:, 0:129], o1[:, 0:129], ALU.add)
                    p.pfree(ho1)
                    if upd:
                        p.copy(Cb[z][i][:, 0:129], Cs[z][i][:, 0:129], eng='pool')
                    yield
                    p.act(den[c], num[c][:, 128:129], AF.Abs)
                    yield
                    p.ts(den[c], den[c], 1.0, ALU.max)
                    p.recip(den[c], den[c])
                    yield
                    yv = yy[i][:, tt, :].on(ybufs[i][tt])
                    p.stt(yv, num[c][:, 0:128], den[c], yv, ALU.mult, ALU.add)
                    yield

            self.run_rr([ml_chain(z, i, heads[i], z * 2 + i) for z in range(2) for i in range(2)])
            m4_ = p.mark()
            scratch = (p.sb("hn_sq", [128, NT, 128], F32), p.sb("hn_ss", [128, NT], F32), p.sb("hn_yn", [128, NT, 128], BF16),
                       p.sb("hn_sg", [128, LTOK], BF16), p.sb("hn_wgt", [128, 8, 128], BF16))
            for i, h in enumerate(heads):
                self.headnorm_gate(yy[i], ybufs[i], h, C_MO + h * 128, AF.Sigmoid, hT, mixT, scratch)
            p.release(m2)
        self.ev_hT, self.ev_mixT, self.ev_mark = hT, mixT, m
        return hT, mixT, m

    def even_out(self, b, xi, xo, hT, mixT, m):
        p = self.p
        l = 0
        wout = p.sb("ev_wout", [128, 8, D], BF16)
        for c in range(8):
            p.dma(wout[:, c, :], self.w_out[c * 128:(c + 1) * 128, :], eng='pool')
        G = [p.sb("evo_G%d" % i, [128, D], F32) for i in range(2)]
        gtmp = p.sb("evo_gtmp", [128, 128], F32)
        st = self.res_state(2)
        self.gate_row(G[0], l, 2, b, gtmp)
        self.gate_row(G[1], l, 2, 4, gtmp)
        for tt in range(NT):
            yb = [self.bank(), self.bank()]
            for h in range(2):
                for c in range(8):
                    p.mm(yb[h], mixT[:, c, tt * 128:(tt + 1) * 128], wout[:, c, h * 512:(h + 1) * 512], start=(c == 0), stop=(c == 7))
            self.residual_tile(b, tt, xi, xo, yb, G[1] if tt < 2 else G[0], st[tt % 2])
        p.release(m)


def _gla(self, b, hT, mixT):
    p = self.p
    m = p.mark()
    rmask = p.sb("gl_rmask", [128, LTOK], BF16)
    p.memset(rmask, 1.0, eng='pool')
    p.memset(rmask.rr("p (n t) -> p n t", t=128)[:, :, 0:1], 0.0, eng='pool')
    w2b = p.sb("gl_w2b", [16, 2, 256], BF16)
    p.dma(w2b, self.gla_w2.rr("z r c -> r z c"), eng='pool')
    wlr = p.sb("gl_wlr", [128, 8, 32], BF16)
    p.dma(wlr, self.w_in[:, C_GLR:C_GLR + 32].rr("(c p) n -> p c n", p=128), eng='pool')
    glrT = p.sb("gl_glrT", [16, 2, LTOK], BF16)
    for z in range(2):
        for blk in range(5):
            t0, n = blk * 512, (512 if blk < 4 else 256)
            ps = self.bank()
            for kc in range(8):
                p.mm(ps[0:16, 0:n], wlr[:, kc, z * 16:(z + 1) * 16], hT[:, kc, t0:t0 + n], start=(kc == 0), stop=(kc == 7))
            p.copy(glrT[:, z, t0:t0 + n], ps[0:16, 0:n], eng='act')
    nb = p.sb("gl_nb", [128, 4], F32)
    p.ts(nb, self.col('glab'), -1.0, ALU.mult)
    for cp in range(2):
        m2 = p.mark()
        qtil = [p.sb("gl_qtil%d" % z, [128, LTOK], BF16) for z in range(2)]
        khat = [p.sb("gl_khat%d" % z, [128, LTOK], BF16) for z in range(2)]
        ktl = [p.sb("gl_ktl%d" % z, [128, LTOK], BF16) for z in range(2)]
        dec = [p.sb("gl_dec%d" % z, [128, NT], F32) for z in range(2)]
        vv = [p.sb("gl_v%d" % i, [128, NT, 128], BF16) for i in range(2)]
        yy = [p.sb("gl_y%d" % i, [128, NT, 128], F32) for i in range(2)]
        ybufs = [[Buf("gly%d_%d" % (i, t)) for t in range(NT)] for i in range(2)]
        m3 = p.mark()
        qT = p.sb("gl_qT", [128, LTOK], BF16)
        kT = p.sb("gl_kT", [128, LTOK], BF16)
        lb = p.sb("gl_l", [128, LTOK], F32)
        P = p.sb("gl_P", [128, LTOK], F32)
        tmp = p.sb("gl_tmp", [128, LTOK], F32)
        E = p.sb("gl_E", [128, LTOK], BF16)
        w = [p.sb("gl_w%d" % i, [128, 8, 128], BF16) for i in range(2)]
        for dst, c0, wi in ((qT, C_GQ + cp * 128, 0), (kT, C_GK + cp * 128, 1)):
            p.dma(w[wi], self.w_in[:, c0:c0 + 128].rr("(c p) n -> p c n", p=128), eng='pool')
            for blk in range(5):
                t0, n = blk * 512, (512 if blk < 4 else 256)
                ps = self.bank()
                for kc in range(8):
                    p.mm(ps[:, 0:n], w[wi][:, kc, :], hT[:, kc, t0:t0 + n], start=(kc == 0), stop=(kc == 7))
                p.copy(dst[:, t0:t0 + n], ps[:, 0:n], eng='act')
        P3 = P.rr("p (n t) -> p n t", t=128)
        Ptot = P3[:, :, 127:128]
        for z in range(2):
            for blk in range(5):
                t0, n = blk * 512, (512 if blk < 4 else 256)
                ps = self.bank()
                p.mm(ps[:, 0:n], w2b[:, z, cp * 128:(cp + 1) * 128], glrT[:, z, t0:t0 + n])
                p.act(lb[:, t0:t0 + n], ps[:, 0:n], AF.Exp, scale=-1.0, bias=nb[:, z * 2 + cp:z * 2 + cp + 1])
            p.act(lb, lb, AF.Ln, bias=1.0)
            p.scan(P, rmask, lb, 0.0, ALU.mult, ALU.add)
            p.act(dec[z], Ptot.rr("p n o -> p (n o)"), AF.Exp, scale=-1.0 / 16)
            t3 = tmp.rr("p (n t) -> p n t", t=128)
            if z == 0:
                p.act(E, P, AF.Exp, scale=-1.0 / 16, bias=LNS_GLA)
                p.tt(qtil[z], qT, E, ALU.mult)
                p.act(E, P, AF.Exp, scale=1.0 / 16)
                p.tt(khat[z], kT, E, ALU.mult, eng='pool')
                p.tt(t3, P3, Ptot.bcast([128, NT, 128]), ALU.subtract)
                p.act(E, tmp, AF.Exp, scale=1.0 / 16)
                p.tt(ktl[z], kT, E, ALU.mult)
            else:
                p.tt(t3, Ptot.bcast([128, NT, 128]), P3, ALU.subtract)
                p.tt(tmp, tmp, lb, ALU.add, eng='pool')
                p.act(E, tmp, AF.Exp, scale=-1.0 / 16, bias=LNS_GLA)
                p.tt(qtil[z], qT, E, ALU.mult)
                p.act(E, tmp, AF.Exp, scale=1.0 / 16)
                p.tt(khat[z], kT, E, ALU.mult, eng='pool')
                p.tt(tmp, lb, P, ALU.subtract)
                p.act(E, tmp, AF.Exp, scale=1.0 / 16)
                p.tt(ktl[z], kT, E, ALU.mult)
        for i in range(2):
            h = cp * 2 + i
            p.dma(w[i], self.w_in[:, C_GV + h * 128:C_GV + (h + 1) * 128].rr("(c p) n -> p c n", p=128), eng='pool')
            for g0 in range(0, NT, 4):
                n = min(4, NT - g0)
                ps = self.bank()
                for j in range(n):
                    tt = g0 + j
                    for kc in range(8):
                        p.mm(ps[:, j * 128:(j + 1) * 128], hT[:, kc, tt * 128:(tt + 1) * 128], w[i][:, kc, :], start=(kc == 0), stop=(kc == 7))
                p.copy(vv[i][:, g0:g0 + n, :], ps[:, 0:n * 128].rr("p (t n) -> p t n", n=128), eng='act')
            p.memset(yy[i], 0.0, eng='pool')
        p.release(m3)
        S = [[p.sb("gl_S%d%d" % (z, i), [128, 128], F32) for i in range(2)] for z in range(2)]
        Sb = [[p.sb("gl_Sb%d%d" % (z, i), [128, 128], BF16) for i in range(2)] for z in range(2)]
        for z in range(2):
            for i in range(2):
                p.memset(S[z][i], 0.0, eng='pool')
                p.memset(Sb[z][i], 0.0, eng='pool')
        AT = [[p.sb("gl_AT%d_%d" % (c, k2), [128, 128], BF16) for k2 in range(2)] for c in range(4)]
        ktok = [[p.sb("gl_ktok%d_%d" % (c, k2), [128, 64], BF16) for k2 in range(2)] for c in range(4)]

        def gl_chain(z, i, c):
            b0, b1 = 2 * c, 2 * c + 1
            pr = slice(i * 64, (i + 1) * 64)
            for step in range(NT):
                tt = ORD[z][step]
                ts_ = slice(tt * 128, (tt + 1) * 128)
                k2 = step % 2
                upd = step < NT - 1
                sc, hsc = p.palloc(1, b0)
                p.mm(sc[:, 0:128], khat[z][pr, ts_], qtil[z][pr, ts_])
                if upd:
                    kp, hkp = p.palloc(1, b0)
                    kpb = kp.bc(BF16)
                    p.tr(kpb[:, 0:64], ktl[z][pr, ts_], self.identb[pr, pr])
                yield
                p.tt(AT[c][k2], sc[:, 0:128], self.maskb[z], ALU.mult)
                p.pfree(hsc)
                if upd:
                    p.copy(ktok[c][k2], kpb[:, 0:64], eng='act')
                    p.pfree(hkp)
                yield
                o, ho = p.palloc(1, b1)
                p.mm(o[:, 0:128], AT[c][k2], vv[i][:, tt, :], start=True, stop=False)
                p.mm(o[:, 0:128], qtil[z][pr, ts_], Sb[z][i][pr, :], start=False, stop=True)
                if upd:
                    su, hsu = p.palloc(1, b0)
                    p.mm(su[pr, 0:128], ktok[c][k2], vv[i][:, tt, :])
                yield
                yv = yy[i][:, tt, :].on(ybufs[i][tt])
                p.tt(yv, yv, o[:, 0:128], ALU.add)
                p.pfree(ho)
                if upd:
                    p.stt(S[z][i][pr, :], S[z][i][pr, :], dec[z][pr, tt:tt + 1], su[pr, 0:128], ALU.mult, ALU.add)
                    p.pfree(hsu)
                yield
                if upd:
                    p.copy(Sb[z][i][pr, :], S[z][i][pr, :], eng='pool')
                yield

        self.run_rr([gl_chain(z, i, z * 2 + i) for z in range(2) for i in range(2)])
        scratch = (p.sb("hn_sq", [128, NT, 128], F32), p.sb("hn_ss", [128, NT], F32), p.sb("hn_yn", [128, NT, 128], BF16),
                   p.sb("hn_sg", [128, LTOK], BF16), p.sb("hn_wgt", [128, 8, 128], BF16))
        for i in range(2):
            h = cp * 2 + i
            self.headnorm_gate(yy[i], ybufs[i], 4 + h, C_GG + h * 128, AF.Silu, hT, mixT, scratch)
        p.release(m2)
    p.release(m)


KE.gla = _gla


C0 = float(np.exp(-0.5))
RW_LN_EPS = 64e-5


def host_small_rw(pk, inp):
    pk.add('mu', fm(inp['rw_mu'][0]))
    pk.add('w0', fm(inp['rw_w0'][0]))
    pk.add('a0', fm(inp['rw_a0'][0]))
    pk.add('kv', fm(inp['rw_kvec'][0]))
    pk.add('lnx', fm(inp['rw_lnx'][0]))


def _setup_rw(self):
    p = self.p
    NB = self.NB
    self.w_rkv = p.dram("rw_w_rkv", [3, D, D], F32, kind="ExternalInput")
    self.w_o = p.dram("rw_w_o", [D, D], F32, kind="ExternalInput")
    self.rw_w1 = p.dram("rw_w1", [2, D, 64], F32, kind="ExternalInput")
    self.rw_w2 = p.dram("rw_w2", [2, 64, D], F32, kind="ExternalInput")
    self.rw_a1 = p.dram("rw_a1", [D, 64], F32, kind="ExternalInput")
    self.rw_a2 = p.dram("rw_a2", [64, D], F32, kind="ExternalInput")
    self.rw_g1 = p.dram("rw_g1", [D, 128], F32, kind="ExternalInput")
    self.rw_g2 = p.dram("rw_g2", [128, D], F32, kind="ExternalInput")
    self.scr = []
    if getattr(self, 'dbg', False):
        self.dbg_y = p.dram("dbg_y", [8, 128, NT * 128], F32, kind="ExternalOutput")
    for b in range(NB):
        d = {}
        for nm in ('r', 'k', 'v', 'o'):
            d[nm] = p.dram("scr_%s%d" % (nm, b), [8, 128, LTOK], BF16, kind="ExternalOutput" if getattr(self, 'dbg', False) else "Internal")
            d[nm + 'buf'] = [Buf("scr_%s%d_%d" % (nm, b, c)) for c in range(8)]
        self.scr.append(d)


def _setup_rw_consts(self):
    p = self.p
    p.release(self.common_mark)
    self.strict = []
    self.M4 = []
    for z in range(2):
        st = p.sb("strict%d" % z, [128, 128], F32)
        pat, cm = ([[1, 128]], -1) if z == 0 else ([[-1, 128]], 1)
        p.aselect(st, self.ones, pat, ALU.is_gt, 0.0, 0, cm)
        self.strict.append(st)
    for z in range(2):
        m4 = p.sb("M4%d" % z, [128, 4, 128], F32)
        p.ts(m4[:, 0, :], self.strict[z], -1.0, ALU.mult, eng='pool')
        p.ts(m4[:, 1, :], self.tri[z], -1.0, ALU.mult, eng='pool')
        p.copy(m4[:, 2, :], self.strict[z], eng='pool')
        p.copy(m4[:, 3, :], self.tri[z], eng='pool')
        self.M4.append(m4)
    self.nstrictT = []
    for z in range(2):
        t = p.sb("nstrictT%d" % z, [128, 128], F32)
        p.ts(t, self.strict[1 - z], -1.0, ALU.mult, eng='pool')
        self.nstrictT.append(t)
    self.masks3 = p.sb("masks3", [128, 3, 128], BF16)
    bd64 = p.sb("bd64", [128, 128], BF16)
    p.memset(self.masks3[:, 0, :], 0.0, eng='pool')
    p.memset(bd64, 0.0, eng='pool')
    for i in range(4):
        p.memset(self.masks3[32 * i:32 * i + 32, 0, 32 * i:32 * i + 32], 1.0, eng='pool')
    for i in range(2):
        p.memset(bd64[64 * i:64 * i + 64, 64 * i:64 * i + 64], 1.0, eng='pool')
    p.tt(self.masks3[:, 1, :], bd64, self.masks3[:, 0, :], ALU.subtract, eng='pool')
    p.ts(self.masks3[:, 2, :], bd64, -1.0, ALU.mult, 1.0, ALU.add, eng='pool')
    self.bones = p.sb("bones", [128, 128], BF16)
    p.memset(self.bones, 0.0, eng='pool')
    p.memset(self.bones[0:64, 0:64], 1.0, eng='pool')
    p.memset(self.bones[64:128, 64:128], 1.0, eng='pool')
    self.omu = p.sb("omu", [128, 48], F32)
    p.ts(self.omu, self.col('mu'), -1.0, ALU.mult, 1.0, ALU.add)
    self.okv1 = p.sb("okv1", [128, 8], F32)
    p.ts(self.okv1, self.col('kv')[:, 8:16], -1.0, ALU.mult, 1.0, ALU.add)
    self.base_mark = p.mark()


def _rw_mix_block(self, xb, hT, mi, blk):
    p = self.p
    mu = self.col('mu')
    t0, n = (0, 256) if blk == 0 else (256 + (blk - 1) * 512, 512)
    for c in range(8):
        muc = mu[:, mi * 8 + c:mi * 8 + c + 1]
        p.act(xb[:, c, 0:n], hT[:, c, t0:t0 + n], AF.Identity, scale=self.omu[:, mi * 8 + c:mi * 8 + c + 1])
        kind = c // 2
        if blk == 0:
            if kind in (0, 2):
                p.stt(xb[:, c, 1:256], hT[:, c, 0:255], muc, xb[:, c, 1:256], ALU.mult, ALU.add)
            else:
                p.stt(xb[:, c, 0:255], hT[:, c, 1:256], muc, xb[:, c, 0:255], ALU.mult, ALU.add)
        else:
            r0 = (blk - 1) * 8
            xv = xb[:, c, :].rr("p (r w) -> p r w", w=64)
            hv = hT[:, c, 256:LTOK].rr("p (r w) -> p r w", w=64)
            if kind == 0:
                p.stt(xv[:, :, 1:64], hv[:, r0:r0 + 8, 0:63], muc, xv[:, :, 1:64], ALU.mult, ALU.add)
            elif kind == 1:
                p.stt(xv[:, :, 0:63], hv[:, r0:r0 + 8, 1:64], muc, xv[:, :, 0:63], ALU.mult, ALU.add)
            elif kind == 2:
                lo = 1 if r0 == 0 else 0
                p.stt(xv[:, lo:8, :], hv[:, r0 + lo - 1:r0 + 7, :], muc, xv[:, lo:8, :], ALU.mult, ALU.add)
            else:
                hi = 7 if r0 == 24 else 8
                p.stt(xv[:, 0:hi, :], hv[:, r0 + 1:r0 + hi + 1, :], muc, xv[:, 0:hi, :], ALU.mult, ALU.add)
    return t0, n


def _rw_phase1(self, b, hT, tw, ta, tg):
    p = self.p
    scr = self.scr[b]
    m = p.mark()
    xblk = [p.sb("rw_xb%d" % i, [128, 8, 512], BF16) for i in range(2)]
    Wr = [p.sb("rw_W%d" % i, [128, 8, D], BF16) for i in range(2)]
    stage = [p.sb("rw_stage%d" % i, [128, 512], BF16) for i in range(4)]
    ns = 0
    nx = 0
    for wi, (mi, kind) in enumerate(((0, 'r'), (2, 'k'), (3, 'v'))):
        W = Wr[wi % 2]
        idx = 'rkv'.index(kind)
        for c in range(8):
            p.dma(W[:, c, :], self.w_rkv[idx, c * 128:(c + 1) * 128, :], eng='pool')
        for blk in range(5):
            xb = xblk[nx % 2]
            nx += 1
            t0, n = self.rw_mix_block(xb, hT, mi, blk)
            for oc in range(8):
                ps = self.bank()
                for kc in range(8):
                    p.mm(ps[:, 0:n], W[:, kc, oc * 128:(oc + 1) * 128], xb[:, kc, 0:n], start=(kc == 0), stop=(kc == 7))
                sg = stage[ns % 4]
                ns += 1
                p.copy(sg[:, 0:n], ps[:, 0:n], eng='act')
                p.dma(scr[kind][oc, :, t0:t0 + n].on(scr[kind + 'buf'][oc]), sg[:, 0:n])
    W0 = p.sb("rw_lr0", [128, 8, 320], BF16)
    Wa = p.sb("rw_lra", [128, 8, 320], BF16)
    Wb = p.sb("rw_lrb", [128, 8, 320], BF16)
    for z in range(2):
        p.dma(W0[:, :, z * 64:(z + 1) * 64], self.rw_w1[z].rr("(c p) r -> p c r", p=128), eng='pool')
    p.dma(W0[:, :, 128:192], self.rw_a1.rr("(c p) r -> p c r", p=128), eng='pool')
    p.dma(W0[:, :, 192:320], self.rw_g1.rr("(c p) r -> p c r", p=128), eng='pool')
    mu = self.col('mu')
    for (c0, c1, mi) in ((0, 128, 1), (128, 192, 4), (192, 320, 5)):
        wdt = c1 - c0
        mub = mu[:, mi * 8:(mi + 1) * 8].rr("p (c o) -> p c o", o=1).bcast([128, 8, wdt])
        omb = self.omu[:, mi * 8:(mi + 1) * 8].rr("p (c o) -> p c o", o=1).bcast([128, 8, wdt])
        p.tt(Wa[:, :, c0:c1], W0[:, :, c0:c1], omb, ALU.mult)
        p.tt(Wb[:, :, c0:c1], W0[:, :, c0:c1], mub, ALU.mult, eng='pool')
    for blk in range(5):
        hs = xblk[nx % 2]
        nx += 1
        t0, n = (0, 256) if blk == 0 else (256 + (blk - 1) * 512, 512)
        p.memset(hs, 0.0, eng='pool')
        for c in range(8):
            kindc = c // 2
            if blk == 0:
                if kindc in (0, 2):
                    p.copy(hs[:, c, 1:256], hT[:, c, 0:255], eng='act')
                else:
                    p.copy(hs[:, c, 0:255], hT[:, c, 1:256], eng='act')
            else:
                r0 = (blk - 1) * 8
                xv = hs[:, c, :].rr("p (r w) -> p r w", w=64)
                hv = hT[:, c, 256:LTOK].rr("p (r w) -> p r w", w=64)
                if kindc == 0:
                    p.copy(xv[:, :, 1:64], hv[:, r0:r0 + 8, 0:63], eng='act')
                elif kindc == 1:
                    p.copy(xv[:, :, 0:63], hv[:, r0:r0 + 8, 1:64], eng='act')
                elif kindc == 2:
                    lo = 1 if r0 == 0 else 0
                    p.copy(xv[:, lo:8, :], hv[:, r0 + lo - 1:r0 + 7, :], eng='act')
                else:
                    hi = 7 if r0 == 24 else 8
                    p.copy(xv[:, 0:hi, :], hv[:, r0 + 1:r0 + hi + 1, :], eng='act')
        for (c0, c1, dst, fn) in ((0, 64, tw[0], AF.Tanh), (64, 128, tw[1], AF.Tanh), (128, 192, ta, None), (192, 320, tg, AF.Sigmoid)):
            M = c1 - c0
            ps = self.bank()
            for kc in range(8):
                p.mm(ps[0:M, 0:n], Wa[:, kc, c0:c1], hT[:, kc, t0:t0 + n], start=(kc == 0), stop=False)
            for kc in range(8):
                p.mm(ps[0:M, 0:n], Wb[:, kc, c0:c1], hs[:, kc, 0:n], start=False, stop=(kc == 7))
            if fn is None:
                p.copy(dst[0:M, t0:t0 + n], ps[0:M, 0:n], eng='act')
            else:
                p.act(dst[0:M, t0:t0 + n], ps[0:M, 0:n], fn)
    p.release(m)


KE.setup_rw = _setup_rw
KE.setup_rw_consts = _setup_rw_consts
KE.rw_mix_block = _rw_mix_block
KE.rw_phase1 = _rw_phase1


BLKS = [(0, 512), (512, 512), (1024, 512), (1536, 512), (2048, 256)]


def _rw_chunk(self, b, cc, tw, ta, tg, last):
    p = self.p
    scr = self.scr[b]
    m = p.mark()
    kv = self.col('kv')
    rT = p.sb("rc_rT", [128, LTOK], BF16)
    kT = p.sb("rc_kT", [128, LTOK], BF16)
    vT = p.sb("rc_vT", [128, LTOK], BF16)
    for t_, nm in ((rT, 'r'), (kT, 'k'), (vT, 'v')):
        p.dma(t_, scr[nm][cc].on(scr[nm + 'buf'][cc]))
    KR = [p.sb("rc_KR%d" % z, [128, NT, 2, 128], BF16) for z in range(2)]
    khat = [p.sb("rc_khat%d" % z, [128, LTOK], BF16) for z in range(2)]
    bhat = [p.sb("rc_bhat%d" % z, [128, LTOK], BF16) for z in range(2)]
    kT4 = [p.sb("rc_kT4%d" % z, [128, LTOK], BF16) for z in range(2)]
    nbT4 = [p.sb("rc_nbT4%d" % z, [128, LTOK], BF16) for z in range(2)]
    gam = [p.sb("rc_gam%d" % z, [128, NT], F32) for z in range(2)]
    Vtok = p.sb("rc_Vtok", [128, NT, 128], BF16)
    y = p.sb("rc_y", [128, NT, 128], F32)
    yz = [y, p.sb("rc_y1", [128, NT, 128], F32)]
    ybufs = [[Buf("rcy%d_%d" % (t, hh)) for hh in range(2)] for t in range(NT)]
    m3 = p.mark()
    aT = p.sb("rc_aT", [128, LTOK], BF16)
    kap = p.sb("rc_kap", [128, LTOK], BF16)
    bet = p.sb("rc_bet", [128, LTOK], BF16)
    sig = p.sb("rc_sig", [128, LTOK], F32)
    P = p.sb("rc_P", [128, LTOK], F32)
    t1 = p.sb("rc_t1", [128, LTOK], F32)
    t2 = p.sb("rc_t2", [128, LTOK], F32)
    E = p.sb("rc_E", [128, LTOK], BF16)
    E2 = p.sb("rc_E2", [128, LTOK], BF16)
    rmask = self.rw_rmask
    w2b = self.rw_w2all[:, :, cc * 128:(cc + 1) * 128]
    a2b = self.rw_a2all[:, cc * 128:(cc + 1) * 128]
    for (t0, n) in BLKS:
        ps = self.bank()
        p.mm(ps[:, 0:n], a2b, ta[0:64, t0:t0 + n])
        p.act(aT[:, t0:t0 + n], ps[:, 0:n], AF.Sigmoid, bias=self.col('a0')[:, cc:cc + 1])
    p.act(E, kT, AF.Square, scale=kv[:, cc:cc + 1])
    for (t0, n) in BLKS:
        ps = self.bank()
        p.mm(ps[:, 0:n], self.bones, E[:, t0:t0 + n])
        p.act(t2[:, t0:t0 + n], ps[:, 0:n], AF.Sqrt)
    p.ts(t2, t2, 1e-12, ALU.max)
    p.recip(t2, t2)
    p.stt(kap, kT, kv[:, cc:cc + 1], t2, ALU.mult, ALU.mult)
    p.tt(bet, kap, aT, ALU.mult, eng='pool')
    p.ts(t1, aT, kv[:, 8 + cc:8 + cc + 1], ALU.mult, self.okv1[:, cc:cc + 1], ALU.add)
    p.tt(kT, kT, t1, ALU.mult)
    P3 = P.rr("p (n t) -> p n t", t=128)
    Ptot = P3[:, :, 127:128]
    Pb = Ptot.bcast([128, NT, 128])
    t13 = t1.rr("p (n t) -> p n t", t=128)
    t23 = t2.rr("p (n t) -> p n t", t=128)
    v3 = lambda x: x.rr("p (n t) -> p n t", t=128)
    for z in range(2):
        for (t0, n) in BLKS:
            ps = self.bank()
            p.mm(ps[:, 0:n], w2b[:, z, :], tw[z][0:64, t0:t0 + n])
            p.act(sig[:, t0:t0 + n], ps[:, 0:n], AF.Sigmoid, bias=self.col('w0')[:, z * 8 + cc:z * 8 + cc + 1])
        p.scan(P, rmask, sig, 0.0, ALU.mult, ALU.add)
        p.act(gam[z], Ptot.rr("p n o -> p (n o)"), AF.Exp, scale=-C0)
        if z == 0:
            G = P
            p.tt(t1, P, sig, ALU.subtract)
            p.tt(t23, P3, Pb, ALU.subtract)
        else:
            p.tt(t13, Pb, P3, ALU.subtract)
            p.tt(t2, sig, P, ALU.subtract)
            G = sig
            p.tt(sig, t1, sig, ALU.add, eng='pool')
        p.act(E, G, AF.Exp, scale=-C0)
        p.tt(KR[z][:, :, 1, :], v3(rT), v3(E), ALU.mult)
        p.act(E2, t1, AF.Exp, scale=-C0)
        p.tt(KR[z][:, :, 0, :], v3(kap), v3(E2), ALU.mult, eng='pool')
        p.act(E, G, AF.Exp, scale=C0)
        p.tt(khat[z], kT, E, ALU.mult)
        p.tt(bhat[z], bet, E, ALU.mult, eng='pool')
        p.act(E2, t2, AF.Exp, scale=C0)
        p.tt(kT4[z], kT, E2, ALU.mult)
        p.stt(nbT4[z], bet, -1.0, E2, ALU.mult, ALU.mult)
    for g0 in range(0, NT, 8):
        n = min(8, NT - g0)
        ps = self.bank()
        psb = ps.bc(BF16)
        for i in range(n):
            p.tr(psb[:, i * 128:(i + 1) * 128], vT[:, (g0 + i) * 128:(g0 + i + 1) * 128], self.identb)
        p.copy(Vtok[:, g0:g0 + n, :], psb[:, 0:n * 128].rr("p (t c) -> p t c", c=128), eng='act')
    p.release(m3)
    mring = p.mark()
    NR = 12
    A4 = [p.sb("rs_A4%d" % i, [128, 4, 128], BF16) for i in range(NR)]
    SQ = [[p.sb("rs_SQ%d_%d" % (i, j), [128, 2, 128], BF16) for j in range(2)] for i in range(NR)]
    XXr = [p.sb("rs_XX%d" % i, [128, 2, 128], BF16) for i in range(NR)]
    Q0T = [p.sb("rs_Q0T%d" % i, [128, 128], BF16) for i in range(NR)]
    QM = [p.sb("rs_QM%d" % i, [128, 3, 128], BF16) for i in range(NR)]
    QMT = [p.sb("rs_QMT%d" % i, [128, 3, 128], BF16) for i in range(NR)]
    Y1r = [p.sb("rs_Y1%d" % i, [128, 128], BF16) for i in range(NR)]
    KB = [p.sb("rs_KB%d" % i, [128, 2, 128], BF16) for i in range(6)]
    Wb = [p.sb("rs_Wb%d" % i, [128, 64], BF16) for i in range(NR)]
    Ub = [p.sb("rs_Ub%d" % i, [128, 64], BF16) for i in range(NR)]
    H = [[p.sb("rs_H%d%d" % (z, hh), [128, 64], F32) for hh in range(2)] for z in range(2)]
    Hb = [[p.sb("rs_Hb%d%d" % (z, hh), [128, 64], BF16) for hh in range(2)] for z in range(2)]
    for z in range(2):
        for hh in range(2):
            p.memset(H[z][hh], 0.0, eng='pool')
            p.memset(Hb[z][hh], 0.0, eng='pool')
    units = [(step, z, hh) for step in range(NT) for z in range(2) for hh in range(2)]
    prep = {}

    def stage_a(u, bk, r):
        step, z, hh = units[u]
        tt = ORD[z][step]
        ts_ = slice(tt * 128, (tt + 1) * 128)
        pr = slice(hh * 64, (hh + 1) * 64)
        kb = KB[(u // 2) % 6]
        if hh == 0:
            kps, hk = p.palloc(1, bk)
            psb = kps.bc(BF16)
            p.tr(psb[:, 0:128], kT4[z][:, ts_], self.identb)
            p.tr(psb[:, 128:256], nbT4[z][:, ts_], self.identb)
        kr = KR[z][pr, tt].rr("p j t -> p (j t)")
        sc1, hsc1 = p.palloc(2, bk)
        p.mm(sc1[:, 0:256], bhat[z][pr, ts_], kr)
        q0, hq0 = p.palloc(1, bk)
        p.mm(q0[:, 0:128], KR[z][pr, tt, 0, :], bhat[z][pr, ts_])
        yield
        if hh == 0:
            p.copy(kb, psb[:, 0:256].rr("p (j c) -> p j c", j=2), eng='act')
            p.pfree(hk)
        p.tt(A4[r][:, 0:2, :], sc1.rr("p (j t) -> p j t", j=2), self.M4[z][:, 0:2, :], ALU.mult)
        p.pfree(hsc1)
        p.tt(Q0T[r], q0[:, 0:128], self.nstrictT[z], ALU.mult)
        p.pfree(hq0)
        sc2, hsc2 = p.palloc(2, bk)
        p.mm(sc2[:, 0:256], khat[z][pr, ts_], kr)
        yield
        p.tt(A4[r][:, 2:4, :], sc2.rr("p (j t) -> p j t", j=2), self.M4[z][:, 2:4, :], ALU.mult)
        p.pfree(hsc2)
        p.tt(QM[r], A4[r][:, 0:1, :].bcast([128, 3, 128]), self.masks3, ALU.mult, eng='pool')
        p.tt(QMT[r], Q0T[r].rr("p (o t) -> p o t", o=1).bcast([128, 3, 128]), self.masks3, ALU.mult, eng='pool')
        XX = XXr[r]
        XXf = XX.rr("p j t -> p (j t)")
        p.tt(XX[:, 0, :], QM[r][:, 0, :], self.identb, ALU.add, eng='pool')
        p.tt(XX[:, 1, :], QMT[r][:, 0, :], self.identb, ALU.add, eng='pool')
        yield
        cq, ct = QM[r][:, 0, :], QMT[r][:, 0, :]
        xq, xt = XX[:, 0, :], XX[:, 1, :]
        ps, h1 = p.palloc(2, bk)
        p.mm(ps[:, 0:128], ct, cq)
        p.mm(ps[:, 128:256], cq, ct)
        yield
        nxt = SQ[r][0]
        p.copy(nxt.rr("p j t -> p (j t)"), ps[:, 0:256], eng='act')
        p.pfree(h1)
        cq, ct = nxt[:, 0, :], nxt[:, 1, :]
        yield
        for lev in range(1, 5):
            ps2, h2 = p.palloc(2, bk)
            p.mm(ps2[:, 0:128], ct, xq)
            p.mm(ps2[:, 128:256], xq, ct)
            if lev < 4:
                ps, h1 = p.palloc(2, bk)
                p.mm(ps[:, 0:128], ct, cq)
                p.mm(ps[:, 128:256], cq, ct)
            yield
            p.tt(XXf, XXf, ps2[:, 0:256], ALU.add)
            p.pfree(h2)
            if lev < 4:
                nxt = SQ[r][lev % 2]
                p.copy(nxt.rr("p j t -> p (j t)"), ps[:, 0:256], eng='act')
                p.pfree(h1)
                cq, ct = nxt[:, 0, :], nxt[:, 1, :]
            yield
        for lvl in (1, 2):
            C, CT = QM[r][:, lvl, :], QMT[r][:, lvl, :]
            ps, h1 = p.palloc(1, bk)
            p.mm(ps[:, 0:128], CT, xq)
            yield
            p.copy(Y1r[r], ps[:, 0:128], eng='act')
            p.pfree(h1)
            yield
            ps2, h2 = p.palloc(2, bk)
            p.mm(ps2[:, 0:128], xt, Y1r[r])
            if lvl == 1:
                p.mm(ps2[:, 128:256], Y1r[r], xt)
            yield
            if lvl == 1:
                p.tt(XXf, XXf, ps2[:, 0:256], ALU.add)
            else:
                p.tt(XX[:, 0, :], XX[:, 0, :], ps2[:, 0:128], ALU.add)
            p.pfree(h2)
            yield
        prep[u] = (r, kb, XX[:, 0, :])

    def stage_b(u, bk, bq):
        step, z, hh = units[u]
        tt = ORD[z][step]
        ts_ = slice(tt * 128, (tt + 1) * 128)
        pr = slice(hh * 64, (hh + 1) * 64)
        cs = slice(hh * 64, (hh + 1) * 64)
        r, kb, xfin = prep.pop(u)
        a4 = A4[r]
        hb = Hb[z][hh]
        vt = Vtok[:, tt, cs]
        while True:
            try:
                w, hw = p.palloc(1, bk)
                break
            except RuntimeError:
                yield
        p.mm(w[:, 0:64], KR[z][pr, tt, 0, :], hb[pr, :], start=True, stop=False)
        p.mm(w[:, 0:64], a4[:, 2, :], vt, start=False, stop=True)
        yield
        p.copy(Wb[r], w[:, 0:64], eng='act')
        p.pfree(hw)
        yield
        while True:
            try:
                uu, hu_ = p.palloc(1, bk)
                break
            except RuntimeError:
                yield
        p.mm(uu[:, 0:64], xfin, Wb[r])
        yield
        p.copy(Ub[r], uu[:, 0:64], eng='act')
        p.pfree(hu_)
        yield
        while True:
            try:
                yb, hy = p.palloc(1, bk)
                break
            except RuntimeError:
                yield
        p.mm(yb[:, 0:64], KR[z][pr, tt, 1, :], hb[pr, :], start=True, stop=False)
        p.mm(yb[:, 0:64], a4[:, 3, :], vt, start=False, stop=False)
        p.mm(yb[:, 0:64], a4[:, 1, :], Ub[r], start=False, stop=True)
        if step < NT - 1:
            while True:
                try:
                    hu, hh_ = p.palloc(1, bk)
                    break
                except RuntimeError:
                    yield
            p.mm(hu[pr, 0:64], kb[:, 0, cs], vt, start=True, stop=False)
            p.mm(hu[pr, 0:64], kb[:, 1, cs], Ub[r], start=False, stop=True)
        yield
        if step < NT - 1:
            p.stt(H[z][hh][pr, :], H[z][hh][pr, :], gam[z][pr, tt:tt + 1], hu[pr, 0:64], ALU.mult, ALU.add)
            p.pfree(hh_)
            p.copy(hb[pr, :], H[z][hh][pr, :], eng='pool')
        p.copy(yz[z][:, tt, cs].on(ybufs[tt][hh]), yb[:, 0:64], eng='act')
        p.pfree(hy)
        yield

    NA = int(os.environ.get('RW_NA', '6'))
    BFIRST = int(os.environ.get('RW_BFIRST', '0'))
    nU = len(units)
    free_r = list(range(NR))
    a_banks = list(range(NA))
    NBB = 8 - NA
    act_a = []
    act_b = []
    a_done = set()
    b_emitted = set()
    next_a = 0
    next_b = [0, 1, 2, 3]
    rmap = {}
    it_ = 0
    last_start = -100
    STAG = int(os.environ.get('RW_STAG', '4'))
    BPRIO = int(os.environ.get('RW_BPRIO', '1'))
    LOOKA = int(os.environ.get('RW_LOOK', '10'))
    while len(b_emitted) < nU:
        it_ += 1
        while next_a < nU and a_banks and free_r and next_a < min(next_b) + LOOKA and it_ - last_start >= STAG:
            last_start = it_
            bk = a_banks.pop(0)
            r = free_r.pop(0)
            rmap[next_a] = r
            act_a.append((stage_a(next_a, bk, r), next_a, bk))
            next_a += 1
        for j in range(4):
            u = next_b[j]
            if u < nU and u in a_done and not any(x[2] == j for x in act_b):
                act_b.append((stage_b(u, NA + (j % NBB), None), u, j))
                next_b[j] = u + 4
        def adv_a():
            nonlocal act_a
            nxt_a = []
            for g, u, bk in act_a:
                try:
                    next(g)
                    nxt_a.append((g, u, bk))
                except StopIteration:
                    a_done.add(u)
                    a_banks.append(bk)
            act_a = nxt_a

        def adv_b():
            nonlocal act_b
            for _rep in range(BPRIO):
                nxt_b = []
                for g, u, j in act_b:
                    try:
                        next(g)
                        nxt_b.append((g, u, j))
                    except StopIteration:
                        b_emitted.add(u)
                        free_r.append(rmap.pop(u))
                act_b = nxt_b

        if BFIRST:
            adv_b()
            adv_a()
        else:
            adv_a()
            adv_b()
    p.release(mring)
    if getattr(self, 'dbg', False):
        for tt in range(NT):
            for hh in range(2):
                p.tt(y[:, tt, hh * 64:(hh + 1) * 64], y[:, tt, hh * 64:(hh + 1) * 64].on(ybufs[tt][hh]), y[:, tt, hh * 64:(hh + 1) * 64].on(ybufs[tt][hh]), ALU.max)
        p.dma(self.dbg_y[cc], y.rr("p t c -> p (t c)"))
    g2b = self.rw_g2all[:, cc * 128:(cc + 1) * 128]
    s1 = p.sb("rp_s1", [128, 36], F32)
    s2 = p.sb("rp_s2", [128, 36], F32)
    sq = p.sb("rp_sq", [128, 36, 64], F32)
    yn = p.sb("rp_yn", [128, NT, 128], BF16)
    lnT = p.sb("rp_lnT", [128, LTOK], BF16)
    prod = p.sb("rp_prod", [128, LTOK], BF16)
    y3 = y.rr("p t (h c) -> p (t h) c", c=64)
    allb = [ybufs[t_][h_] for t_ in range(NT) for h_ in range(2)]
    yall = V(y.ap, allb)
    y1all = V(yz[1].ap, allb)
    p.tt(yall, yall, y1all, ALU.add)
    p.tt(sq, V(y3.ap, allb), V(y3.ap, allb), ALU.mult)
    p.reduce(s2, sq, ALU.add)
    p.reduce(s1, y3, ALU.add)
    p.ts(s1, s1, 1.0 / 64, ALU.mult)
    p.tt(sq[:, :, 0], s1, s1, ALU.mult)
    p.stt(s2, s2, 1.0 / 64, sq[:, :, 0], ALU.mult, ALU.subtract)
    p.act(s2, s2, AF.Sqrt, bias=RW_LN_EPS)
    p.recip(s2, s2)
    yn3 = yn.rr("p t (h c) -> p (t h) c", c=64)
    p.tt(sq, y3, s1.rr("p (n o) -> p n o", o=1).bcast([128, 36, 64]), ALU.subtract)
    p.tt(yn3, sq, s2.rr("p (n o) -> p n o", o=1).bcast([128, 36, 64]), ALU.mult)
    lnx = self.col('lnx')
    for g0 in range(0, NT, 8):
        n = min(8, NT - g0)
        ps = self.bank()
        psb = ps.bc(BF16)
        for i in range(n):
            p.tr(psb[:, i * 128:(i + 1) * 128], yn[:, g0 + i, :], self.identb)
        p.ts(lnT[:, g0 * 128:(g0 + n) * 128], psb[:, 0:n * 128], lnx[:, cc:cc + 1], ALU.mult, lnx[:, 8 + cc:8 + cc + 1], ALU.add)
    p.stt(prod, rT, kv[:, 16 + cc:16 + cc + 1], kT, ALU.mult, ALU.mult)
    stage = [p.sb("rp_stage%d" % i, [128, 512], BF16) for i in range(2)]
    tmpb = [p.sb("rp_tmp%d" % i, [128, 512], F32) for i in range(2)]
    for i, (t0, n) in enumerate(BLKS):
        psA = self.bank()
        p.mm(psA[:, 0:n], self.bones, prod[:, t0:t0 + n])
        psG = self.bank()
        p.mm(psG[:, 0:n], g2b, tg[:, t0:t0 + n])
        tb = tmpb[i % 2]
        p.tt(tb[:, 0:n], psA[:, 0:n], vT[:, t0:t0 + n], ALU.mult)
        p.tt(tb[:, 0:n], tb[:, 0:n], lnT[:, t0:t0 + n], ALU.add, eng='pool')
        sg = stage[i % 2]
        p.tt(sg[:, 0:n], psG[:, 0:n], tb[:, 0:n], ALU.mult)
        p.dma(scr['o'][cc, :, t0:t0 + n].on(scr['obuf'][cc]), sg[:, 0:n])
    p.release(m)


def _rw_mixer(self, b, xi, xo, last=True):
    p = self.p
    l = 1
    m = p.mark()
    self.rw_rmask = p.sb("rw_rmask", [128, LTOK], BF16)
    p.memset(self.rw_rmask, 1.0, eng='pool')
    p.memset(self.rw_rmask.rr("p (n t) -> p n t", t=128)[:, :, 0:1], 0.0, eng='pool')
    self.rw_w2all = p.sb("rw_w2all", [64, 2, D], BF16)
    p.dma(self.rw_w2all, self.rw_w2.rr("z r c -> r z c"), eng='pool')
    self.rw_a2all = p.sb("rw_a2all", [64, D], BF16)
    p.dma(self.rw_a2all, self.rw_a2, eng='pool')
    self.rw_g2all = p.sb("rw_g2all", [128, D], BF16)
    p.dma(self.rw_g2all, self.rw_g2, eng='pool')
    tw = [p.sb("rw_tw%d" % z, [128, LTOK], BF16) for z in range(2)]
    ta = p.sb("rw_ta", [128, LTOK], BF16)
    tg = p.sb("rw_tg", [128, LTOK], BF16)
    mh = p.mark()
    hT = p.sb("rw_hT", [128, 8, LTOK], BF16)
    self.prenorm(b, xi, l, 0, hT)
    self.rw_phase1(b, hT, tw, ta, tg)
    p.release(mh)
    for cc in range(8):
        self.rw_chunk(b, cc, tw, ta, tg, last)
    p.release(mh)
    wo = p.sb("rw_wo", [128, 8, D], BF16)
    for c in range(8):
        p.dma(wo[:, c, :], self.w_o[c * 128:(c + 1) * 128, :], eng='pool')
    oT = p.sb("rw_oT", [128, 8, LTOK], BF16)
    for c in range(8):
        p.dma(oT[:, c, :], self.scr[b]['o'][c].on(self.scr[b]['obuf'][c]))
    G = [p.sb("rwo_G%d" % i, [128, D], F32) for i in range(2)]
    gtmp = p.sb("rwo_gtmp", [128, 128], F32)
    st = self.res_state(2)
    self.gate_row(G[0], l, 2, b, gtmp)
    if not last:
        self.gate_row(G[1], l, 2, 4, gtmp)
    for tt in (range(2, NT) if last else range(NT)):
        yb = [self.bank(), self.bank()]
        for h in range(2):
            for c in range(8):
                p.mm(yb[h], oT[:, c, tt * 128:(tt + 1) * 128], wo[:, c, h * 512:(h + 1) * 512], start=(c == 0), stop=(c == 7))
        self.residual_tile(b, tt, xi, xo, yb, G[1] if tt < 2 else G[0], st[tt % 2])
    p.release(m)


KE.rw_chunk = _rw_chunk
KE.rw_mixer = _rw_mixer


_NC_CACHE = {}
W_NAMES = ('w_mod', 'ffn_w_up', 'ffn_w_down')
EV_NAMES = ('ev_w_in', 'ev_w_out', 'ev_gla_w2')
RW_NAMES = ('rw_w_rkv', 'rw_w_o', 'rw_w1', 'rw_w2', 'rw_a1', 'rw_a2', 'rw_g1', 'rw_g2')


def build_program(NB, pk):
    k = KE(NB, pk.cols, pk.n, dbg=False)
    k.setup_even()
    k.setup_rw()
    k.mod_stage(0)
    for b in range(NB):
        hT, mixT, m = k.even_mixer(b, 0, 1)
        k.gla(b, hT, mixT)
        k.even_out(b, 0, 1, hT, mixT, m)
        k.ffn(b, 0, 1, 2, do_ctx=True)
    k.setup_rw_consts()
    k.mod_stage(1)
    for b in range(NB):
        k.rw_mixer(b, 2, 3, last=True)
        k.ffn(b, 1, 3, 4, do_ctx=False)
    return k.p.finish()


def kernel(**inp):
    inp = {k_: np.asarray(v_, dtype=np.float32) for k_, v_ in inp.items()}
    NCORE = 8
    B = inp['x'].shape[0]
    NB = B // NCORE
    pk = host_small(inp)
    host_small_even(pk, inp)
    host_small_rw(pk, inp)
    small = pk.pack()
    rows = np.zeros((16, D), np.float32)
    rows[0:8] = inp['norm_g'].reshape(8, D)
    rows[8, :16] = inp['ev_b_gates'][0]
    nc = build_program(NB, pk)
    shared = {"small": small, "rows": rows}
    for nm in W_NAMES:
        shared[nm] = np.ascontiguousarray(inp[nm])
    for nm in EV_NAMES + RW_NAMES:
        shared[nm] = np.ascontiguousarray(inp[nm][0])
    in_maps = []
    for c in range(NCORE):
        sl = slice(c * NB, (c + 1) * NB)
        cc = np.zeros((6, D), np.float32)
        cc[0:NB] = inp['c'][sl]
        cc[4] = inp['c_ctx']
        ccT = np.ascontiguousarray(cc.reshape(6, 8, 128).transpose(2, 1, 0).reshape(128, 48))
        xcat = np.ascontiguousarray(np.concatenate([inp['ctx'][sl], inp['x'][sl]], axis=1))
        d = dict(shared)
        d["xcat"] = xcat
        d["ccT"] = ccT
        in_maps.append(d)
    res = run_bass_kernel_spmd(nc, in_maps, core_ids=list(range(NCORE)))
    out = np.concatenate([np.asarray(r["out"]) for r in res.results], axis=0)
    return out.astype(np.float32)
```
